# Optimizing a Trainium2 kernel written in Bass

```python
import math
import jax, jax.numpy as jnp
from jax import lax
import numpy as np

D_MODEL = 1024
BATCH = 8
SEQ = 2048
DEPTH = 2
DEC_BATCH = 128
DEC_SEQ = 8
PAST_LEN = 16384
PAGE_SIZE = 128

MIX_WIDTH = 2 * D_MODEL
SSD_WIDTH = MIX_WIDTH // 2
SSD_HEAD_DIM = 64
SSD_HEADS = SSD_WIDTH // SSD_HEAD_DIM
SSD_GROUPS = 2
SSD_HPG = SSD_HEADS // SSD_GROUPS
SSD_STATE = 128
SSD_CONV_DIM = SSD_WIDTH + 2 * SSD_GROUPS * SSD_STATE
ML_WIDTH = MIX_WIDTH // 4
ML_HEADS = 4
ML_HEAD_DIM = ML_WIDTH // ML_HEADS
LRU_WIDTH = MIX_WIDTH // 4
LRU_BLOCKS = 4
LRU_BLOCK_DIM = LRU_WIDTH // LRU_BLOCKS
LRU_C = 8.0
CONV_W = 4
CHUNK = 128
D_FF = 4 * D_MODEL
EPS = 1e-6
IN_SEGMENTS = (SSD_WIDTH, SSD_CONV_DIM, SSD_HEADS, ML_WIDTH, ML_WIDTH, ML_WIDTH, ML_WIDTH, ML_HEADS, ML_HEADS, LRU_WIDTH, LRU_WIDTH)
IN_DIM = sum(IN_SEGMENTS)

kernel_name = 'hymba_ssd_mlstm_rglru_decoder_step'


def _split_points():
    pts, acc = [], 0
    for s in IN_SEGMENTS[:-1]:
        acc += s
        pts.append(acc)
    return pts


def rmsnorm(x, w):
    xf = x.astype(jnp.float32)
    return xf * lax.rsqrt(jnp.mean(xf * xf, axis=-1, keepdims=True) + EPS) * w.astype(jnp.float32)


def causal_conv(x, buf, w, b):
    T = x.shape[1]
    xpad = jnp.concatenate([buf.astype(jnp.float32), x], axis=1)
    w = w.astype(jnp.float32)
    out = b.astype(jnp.float32) + xpad[:, 0:T] * w[0]
    for j in range(1, CONV_W):
        out = out + xpad[:, j:j + T] * w[j]
    return out, xpad[:, xpad.shape[1] - (CONV_W - 1):]


def causal_mask(L):
    i = jnp.arange(L)
    return i[:, None] >= i[None, :]


def to_chunks(a, L):
    return a.reshape((a.shape[0], a.shape[1] // L, L) + a.shape[2:]).swapaxes(0, 1)


def from_chunks(a):
    a = a.swapaxes(0, 1)
    return a.reshape((a.shape[0], a.shape[1] * a.shape[2]) + a.shape[3:])


def ssd_chunked(x, dt, a, bm, cm, s0):
    L = math.gcd(x.shape[1], CHUNK)
    mask = causal_mask(L)[None, :, :, None, None]

    def step(s, inp):
        xc, dtc, bc, cc = inp
        cum = jnp.cumsum(dtc * a, axis=1)
        decay = jnp.exp(jnp.where(mask, cum[:, :, None] - cum[:, None, :], -jnp.inf))
        att = jnp.einsum('btgn,bsgn->btsg', cc, bc)[..., None] * decay
        xdt = xc * dtc[..., None]
        y = (jnp.einsum('btsge,bsgep->btgep', att, xdt)
             + jnp.einsum('btgn,bgepn->btgep', cc, s) * jnp.exp(cum)[..., None])
        xend = xdt * jnp.exp(cum[:, -1:] - cum)[..., None]
        s = jnp.exp(cum[:, -1])[..., None, None] * s + jnp.einsum('bsgn,bsgep->bgepn', bc, xend)
        return s, y

    s, ys = lax.scan(step, s0, tuple(to_chunks(t, L) for t in (x, dt, bm, cm)))
    return from_chunks(ys), s


def mlstm_chunked(q, k, v, ig, lf, c0, n0, m0):
    L = math.gcd(q.shape[1], CHUNK)
    mask = causal_mask(L)[None, :, :, None]

    def step(carry, inp):
        c, n, m = carry
        qc, kc, vc, ic, fc = inp
        bc = jnp.cumsum(fc, axis=1)
        dmat = jnp.where(mask, bc[:, :, None] - bc[:, None, :] + ic[:, None, :], -jnp.inf)
        g = bc + m[:, None]
        mt = jnp.maximum(g, jnp.max(dmat, axis=2))
        w = jnp.exp(dmat - mt[:, :, None]) * jnp.einsum('bthd,bshd->btsh', qc, kc)
        inter = jnp.exp(g - mt)
        num = jnp.einsum('btsh,bshd->bthd', w, vc) + inter[..., None] * jnp.einsum('bhvd,bthd->bthv', c, qc)
        den = jnp.sum(w, axis=2) + inter * jnp.einsum('bhd,bthd->bth', n, qc)
        hout = num / jnp.maximum(jnp.abs(den), jnp.exp(-mt))[..., None]
        m_new = mt[:, -1]
        w_end = jnp.exp(bc[:, -1:] - bc + ic - m_new[:, None])
        dc = jnp.exp(bc[:, -1] + m - m_new)
        c = dc[..., None, None] * c + jnp.einsum('bshv,bshd->bhvd', vc * w_end[..., None], kc)
        n = dc[..., None] * n + jnp.einsum('bsh,bshd->bhd', w_end, kc)
        return (c, n, m_new), hout

    (c, n, m), hs = lax.scan(step, (c0, n0, m0), tuple(to_chunks(t, L) for t in (q, k, v, ig, lf)))
    return from_chunks(hs), c, n, m


def block_diag(x, w):
    b, T = x.shape[0], x.shape[1]
    xb = x.reshape(b, T, LRU_BLOCKS, LRU_BLOCK_DIM)
    return jnp.einsum('btkc,kcd->btkd', xb, w.astype(jnp.float32)).reshape(b, T, LRU_WIDTH)


def lru_scan(a, u, h0):
    u = u.at[:, 0].add(a[:, 0] * h0)

    def combine(left, right):
        return left[0] * right[0], right[0] * left[1] + right[1]

    _, hs = lax.associative_scan(combine, (a, u), axis=1)
    return hs, hs[:, -1]


def mixer(h, l, P, st, start_pos):
    f32 = jnp.float32
    b, T = h.shape[0], h.shape[1]
    ssm0, sconv0, mc0, mn0, mm0, lh0, lconv0 = (s.astype(f32) for s in st)
    u = jnp.matmul(h, P['w_in'][l]).astype(f32)
    z, xbc, dt_raw, q, k, v, o, ig, fg, xr, gr = jnp.split(u, _split_points(), axis=-1)

    xbc, sconv = causal_conv(xbc, sconv0, P['ssd_conv_w'][l], P['ssd_conv_b'][l])
    xbc = jax.nn.silu(xbc)
    xs, bm, cm = jnp.split(xbc, [SSD_WIDTH, SSD_WIDTH + SSD_GROUPS * SSD_STATE], axis=-1)
    dt = jax.nn.softplus(dt_raw + P['ssd_dt_bias'][l].astype(f32))
    a = -jnp.exp(P['ssd_a_log'][l].astype(f32))
    y, ssm = ssd_chunked(
        xs.reshape(b, T, SSD_GROUPS, SSD_HPG, SSD_HEAD_DIM),
        dt.reshape(b, T, SSD_GROUPS, SSD_HPG),
        a.reshape(SSD_GROUPS, SSD_HPG),
        bm.reshape(b, T, SSD_GROUPS, SSD_STATE),
        cm.reshape(b, T, SSD_GROUPS, SSD_STATE),
        ssm0.reshape(b, SSD_GROUPS, SSD_HPG, SSD_HEAD_DIM, SSD_STATE))
    y = y.reshape(b, T, SSD_WIDTH) + xs * jnp.repeat(P['ssd_d'][l].astype(f32), SSD_HEAD_DIM)
    y_ssd = rmsnorm(y * jax.nn.silu(z), P['ssd_norm_w'][l])
    ssm = ssm.reshape(b, SSD_HEADS, SSD_HEAD_DIM, SSD_STATE)

    hd = (b, T, ML_HEADS, ML_HEAD_DIM)
    hm, mc, mn, mm = mlstm_chunked(
        q.reshape(hd), k.reshape(hd) * (ML_HEAD_DIM ** -0.5), v.reshape(hd),
        ig + P['ml_i_bias'][l].astype(f32),
        jax.nn.log_sigmoid(fg + P['ml_f_bias'][l].astype(f32)),
        mc0, mn0, mm0)
    y_ml = rmsnorm(hm, P['ml_norm_w'][l].reshape(ML_HEADS, ML_HEAD_DIM)).reshape(b, T, ML_WIDTH) * jax.nn.sigmoid(o)

    xr, lconv = causal_conv(xr, lconv0, P['lru_conv_w'][l], P['lru_conv_b'][l])
    r = jax.nn.sigmoid(block_diag(xr, P['lru_wa'][l]) + P['lru_ba'][l].astype(f32))
    i = jax.nn.sigmoid(block_diag(xr, P['lru_wx'][l]) + P['lru_bx'][l].astype(f32))
    log_a = -LRU_C * r * jax.nn.softplus(-P['lru_lambda'][l].astype(f32))
    pos = start_pos + jnp.arange(T)
    mult = jnp.where((pos == 0)[None, :, None], 1.0, jnp.sqrt(-jnp.expm1(2.0 * log_a)))
    hr, lh = lru_scan(jnp.exp(log_a), mult * i * xr, lh0)
    y_lru = hr * jax.nn.gelu(gr)

    ycat = jnp.concatenate([y_ssd, y_ml, y_lru], axis=-1).astype(h.dtype)
    out = jnp.matmul(ycat, P['w_out'][l])
    return out, (ssm, sconv, mc, mn, mm, lh, lconv)


def zero_states(b):
    z = lambda s: jnp.zeros(s, jnp.float32)
    return (z((DEPTH, b, SSD_HEADS, SSD_HEAD_DIM, SSD_STATE)),
            z((DEPTH, b, CONV_W - 1, SSD_CONV_DIM)),
            z((DEPTH, b, ML_HEADS, ML_HEAD_DIM, ML_HEAD_DIM)),
            z((DEPTH, b, ML_HEADS, ML_HEAD_DIM)),
            z((DEPTH, b, ML_HEADS)),
            z((DEPTH, b, LRU_WIDTH)),
            z((DEPTH, b, CONV_W - 1, LRU_WIDTH)))


def trunk(x, c, states, P, start_pos):
    f32 = jnp.float32
    new = [[] for _ in states]
    cs = jax.nn.silu(c.astype(f32))
    for l in range(DEPTH):
        mod = jnp.matmul(cs, P['ada_w'][l].astype(f32)) + P['ada_b'][l].astype(f32)
        sh1, sc1, g1, sh2, sc2, g2 = (m[:, None, :] for m in jnp.split(mod, 6, axis=-1))
        hn = (rmsnorm(x, P['norm1_w'][l]) * (1.0 + sc1) + sh1).astype(x.dtype)
        mix, st_new = mixer(hn, l, P, tuple(s[l] for s in states), start_pos)
        x = (x.astype(f32) + g1 * mix.astype(f32)).astype(x.dtype)
        hn = (rmsnorm(x, P['norm2_w'][l]) * (1.0 + sc2) + sh2).astype(x.dtype)
        ff = jnp.matmul(jnp.square(jax.nn.relu(jnp.matmul(hn, P['mlp_up'][l]))), P['mlp_down'][l])
        x = (x.astype(f32) + g2 * ff.astype(f32)).astype(x.dtype)
        for lst, s in zip(new, st_new):
            lst.append(s)
    y = rmsnorm(x, P['final_norm_w']).astype(x.dtype)
    return y, tuple(jnp.stack(lst) for lst in new)


def setup_inputs(seed: int = 0) -> dict:
    key = jax.random.key(seed)
    ks = iter(jax.random.split(key, 48))
    f32 = jnp.float32

    def nrm(shape, scale):
        return scale * jax.random.normal(next(ks), shape, f32)

    def unif(shape, lo, hi):
        return jax.random.uniform(next(ks), shape, f32, lo, hi)

    dt0 = jnp.exp(unif((DEPTH, SSD_HEADS), math.log(1e-3), math.log(1e-1)))
    lam_a = unif((DEPTH, LRU_WIDTH), 0.9, 0.999)
    return {
        'x_prompt': nrm((BATCH, SEQ, D_MODEL), 1.0),
        'x_sample': nrm((DEC_BATCH, DEC_SEQ, D_MODEL), 1.0),
        'c_prompt': nrm((BATCH, D_MODEL), 1.0),
        'c_sample': nrm((DEC_BATCH, D_MODEL), 1.0),
        'state_ssm': nrm((DEPTH, DEC_BATCH, SSD_HEADS, SSD_HEAD_DIM, SSD_STATE), 0.1),
        'state_ssd_conv': nrm((DEPTH, DEC_BATCH, CONV_W - 1, SSD_CONV_DIM), 1.0),
        'state_mlstm_c': nrm((DEPTH, DEC_BATCH, ML_HEADS, ML_HEAD_DIM, ML_HEAD_DIM), 0.1),
        'state_mlstm_n': nrm((DEPTH, DEC_BATCH, ML_HEADS, ML_HEAD_DIM), 0.1),
        'state_mlstm_m': nrm((DEPTH, DEC_BATCH, ML_HEADS), 1.0),
        'state_lru_h': nrm((DEPTH, DEC_BATCH, LRU_WIDTH), 0.5),
        'state_lru_conv': nrm((DEPTH, DEC_BATCH, CONV_W - 1, LRU_WIDTH), 1.0),
        'ada_w': nrm((DEPTH, D_MODEL, 6 * D_MODEL), D_MODEL ** -0.5),
        'ada_b': nrm((DEPTH, 6 * D_MODEL), 0.01),
        'norm1_w': 1.0 + nrm((DEPTH, D_MODEL), 0.01),
        'norm2_w': 1.0 + nrm((DEPTH, D_MODEL), 0.01),
        'w_in': nrm((DEPTH, D_MODEL, IN_DIM), D_MODEL ** -0.5),
        'ssd_conv_w': nrm((DEPTH, CONV_W, SSD_CONV_DIM), CONV_W ** -0.5),
        'ssd_conv_b': nrm((DEPTH, SSD_CONV_DIM), 0.01),
        'ssd_dt_bias': dt0 + jnp.log(-jnp.expm1(-dt0)),
        'ssd_a_log': jnp.log(unif((DEPTH, SSD_HEADS), 1.0, 16.0)),
        'ssd_d': 1.0 + nrm((DEPTH, SSD_HEADS), 0.1),
        'ssd_norm_w': 1.0 + nrm((DEPTH, SSD_WIDTH), 0.01),
        'ml_i_bias': nrm((DEPTH, ML_HEADS), 0.1),
        'ml_f_bias': jnp.linspace(3.0, 6.0, ML_HEADS, dtype=f32)[None, :] + nrm((DEPTH, ML_HEADS), 0.1),
        'ml_norm_w': 1.0 + nrm((DEPTH, ML_WIDTH), 0.01),
        'lru_conv_w': nrm((DEPTH, CONV_W, LRU_WIDTH), CONV_W ** -0.5),
        'lru_conv_b': nrm((DEPTH, LRU_WIDTH), 0.01),
        'lru_wa': nrm((DEPTH, LRU_BLOCKS, LRU_BLOCK_DIM, LRU_BLOCK_DIM), LRU_BLOCK_DIM ** -0.5),
        'lru_ba': nrm((DEPTH, LRU_WIDTH), 0.01),
        'lru_wx': nrm((DEPTH, LRU_BLOCKS, LRU_BLOCK_DIM, LRU_BLOCK_DIM), LRU_BLOCK_DIM ** -0.5),
        'lru_bx': nrm((DEPTH, LRU_WIDTH), 0.01),
        'lru_lambda': jnp.log(lam_a) - jnp.log1p(-lam_a),
        'w_out': nrm((DEPTH, MIX_WIDTH, D_MODEL), MIX_WIDTH ** -0.5),
        'mlp_up': nrm((DEPTH, D_MODEL, D_FF), D_MODEL ** -0.5),
        'mlp_down': nrm((DEPTH, D_FF, D_MODEL), D_FF ** -0.5),
        'final_norm_w': 1.0 + nrm((D_MODEL,), 0.01),
    }


def reference(x_prompt, x_sample, c_prompt, c_sample,
              state_ssm, state_ssd_conv, state_mlstm_c, state_mlstm_n, state_mlstm_m,
              state_lru_h, state_lru_conv,
              ada_w, ada_b, norm1_w, norm2_w, w_in,
              ssd_conv_w, ssd_conv_b, ssd_dt_bias, ssd_a_log, ssd_d, ssd_norm_w,
              ml_i_bias, ml_f_bias, ml_norm_w,
              lru_conv_w, lru_conv_b, lru_wa, lru_ba, lru_wx, lru_bx, lru_lambda,
              w_out, mlp_up, mlp_down, final_norm_w):
    P = dict(ada_w=ada_w, ada_b=ada_b, norm1_w=norm1_w, norm2_w=norm2_w, w_in=w_in,
             ssd_conv_w=ssd_conv_w, ssd_conv_b=ssd_conv_b, ssd_dt_bias=ssd_dt_bias,
             ssd_a_log=ssd_a_log, ssd_d=ssd_d, ssd_norm_w=ssd_norm_w,
             ml_i_bias=ml_i_bias, ml_f_bias=ml_f_bias, ml_norm_w=ml_norm_w,
             lru_conv_w=lru_conv_w, lru_conv_b=lru_conv_b, lru_wa=lru_wa, lru_ba=lru_ba,
             lru_wx=lru_wx, lru_bx=lru_bx, lru_lambda=lru_lambda,
             w_out=w_out, mlp_up=mlp_up, mlp_down=mlp_down, final_norm_w=final_norm_w)
    y_prompt, (p_ssm, p_sconv, p_mc, p_mn, p_mm, p_lh, p_lconv) = trunk(
        x_prompt, c_prompt, zero_states(x_prompt.shape[0]), P, 0)
    y_sample, (s_ssm, s_sconv, s_mc, s_mn, s_mm, s_lh, s_lconv) = trunk(
        x_sample, c_sample,
        (state_ssm, state_ssd_conv, state_mlstm_c, state_mlstm_n, state_mlstm_m, state_lru_h, state_lru_conv),
        P, PAST_LEN)
    return (y_prompt, y_sample,
            p_ssm, p_sconv, p_mc, p_mn, p_mm, p_lh, p_lconv,
            s_ssm, s_sconv, s_mc, s_mn, s_mm, s_lh, s_lconv)
```

```python
import numpy as np
import concourse.bass as bass
import concourse.mybir as mybir
from concourse.bass_utils import run_bass_kernel_spmd

F32 = mybir.dt.float32
BF16 = mybir.dt.bfloat16
ALU = mybir.AluOpType
AF = mybir.ActivationFunctionType
AX = mybir.AxisListType

NCORES = 8
D = 1024
TP = 2048
NS = 16
TS = 8
NTOK = TP + NS * TS
DEPTH = 2
IN_DIM = 5656
EPS = 1e-6
NEG = -30000.0


class Buf:
    def __init__(self, ap, name=""):
        self.ap = ap
        self.name = name
        self.last_w = None
        self.readers = []

    def __getitem__(self, key):
        return View(self, self.ap[key])

    def v(self):
        return View(self, self.ap)


class View:
    def __init__(self, buf, ap):
        self.buf = buf
        self.ap = ap

    def __getitem__(self, key):
        return View(self.buf, self.ap[key])

    def rr(self, s, **kw):
        return View(self.buf, self.ap.rearrange(s, **kw))

    def bc(self, shape):
        return View(self.buf, self.ap.to_broadcast(list(shape)))

    def unsq(self, axis):
        return View(self.buf, self.ap.unsqueeze(axis))

    def bitcast(self, dt):
        return View(self.buf, self.ap.bitcast(dt))


class Op:
    __slots__ = ("eng", "meth", "kw", "reads", "writes", "deps", "idx", "is_dma", "inc_val", "key")


WRITE_KEYS = ("out", "accum_out")
ENGS = ("pe", "act", "dve", "pool", "sp")


class Prog:
    def __init__(self, nc):
        self.nc = nc
        self.ops = []
        self.bar = set()
        self.bpoints = []
        self.capture = None
        self.arena = None
        self.aoff = 0
        self.asize = 0

    def barrier(self):
        last = {}
        dm = set()
        for op in self.ops:
            if op.is_dma:
                dm.add(op.idx)
            else:
                last[op.eng] = op.idx
        self.bar = set(last.values()) | dm
        self.aoff = 0
        self.bpoints.append(len(self.ops))

    def ov_top(self, name, shape, dtype):
        n = 1
        for d in shape[1:]:
            n *= d
        nbytes = n * (4 if dtype == F32 else 2)
        n4 = (nbytes + 31) // 32 * 8
        self.atop = self.asize - n4
        ap = self.arena[:, self.atop:self.atop + n4]
        if dtype != F32:
            ap = ap.bitcast(dtype)
        ap = ap[:, 0:n]
        if len(shape) == 3:
            ap = ap.rearrange("p (a b) -> p a b", a=shape[1])
        return Buf(ap, name)

    def ov(self, name, shape, dtype):
        n = 1
        for d in shape[1:]:
            n *= d
        nbytes = n * (4 if dtype == F32 else 2)
        n4 = (nbytes + 31) // 32 * 8
        assert self.aoff + n4 <= getattr(self, "atop", self.asize), (name, self.aoff, n4, self.asize)
        ap = self.arena[:, self.aoff:self.aoff + n4]
        self.aoff += n4
        if dtype != F32:
            ap = ap.bitcast(dtype)
        ap = ap[:, 0:n]
        if len(shape) == 3:
            ap = ap.rearrange("p (a b) -> p a b", a=shape[1])
        return Buf(ap, name)

    def sb(self, name, shape, dtype):
        t = self.nc.alloc_sbuf_tensor(name, list(shape), dtype)
        return Buf(t.ap(), name)

    def ps(self, name, shape, dtype=F32):
        t = self.nc.alloc_psum_tensor(name, list(shape), dtype)
        return Buf(t.ap(), name)

    def do(self, eng, meth, extra_reads=(), extra_writes=(), **kw):
        if self.capture is not None:
            self.capture.append((eng, meth, extra_reads, extra_writes, kw))
            return None
        reads, writes, real = [], [], {}
        for k, v in kw.items():
            if isinstance(v, View):
                (writes if k in WRITE_KEYS else reads).append(v.buf)
                real[k] = v.ap
            elif isinstance(v, Buf):
                (writes if k in WRITE_KEYS else reads).append(v)
                real[k] = v.ap
            else:
                real[k] = v
        for b in extra_reads:
            reads.append(b.buf if isinstance(b, View) else b)
        for b in extra_writes:
            writes.append(b.buf if isinstance(b, View) else b)
        op = Op()
        op.eng, op.meth, op.kw, op.reads, op.writes = eng, meth, real, reads, writes
        op.is_dma = meth in ("dma_start",)
        op.inc_val = None
        op.key = None
        op.idx = len(self.ops)
        deps = set()
        for r in reads:
            if r.last_w is not None:
                deps.add(r.last_w)
        for w in writes:
            if w.last_w is not None:
                deps.add(w.last_w)
            deps.update(w.readers)
        for r in reads:
            r.readers.append(op.idx)
        for w in writes:
            w.last_w = op.idx
            w.readers = []
        deps.discard(op.idx)
        deps |= self.bar
        if eng == "pe":
            deps = {d for d in deps if self.ops[d].eng != "pe"}
        op.deps = deps
        self.ops.append(op)
        return op

    def emit(self):
        nc = self.nc
        ops = self.ops
        has_dep = [False] * len(ops)
        for op in ops:
            for d in op.deps:
                has_dep[d] = True
        eng_sem = {e: nc.alloc_semaphore("sem_" + e) for e in ENGS}
        eng_cnt = {e: 0 for e in ENGS}
        dma_sems = {}
        cur = {}
        swsems = {}
        free = []
        bset = set(self.bpoints)
        for op in ops:
            if op.idx in bset:
                free.extend(cur.values())
                cur = {}
            if op.is_dma:
                key = op.writes[0] if op.writes else op.reads[0]
                if op.eng == "pool":
                    ent = swsems.get(id(key))
                    if ent is None:
                        ent = [nc.alloc_semaphore("swsem%d" % len(swsems)), 0]
                        swsems[id(key)] = ent
                        dma_sems["sw%d" % len(swsems)] = ent
                else:
                    ent = cur.get(id(key))
                    if ent is None:
                        if free:
                            ent = free.pop()
                        else:
                            ent = [nc.alloc_semaphore("dsem%d" % len(dma_sems)), 0]
                            dma_sems[len(dma_sems)] = ent
                        cur[id(key)] = ent
                ent[1] += 16
                op.inc_val = (ent[0], ent[1], 16)
            elif has_dep[op.idx]:
                eng_cnt[op.eng] += 1
                op.inc_val = (eng_sem[op.eng], eng_cnt[op.eng], 1)
        streams = {e: [] for e in ENGS}
        for op in ops:
            streams[op.eng].append(op)
        engmap = {"pe": "tensor", "act": "scalar", "dve": "vector", "pool": "gpsimd", "sp": "sync"}

        def run_stream(ename):
            def body(eng):
                waited = {}
                for op in streams[ename]:
                    need = {}
                    for d in op.deps:
                        s, val, _ = ops[d].inc_val
                        k = id(s)
                        if waited.get(k, 0) >= val:
                            continue
                        if k not in need or need[k][1] < val:
                            need[k] = (s, val)
                    for k, (s, val) in need.items():
                        eng.wait_ge(s, val)
                        waited[k] = val
                    ins = getattr(eng, op.meth)(**op.kw)
                    if op.inc_val is not None:
                        ins.then_inc(op.inc_val[0], op.inc_val[2])
                if ename == "sp":
                    for key, (s, tot) in dma_sems.items():
                        eng.wait_ge(s, tot)
            return body

        with nc.Block() as block:
            for e in ENGS:
                getattr(block, engmap[e])(run_stream(e))


def build_program():
    nc = bass.Bass("TRN2", target_bir_lowering=False)
    P = Prog(nc)

    def din(name, shape):
        return nc.dram_tensor(name, list(shape), F32, kind="ExternalInput").ap()

    def dout(name, shape):
        return nc.dram_tensor(name, list(shape), F32, kind="ExternalOutput").ap()

    xp = din("xp", [TP, D])
    xs = din("xs", [NS * TS, D])
    c17 = din("c17", [1 + NS, D])
    st_ssm = din("st_ssm", [DEPTH, NS, 16, 64, 128])
    st_sconv = din("st_sconv", [DEPTH, NS, 3, 1536])
    st_mc = din("st_mc", [DEPTH, NS, 4, 128, 128])
    st_mn = din("st_mn", [DEPTH, NS, 4, 128])
    st_mm = din("st_mm", [DEPTH, NS, 4])
    st_lh = din("st_lh", [DEPTH, NS, 512])
    st_lconv = din("st_lconv", [DEPTH, NS, 3, 512])
    ada_w = din("ada_w", [DEPTH, D, 6 * D])
    ada_b = din("ada_b", [DEPTH, 6 * D])
    norm1_w = din("norm1_w", [DEPTH, D])
    norm2_w = din("norm2_w", [DEPTH, D])
    w_in = din("w_in", [DEPTH, D, IN_DIM])
    ssd_conv_w = din("ssd_conv_w", [DEPTH, 4, 1536])
    ssd_conv_b = din("ssd_conv_b", [DEPTH, 1536])
    ssd_dt_bias = din("ssd_dt_bias", [DEPTH, 16])
    ssd_a_log = din("ssd_a_log", [DEPTH, 16])
    ssd_d = din("ssd_d", [DEPTH, 16])
    ssd_norm_w = din("ssd_norm_w", [DEPTH, 1024])
    ml_i_bias = din("ml_i_bias", [DEPTH, 4])
    ml_f_bias = din("ml_f_bias", [DEPTH, 4])
    ml_norm_w = din("ml_norm_w", [DEPTH, 512])
    lru_conv_w = din("lru_conv_w", [DEPTH, 4, 512])
    lru_conv_b = din("lru_conv_b", [DEPTH, 512])
    lru_wa = din("lru_wa", [DEPTH, 4, 128, 128])
    lru_ba = din("lru_ba", [DEPTH, 512])
    lru_wx = din("lru_wx", [DEPTH, 4, 128, 128])
    lru_bx = din("lru_bx", [DEPTH, 512])
    lru_lambda = din("lru_lambda", [DEPTH, 512])
    w_out = din("w_out", [DEPTH, 2 * D, D])
    mlp_up = din("mlp_up", [DEPTH, D, 4 * D])
    mlp_down = din("mlp_down", [DEPTH, 4 * D, D])
    final_norm_w = din("final_norm_w", [1, D])
    y_p = dout("y_p", [TP, D])
    y_s = dout("y_s", [NS * TS, D])
    o_ssm = dout("o_ssm", [DEPTH, 1 + NS, 16, 64, 128])
    o_sconv = dout("o_sconv", [DEPTH, 1 + NS, 3, 1536])
    o_mc = dout("o_mc", [DEPTH, 1 + NS, 4, 128, 128])
    o_mn = dout("o_mn", [DEPTH, 1 + NS, 4, 128])
    o_mm = dout("o_mm", [DEPTH, 1 + NS, 4])
    o_lh = dout("o_lh", [DEPTH, 1 + NS, 512])
    o_lconv = dout("o_lconv", [DEPTH, 1 + NS, 3, 512])

    def mm(out, lhsT, rhs, start=True, stop=True):
        P.do("pe", "matmul", out=out, lhsT=lhsT, rhs=rhs, start=start, stop=stop)

    def tr(out, in_, ident):
        P.do("pe", "transpose", out=out, in_=in_, identity=ident)

    def act(out, in_, func, bias=0.0, scale=1.0, eng="act", **kw):
        P.do("act", "activation", out=out, in_=in_, func=func, bias=bias, scale=scale, **kw)

    def tt(out, in0, in1, op, eng="dve"):
        P.do(eng, "tensor_tensor", out=out, in0=in0, in1=in1, op=op)

    def ts(out, in0, s1, s2, op0, op1=None, eng="dve"):
        if op1 is None:
            P.do(eng, "tensor_scalar", out=out, in0=in0, scalar1=s1, scalar2=None, op0=op0)
        else:
            P.do(eng, "tensor_scalar", out=out, in0=in0, scalar1=s1, scalar2=s2, op0=op0, op1=op1)

    def stt(out, in0, scalar, in1, op0, op1):
        P.do("dve", "scalar_tensor_tensor", out=out, in0=in0, scalar=scalar, in1=in1, op0=op0, op1=op1)

    def cp(out, in_, eng="dve"):
        P.do(eng, "tensor_copy", out=out, in_=in_)

    def dma(out, in_, eng="sp"):
        P.do(eng, "dma_start", out=out, in_=in_)

    def dma_slow(out, in_, eng="sp"):
        P.do(eng, "dma_start", out=out, in_=in_, allow_slow_non_contiguous=True)

    def memset(buf_view, val, eng="pool"):
        P.do(eng, "memset", ap=buf_view.ap, constant=val, extra_writes=[buf_view.buf])

    identf = P.sb("identf", [128, 128], F32)
    identb = P.sb("identb", [128, 128], BF16)
    ones_mean = P.sb("ones_mean", [128, 128], BF16)
    onesf = P.sb("onesf", [128, 128], F32)
    ucum = P.sb("ucum", [128, 128], F32)
    negrep = P.sb("negrep", [128, 4, 128], BF16)
    negT = P.sb("negT", [128, 4, 128], BF16)
    sel127 = P.sb("sel127", [128, 128], F32)
    sel7 = P.sb("sel7", [128, 128], F32)
    memset(identf.v(), 1.0)
    P.do("pool", "affine_select", out=identf.v(), in_=identf.v(), pattern=[[-1, 128]], compare_op=ALU.is_equal,
         fill=0.0, base=0, channel_multiplier=1)
    cp(identb.v(), identf.v(), eng="pool")
    memset(ones_mean.v(), 1.0 / 1024.0)
    memset(onesf.v(), 1.0)
    memset(ucum.v(), 1.0)
    P.do("pool", "affine_select", out=ucum.v(), in_=ucum.v(), pattern=[[1, 128]], compare_op=ALU.is_ge,
         fill=0.0, base=0, channel_multiplier=-1)
    negtmp = P.sb("negtmp", [128, 128], F32)
    memset(negtmp.v(), 0.0)
    P.do("pool", "affine_select", out=negtmp.v(), in_=negtmp.v(), pattern=[[1, 128]], compare_op=ALU.is_ge,
         fill=NEG, base=0, channel_multiplier=-1)
    for i in range(4):
        cp(negrep[:, i, :], negtmp.v(), eng="pool")
    memset(negtmp.v(), 0.0)
    P.do("pool", "affine_select", out=negtmp.v(), in_=negtmp.v(), pattern=[[-1, 128]], compare_op=ALU.is_ge,
         fill=NEG, base=0, channel_multiplier=1)
    for i in range(4):
        cp(negT[:, i, :], negtmp.v(), eng="pool")
    for selt, row in ((sel127, 127), (sel7, 7)):
        memset(selt.v(), 1.0)
        P.do("pool", "affine_select", out=selt.v(), in_=selt.v(), pattern=[[0, 128]], compare_op=ALU.is_equal,
             fill=0.0, base=-row, channel_multiplier=1)

    PA = P.ps("PA", [128, 1024], F32)
    PB = P.ps("PB", [128, 2048], F32)
    PC = P.ps("PC", [128, 512], F32)
    PF = P.ps("PF", [128, 512], F32)

    x = P.sb("x", [128, 8, NTOK], F32)
    hn_d = nc.dram_tensor("hn_d", [NTOK // 128, 128, 1024], BF16, kind="Internal").ap()
    NCH = NTOK // 128
    xc = [Buf(x.ap[:, :, c * 128:(c + 1) * 128], "x%d" % c) for c in range(NCH)]
    hd = [Buf(hn_d[c].rearrange("p (k t) -> p k t", k=8), "hd%d" % c) for c in range(NCH)]


    def acp(out, in_):
        P.do("act", "activation", out=out, in_=in_, func=AF.Copy)

    Wbuf = Wo = None
    stage = P.sb("stage", [128, 128], F32)
    parA = P.sb("parA", [128, 104], F32)
    parB = P.sb("parB", [128, 60], F32)
    mod = P.sb("mod", [128, 48, 17], F32)
    s1 = P.sb("s1", [128, 8, 17], F32)
    s2 = P.sb("s2", [128, 8, 17], F32)
    bc_dtb = P.sb("bc_dtb", [128, 16], F32)
    bc_A = P.sb("bc_A", [128, 16], F32)
    bc_D = P.sb("bc_D", [128, 16], F32)
    bc_ib = P.sb("bc_ib", [128, 4], F32)
    bc_fb = P.sb("bc_fb", [128, 4], F32)
    lru_c1 = P.sb("lru_c1", [128, 4], F32)
    fnw = P.sb("fnw", [128, 8], F32)
    cs_fm = P.sb("cs_fm", [128, 8, 17], BF16)
    eps_t = P.sb("eps_t", [128, 1], F32)
    onesb = P.sb("onesb", [128, 1], BF16)
    memset(eps_t.v(), EPS)
    memset(onesb.v(), 1.0)
    sconv_io = P.sb("sconv_io", [128, 12, 48], F32)
    lconv_io = P.sb("lconv_io", [128, 4, 48], F32)
    lh_io = P.sb("lh_io", [128, 4, 16], F32)
    mn_io = P.sb("mn_io", [128, 64], F32)
    A4 = nc.sbuf_bytes_remaining // 4 - 64
    A4 = A4 // 8 * 8
    P.arena = nc.alloc_sbuf_tensor("arena", [128, A4], F32).ap()
    P.asize = A4

    xins = [P.ov("xin%d" % i, [128, 1024], F32) for i in range(3)]
    pbs = [PA, Buf(PB.ap[:, 0:1024], "pb1"), Buf(PB.ap[:, 1024:2048], "pb2")]
    for c in range(NCH):
        src = xp[c * 128:(c + 1) * 128, :] if c < 16 else xs[:, :]
        xin = xins[c % 3]
        pb = pbs[c % 3]
        dma(xin.v(), src)
        for k in range(8):
            tr(pb[:, k * 128:(k + 1) * 128], xin[:, k * 128:(k + 1) * 128], identf.v())
        if c % 2 == 0:
            cp(xc[c].v(), pb.v().rr("p (k t) -> p k t", k=8))
        else:
            acp(xc[c].v(), pb.v().rr("p (k t) -> p k t", k=8))
    P.barrier()

    cin = P.ov("cin", [128, 1024], F32)
    dma(cin[0:17, :], c17)
    act(cin[0:17, :], cin[0:17, :], AF.Silu)
    for k in range(8):
        tr(PC[:, k * 17:(k + 1) * 17], cin[0:17, k * 128:(k + 1) * 128], identf[0:17, 0:17])
    cp(cs_fm.v(), PC[:, 0:136].rr("p (k s) -> p k s", k=8))

    def load_fm(dst_view, rows_list):
        r0 = 0
        for ap in rows_list:
            r = ap.shape[0]
            dma(stage[r0:r0 + r, :], ap)
            r0 += r
        tr(PC[:, 0:r0], stage[0:r0, :], identf[0:r0, 0:r0])
        cp(dst_view, PC[:, 0:r0])

    def load_bc(dst_view, ap_row, n):
        dma(dst_view, ap_row.to_broadcast([128, n]))

    load_fm(fnw.v(), [final_norm_w.rearrange("o (k p) -> (o k) p", p=128)])

    def alloc_norm():
        return ([P.ov("adaslab%d" % i, [128, 8, 512], BF16) for i in range(2)],
                [P.ov("sq%d" % i, [128, 8, 128], BF16) for i in range(2)],
                [P.ov("rstd%d" % i, [128, 128], F32) for i in range(2)],
                [P.ov("ntmp%d" % i, [128, 8, 128], F32) for i in range(2)])
    adaslabs = sqs = rstds = ntmps = None
    nbanks = [PC, PF]

    def rmsnorm_mod(c, sc_t, sh_off, dstb):
        sq, rstd, ntmp, nb = sqs[c % 2], rstds[c % 2], ntmps[c % 2], nbanks[c % 2]
        P.do("pool" if c % 2 else "act", "tensor_tensor" if c % 2 else "activation",
             **(dict(out=sq.v(), in0=xc[c].v(), in1=xc[c].v(), op=ALU.mult) if c % 2 else
                dict(out=sq.v(), in_=xc[c].v(), func=AF.Square)))
        for k in range(8):
            mm(nb[:, 0:128], ones_mean.v(), sq[:, k, :], start=(k == 0), stop=(k == 7))
        act(rstd.v(), nb[:, 0:128], AF.Ln, bias=eps_t[:, 0:1], scale=1.0)
        act(rstd.v(), rstd.v(), AF.Exp, scale=-0.5)
        tt(ntmp.v(), xc[c].v(), rstd.v().unsq(1).bc([128, 8, 128]), ALU.mult)
        if c < 16:
            for k in range(8):
                act(dstb[:, k, :], ntmp[:, k, :], AF.Identity, bias=mod[:, sh_off + k, 0:1], scale=sc_t[:, k, 0:1])
        else:
            for k in range(8):
                tt(ntmp[:, k, :].rr("p (b t) -> p b t", t=8), ntmp[:, k, :].rr("p (b t) -> p b t", t=8),
                   sc_t[:, k, 1:17].unsq(2).bc([128, 16, 8]), ALU.mult)
                tt(dstb[:, k, :].rr("p (b t) -> p b t", t=8), ntmp[:, k, :].rr("p (b t) -> p b t", t=8),
                   mod[:, sh_off + k, 1:17].unsq(2).bc([128, 16, 8]), ALU.add)


    zs = xpad = cacc = xbc = xB = sm = Rb = dec = MT = xdt = xend = yy = y2 = ynb = yfm = Snat = STb = ss = None

    def alloc_ssd():
        return dict(zs=P.ov("zs", [128, 1024], BF16), xpad=P.ov("xpad", [128, 12, 131], F32),
                    cacc=P.ov("cacc", [128, 128], F32), xbc=P.ov("xbc", [128, 12, 128], BF16),
                    xB=P.ov("xB", [128, 1280], BF16), sm=P.ov("sm", [128, 256], F32),
                    Rb=P.ov("Rb", [128, 4, 128], F32), dec=P.ov("dec", [128, 4, 128], F32),
                    MT=P.ov("MT", [128, 16, 128], BF16), xdt=P.ov("xdt", [128, 1024], BF16),
                    yy=P.ov("yy", [128, 1024], F32), ynb=P.ov("ynb", [128, 1024], BF16),
                    yfm=P.ov("yfm", [128, 8, 128], BF16), Snat=P.ov("Snat", [128, 8, 128], F32),
                    ss=P.ov("ss", [128, 8], F32))

    def w_out_and_update(l, tok0, L, nk, seqcol):
        c = tok0 // 128
        o = tok0 - c * 128
        for oc in range(8):
            for k in range(nk):
                mm(PB[:, oc * 128:oc * 128 + L], Wo[:, k, oc * 128:(oc + 1) * 128], yfm[:, k, 0:L],
                   start=(k == 0), stop=(k == nk - 1))
        for oc in range(8):
            stt(xc[c][:, oc, o:o + L], PB[:, oc * 128:oc * 128 + L], mod[:, 16 + oc, seqcol:seqcol + 1],
                xc[c][:, oc, o:o + L], ALU.mult, ALU.add)

    def units():
        for c in range(16):
            yield (c * 128, 128, 0, c == 0, c == 15, None)
        for b in range(NS):
            yield (TP + b * TS, TS, 1 + b, True, True, b)

    def ssd_unit(l, tok0, L, seqcol, first, last, b):
        c = tok0 // 128
        o = tok0 - c * 128
        if b is None:
            H = hbuf[c % 2]
            dma(H.v(), hd[c].v())
            o = 0
        else:
            H = hsamp
        sel = sel127 if L == 128 else sel7
        if b is not None:
            dma(Snat.v(), st_ssm[l, b].rearrange("(hp h2) p n -> (h2 p) hp n", h2=2))
            for j in range(3):
                dma_slow(xpad[:, :, j], st_sconv[l, b, j].rearrange("(k p) -> p k", p=128))
        elif first:
            memset(Snat.v(), 0.0)
            memset(xpad[:, :, 0:3], 0.0)
        for j in range(2):
            for k in range(8):
                mm(PA[0:L, j * 512:(j + 1) * 512], H[:, k, o:o + L], Wbuf[:, k, j * 512:(j + 1) * 512],
                   start=(k == 0), stop=(k == 7))
        act(zs[0:L, :], PA[0:L, :], AF.Silu)
        for oc in range(12):
            for k in range(8):
                mm(PB[:, oc * 128:oc * 128 + L], Wbuf[:, k, 1024 + oc * 128:1024 + (oc + 1) * 128], H[:, k, o:o + L],
                   start=(k == 0), stop=(k == 7))
        acp(xpad[:, :, 3:3 + L], PB[:, 0:1536].rr("p (k t) -> p k t", k=12)[:, :, 0:L])
        for k in range(8):
            mm(PC[0:L, 0:16], H[:, k, o:o + L], Wbuf[:, k, 2560:2576], start=(k == 0), stop=(k == 7))
        dt = sm[0:L, 0:16]
        dtA = sm[0:L, 16:32]
        cum = sm[0:L, 32:48]
        tt(dt, PC[0:L, 0:16], bc_dtb[0:L, :], ALU.add)
        act(dt, dt, AF.Exp)
        act(dt, dt, AF.Ln, bias=1.0)
        tt(dtA, dt, bc_A[0:L, :], ALU.mult)
        for oc in range(12):
            ts(cacc[:, 0:L], xpad[:, oc, 0:L], parB[:, oc:oc + 1], parB[:, 48 + oc:49 + oc], ALU.mult, ALU.add)
            for j in range(1, 4):
                stt(cacc[:, 0:L], xpad[:, oc, j:j + L], parB[:, j * 12 + oc:j * 12 + oc + 1], cacc[:, 0:L], ALU.mult, ALU.add)
            act(xbc[:, oc, 0:L], cacc[:, 0:L], AF.Silu)
        if last:
            for j in range(3):
                dma_slow(o_sconv[l, seqcol, j].rearrange("(k p) -> p k", p=128), xpad[:, :, L + j])
        elif b is None:
            cp(xpad[:, :, 0:3], xpad[:, :, L:L + 3], eng="pool")
        PAb = PA.v().bitcast(BF16)
        for oc in range(10):
            tr(PAb[0:L, oc * 128:(oc + 1) * 128], xbc[:, oc, 0:L], identb.v())
        acp(xB[0:L, :], PAb[0:L, 0:1280])
        mm(PC[0:L, 16:32], ucum[0:L, 0:L], dtA)
        cp(cum, PC[0:L, 16:32])
        for g in range(2):
            mm(PF[0:L, g * 128:g * 128 + L], xbc[:, 8 + g, 0:L], xbc[:, 10 + g, 0:L])
        for q in range(4):
            g = q // 2
            tt(Rb[0:L, :, 0:L], ucum[0:L, 0:L].unsq(1).bc([L, 4, L]), dtA[:, 4 * q:4 * q + 4].unsq(2).bc([L, 4, L]), ALU.mult)
            outv = PB[0:L, q * 512:(q + 1) * 512].rr("p (h t) -> p h t", h=4)[:, :, 0:L]
            mm(outv, onesf[0:L, 0:L], Rb[0:L, :, 0:L], start=True, stop=False)
            mm(outv, identb[0:L, 0:L], negrep[0:L, :, 0:L], start=False, stop=True)
            tt(dec[0:L, :, 0:L], outv, cum[:, 4 * q:4 * q + 4].unsq(2).bc([L, 4, L]), ALU.subtract)
            act(dec[0:L, :, 0:L], dec[0:L, :, 0:L], AF.Exp)
            tt(MT[0:L, 4 * q:4 * q + 4, 0:L], dec[0:L, :, 0:L],
               PF[0:L, g * 128:g * 128 + L].unsq(1).bc([L, 4, L]), ALU.mult)
        tt(xdt[0:L, :].rr("t (h p) -> t h p", p=64), xB[0:L, 0:1024].rr("t (h p) -> t h p", p=64),
           dt.unsq(2).bc([L, 16, 64]), ALU.mult)
        for h in range(16):
            mm(PA[0:L, h * 64:(h + 1) * 64], MT[0:L, h, 0:L], xdt[0:L, h * 64:(h + 1) * 64])
        for hp in range(8):
            tr(PB[:, hp * 128:(hp + 1) * 128], Snat[:, hp, :], identf.v())
        acp(STb.v(), PB[:, 0:1024])
        for g in range(2):
            mm(PB[0:L, 1024 + g * 512:1024 + (g + 1) * 512], xbc[:, 10 + g, 0:L], STb[:, g * 512:(g + 1) * 512])
        ecum = sm[0:L, 48:64]
        act(ecum, cum, AF.Exp)
        tt(yy[0:L, :].rr("t (h p) -> t h p", p=64), PB[0:L, 1024:2048].rr("t (h p) -> t h p", p=64),
           ecum.unsq(2).bc([L, 16, 64]), ALU.mult)
        tt(yy[0:L, :], yy[0:L, :], PA[0:L, :], ALU.add)
        tt(y2[0:L, :].rr("t (h p) -> t h p", p=64), xB[0:L, 0:1024].rr("t (h p) -> t h p", p=64),
           bc_D[0:L, :].unsq(2).bc([L, 16, 64]), ALU.mult)
        tt(yy[0:L, :], yy[0:L, :], y2[0:L, :], ALU.add)
        tt(yy[0:L, :], yy[0:L, :], zs[0:L, :], ALU.mult)
        act(y2[0:L, :], yy[0:L, :], AF.Square, accum_out=ss[0:L, 0:1])
        act(ss[0:L, 1:2], ss[0:L, 0:1], AF.Ln, bias=eps_t[0:L, 0:1], scale=1.0 / 1024.0)
        act(ss[0:L, 2:3], ss[0:L, 1:2], AF.Exp, scale=-0.5)
        ts(ynb[0:L, :], yy[0:L, :], ss[0:L, 2:3], None, ALU.mult)
        for k in range(8):
            tr(PAb[:, k * 128:k * 128 + L], ynb[0:L, k * 128:(k + 1) * 128], identb[0:L, 0:L])
        tt(yfm[:, :, 0:L], PAb[:, 0:1024].rr("p (k t) -> p k t", k=8)[:, :, 0:L],
           parA[:, 64:72].unsq(2).bc([128, 8, L]), ALU.mult)
        w_out_and_update(l, tok0, L, 8, seqcol)
        mm(PC[:, 32:48], sel[0:L, :], cum)
        cl = sm[:, 64:80]
        cp(cl, PC[:, 32:48])
        eend = sm[0:L, 80:96]
        tt(eend, cl[0:L, :], cum, ALU.subtract)
        act(eend, eend, AF.Exp)
        tt(xend[0:L, :].rr("t (h p) -> t h p", p=64), xdt[0:L, :].rr("t (h p) -> t h p", p=64),
           eend.unsq(2).bc([L, 16, 64]), ALU.mult)
        dcy = sm[:, 96:112]
        act(dcy, cl, AF.Exp)
        for hp in range(8):
            g = hp // 4
            mm(PA[:, hp * 128:(hp + 1) * 128], xend[0:L, hp * 128:(hp + 1) * 128], xB[0:L, 1024 + g * 128:1024 + (g + 1) * 128])
        dcyv = dcy.rr("p (hp h2) -> p hp h2", h2=2)
        for h2 in range(2):
            rs = slice(64 * h2, 64 * h2 + 64)
            tt(Snat[rs, :, :], Snat[rs, :, :], dcyv[rs, :, h2:h2 + 1].bc([64, 8, 128]), ALU.mult)
            tt(Snat[rs, :, :], Snat[rs, :, :], PA[rs, :].rr("p (hp n) -> p hp n", hp=8), ALU.add)
        if last:
            dma(o_ssm[l, seqcol].rearrange("(hp h2) p n -> (h2 p) hp n", h2=2), Snat.v())

    hbuf = hsamp = None
    lpad = lx = lxb = lri = la = lu = lh = lg = hstate = wg = None

    def alloc_lru():
        return dict(lpad=P.ov("lpad", [128, 4, 131], F32), lx=P.ov("lx", [128, 4, 128], F32),
                    lxb=P.ov("lxb", [128, 4, 128], BF16), lri=P.ov("lri", [128, 8, 128], F32),
                    la=P.ov("la", [128, 4, 128], F32), lu=P.ov("lu", [128, 4, 128], F32),
                    lh=P.ov("lh", [128, 4, 128], F32), lg=P.ov("lg", [128, 4, 128], F32),
                    hstate=P.ov("hstate", [128, 4], F32), wg=P.ov("wg", [128, 8, 128], BF16),
                    yfm=P.ov("yfm", [128, 8, 128], BF16))

    def lru_unit(l, tok0, L, seqcol, first, last, b):
        c = tok0 // 128
        o = tok0 - c * 128
        if b is None:
            H = hbuf[c % 2]
            dma(H.v(), hd[c].v())
            o = 0
        else:
            H = hsamp
        if b is not None:
            dma_slow(hstate.v(), st_lh[l, b].rearrange("(k p) -> p k", p=128))
            for j in range(3):
                dma_slow(lpad[:, :, j], st_lconv[l, b, j].rearrange("(k p) -> p k", p=128))
        elif first:
            memset(hstate.v(), 0.0)
            memset(lpad[:, :, 0:3], 0.0)
        for oc in range(8):
            for k in range(8):
                mm(PA[:, oc * 128:oc * 128 + L], Wbuf[:, k, oc * 128:(oc + 1) * 128], H[:, k, o:o + L],
                   start=(k == 0), stop=(k == 7))
        PAv = PA.v().rr("p (k t) -> p k t", k=8)
        acp(lpad[:, :, 3:3 + L], PAv[:, 0:4, 0:L])
        act(lg[:, :, 0:L], PAv[:, 4:8, 0:L], AF.Gelu)
        def cwv(j):
            return parA[:, 72 + j * 4:72 + (j + 1) * 4].unsq(2).bc([128, 4, L])
        tt(lx[:, :, 0:L], lpad[:, :, 0:L], cwv(0), ALU.mult)
        for j in range(1, 4):
            tt(lu[:, :, 0:L], lpad[:, :, j:j + L], cwv(j), ALU.mult)
            tt(lx[:, :, 0:L], lx[:, :, 0:L], lu[:, :, 0:L], ALU.add)
        tt(lx[:, :, 0:L], lx[:, :, 0:L], parA[:, 88:92].unsq(2).bc([128, 4, L]), ALU.add)
        cp(lxb[:, :, 0:L], lx[:, :, 0:L])
        if last:
            for j in range(3):
                dma_slow(o_lconv[l, seqcol, j].rearrange("(k p) -> p k", p=128), lpad[:, :, L + j])
        elif b is None:
            cp(lpad[:, :, 0:3], lpad[:, :, L:L + 3], eng="pool")
        for k in range(4):
            mm(PC[:, k * 128:k * 128 + L], wg[:, k, :], lxb[:, k, 0:L])
        for k in range(4):
            mm(PF[:, k * 128:k * 128 + L], wg[:, 4 + k, :], lxb[:, k, 0:L])
        tt(lri[:, 0:4, 0:L], PC.v().rr("p (k t) -> p k t", k=4)[:, :, 0:L], parA[:, 92:96].unsq(2).bc([128, 4, L]), ALU.add)
        tt(lri[:, 4:8, 0:L], PF.v().rr("p (k t) -> p k t", k=4)[:, :, 0:L], parA[:, 96:100].unsq(2).bc([128, 4, L]), ALU.add)
        act(lri[:, :, 0:L], lri[:, :, 0:L], AF.Sigmoid)
        for k in range(4):
            act(la[:, k, 0:L], lri[:, k, 0:L], AF.Exp, scale=lru_c1[:, k:k + 1])
        tt(lu[:, :, 0:L], la[:, :, 0:L], la[:, :, 0:L], ALU.mult)
        ts(lu[:, :, 0:L], lu[:, :, 0:L], -1.0, 1.0, ALU.mult, ALU.add)
        ts(lu[:, :, 0:L], lu[:, :, 0:L], 0.0, None, ALU.max)
        act(lu[:, :, 0:L], lu[:, :, 0:L], AF.Sqrt)
        if b is None and first:
            memset(lu[:, :, 0:1], 1.0)
        tt(lu[:, :, 0:L], lu[:, :, 0:L], lri[:, 4:8, 0:L], ALU.mult)
        tt(lu[:, :, 0:L], lu[:, :, 0:L], lx[:, :, 0:L], ALU.mult)
        for k in range(4):
            P.do("dve", "tensor_tensor_scan", out=lh[:, k, 0:L], data0=la[:, k, 0:L], data1=lu[:, k, 0:L],
                 initial=hstate[:, k:k + 1], op0=ALU.mult, op1=ALU.add)
        cp(hstate.v(), lh[:, :, L - 1])
        if last:
            dma_slow(o_lh[l, seqcol].rearrange("(k p) -> p k", p=128), hstate.v())
        tt(yfm[:, 0:4, 0:L], lh[:, :, 0:L], lg[:, :, 0:L], ALU.mult)
        w_out_and_update(l, tok0, L, 4, seqcol)

    bc_mlw = None
    qk = ktm = vtm = vw = osig = g4 = R4 = d4 = w4 = Cnat = CTb = nfm = nfb = mbc = num = num2 = None

    def alloc_ml():
        return dict(qk=P.ov("qk", [128, 8, 128], BF16), ktm=P.ov("ktm", [128, 512], BF16),
                    vtm=P.ov("vtm", [128, 512], BF16), vw=P.ov("vw", [128, 512], BF16),
                    osig=P.ov("osig", [128, 512], F32), g4=P.ov("g4", [128, 128], F32),
                    R4=P.ov("R4", [128, 4, 128], F32), d4=P.ov("d4", [128, 4, 128], F32),
                    w4=P.ov("w4", [128, 4, 128], BF16), Cnat=P.ov("Cnat", [128, 4, 128], F32),
                    CTb=P.ov("CTb", [128, 4, 128], BF16), nfm=P.ov("nfm", [128, 4], F32),
                    nfb=P.ov("nfb", [128, 4], BF16), mbc=P.ov("mbc", [128, 4], F32),
                    num=P.ov("num", [128, 512], F32), num2=P.ov("num2", [128, 512], F32), bc_mlw=P.ov("bc_mlw", [128, 512], F32),
                    ynb=P.ov("ynb", [128, 1024], BF16), yfm=P.ov("yfm", [128, 8, 128], BF16))

    KSC = 128.0 ** -0.5

    def ml_unit(l, tok0, L, seqcol, first, last, b):
        c = tok0 // 128
        o = tok0 - c * 128
        if b is None:
            H = hbuf[c % 2]
            dma(H.v(), hd[c].v())
            o = 0
        else:
            H = hsamp
        sel = sel127 if L == 128 else sel7
        if b is not None:
            dma(Cnat.v(), st_mc[l, b].rearrange("h v d -> v h d"))
            dma_slow(nfm.v(), st_mn[l, b].rearrange("h d -> d h"))
            dma(mbc.v(), st_mm[l, b:b + 1, :].to_broadcast([128, 4]))
        elif first:
            memset(Cnat.v(), 0.0)
            memset(nfm.v(), 0.0)
            memset(mbc.v(), 0.0)
        for oc in range(8):
            for k in range(8):
                mm(PA[:, oc * 128:oc * 128 + L], Wbuf[:, k, oc * 128:(oc + 1) * 128], H[:, k, o:o + L],
                   start=(k == 0), stop=(k == 7))
        PAv = PA.v().rr("p (k t) -> p k t", k=8)
        acp(qk[:, 0:4, 0:L], PAv[:, 0:4, 0:L])
        act(qk[:, 4:8, 0:L], PAv[:, 4:8, 0:L], AF.Copy, scale=KSC)
        for j in range(3):
            for k in range(8):
                mm(PB[0:L, j * 512:(j + 1) * 512], H[:, k, o:o + L], Wbuf[:, k, 512 + j * 512:512 + (j + 1) * 512],
                   start=(k == 0), stop=(k == 7))
        for k in range(8):
            mm(PC[0:L, 0:8], H[:, k, o:o + L], Wbuf[:, k, 2048:2056], start=(k == 0), stop=(k == 7))
        act(ktm[0:L, :], PB[0:L, 0:512], AF.Copy, scale=KSC)
        acp(vtm[0:L, :], PB[0:L, 512:1024])
        act(osig[0:L, :], PB[0:L, 1024:1536], AF.Sigmoid)
        ig = g4[0:L, 0:4]
        lf = g4[0:L, 4:8]
        bcs = g4[0:L, 8:12]
        a_s = g4[0:L, 12:16]
        tt(ig, PC[0:L, 0:4], bc_ib[0:L, :], ALU.add)
        tt(lf, PC[0:L, 4:8], bc_fb[0:L, :], ALU.add)
        act(lf, lf, AF.Exp, scale=-1.0)
        act(lf, lf, AF.Ln, bias=1.0)
        ts(lf, lf, -1.0, None, ALU.mult)
        mm(PC[0:L, 8:12], ucum[0:L, 0:L], lf)
        cp(bcs, PC[0:L, 8:12])
        tt(a_s, ig, bcs, ALU.subtract)
        tt(R4[0:L, :, 0:L], identf[0:L, 0:L].unsq(1).bc([L, 4, L]), a_s.unsq(2).bc([L, 4, L]), ALU.mult)
        outv = PF.v().rr("p (h t) -> p h t", h=4)[0:L, :, 0:L]
        mm(outv, onesf[0:L, 0:L], R4[0:L, :, 0:L], start=True, stop=False)
        mm(outv, identb[0:L, 0:L], negT[0:L, :, 0:L], start=False, stop=True)
        cm = g4[0:L, 16:20]
        P.do("dve", "tensor_reduce", out=cm, in_=outv, axis=AX.X, op=ALU.max)
        r = g4[0:L, 20:24]
        tt(r, cm, mbc[0:L, :], ALU.max)
        mt = g4[0:L, 24:28]
        tt(mt, bcs, r, ALU.add)
        negr = g4[0:L, 28:32]
        ts(negr, r, -1.0, None, ALU.mult)
        tt(R4[0:L, :, 0:L], identf[0:L, 0:L].unsq(1).bc([L, 4, L]), negr.unsq(2).bc([L, 4, L]), ALU.mult)
        mm(outv, onesf[0:L, 0:L], R4[0:L, :, 0:L], start=True, stop=False)
        mm(outv, identb[0:L, 0:L], negrep[0:L, :, 0:L], start=False, stop=True)
        tt(d4[0:L, :, 0:L], outv, a_s.unsq(2).bc([L, 4, L]), ALU.add)
        act(d4[0:L, :, 0:L], d4[0:L, :, 0:L], AF.Exp)
        outc = PC.v().rr("p (h t) -> p h t", h=4)
        for h in range(4):
            mm(outc[0:L, h, 0:L], qk[:, 4 + h, 0:L], qk[:, h, 0:L])
        tt(w4[0:L, :, 0:L], d4[0:L, :, 0:L], outc[0:L, :, 0:L], ALU.mult)
        for h in range(4):
            mm(PA[0:L, h * 128:(h + 1) * 128], w4[0:L, h, 0:L], vtm[0:L, h * 128:(h + 1) * 128])
        for h in range(4):
            mm(PA[0:L, 512 + h:513 + h], w4[0:L, h, 0:L], onesb[0:L, :])
        for h in range(4):
            tr(PB[:, h * 128:(h + 1) * 128], Cnat[:, h, :], identf.v())
        acp(CTb.v(), PB[:, 0:512].rr("p (h v) -> p h v", h=4))
        cp(nfb.v(), nfm.v())
        for h in range(4):
            mm(PB[0:L, 512 + h * 128:512 + (h + 1) * 128], qk[:, h, 0:L], CTb[:, h, :])
        for h in range(4):
            mm(PB[0:L, 1024 + h:1025 + h], qk[:, h, 0:L], nfb[:, h:h + 1])
        inter = g4[0:L, 32:36]
        tt(inter, mbc[0:L, :], r, ALU.subtract)
        act(inter, inter, AF.Exp)
        tt(num[0:L, :].rr("t (h v) -> t h v", h=4), PB[0:L, 512:1024].rr("t (h v) -> t h v", h=4),
           inter.unsq(2).bc([L, 4, 128]), ALU.mult)
        tt(num[0:L, :], num[0:L, :], PA[0:L, 0:512], ALU.add)
        den = g4[0:L, 36:40]
        tt(den, PB[0:L, 1024:1028], inter, ALU.mult)
        tt(den, den, PA[0:L, 512:516], ALU.add)
        act(den, den, AF.Abs)
        emt = g4[0:L, 40:44]
        act(emt, mt, AF.Exp, scale=-1.0)
        tt(den, den, emt, ALU.max)
        P.do("dve", "reciprocal", out=den, in_=den)
        tt(num[0:L, :].rr("t (h v) -> t h v", h=4), num[0:L, :].rr("t (h v) -> t h v", h=4),
           den.unsq(2).bc([L, 4, 128]), ALU.mult)
        tt(num2[0:L, :], num[0:L, :], num[0:L, :], ALU.mult)
        ssq = g4[0:L, 44:48]
        P.do("dve", "tensor_reduce", out=ssq, in_=num2[0:L, :].rr("t (h v) -> t h v", h=4), axis=AX.X, op=ALU.add)
        act(ssq, ssq, AF.Sqrt, bias=eps_t[0:L, 0:1], scale=1.0 / 128.0)
        P.do("dve", "reciprocal", out=ssq, in_=ssq)
        tt(num[0:L, :].rr("t (h v) -> t h v", h=4), num[0:L, :].rr("t (h v) -> t h v", h=4),
           ssq.unsq(2).bc([L, 4, 128]), ALU.mult)
        tt(num[0:L, :], num[0:L, :], bc_mlw[0:L, :], ALU.mult)
        tt(ynb[0:L, 0:512], num[0:L, :], osig[0:L, :], ALU.mult)
        PAb = PA.v().bitcast(BF16)
        for k in range(4):
            tr(PAb[:, k * 128:k * 128 + L], ynb[0:L, k * 128:(k + 1) * 128], identb[0:L, 0:L])
        cp(yfm[:, 0:4, 0:L], PAb[:, 0:512].rr("p (k t) -> p k t", k=4)[:, :, 0:L])
        w_out_and_update(l, tok0, L, 4, seqcol)
        bm = g4[0:L, 48:56]
        cp(bm[:, 0:4], bcs)
        cp(bm[:, 4:8], mt)
        mm(PC[:, 0:8], sel[0:L, :], bm)
        last8 = g4[:, 56:64]
        cp(last8, PC[:, 0:8])
        wend = g4[0:L, 64:68]
        tt(wend, last8[0:L, 0:4], last8[0:L, 4:8], ALU.subtract)
        tt(wend, wend, a_s, ALU.add)
        act(wend, wend, AF.Exp)
        dc = g4[:, 68:72]
        tt(dc, last8[:, 0:4], mbc.v(), ALU.add)
        tt(dc, dc, last8[:, 4:8], ALU.subtract)
        act(dc, dc, AF.Exp)
        cp(mbc.v(), last8[:, 4:8])
        tt(vw[0:L, :].rr("t (h v) -> t h v", h=4), vtm[0:L, :].rr("t (h v) -> t h v", h=4),
           wend.unsq(2).bc([L, 4, 128]), ALU.mult)
        wendb = g4[0:L, 72:76].bitcast(BF16)[:, 0:4]
        cp(wendb, wend)
        for h in range(4):
            mm(PB[:, h * 128:(h + 1) * 128], vw[0:L, h * 128:(h + 1) * 128], ktm[0:L, h * 128:(h + 1) * 128])
        for h in range(4):
            mm(PB[:, 512 + h:513 + h], ktm[0:L, h * 128:(h + 1) * 128], wendb[:, h:h + 1])
        tt(Cnat.v(), Cnat.v(), dc.unsq(2).bc([128, 4, 128]), ALU.mult)
        tt(Cnat.v(), Cnat.v(), PB[:, 0:512].rr("p (h d) -> p h d", h=4), ALU.add)
        tt(nfm.v(), nfm.v(), dc, ALU.mult)
        tt(nfm.v(), nfm.v(), PB[:, 512:516], ALU.add)
        if last:
            dma(o_mc[l, seqcol].rearrange("h v d -> v h d"), Cnat.v())
            dma_slow(o_mn[l, seqcol].rearrange("h d -> d h"), nfm.v())
            dma(o_mm[l, seqcol:seqcol + 1, :], mbc[0:1, :])

    hid = upw = dnw = rl = modraw = None


    def run_pipeline(gens):
        def collect(g, until_early):
            P.capture = []
            try:
                while True:
                    v = next(g)
                    if until_early and v == "EARLY_DONE":
                        break
            except StopIteration:
                pass
            ops_ = P.capture
            P.capture = None
            return ops_

        sim = {"eng": {e: 0.0 for e in ENGS}, "w": {}, "r": {}}

        def bufs_of(it):
            eng, meth, er, ew, kw = it
            reads, writes = [], []
            for k, v in kw.items():
                bb = v.buf if isinstance(v, View) else (v if isinstance(v, Buf) else None)
                if bb is not None:
                    (writes if k in WRITE_KEYS else reads).append(bb)
            for x_ in er:
                reads.append(x_.buf if isinstance(x_, View) else x_)
            for x_ in ew:
                writes.append(x_.buf if isinstance(x_, View) else x_)
            return reads, writes

        def fsize(v):
            ap = v.ap if isinstance(v, (View, Buf)) else v
            n = 1
            for d in ap.shape[1:]:
                n *= d
            return n

        def dur_of(it):
            eng, meth, er, ew, kw = it
            if meth == "dma_start":
                o_ = kw["out"]
                return 2.0 + fsize(o_) * 128 * 4 / 150e3
            if eng == "pe":
                n = fsize(kw["rhs"]) if "rhs" in kw else 128
                d_ = max(n, 64) / 1200.0
                lt = kw.get("lhsT", kw.get("in_"))
                if lt is not None and (lt.ap if isinstance(lt, (View, Buf)) else lt).dtype == F32:
                    d_ *= 4 if meth == "matmul" else 1
                return d_
            o_ = kw.get("out", kw.get("ap"))
            n = fsize(o_) if o_ is not None else 64
            if eng == "dve":
                return 0.07 + n / 960.0
            if eng == "act":
                return 0.22 + n / 1200.0
            return 0.12 + n / 400.0

        def est_start(it):
            reads, writes = bufs_of(it)
            t = sim["eng"][it[0]]
            for r_ in reads:
                t = max(t, sim["w"].get(id(r_), 0.0) + 0.15)
            for w_ in writes:
                t = max(t, sim["w"].get(id(w_), 0.0) + 0.15, sim["r"].get(id(w_), 0.0) + 0.15)
            return t, reads, writes

        def commit(it, t, reads, writes):
            d_ = dur_of(it)
            issue = 0.03 if it[1] != "dma_start" else 0.1
            sim["eng"][it[0]] = (t + d_) if it[1] != "dma_start" else (t + issue)
            for r_ in reads:
                sim["r"][id(r_)] = max(sim["r"].get(id(r_), 0.0), t + d_)
            for w_ in writes:
                sim["w"][id(w_)] = t + d_
            P.do(it[0], it[1], it[2], it[3], **it[4])

        def merge(A, B):
            lst = list(B) + list(A)
            n = len(lst)
            lastw, readers = {}, {}
            preds = [set() for _ in range(n)]
            rws = []
            for idx, it in enumerate(lst):
                reads, writes = bufs_of(it)
                rws.append((reads, writes))
                for r_ in reads:
                    if id(r_) in lastw:
                        preds[idx].add(lastw[id(r_)])
                for w_ in writes:
                    if id(w_) in lastw:
                        preds[idx].add(lastw[id(w_)])
                    preds[idx].update(readers.get(id(w_), ()))
                for r_ in reads:
                    readers.setdefault(id(r_), []).append(idx)
                for w_ in writes:
                    lastw[id(w_)] = idx
                    readers[id(w_)] = []
                preds[idx].discard(idx)
            succs = [[] for _ in range(n)]
            indeg = [0] * n
            for idx in range(n):
                indeg[idx] = len(preds[idx])
                for p_ in preds[idx]:
                    succs[p_].append(idx)
            ready = [i for i in range(n) if indeg[i] == 0]
            while ready:
                best = None
                bt = None
                for i in ready:
                    t = est_start(lst[i])
                    if bt is None or (t[0], i) < (bt[0], best):
                        best, bt = i, t
                ready.remove(best)
                commit(lst[best], *bt)
                for s_ in succs[best]:
                    indeg[s_] -= 1
                    if indeg[s_] == 0:
                        ready.append(s_)
                n -= 1
            assert n == 0, "scheduler dropped ops"

        WIN = 32
        gens = list(gens)
        for w0 in range(0, len(gens), WIN):
            allops = []
            for g in gens[w0:w0 + WIN]:
                allops.extend(collect(g, False))
            merge(allops, [])

    EA = EB = EC = EY = LA = LB = LC = None
    xbc2 = xB2 = xdt2 = sm2 = yi2 = cacc_t = caccs = ctmp_t = cts = Rbs = decs = None
    xpads = zss = yys = ynbs2 = Snats = sss = yfms_s = zsfm_all = None

    def ssd_gen(l, tok0, L, seqcol, first, last, b, ctx):
        c = tok0 // 128
        sel = sel127 if L == 128 else sel7
        xbc, xB, xdt, sm, yi = xbc2[ctx], xB2[ctx], xdt2[ctx], sm2[ctx], yi2[ctx]
        if b is None:
            xpad_, zs_, yy_, ynb_, Snat_, ss_, yfm_ = xpad, zs, yy, ynb, Snat, ss, yfm
        else:
            xpad_, zs_, yy_, ynb_, Snat_, ss_, yfm_ = xpads[b], zss[ctx], yys[ctx], ynbs2[ctx], Snats[ctx], sss[ctx], yfms_s[b]
        xend_ = zs_
        STb_ = ynb_
        if b is None:
            H = hbuf[ctx]
            dma(H.v(), hd[c].v())
            o = 0
        else:
            H = hsamp
            o = tok0 - c * 128
        if b is not None:
            cp(xpad_[:, :, 0:3], sconv_io[:, :, 3 * b:3 * b + 3], eng="pool")
        elif first:
            memset(xpad_[:, :, 0:3], 0.0)
        for r in range(3 if b is None else 0):
            bank = (EA, EB)[r % 2]
            for oo in range(4):
                oc = r * 4 + oo
                for k in range(8):
                    mm(bank[:, oo * 128:oo * 128 + L], Wbuf[:, k, 1024 + oc * 128:1024 + (oc + 1) * 128], H[:, k, o:o + L],
                       start=(k == 0), stop=(k == 7))
            acp(xpad_[:, r * 4:(r + 1) * 4, 3:3 + L], bank.v().rr("p (k t) -> p k t", k=4)[:, :, 0:L])
            yield
        for k in range(8):
            mm(EC[0:L, 0:16], H[:, k, o:o + L], Wbuf[:, k, 2560:2576], start=(k == 0), stop=(k == 7))
        dt = sm[0:L, 0:16]
        dtA = sm[0:L, 16:32]
        cum = sm[0:L, 32:48]
        tt(dt, EC[0:L, 0:16], bc_dtb[0:L, :], ALU.add)
        act(dt, dt, AF.Exp)
        act(dt, dt, AF.Ln, bias=1.0)
        tt(dtA, dt, bc_A[0:L, :], ALU.mult)
        mm(EC[0:L, 16:32], ucum[0:L, 0:L], dtA)
        cp(cum, EC[0:L, 16:32])
        yield
        call = View(caccs[0], cacc_t.ap[:, :, 0:L])
        ctall = View(cts[0], ctmp_t.ap[:, :, 0:L])
        if L == 128:
            for oc in range(12):
                ts(caccs[oc][:, 0:L], xpad_[:, oc, 0:L], parB[:, oc:oc + 1], parB[:, 48 + oc:49 + oc], ALU.mult, ALU.add)
            for j in range(1, 4):
                for oc in range(12):
                    stt(caccs[oc][:, 0:L], xpad_[:, oc, j:j + L], parB[:, j * 12 + oc:j * 12 + oc + 1], caccs[oc][:, 0:L], ALU.mult, ALU.add)
        else:
            def cwv(j):
                return parB[:, j * 12:(j + 1) * 12].unsq(2).bc([128, 12, L])
            P.do("dve", "tensor_tensor", out=call, in0=xpad_[:, :, 0:L], in1=cwv(0), op=ALU.mult, extra_writes=caccs[1:])
            for j in range(1, 4):
                P.do("dve", "tensor_tensor", out=ctall, in0=xpad_[:, :, j:j + L], in1=cwv(j), op=ALU.mult, extra_writes=cts[1:])
                P.do("dve", "tensor_tensor", out=call, in0=call, in1=ctall, op=ALU.add, extra_reads=cts[1:], extra_writes=caccs[1:])
            P.do("dve", "tensor_tensor", out=call, in0=call, in1=parB[:, 48:60].unsq(2).bc([128, 12, L]), op=ALU.add, extra_writes=caccs[1:])
        P.do("act", "activation", out=ctall, in_=call, func=AF.Exp, bias=0.0, scale=-1.0, extra_reads=caccs[1:], extra_writes=cts[1:])
        P.do("act", "activation", out=ctall, in_=ctall, func=AF.Ln, bias=1.0, scale=1.0, extra_writes=cts[1:])
        P.do("act", "activation", out=ctall, in_=ctall, func=AF.Exp, bias=0.0, scale=-1.0, extra_writes=cts[1:])
        P.do("dve", "tensor_tensor", out=xbc[:, :, 0:L], in0=call, in1=ctall, op=ALU.mult, extra_reads=caccs[1:] + cts[1:])
        yield
        if b is not None:
            cp(sconv_io[:, :, 3 * b:3 * b + 3], xpad_[:, :, L:L + 3], eng="pool")
        elif last:
            for j in range(3):
                dma_slow(o_sconv[l, seqcol, j].rearrange("(k p) -> p k", p=128), xpad_[:, :, L + j])
        else:
            cp(xpad_[:, :, 0:3], xpad_[:, :, L:L + 3], eng="pool")
        EBb = EB.v().bitcast(BF16)
        EAb = EA.v().bitcast(BF16)
        for oc in range(8):
            tr(EBb[0:L, oc * 128:(oc + 1) * 128], xbc[:, oc, 0:L], identb.v())
        acp(xB[0:L, 0:1024], EBb[0:L, 0:1024])
        for oc in range(8, 10):
            tr(EAb[0:L, (oc - 8) * 128:(oc - 7) * 128], xbc[:, oc, 0:L], identb.v())
        acp(xB[0:L, 1024:1280], EAb[0:L, 0:256])
        for g in range(2):
            mm(EC[0:L, 128 + g * 128:128 + g * 128 + L], xbc[:, 8 + g, 0:L], xbc[:, 10 + g, 0:L])
        tt(xdt[0:L, :].rr("t (h p) -> t h p", p=64), xB[0:L, 0:1024].rr("t (h p) -> t h p", p=64),
           dt.unsq(2).bc([L, 16, 64]), ALU.mult)
        yield
        for q in range(4):
            g = q // 2
            bank = (EA, EB)[q % 2]
            Rb = Rbs[q % 2]
            dec = decs[q % 2]
            tt(Rb[0:L, :, 0:L], ucum[0:L, 0:L].unsq(1).bc([L, 4, L]), dtA[:, 4 * q:4 * q + 4].unsq(2).bc([L, 4, L]), ALU.mult, eng="pool")
            outv = bank[0:L, :].rr("p (h t) -> p h t", h=4)[:, :, 0:L]
            if L == 128:
                mm(outv, onesf[0:L, 0:L], Rb[0:L, :, 0:L], start=True, stop=False)
                mm(outv, identb[0:L, 0:L], negrep[0:L, :, 0:L], start=False, stop=True)
            else:
                for hh in range(4):
                    mm(outv[:, hh, :], onesf[0:L, 0:L], Rb[0:L, hh, 0:L], start=True, stop=False)
                    mm(outv[:, hh, :], identb[0:L, 0:L], negrep[0:L, hh, 0:L], start=False, stop=True)
            tt(dec[0:L, :, 0:L], outv, cum[:, 4 * q:4 * q + 4].unsq(2).bc([L, 4, L]), ALU.subtract)
            act(dec[0:L, :, 0:L], dec[0:L, :, 0:L], AF.Exp)
            tt(MT[0:L, 4 * q:4 * q + 4, 0:L], dec[0:L, :, 0:L],
               EC[0:L, 128 + g * 128:128 + g * 128 + L].unsq(1).bc([L, 4, L]), ALU.mult)
            for h in range(4 * q, 4 * q + 4):
                mm(EY[0:L, h * 64:(h + 1) * 64], MT[0:L, h, 0:L], xdt[0:L, h * 64:(h + 1) * 64])
            yield
        acp(yi[0:L, :], EY[0:L, :])
        yield "EARLY_DONE"
        if b is not None:
            dma(Snat_.v(), st_ssm[l, b].rearrange("(hp h2) p n -> (h2 p) hp n", h2=2))
        elif first:
            memset(Snat_.v(), 0.0)
        y2 = yi
        ecum = sm[0:L, 48:64]
        act(ecum, cum, AF.Exp)
        mm(LC[:, 0:16], sel[0:L, :], cum)
        cl = sm[:, 64:80]
        cp(cl, LC[:, 0:16])
        eend = sm[0:L, 80:96]
        tt(eend, cl[0:L, :], cum, ALU.subtract)
        act(eend, eend, AF.Exp)
        tt(xend_[0:L, :].rr("t (h p) -> t h p", p=64), xdt[0:L, :].rr("t (h p) -> t h p", p=64),
           eend.unsq(2).bc([L, 16, 64]), ALU.mult, eng="pool")
        dcy = sm[:, 96:112]
        act(dcy, cl, AF.Exp)
        dcyv = dcy.rr("p (hp h2) -> p hp h2", h2=2)
        for g_ in range(2):
            for hh in range(4):
                tr(LC[:, hh * 128:(hh + 1) * 128], Snat_[:, 4 * g_ + hh, :], identf.v())
            acp(STb_[:, g_ * 512:(g_ + 1) * 512], LC.v())
        yield
        for r in range(2):
            bank = (LA, LB)[r]
            for i in range(4):
                hp = r * 4 + i
                mm(bank[:, i * 128:(i + 1) * 128], xend_[0:L, hp * 128:(hp + 1) * 128], xB[0:L, 1024 + r * 128:1024 + (r + 1) * 128])
            for h2 in range(2):
                rs = slice(64 * h2, 64 * h2 + 64)
                tt(Snat_[rs, 4 * r:4 * r + 4, :], Snat_[rs, 4 * r:4 * r + 4, :], dcyv[rs, 4 * r:4 * r + 4, h2:h2 + 1].bc([64, 4, 128]), ALU.mult, eng="pool")
                tt(Snat_[rs, 4 * r:4 * r + 4, :], Snat_[rs, 4 * r:4 * r + 4, :], bank[rs, :].rr("p (hp n) -> p hp n", hp=4), ALU.add)
            yield
        if last:
            dma(o_ssm[l, seqcol].rearrange("(hp h2) p n -> (h2 p) hp n", h2=2), Snat_.v())
        for g_ in range(2):
            bank = (LA, LB)[g_]
            mm(bank[0:L, :], xbc[:, 10 + g_, 0:L], STb_[:, g_ * 512:(g_ + 1) * 512])
            tt(yy_[0:L, g_ * 512:(g_ + 1) * 512].rr("t (h p) -> t h p", p=64), bank[0:L, :].rr("t (h p) -> t h p", p=64),
               ecum[:, 8 * g_:8 * g_ + 8].unsq(2).bc([L, 8, 64]), ALU.mult)
        yield
        tt(yy_[0:L, :], yy_[0:L, :], yi[0:L, :], ALU.add, eng="pool")
        for j in range(2):
            bank = (LA, LB)[j]
            if b is None:
                for k in range(8):
                    mm(bank[0:L, :], H[:, k, o:o + L], Wbuf[:, k, j * 512:(j + 1) * 512], start=(k == 0), stop=(k == 7))
                zt = y2[0:L, j * 512:(j + 1) * 512]
                act(zt, bank[0:L, :], AF.Exp, scale=-1.0)
                act(zt, zt, AF.Ln, bias=1.0)
                act(zt, zt, AF.Exp, scale=-1.0)
                tt(zs_[0:L, j * 512:(j + 1) * 512], bank[0:L, :], zt, ALU.mult)
            else:
                bankb = bank.v().bitcast(BF16)
                for kk in range(4):
                    tr(bankb[0:L, kk * 128:(kk + 1) * 128], zsfm_all[:, 4 * j + kk, o:o + L], identb.v())
                acp(zs_[0:L, j * 512:(j + 1) * 512], bankb[0:L, 0:512])
        yield
        tt(y2[0:L, :].rr("t (h p) -> t h p", p=64), xB[0:L, 0:1024].rr("t (h p) -> t h p", p=64),
           bc_D[0:L, :].unsq(2).bc([L, 16, 64]), ALU.mult, eng="pool")
        tt(yy_[0:L, :], yy_[0:L, :], y2[0:L, :], ALU.add, eng="pool")
        tt(yy_[0:L, :], yy_[0:L, :], zs_[0:L, :], ALU.mult)
        act(y2[0:L, :], yy_[0:L, :], AF.Square, accum_out=ss_[0:L, 0:1])
        act(ss_[0:L, 1:2], ss_[0:L, 0:1], AF.Ln, bias=eps_t[0:L, 0:1], scale=1.0 / 1024.0)
        act(ss_[0:L, 2:3], ss_[0:L, 1:2], AF.Exp, scale=-0.5)
        act(ynb_[0:L, :], yy_[0:L, :], AF.Copy, scale=ss_[0:L, 2:3])
        yield
        LCb = LC.v().bitcast(BF16)
        for k in range(8):
            tr(LCb[:, k * 128:k * 128 + L], ynb_[0:L, k * 128:(k + 1) * 128], identb[0:L, 0:L])
        tt(yfm_[:, :, 0:L], LCb[:, 0:1024].rr("p (k t) -> p k t", k=8)[:, :, 0:L],
           parA[:, 64:72].unsq(2).bc([128, 8, L]), ALU.mult)
        yield
        for r in range(2 if b is None else 0):
            bank = (LA, LB)[r]
            for oo in range(4):
                oc = r * 4 + oo
                for k in range(8):
                    mm(bank[:, oo * 128:oo * 128 + L], Wo[:, k, oc * 128:(oc + 1) * 128], yfm_[:, k, 0:L],
                       start=(k == 0), stop=(k == 7))
            for oo in range(4):
                oc = r * 4 + oo
                stt(xc[c][:, oc, o:o + L], bank[:, oo * 128:oo * 128 + L], mod[:, 16 + oc, seqcol:seqcol + 1],
                    xc[c][:, oc, o:o + L], ALU.mult, ALU.add)
            yield

    qk2 = ktm2 = vtm2 = osig2 = g42 = numi2 = None
    R4s = d4s = w4s = CTbs = nfbs = nums = num2s = ynbs = vws = None
    qk_s = kvo_s = yfms_m = kvo_all = None

    def ml_gen(l, tok0, L, seqcol, first, last, b, ctx):
        c = tok0 // 128
        sel = sel127 if L == 128 else sel7
        uidx = ctx
        ctx4 = uidx % 4
        p2 = uidx % 2
        qk, ktm, vtm, osig, g4, numi = qk2[ctx4], ktm2[ctx4], vtm2[ctx4], osig2[ctx4], g42[ctx4], numi2[ctx4]
        R4, d4, w4 = R4s[p2], d4s[p2], w4s[p2]
        CTb, nfb, num, num2, ynb, yfm, vw = CTbs[p2], nfbs[p2], nums[p2], num2s[p2], ynbs[p2], yfms[p2], vws[p2]
        if b is not None:
            qk = qk_s[b]
            yfm = yfms_m[b]
        if b is None:
            H = hbuf[ctx4]
            dma(H.v(), hd[c].v())
            o = 0
        else:
            H = hsamp
            o = tok0 - c * 128
        for r in range(2 if b is None else 0):
            bank = (EA, EB)[r]
            for oo in range(4):
                oc = r * 4 + oo
                for k in range(8):
                    mm(bank[:, oo * 128:oo * 128 + L], Wbuf[:, k, oc * 128:(oc + 1) * 128], H[:, k, o:o + L],
                       start=(k == 0), stop=(k == 7))
            bv = bank.v().rr("p (k t) -> p k t", k=4)[:, :, 0:L]
            if r == 0:
                acp(qk[:, 0:4, 0:L], bv)
            else:
                act(qk[:, 4:8, 0:L], bv, AF.Copy, scale=KSC)
        yield
        if b is not None:
            EAb = EA.v().bitcast(BF16)
            EBb = EB.v().bitcast(BF16)
            kva = View(kvo_s[0], kvo_all.ap)
            for kk in range(12):
                dstb = EAb if kk < 8 else EBb
                k2 = kk if kk < 8 else kk - 8
                P.do("pe", "transpose", out=dstb[0:L, k2 * 128:(k2 + 1) * 128], in_=kva[:, kk, o:o + L],
                     identity=identb.v(), extra_reads=kvo_s[1:])
            acp(ktm[0:L, :], EAb[0:L, 0:512])
            acp(vtm[0:L, :], EAb[0:L, 512:1024])
            acp(osig[0:L, :], EBb[0:L, 0:512])
        for j in range(3 if b is None else 0):
            bank = (EA, EB)[j % 2]
            for k in range(8):
                mm(bank[0:L, :], H[:, k, o:o + L], Wbuf[:, k, 512 + j * 512:512 + (j + 1) * 512], start=(k == 0), stop=(k == 7))
            if j == 0:
                act(ktm[0:L, :], bank[0:L, :], AF.Copy, scale=KSC)
            elif j == 1:
                acp(vtm[0:L, :], bank[0:L, :])
            else:
                act(osig[0:L, :], bank[0:L, :], AF.Exp, scale=-1.0)
                act(osig[0:L, :], osig[0:L, :], AF.Ln, bias=1.0)
                act(osig[0:L, :], osig[0:L, :], AF.Exp, scale=-1.0)
        for k in range(8):
            mm(EC[0:L, 0:8], H[:, k, o:o + L], Wbuf[:, k, 2048:2056], start=(k == 0), stop=(k == 7))
        yield
        ig = g4[0:L, 0:4]
        lf = g4[0:L, 4:8]
        bcs = g4[0:L, 8:12]
        a_s = g4[0:L, 12:16]
        rp = g4[0:L, 16:20]
        negrp = g4[0:L, 20:24]
        tt(ig, EC[0:L, 0:4], bc_ib[0:L, :], ALU.add)
        tt(lf, EC[0:L, 4:8], bc_fb[0:L, :], ALU.add)
        act(lf, lf, AF.Exp, scale=-1.0)
        act(lf, lf, AF.Ln, bias=1.0)
        ts(lf, lf, -1.0, None, ALU.mult)
        mm(EC[0:L, 8:12], ucum[0:L, 0:L], lf)
        cp(bcs, EC[0:L, 8:12])
        tt(a_s, ig, bcs, ALU.subtract)
        tt(R4[0:L, :, 0:L], identf[0:L, 0:L].unsq(1).bc([L, 4, L]), a_s.unsq(2).bc([L, 4, L]), ALU.mult)
        outv = EB.v().rr("p (h t) -> p h t", h=4)[0:L, :, 0:L]
        if L == 128:
            mm(outv, onesf[0:L, 0:L], R4[0:L, :, 0:L], start=True, stop=False)
            mm(outv, identb[0:L, 0:L], negT[0:L, :, 0:L], start=False, stop=True)
        else:
            for hh in range(4):
                mm(outv[:, hh, :], onesf[0:L, 0:L], R4[0:L, hh, 0:L], start=True, stop=False)
                mm(outv[:, hh, :], identb[0:L, 0:L], negT[0:L, hh, 0:L], start=False, stop=True)
        P.do("dve", "tensor_reduce", out=rp, in_=outv, axis=AX.X, op=ALU.max)
        ts(negrp, rp, -1.0, None, ALU.mult)
        yield
        tt(R4[0:L, :, 0:L], identf[0:L, 0:L].unsq(1).bc([L, 4, L]), negrp.unsq(2).bc([L, 4, L]), ALU.mult)
        outv2 = EA.v().rr("p (h t) -> p h t", h=4)[0:L, :, 0:L]
        if L == 128:
            mm(outv2, onesf[0:L, 0:L], R4[0:L, :, 0:L], start=True, stop=False)
            mm(outv2, identb[0:L, 0:L], negrep[0:L, :, 0:L], start=False, stop=True)
        else:
            for hh in range(4):
                mm(outv2[:, hh, :], onesf[0:L, 0:L], R4[0:L, hh, 0:L], start=True, stop=False)
                mm(outv2[:, hh, :], identb[0:L, 0:L], negrep[0:L, hh, 0:L], start=False, stop=True)
        tt(d4[0:L, :, 0:L], outv2, a_s.unsq(2).bc([L, 4, L]), ALU.add)
        act(d4[0:L, :, 0:L], d4[0:L, :, 0:L], AF.Exp)
        outc = EY[:, 0:512].rr("p (h t) -> p h t", h=4)
        for h in range(4):
            mm(outc[0:L, h, 0:L], qk[:, 4 + h, 0:L], qk[:, h, 0:L])
        tt(w4[0:L, :, 0:L], d4[0:L, :, 0:L], outc[0:L, :, 0:L], ALU.mult)
        for h in range(4):
            mm(EY[0:L, 512 + h * 128:512 + (h + 1) * 128], w4[0:L, h, 0:L], vtm[0:L, h * 128:(h + 1) * 128])
        for h in range(4):
            mm(EC[0:L, 16 + h:17 + h], w4[0:L, h, 0:L], onesb[0:L, :])
        acp(numi[0:L, 0:512], EY[0:L, 512:1024])
        cp(numi[0:L, 512:516], EC[0:L, 16:20])
        yield "EARLY_DONE"
        if b is not None:
            dma(Cnat.v(), st_mc[l, b].rearrange("h v d -> v h d"))
            cp(nfm.v(), mn_io[:, 4 * b:4 * b + 4], eng="pool")
            dma(mbc.v(), st_mm[l, b:b + 1, :].to_broadcast([128, 4]))
        elif first:
            memset(Cnat.v(), 0.0)
            memset(nfm.v(), 0.0)
            memset(mbc.v(), 0.0)
        r_ = g4[0:L, 24:28]
        f_ = g4[0:L, 28:32]
        inter = g4[0:L, 32:36]
        mt = g4[0:L, 36:40]
        emt = g4[0:L, 40:44]
        den = g4[0:L, 44:48]
        den2 = g4[0:L, 48:52]
        ssq = g4[0:L, 52:56]
        tt(r_, rp, mbc[0:L, :], ALU.max)
        tt(f_, rp, r_, ALU.subtract)
        act(f_, f_, AF.Exp)
        tt(inter, mbc[0:L, :], r_, ALU.subtract)
        act(inter, inter, AF.Exp)
        tt(mt, bcs, r_, ALU.add)
        act(emt, mt, AF.Exp, scale=-1.0)
        for h in range(4):
            tr(LC[:, h * 128:(h + 1) * 128], Cnat[:, h, :], identf.v())
        acp(CTb.v(), LC.v().rr("p (h v) -> p h v", h=4))
        cp(nfb.v(), nfm.v())
        for h in range(4):
            mm(LA[0:L, h * 128:(h + 1) * 128], qk[:, h, 0:L], CTb[:, h, :])
        for h in range(4):
            mm(LB[0:L, h:h + 1], qk[:, h, 0:L], nfb[:, h:h + 1])
        yield
        tt(num[0:L, :].rr("t (h v) -> t h v", h=4), numi[0:L, 0:512].rr("t (h v) -> t h v", h=4),
           f_.unsq(2).bc([L, 4, 128]), ALU.mult)
        tt(num2[0:L, :].rr("t (h v) -> t h v", h=4), LA[0:L, :].rr("t (h v) -> t h v", h=4),
           inter.unsq(2).bc([L, 4, 128]), ALU.mult)
        tt(num[0:L, :], num[0:L, :], num2[0:L, :], ALU.add, eng="pool")
        tt(den, numi[0:L, 512:516], f_, ALU.mult)
        tt(den2, LB[0:L, 0:4], inter, ALU.mult)
        tt(den, den, den2, ALU.add)
        act(den, den, AF.Abs)
        tt(den, den, emt, ALU.max)
        P.do("dve", "reciprocal", out=den, in_=den)
        tt(num[0:L, :].rr("t (h v) -> t h v", h=4), num[0:L, :].rr("t (h v) -> t h v", h=4),
           den.unsq(2).bc([L, 4, 128]), ALU.mult)
        yield
        tt(num2[0:L, :], num[0:L, :], num[0:L, :], ALU.mult, eng="pool")
        P.do("dve", "tensor_reduce", out=ssq, in_=num2[0:L, :].rr("t (h v) -> t h v", h=4), axis=AX.X, op=ALU.add)
        act(ssq, ssq, AF.Ln, bias=eps_t[0:L, 0:1], scale=1.0 / 128.0)
        act(ssq, ssq, AF.Exp, scale=-0.5)
        tt(num[0:L, :].rr("t (h v) -> t h v", h=4), num[0:L, :].rr("t (h v) -> t h v", h=4),
           ssq.unsq(2).bc([L, 4, 128]), ALU.mult)
        tt(num[0:L, :], num[0:L, :], bc_mlw[0:L, :], ALU.mult, eng="pool")
        tt(ynb[0:L, 0:512], num[0:L, :], osig[0:L, :], ALU.mult)
        LCb = LC.v().bitcast(BF16)
        for k in range(4):
            tr(LCb[:, k * 128:k * 128 + L], ynb[0:L, k * 128:(k + 1) * 128], identb[0:L, 0:L])
        cp(yfm[:, 0:4, 0:L], LCb[:, 0:512].rr("p (k t) -> p k t", k=4)[:, :, 0:L])
        yield
        for r in range(2 if b is None else 0):
            bank = (LA, LB)[r]
            for oo in range(4):
                oc = r * 4 + oo
                for k in range(4):
                    mm(bank[:, oo * 128:oo * 128 + L], Wo[:, k, oc * 128:(oc + 1) * 128], yfm[:, k, 0:L],
                       start=(k == 0), stop=(k == 3))
            for oo in range(4):
                oc = r * 4 + oo
                stt(xc[c][:, oc, o:o + L], bank[:, oo * 128:oo * 128 + L], mod[:, 16 + oc, seqcol:seqcol + 1],
                    xc[c][:, oc, o:o + L], ALU.mult, ALU.add)
            yield
        bm = g4[0:L, 56:64]
        cp(bm[:, 0:4], bcs)
        cp(bm[:, 4:8], mt)
        mm(LC[:, 0:8], sel[0:L, :], bm)
        last8 = g4[:, 64:72]
        cp(last8, LC[:, 0:8])
        wend = g4[0:L, 72:76]
        tt(wend, last8[0:L, 0:4], last8[0:L, 4:8], ALU.subtract)
        tt(wend, wend, a_s, ALU.add)
        act(wend, wend, AF.Exp)
        dc = g4[:, 76:80]
        tt(dc, last8[:, 0:4], mbc.v(), ALU.add)
        tt(dc, dc, last8[:, 4:8], ALU.subtract)
        act(dc, dc, AF.Exp)
        cp(mbc.v(), last8[:, 4:8])
        tt(vw[0:L, :].rr("t (h v) -> t h v", h=4), vtm[0:L, :].rr("t (h v) -> t h v", h=4),
           wend.unsq(2).bc([L, 4, 128]), ALU.mult)
        wendb = g4[0:L, 80:84].bitcast(BF16)[:, 0:4]
        cp(wendb, wend)
        yield
        for h in range(4):
            mm(LA[:, h * 128:(h + 1) * 128], vw[0:L, h * 128:(h + 1) * 128], ktm[0:L, h * 128:(h + 1) * 128])
        for h in range(4):
            mm(LB[:, h:h + 1], ktm[0:L, h * 128:(h + 1) * 128], wendb[:, h:h + 1])
        tt(Cnat.v(), Cnat.v(), dc.unsq(2).bc([128, 4, 128]), ALU.mult, eng="pool")
        tt(Cnat.v(), Cnat.v(), LA.v().rr("p (h d) -> p h d", h=4), ALU.add)
        tt(nfm.v(), nfm.v(), dc, ALU.mult)
        tt(nfm.v(), nfm.v(), LB[:, 0:4], ALU.add)
        if last:
            dma(o_mc[l, seqcol].rearrange("h v d -> v h d"), Cnat.v())
            if b is not None:
                cp(mn_io[:, 4 * b:4 * b + 4], nfm.v(), eng="pool")
            else:
                dma_slow(o_mn[l, seqcol].rearrange("h d -> d h"), nfm.v())
            dma(o_mm[l, seqcol:seqcol + 1, :], mbc[0:1, :])
        yield

    la2 = lu2 = lg2 = lxs = lx_t = None
    lpads = lx_ts = lxss = lxbs = lris = lhs = yfms = EAs = EBs = LAs = LBs = None
    lpads_s = lgs_s = yfms_l = None

    def lru_gen(l, tok0, L, seqcol, first, last, b, ctx):
        c = tok0 // 128
        uidx = ctx
        ctx4 = uidx % 4
        p2 = uidx % 2
        la, lu, lg = la2[ctx4], lu2[ctx4], lg2[ctx4]
        lpad, lx_t, lxs, lxb, lri = lpads[p2], lx_ts[p2], lxss[p2], lxbs[p2], lris[p2]
        lpad_next = lpads[1 - p2]
        lh, yfm = lhs[p2], yfms[p2]
        if b is not None:
            lpad = lpads_s[b]
            lg = lgs_s[b]
            yfm = yfms_l[b]
        EA, EB = EAs[p2], EBs[p2]
        LA, LB = LAs[p2], LBs[p2]
        if b is None:
            H = hbuf[ctx4]
            dma(H.v(), hd[c].v())
            o = 0
        else:
            H = hsamp
            o = tok0 - c * 128
        if b is not None:
            cp(lpad[:, :, 0:3], lconv_io[:, :, 3 * b:3 * b + 3], eng="pool")
        elif first:
            memset(lpad[:, :, 0:3], 0.0)
        for r in range(2 if b is None else 0):
            bank = (EA, EB)[r]
            for oo in range(4):
                oc = r * 4 + oo
                for k in range(8):
                    mm(bank[:, oo * 128:oo * 128 + L], Wbuf[:, k, oc * 128:(oc + 1) * 128], H[:, k, o:o + L],
                       start=(k == 0), stop=(k == 7))
            bv = bank.v().rr("p (k t) -> p k t", k=4)[:, :, 0:L]
            if r == 0:
                acp(lpad[:, :, 3:3 + L], bv)
            else:
                act(lg[:, :, 0:L], bv, AF.Gelu)
        yield
        for k in range(4):
            ts(lxs[k][:, 0:L], lpad[:, k, 0:L], parA[:, 72 + k:73 + k], parA[:, 88 + k:89 + k], ALU.mult, ALU.add)
        for j in range(1, 4):
            for k in range(4):
                stt(lxs[k][:, 0:L], lpad[:, k, j:j + L], parA[:, 72 + j * 4 + k:73 + j * 4 + k], lxs[k][:, 0:L], ALU.mult, ALU.add)
        lxall = View(lxs[0], lx_t.ap[:, :, 0:L])
        P.do("act", "activation", out=lxb[:, :, 0:L], in_=lxall, func=AF.Copy, extra_reads=lxs[1:])
        if b is not None:
            cp(lconv_io[:, :, 3 * b:3 * b + 3], lpad[:, :, L:L + 3], eng="pool")
        elif last:
            for j in range(3):
                dma_slow(o_lconv[l, seqcol, j].rearrange("(k p) -> p k", p=128), lpad[:, :, L + j])
        else:
            cp(lpad_next[:, :, 0:3], lpad[:, :, L:L + 3], eng="pool")
        yield
        for k in range(4):
            mm(EA[:, k * 128:k * 128 + L], wg[:, k, :], lxb[:, k, 0:L])
        for k in range(4):
            mm(EB[:, k * 128:k * 128 + L], wg[:, 4 + k, :], lxb[:, k, 0:L])
        tt(lri[:, 0:4, 0:L], EA.v().rr("p (k t) -> p k t", k=4)[:, :, 0:L], parA[:, 92:96].unsq(2).bc([128, 4, L]), ALU.add)
        tt(lri[:, 4:8, 0:L], EB.v().rr("p (k t) -> p k t", k=4)[:, :, 0:L], parA[:, 96:100].unsq(2).bc([128, 4, L]), ALU.add)
        act(lri[:, :, 0:L], lri[:, :, 0:L], AF.Exp, scale=-1.0)
        act(lri[:, :, 0:L], lri[:, :, 0:L], AF.Ln, bias=1.0)
        act(lri[:, :, 0:L], lri[:, :, 0:L], AF.Exp, scale=-1.0)
        for k in range(4):
            act(la[:, k, 0:L], lri[:, k, 0:L], AF.Exp, scale=lru_c1[:, k:k + 1])
        yield
        tt(lu[:, :, 0:L], la[:, :, 0:L], la[:, :, 0:L], ALU.mult)
        ts(lu[:, :, 0:L], lu[:, :, 0:L], -1.0, 1.0, ALU.mult, ALU.add)
        ts(lu[:, :, 0:L], lu[:, :, 0:L], 1e-18, None, ALU.max)
        act(lu[:, :, 0:L], lu[:, :, 0:L], AF.Ln)
        act(lu[:, :, 0:L], lu[:, :, 0:L], AF.Exp, scale=0.5)
        if b is None and first:
            memset(lu[:, :, 0:1], 1.0)
        tt(lu[:, :, 0:L], lu[:, :, 0:L], lri[:, 4:8, 0:L], ALU.mult)
        P.do("dve", "tensor_tensor", out=lu[:, :, 0:L], in0=lu[:, :, 0:L], in1=lxall, op=ALU.mult, extra_reads=lxs[1:])
        yield "EARLY_DONE"
        if b is not None:
            cp(hstate.v(), lh_io[:, :, b], eng="pool")
        elif first:
            memset(hstate.v(), 0.0)
        for k in range(4):
            P.do("dve", "tensor_tensor_scan", out=lh[:, k, 0:L], data0=la[:, k, 0:L], data1=lu[:, k, 0:L],
                 initial=hstate[:, k:k + 1], op0=ALU.mult, op1=ALU.add)
        cp(hstate.v(), lh[:, :, L - 1])
        if b is not None:
            cp(lh_io[:, :, b], hstate.v(), eng="pool")
        elif last:
            dma_slow(o_lh[l, seqcol].rearrange("(k p) -> p k", p=128), hstate.v())
        tt(yfm[:, 0:4, 0:L], lh[:, :, 0:L], lg[:, :, 0:L], ALU.mult)
        yield
        for r in range(2 if b is None else 0):
            bank = (LA, LB)[r]
            for oo in range(4):
                oc = r * 4 + oo
                for k in range(4):
                    mm(bank[:, oo * 128:oo * 128 + L], Wo[:, k, oc * 128:(oc + 1) * 128], yfm[:, k, 0:L],
                       start=(k == 0), stop=(k == 3))
            for oo in range(4):
                oc = r * 4 + oo
                stt(xc[c][:, oc, o:o + L], bank[:, oo * 128:oo * 128 + L], mod[:, 16 + oc, seqcol:seqcol + 1],
                    xc[c][:, oc, o:o + L], ALU.mult, ALU.add)
            yield

    for l in range(DEPTH):
        P.barrier()
        adaslabs, sqs, rstds, ntmps = alloc_norm()
        load_fm(parA.v(), [ada_b[l].rearrange("(k p) -> k p", p=128), norm1_w[l].rearrange("(k p) -> k p", p=128),
                           norm2_w[l].rearrange("(k p) -> k p", p=128), ssd_norm_w[l].rearrange("(k p) -> k p", p=128),
                           lru_conv_w[l].rearrange("j (k p) -> (j k) p", p=128), lru_conv_b[l].rearrange("(k p) -> k p", p=128),
                           lru_ba[l].rearrange("(k p) -> k p", p=128), lru_bx[l].rearrange("(k p) -> k p", p=128),
                           lru_lambda[l].rearrange("(k p) -> k p", p=128)])
        load_fm(parB.v(), [ssd_conv_w[l].rearrange("j (k p) -> (j k) p", p=128), ssd_conv_b[l].rearrange("(k p) -> k p", p=128)])
        load_bc(bc_dtb.v(), ssd_dt_bias[l:l + 1, :], 16)
        load_bc(bc_A.v(), ssd_a_log[l:l + 1, :], 16)
        act(bc_A.v(), bc_A.v(), AF.Exp)
        ts(bc_A.v(), bc_A.v(), -1.0, None, ALU.mult)
        load_bc(bc_D.v(), ssd_d[l:l + 1, :], 16)
        load_bc(bc_ib.v(), ml_i_bias[l:l + 1, :], 4)
        load_bc(bc_fb.v(), ml_f_bias[l:l + 1, :], 4)
        act(lru_c1.v(), parA[:, 100:104], AF.Exp, scale=-1.0)
        act(lru_c1.v(), lru_c1.v(), AF.Ln, bias=1.0)
        ts(lru_c1.v(), lru_c1.v(), -8.0, None, ALU.mult)
        if l == 0:
            for sl in range(12):
                adaslab = adaslabs[sl % 2]
                dma(adaslab.v(), ada_w[l, :, sl * 512:(sl + 1) * 512].rearrange("(k p) n -> p k n", p=128), eng="pool")
                for oc in range(4):
                    j = sl * 4 + oc
                    for k in range(8):
                        mm(PB[:, j * 32:j * 32 + 17], adaslab[:, k, oc * 128:(oc + 1) * 128], cs_fm[:, k, :],
                           start=(k == 0), stop=(k == 7))
            tt(mod.v(), PB[:, 0:1536].rr("p (j s) -> p j s", j=48)[:, :, 0:17], parA[:, 0:48].unsq(2).bc([128, 48, 17]), ALU.add)
        else:
            tt(mod.v(), modraw.v(), parA[:, 0:48].unsq(2).bc([128, 48, 17]), ALU.add)
            P.atop = P.asize
        ts(s1.v(), mod[:, 8:16, :], 1.0, None, ALU.add)
        tt(s1.v(), s1.v(), parA[:, 48:56].unsq(2).bc([128, 8, 17]), ALU.mult)
        ts(s2.v(), mod[:, 32:40, :], 1.0, None, ALU.add)
        tt(s2.v(), s2.v(), parA[:, 56:64].unsq(2).bc([128, 8, 17]), ALU.mult)
        hst = [P.ov("hst%d" % i, [128, 8, 128], BF16) for i in range(2)]
        for c in range(NCH):
            rmsnorm_mod(c, s1, 0, hst[c % 2])
            dma(hd[c].v(), hst[c % 2].v())
        P.barrier()
        Wbuf = P.ov("Wbuf", [128, 8, 2576], BF16)
        Wo = P.ov("Wo", [128, 8, 1024], BF16)
        hbuf = [P.ov("hbuf%d" % i, [128, 8, 128], BF16) for i in range(2)]
        hsamp = P.ov("hsamp", [128, 8, 128], BF16)
        dma(hsamp.v(), hd[16].v())
        xpad = P.ov("xpad", [128, 12, 131], F32)
        cacc_t = P.ov("cacc", [128, 12, 128], F32)
        caccs = [Buf(cacc_t.ap[:, i, :], "cacc%d" % i) for i in range(12)]
        ctmp_t = P.ov("ctmp", [128, 12, 128], F32)
        cts = [Buf(ctmp_t.ap[:, 4 * i:4 * i + 4, :], "ct%d" % i) for i in range(3)]
        Rbs = [cts[2], P.ov("Rb1", [128, 4, 128], F32)]
        decs = [cts[0], cts[1]]
        MT = P.ov("MT", [128, 16, 128], BF16)
        xbc2 = [P.ov("xbc%d" % i, [128, 12, 128], BF16) for i in range(2)]
        xB2 = [P.ov("xB%d" % i, [128, 1280], BF16) for i in range(2)]
        xdt2 = [P.ov("xdt%d" % i, [128, 1024], BF16) for i in range(2)]
        sm2 = [P.ov("sm%d" % i, [128, 128], F32) for i in range(2)]
        yi2 = [P.ov("yi%d" % i, [128, 1024], F32) for i in range(2)]
        zs = P.ov("zs", [128, 1024], BF16)
        yy = P.ov("yy", [128, 1024], F32)
        ynb = P.ov("ynb", [128, 1024], BF16)
        yfm = P.ov("yfm", [128, 8, 128], BF16)
        Snat = P.ov("Snat", [128, 8, 128], F32)
        ss = P.ov("ss", [128, 8], F32)
        xend = zs
        STb = ynb
        EA, EB, EC = Buf(PA.ap[:, 0:512], "EA"), Buf(PA.ap[:, 512:1024], "EB"), Buf(PC.ap, "EC")
        EY = Buf(PB.ap[:, 0:1024], "EY")
        LA, LB, LC = Buf(PB.ap[:, 1024:1536], "LA"), Buf(PB.ap[:, 1536:2048], "LB"), Buf(PF.ap, "LC")
        dma(Wbuf[:, :, 0:2576], w_in[l, :, 0:2576].rearrange("(k p) n -> p k n", p=128), eng="pool")
        dma(Wo.v(), w_out[l, 0:1024, :].rearrange("(k p) n -> p k n", p=128), eng="pool")
        scv = st_sconv[l].rearrange("b j c -> (b j) c")
        dma(yy[0:48, :], scv[:, 0:1024])
        dma(yi2[0][0:48, 0:512], scv[:, 1024:1536])
        for k in range(12):
            src = yy[0:48, k * 128:(k + 1) * 128] if k < 8 else yi2[0][0:48, (k - 8) * 128:(k - 7) * 128]
            bank = EA if k < 8 else EB
            kk = k if k < 8 else k - 8
            tr(bank[:, kk * 48:(kk + 1) * 48], src, identf[0:48, 0:48])
        cp(sconv_io[:, 0:8, :], EA[:, 0:384].rr("p (k t) -> p k t", k=8))
        cp(sconv_io[:, 8:12, :], EB[:, 0:192].rr("p (k t) -> p k t", k=4))
        ulist = list(units())
        run_pipeline([ssd_gen(l, tok0, L, seqcol, first, last, b, i % 2)
                      for i, (tok0, L, seqcol, first, last, b) in enumerate(ulist[0:16])])
        P.barrier()
        Wbuf = P.ov("Wbuf", [128, 8, 2576], BF16)
        Wo = P.ov("Wo", [128, 8, 1024], BF16)
        hbuf = [P.ov("hbuf%d" % i, [128, 8, 128], BF16) for i in range(2)]
        hsamp = P.ov("hsamp", [128, 8, 128], BF16)
        xpad_all = P.ov("xpad_all", [128, 12, 176], F32)
        xpa4 = xpad_all.ap.rearrange("p k (b j) -> p k b j", b=16)
        xpads = [Buf(xpa4[:, :, b_, :], "xpad_s%d" % b_) for b_ in range(16)]
        cacc_t = P.ov("cacc", [128, 12, 8], F32)
        caccs = [Buf(cacc_t.ap[:, i, :], "cacc%d" % i) for i in range(12)]
        ctmp_t = P.ov("ctmp", [128, 12, 8], F32)
        cts = [Buf(ctmp_t.ap[:, 4 * i:4 * i + 4, :], "ct%d" % i) for i in range(3)]
        Rbs = [cts[2], P.ov("Rb1", [128, 4, 8], F32)]
        decs = [cts[0], cts[1]]
        MT = P.ov("MT", [128, 16, 8], BF16)
        xbc2 = [P.ov("xbc%d" % i, [128, 12, 8], BF16) for i in range(2)]
        xB2 = [P.ov("xB%d" % i, [128, 1280], BF16) for i in range(2)]
        xdt2 = [P.ov("xdt%d" % i, [128, 1024], BF16) for i in range(2)]
        sm2 = [P.ov("sm%d" % i, [128, 128], F32) for i in range(2)]
        yi2 = [P.ov("yi%d" % i, [128, 1024], F32) for i in range(2)]
        zss = [P.ov("zs%d" % i, [128, 1024], BF16) for i in range(2)]
        yys = [P.ov("yy%d" % i, [128, 1024], F32) for i in range(2)]
        ynbs2 = [P.ov("ynb%d" % i, [128, 1024], BF16) for i in range(2)]
        Snats = [P.ov("Snat%d" % i, [128, 8, 128], F32) for i in range(2)]
        sss = [P.ov("ss%d" % i, [128, 8], F32) for i in range(2)]
        yfm_all = P.ov("yfm_all", [128, 8, 128], BF16)
        yfms_s = [Buf(yfm_all.ap[:, :, 8 * b_:8 * b_ + 8], "yfm_s%d" % b_) for b_ in range(16)]
        zsfm_all = P.ov("zsfm_all", [128, 8, 128], BF16)
        ztmp = P.ov("ztmp", [128, 512], F32)
        yy = yys[0]
        EA, EB, EC = Buf(PA.ap[:, 0:512], "EA"), Buf(PA.ap[:, 512:1024], "EB"), Buf(PC.ap, "EC")
        EY = Buf(PB.ap[:, 0:1024], "EY")
        LA, LB, LC = Buf(PB.ap[:, 1024:1536], "LA"), Buf(PB.ap[:, 1536:2048], "LB"), Buf(PF.ap, "LC")
        for r in range(3):
            bank = (EA, EB)[r % 2]
            for oo in range(4):
                oc = r * 4 + oo
                for k in range(8):
                    mm(bank[:, oo * 128:(oo + 1) * 128], Wbuf[:, k, 1024 + oc * 128:1024 + (oc + 1) * 128], hsamp[:, k, :],
                       start=(k == 0), stop=(k == 7))
            for oo in range(4):
                oc = r * 4 + oo
                P.do("act", "activation", out=View(xpads[0], xpa4[:, oc, :, 3:11]),
                     in_=bank[:, oo * 128:(oo + 1) * 128].rr("p (b t) -> p b t", b=16), func=AF.Copy,
                     extra_writes=xpads[1:])
        for r in range(2):
            bank = (LA, LB)[r]
            for oo in range(4):
                oc = r * 4 + oo
                for k in range(8):
                    mm(bank[:, oo * 128:(oo + 1) * 128], Wbuf[:, k, oc * 128:(oc + 1) * 128], hsamp[:, k, :],
                       start=(k == 0), stop=(k == 7))
            act(ztmp.v(), bank.v(), AF.Exp, scale=-1.0)
            act(ztmp.v(), ztmp.v(), AF.Ln, bias=1.0)
            act(ztmp.v(), ztmp.v(), AF.Exp, scale=-1.0)
            tt(zsfm_all[:, 4 * r:4 * r + 4, :], bank.v().rr("p (k t) -> p k t", k=4), ztmp.v().rr("p (k t) -> p k t", k=4), ALU.mult)
        run_pipeline([ssd_gen(l, tok0, L, seqcol, first, last, b, i % 2)
                      for i, (tok0, L, seqcol, first, last, b) in enumerate(ulist[16:])])
        yfa = View(yfms_s[0], yfm_all.ap)
        for r in range(2):
            bank = (LA, LB)[r]
            for oo in range(4):
                oc = r * 4 + oo
                for k in range(8):
                    P.do("pe", "matmul", out=bank[:, oo * 128:(oo + 1) * 128], lhsT=Wo[:, k, oc * 128:(oc + 1) * 128],
                         rhs=yfa[:, k, :], start=(k == 0), stop=(k == 7), extra_reads=yfms_s[1:])
            for oo in range(4):
                oc = r * 4 + oo
                tt(ztmp[:, 0:128].rr("p (b t) -> p b t", t=8), bank[:, oo * 128:(oo + 1) * 128].rr("p (b t) -> p b t", t=8),
                   mod[:, 16 + oc, 1:17].unsq(2).bc([128, 16, 8]), ALU.mult)
                tt(xc[16][:, oc, :], xc[16][:, oc, :], ztmp[:, 0:128], ALU.add)
        for k in range(12):
            bank = (EA, EB, LA)[k // 4]
            tr(bank[0:48, (k % 4) * 128:(k % 4 + 1) * 128], sconv_io[:, k, :], identf.v())
        cp(yy[0:48, 0:512], EA[0:48, :])
        cp(yy[0:48, 512:1024], EB[0:48, :])
        cp(yi2[0][0:48, 0:512], LA[0:48, :])
        sco = o_sconv[l, 1:1 + NS].rearrange("b j c -> (b j) c")
        dma(sco[:, 0:1024], yy[0:48, :])
        dma(sco[:, 1024:1536], yi2[0][0:48, 0:512])
        P.barrier()
        Wbuf = P.ov("Wbuf", [128, 8, 2056], BF16)
        Wo = P.ov("Wo", [128, 4, 1024], BF16)
        hsamp = P.ov("hsamp", [128, 8, 128], BF16)
        dma(hsamp.v(), hd[16].v())
        hbuf = [P.ov("hbuf%d" % i, [128, 8, 128], BF16) for i in range(4)]
        R4s = [P.ov("R4%d" % i, [128, 4, 128], F32) for i in range(2)]
        d4s = [P.ov("d4%d" % i, [128, 4, 128], F32) for i in range(2)]
        w4s = [P.ov("w4%d" % i, [128, 4, 128], BF16) for i in range(2)]
        qk2 = [P.ov("qk%d" % i, [128, 8, 128], BF16) for i in range(4)]
        ktm2 = [P.ov("ktm%d" % i, [128, 512], BF16) for i in range(4)]
        vtm2 = [P.ov("vtm%d" % i, [128, 512], BF16) for i in range(4)]
        osig2 = [P.ov("osig%d" % i, [128, 512], F32) for i in range(4)]
        g42 = [P.ov("g4%d" % i, [128, 128], F32) for i in range(4)]
        numi2 = [P.ov("numi%d" % i, [128, 520], F32) for i in range(4)]
        CTbs = [P.ov("CTb%d" % i, [128, 4, 128], BF16) for i in range(2)]
        nfbs = [P.ov("nfb%d" % i, [128, 4], BF16) for i in range(2)]
        nums = [P.ov("num%d" % i, [128, 512], F32) for i in range(2)]
        num2s = [P.ov("num2%d" % i, [128, 512], F32) for i in range(2)]
        ynbs = [P.ov("ynb%d" % i, [128, 512], BF16) for i in range(2)]
        yfms = [P.ov("yfm%d" % i, [128, 4, 128], BF16) for i in range(2)]
        vws = [P.ov("vw%d" % i, [128, 512], BF16) for i in range(2)]
        num = nums[0]
        Cnat = P.ov("Cnat", [128, 4, 128], F32)
        nfm = P.ov("nfm", [128, 4], F32)
        mbc = P.ov("mbc", [128, 4], F32)
        bc_mlw = P.ov("bc_mlw", [128, 512], F32)
        load_bc(bc_mlw.v(), ml_norm_w[l:l + 1, :], 512)
        EA, EB, EC = Buf(PA.ap[:, 0:512], "EA"), Buf(PA.ap[:, 512:1024], "EB"), Buf(PC.ap, "EC")
        EY = Buf(PB.ap[:, 0:1024], "EY")
        LA, LB, LC = Buf(PB.ap[:, 1024:1536], "LA"), Buf(PB.ap[:, 1536:2048], "LB"), Buf(PF.ap, "LC")
        dma(Wbuf[:, :, 0:2056], w_in[l, :, 2576:4632].rearrange("(k p) n -> p k n", p=128), eng="pool")
        dma(Wo[:, 0:4, :], w_out[l, 1024:1536, :].rearrange("(k p) n -> p k n", p=128), eng="pool")
        dma(num[0:64, 0:128], st_mn[l].rearrange("b h d -> (b h) d"))
        tr(EA[:, 0:64], num[0:64, 0:128], identf[0:64, 0:64])
        cp(mn_io.v(), EA[:, 0:64])
        qk_all = P.ov("qk_all", [128, 8, 128], BF16)
        qk_s = [Buf(qk_all.ap[:, :, 8 * b_:8 * b_ + 8], "qk_s%d" % b_) for b_ in range(16)]
        kvo_all = P.ov("kvo_all", [128, 12, 128], BF16)
        kvo_s = [Buf(kvo_all.ap[:, :, 8 * b_:8 * b_ + 8], "kvo_s%d" % b_) for b_ in range(16)]
        yfm_all_m = P.ov("yfm_all_m", [128, 4, 128], BF16)
        yfms_m = [Buf(yfm_all_m.ap[:, :, 8 * b_:8 * b_ + 8], "yfm_m%d" % b_) for b_ in range(16)]
        for r in range(5):
            bank = (EA, EB)[r % 2]
            for oo in range(4):
                oc = (r if r < 2 else r - 1) * 4 + oo
                for k in range(8):
                    mm(bank[:, oo * 128:(oo + 1) * 128], Wbuf[:, k, oc * 128:(oc + 1) * 128], hsamp[:, k, :],
                       start=(k == 0), stop=(k == 7))
            bv = bank.v().rr("p (k t) -> p k t", k=4)
            if r == 0:
                P.do("act", "activation", out=View(qk_s[0], qk_all.ap[:, 0:4, :]), in_=bv, func=AF.Copy, extra_writes=qk_s[1:])
            elif r == 1:
                P.do("act", "activation", out=View(qk_s[0], qk_all.ap[:, 4:8, :]), in_=bv, func=AF.Copy, scale=KSC, extra_writes=qk_s[1:])
            elif r == 2:
                P.do("act", "activation", out=View(kvo_s[0], kvo_all.ap[:, 0:4, :]), in_=bv, func=AF.Copy, scale=KSC, extra_writes=kvo_s[1:])
            elif r == 3:
                P.do("act", "activation", out=View(kvo_s[0], kvo_all.ap[:, 4:8, :]), in_=bv, func=AF.Copy, extra_writes=kvo_s[1:])
            else:
                nt = nums[0].v()
                act(nt, bank.v(), AF.Exp, scale=-1.0)
                act(nt, nt, AF.Ln, bias=1.0)
                P.do("act", "activation", out=View(kvo_s[0], kvo_all.ap[:, 8:12, :]), in_=nt.rr("p (k t) -> p k t", k=4),
                     func=AF.Exp, scale=-1.0, extra_writes=kvo_s[1:])
        run_pipeline([ml_gen(l, tok0, L, seqcol, first, last, b, i)
                      for i, (tok0, L, seqcol, first, last, b) in enumerate(units())])
        yfam = View(yfms_m[0], yfm_all_m.ap)
        for r in range(2):
            bank = (LA, LB)[r]
            for oo in range(4):
                oc = r * 4 + oo
                for k in range(4):
                    P.do("pe", "matmul", out=bank[:, oo * 128:(oo + 1) * 128], lhsT=Wo[:, k, oc * 128:(oc + 1) * 128],
                         rhs=yfam[:, k, :], start=(k == 0), stop=(k == 3), extra_reads=yfms_m[1:])
            for oo in range(4):
                oc = r * 4 + oo
                tt(nums[0][:, 0:128].rr("p (b t) -> p b t", t=8), bank[:, oo * 128:(oo + 1) * 128].rr("p (b t) -> p b t", t=8),
                   mod[:, 16 + oc, 1:17].unsq(2).bc([128, 16, 8]), ALU.mult)
                tt(xc[16][:, oc, :], xc[16][:, oc, :], nums[0][:, 0:128], ALU.add)
        tr(EA[0:64, 0:128], mn_io.v(), identf.v())
        cp(num[0:64, 0:128], EA[0:64, 0:128])
        dma(o_mn[l, 1:1 + NS].rearrange("b h d -> (b h) d"), num[0:64, 0:128])
        P.barrier()
        Wbuf = P.ov("Wbuf", [128, 8, 1024], BF16)
        Wo = P.ov("Wo", [128, 4, 1024], BF16)
        hbuf = [P.ov("hbuf%d" % i, [128, 8, 128], BF16) for i in range(2)]
        hsamp = P.ov("hsamp", [128, 8, 128], BF16)
        dma(hsamp.v(), hd[16].v())
        hbuf = [P.ov("hbuf%d" % i, [128, 8, 128], BF16) for i in range(4)]
        lpads = [P.ov("lpad%d" % i, [128, 4, 131], F32) for i in range(2)]
        lx_ts = [P.ov("lx%d" % i, [128, 4, 128], F32) for i in range(2)]
        lxss = [[Buf(t_.ap[:, i, :], "lxs%d" % i) for i in range(4)] for t_ in lx_ts]
        lxbs = [P.ov("lxb%d" % i, [128, 4, 128], BF16) for i in range(2)]
        lris = [P.ov("lri%d" % i, [128, 8, 128], F32) for i in range(2)]
        la2 = [P.ov("la%d" % i, [128, 4, 128], F32) for i in range(4)]
        lu2 = [P.ov("lu%d" % i, [128, 4, 128], F32) for i in range(4)]
        lg2 = [P.ov("lg%d" % i, [128, 4, 128], F32) for i in range(4)]
        lhs = [P.ov("lh%d" % i, [128, 4, 128], F32) for i in range(2)]
        yfms = [P.ov("yfm%d" % i, [128, 8, 128], BF16) for i in range(2)]
        lh = lhs[0]
        lri = lris[0]
        hstate = P.ov("hstate", [128, 4], F32)
        wg = P.ov("wg", [128, 8, 128], BF16)
        EAs = [Buf(PA.ap[:, 0:512], "EA0"), Buf(PB.ap[:, 0:512], "EA1")]
        EBs = [Buf(PA.ap[:, 512:1024], "EB0"), Buf(PB.ap[:, 512:1024], "EB1")]
        LAs = [Buf(PB.ap[:, 1024:1536], "LA0"), Buf(PC.ap, "LA1")]
        LBs = [Buf(PB.ap[:, 1536:2048], "LB0"), Buf(PF.ap, "LB1")]
        EA, EB = EAs[0], EBs[0]
        dma(Wbuf[:, :, 0:1024], w_in[l, :, 4632:5656].rearrange("(k p) n -> p k n", p=128), eng="pool")
        dma(Wo[:, 0:4, :], w_out[l, 1536:2048, :].rearrange("(k p) n -> p k n", p=128), eng="pool")
        dma(wg[:, 0:4, :], lru_wa[l].rearrange("k c d -> c k d"), eng="pool")
        dma(wg[:, 4:8, :], lru_wx[l].rearrange("k c d -> c k d"), eng="pool")
        lhv = lh.v().rr("p k t -> p (k t)")
        lrv = lri.v().rr("p k t -> p (k t)")
        dma(lhv[0:48, :], st_lconv[l].rearrange("b j c -> (b j) c"))
        dma(lrv[0:16, 0:512], st_lh[l])
        for k in range(4):
            tr(EA[:, k * 48:(k + 1) * 48], lhv[0:48, k * 128:(k + 1) * 128], identf[0:48, 0:48])
            tr(EB[:, k * 16:(k + 1) * 16], lrv[0:16, k * 128:(k + 1) * 128], identf[0:16, 0:16])
        cp(lconv_io.v(), EA[:, 0:192].rr("p (k t) -> p k t", k=4))
        cp(lh_io.v(), EB[:, 0:64].rr("p (k t) -> p k t", k=4))
        lpad_all = P.ov("lpad_all", [128, 4, 176], F32)
        lpa4 = lpad_all.ap.rearrange("p k (b j) -> p k b j", b=16)
        lpads_s = [Buf(lpa4[:, :, b_, :], "lpad_s%d" % b_) for b_ in range(16)]
        lg_all = P.ov("lg_all", [128, 4, 128], F32)
        lgs_s = [Buf(lg_all.ap[:, :, 8 * b_:8 * b_ + 8], "lg_s%d" % b_) for b_ in range(16)]
        yfm_all_l = P.ov("yfm_all_l", [128, 4, 128], BF16)
        yfms_l = [Buf(yfm_all_l.ap[:, :, 8 * b_:8 * b_ + 8], "yfm_l%d" % b_) for b_ in range(16)]
        ltmp = P.ov("ltmp", [128, 128], F32)
        for r in range(2):
            bank = (EAs[1], EBs[1])[r]
            for oo in range(4):
                oc = r * 4 + oo
                for k in range(8):
                    mm(bank[:, oo * 128:(oo + 1) * 128], Wbuf[:, k, oc * 128:(oc + 1) * 128], hsamp[:, k, :],
                       start=(k == 0), stop=(k == 7))
            if r == 0:
                for oo in range(4):
                    P.do("act", "activation", out=View(lpads_s[0], lpa4[:, oo, :, 3:11]),
                         in_=bank[:, oo * 128:(oo + 1) * 128].rr("p (b t) -> p b t", b=16), func=AF.Copy,
                         extra_writes=lpads_s[1:])
            else:
                P.do("act", "activation", out=View(lgs_s[0], lg_all.ap), in_=bank.v().rr("p (k t) -> p k t", k=4),
                     func=AF.Gelu, extra_writes=lgs_s[1:])
        run_pipeline([lru_gen(l, tok0, L, seqcol, first, last, b, i)
                      for i, (tok0, L, seqcol, first, last, b) in enumerate(units())])
        yfal = View(yfms_l[0], yfm_all_l.ap)
        for r in range(2):
            bank = (LAs[0], LBs[0])[r]
            for oo in range(4):
                oc = r * 4 + oo
                for k in range(4):
                    P.do("pe", "matmul", out=bank[:, oo * 128:(oo + 1) * 128], lhsT=Wo[:, k, oc * 128:(oc + 1) * 128],
                         rhs=yfal[:, k, :], start=(k == 0), stop=(k == 3), extra_reads=yfms_l[1:])
            for oo in range(4):
                oc = r * 4 + oo
                tt(ltmp.v().rr("p (b t) -> p b t", t=8), bank[:, oo * 128:(oo + 1) * 128].rr("p (b t) -> p b t", t=8),
                   mod[:, 16 + oc, 1:17].unsq(2).bc([128, 16, 8]), ALU.mult)
                tt(xc[16][:, oc, :], xc[16][:, oc, :], ltmp.v(), ALU.add)
        for k in range(4):
            tr(EA[0:48, k * 128:(k + 1) * 128], lconv_io[:, k, :], identf.v())
            tr(EB[0:16, k * 128:(k + 1) * 128], lh_io[:, k, :], identf.v())
        cp(lhv[0:48, :], EA[0:48, :])
        cp(lrv[0:16, 0:512], EB[0:16, :])
        dma(o_lconv[l, 1:1 + NS].rearrange("b j c -> (b j) c"), lhv[0:48, :])
        dma(o_lh[l, 1:1 + NS], lrv[0:16, 0:512])
        P.barrier()
        adaslabs, sqs, rstds, ntmps = alloc_norm()
        hids = [P.ov("hid%d" % i, [128, 4, 512], BF16) for i in range(2)]
        upws = [P.ov("upw%d" % i, [128, 8, 512], BF16) for i in range(2)]
        dnws = [P.ov("dnw%d" % i, [128, 4, 1024], BF16) for i in range(2)]
        rls = [P.ov("rl%d" % i, [128, 512], F32) for i in range(3)]
        upbanks = [Buf(PA.ap[:, 0:512], "mb0"), Buf(PA.ap[:, 512:1024], "mb1"), Buf(PC.ap, "mb2")]
        adabank = Buf(PF.ap, "adab")
        if l + 1 < DEPTH:
            modraw = P.ov_top("modraw", [128, 48, 17], F32)
        dnbanks = [Buf(PB.ap[:, i * 512:(i + 1) * 512], "db%d" % i) for i in range(4)]
        hn = P.ov("hn_mlp", [128, 8, NTOK], BF16)
        hc = [Buf(hn.ap[:, :, c * 128:(c + 1) * 128], "h%d" % c) for c in range(NCH)]
        for c in range(NCH):
            rmsnorm_mod(c, s2, 24, hc[c])
        P.barrier()
        tiles = [(i * 512, 512) for i in range(4)] + [(2048, 128)]
        cnt = 0
        rcnt = 0
        for e in range(8):
            upw = upws[e % 2]
            dnw = dnws[e % 2]
            dma(upw.v(), mlp_up[l, :, e * 512:(e + 1) * 512].rearrange("(k p) n -> p k n", p=128), eng="pool")
            dma(dnw.v(), mlp_down[l, e * 512:(e + 1) * 512, :].rearrange("(k p) n -> p k n", p=128), eng="pool")
            for (t0, N) in tiles:
                hid = hids[cnt % 2]
                hbufs = [hc[(t0 + i * 128) // 128] for i in range(N // 128)]
                for fo in range(4):
                    ub = upbanks[(cnt * 4 + fo) % 3]
                    rl = rls[rcnt % 3]
                    rcnt += 1
                    for k in range(8):
                        P.do("pe", "matmul", out=ub[:, 0:N], lhsT=upw[:, k, fo * 128:(fo + 1) * 128],
                             rhs=View(hbufs[0], hn.ap[:, k, t0:t0 + N]), start=(k == 0), stop=(k == 7),
                             extra_reads=hbufs[1:])
                    act(rl[:, 0:N], ub[:, 0:N], AF.Relu)
                    tt(hid[:, fo, 0:N], rl[:, 0:N], rl[:, 0:N], ALU.mult, eng="pool")
                for oc in range(8):
                    db = dnbanks[oc % 4]
                    for k in range(4):
                        mm(db[:, 0:N], dnw[:, k, oc * 128:(oc + 1) * 128], hid[:, k, 0:N],
                           start=(k == 0), stop=(k == 3))
                    pv = db[:, 0:N]
                    xbufs = [xc[(t0 + i * 128) // 128] for i in range(N // 128)]
                    xv = View(xbufs[0], x.ap[:, oc, t0:t0 + N])
                    if t0 < TP:
                        P.do("dve", "scalar_tensor_tensor", out=xv, in0=pv, scalar=mod[:, 40 + oc, 0:1], in1=xv,
                             op0=ALU.mult, op1=ALU.add, extra_reads=xbufs[1:], extra_writes=xbufs[1:])
                    else:
                        rl = rls[rcnt % 3]
                        rcnt += 1
                        tt(rl[:, 0:N].rr("p (b t) -> p b t", t=8), pv.rr("p (b t) -> p b t", t=8),
                           mod[:, 40 + oc, 1:17].unsq(2).bc([128, 16, 8]), ALU.mult)
                        tt(xv, xv, rl[:, 0:N], ALU.add)
                cnt += 1
                if l + 1 < DEPTH and cnt % 3 == 1 and cnt // 3 < 12:
                    sl = cnt // 3
                    adaslab = adaslabs[sl % 2]
                    dma(adaslab.v(), ada_w[l + 1, :, sl * 512:(sl + 1) * 512].rearrange("(k p) n -> p k n", p=128), eng="pool")
                    for oc in range(4):
                        j = sl * 4 + oc
                        jj = j % 16
                        for k in range(8):
                            mm(adabank[:, jj * 32:jj * 32 + 17], adaslab[:, k, oc * 128:(oc + 1) * 128], cs_fm[:, k, :],
                               start=(k == 0), stop=(k == 7))
                    if sl % 4 == 3:
                        g0 = (sl // 4) * 16
                        cp(modraw[:, g0:g0 + 16, :], adabank.v().rr("p (j s) -> p j s", j=16)[:, :, 0:17])

    P.barrier()
    adaslabs, sqs, rstds, ntmps = alloc_norm()
    youts = [P.ov("yout%d" % i, [128, 1024], F32) for i in range(2)]
    pbs = [PA, Buf(PB.ap[:, 0:1024], "fpb1")]
    for c in range(NCH):
        sq, rstd, ntmp, nb = sqs[c % 2], rstds[c % 2], ntmps[c % 2], nbanks[c % 2]
        yout, pb = youts[c % 2], pbs[c % 2]
        act(sq.v(), xc[c].v(), AF.Square)
        for k in range(8):
            mm(nb[:, 0:128], ones_mean.v(), sq[:, k, :], start=(k == 0), stop=(k == 7))
        act(rstd.v(), nb[:, 0:128], AF.Ln, bias=eps_t[:, 0:1], scale=1.0)
        act(rstd.v(), rstd.v(), AF.Exp, scale=-0.5)
        tt(ntmp.v(), xc[c].v(), rstd.v().unsq(1).bc([128, 8, 128]), ALU.mult)
        tt(ntmp.v(), ntmp.v(), fnw.v().unsq(2).bc([128, 8, 128]), ALU.mult, eng="pool")
        for k in range(8):
            tr(pb[:, k * 128:(k + 1) * 128], ntmp[:, k, :], identf.v())
        if c % 2 == 0:
            acp(yout.v(), pb.v())
        else:
            cp(yout.v(), pb.v())
        dst = y_p[c * 128:(c + 1) * 128, :] if c < 16 else y_s[:, :]
        dma(dst, yout.v())

    P.emit()
    return nc


_NC_CACHE = {}


def kernel(**inputs):
    f = lambda a: np.ascontiguousarray(np.asarray(a, dtype=np.float32))
    x_prompt = f(inputs["x_prompt"]); x_sample = f(inputs["x_sample"])
    c_prompt = f(inputs["c_prompt"]); c_sample = f(inputs["c_sample"])
    state_ssm = f(inputs["state_ssm"]); state_ssd_conv = f(inputs["state_ssd_conv"])
    state_mlstm_c = f(inputs["state_mlstm_c"]); state_mlstm_n = f(inputs["state_mlstm_n"])
    state_mlstm_m = f(inputs["state_mlstm_m"]); state_lru_h = f(inputs["state_lru_h"])
    state_lru_conv = f(inputs["state_lru_conv"])
    shared = {}
    for name in ("ada_w", "ada_b", "norm1_w", "norm2_w", "w_in", "ssd_conv_w", "ssd_conv_b", "ssd_dt_bias",
                 "ssd_a_log", "ssd_d", "ssd_norm_w", "ml_i_bias", "ml_f_bias", "ml_norm_w", "lru_conv_w",
                 "lru_conv_b", "lru_wa", "lru_ba", "lru_wx", "lru_bx", "lru_lambda", "w_out", "mlp_up", "mlp_down"):
        shared[name] = f(inputs[name])
    shared["final_norm_w"] = f(inputs["final_norm_w"]).reshape(1, D)
    if "nc" not in _NC_CACHE:
        _NC_CACHE["nc"] = build_program()
    nc = _NC_CACHE["nc"]
    in_maps = []
    for i in range(NCORES):
        sl = slice(i * NS, (i + 1) * NS)
        m = dict(shared)
        m["xp"] = x_prompt[i]
        m["xs"] = x_sample[sl].reshape(NS * TS, D)
        m["c17"] = np.concatenate([c_prompt[i:i + 1], c_sample[sl]], axis=0)
        m["st_ssm"] = np.ascontiguousarray(state_ssm[:, sl])
        m["st_sconv"] = np.ascontiguousarray(state_ssd_conv[:, sl])
        m["st_mc"] = np.ascontiguousarray(state_mlstm_c[:, sl])
        m["st_mn"] = np.ascontiguousarray(state_mlstm_n[:, sl])
        m["st_mm"] = np.ascontiguousarray(state_mlstm_m[:, sl])
        m["st_lh"] = np.ascontiguousarray(state_lru_h[:, sl])
        m["st_lconv"] = np.ascontiguousarray(state_lru_conv[:, sl])
        in_maps.append(m)
    res = run_bass_kernel_spmd(nc, in_maps, core_ids=list(range(NCORES)))
    R = res.results
    y_prompt = np.stack([R[i]["y_p"] for i in range(NCORES)], axis=0)
    y_sample = np.concatenate([R[i]["y_s"].reshape(NS, TS, D) for i in range(NCORES)], axis=0)
    outs = [y_prompt, y_sample]
    names = ["o_ssm", "o_sconv", "o_mc", "o_mn", "o_mm", "o_lh", "o_lconv"]
    for nm in names:
        outs.append(np.concatenate([R[i][nm][:, 0:1] for i in range(NCORES)], axis=1))
    for nm in names:
        outs.append(np.concatenate([R[i][nm][:, 1:] for i in range(NCORES)], axis=1))
    return tuple(np.ascontiguousarray(o, dtype=np.float32) for o in outs)
```

```python
import numpy as np
import concourse.bass as bass
import concourse.mybir as mybir
from concourse.bass_utils import run_bass_kernel_spmd

F32 = mybir.dt.float32
BF16 = mybir.dt.bfloat16
ALU = mybir.AluOpType
AF = mybir.ActivationFunctionType
AX = mybir.AxisListType

NCORES = 8
D = 1024
TP = 2048
NS = 16
TS = 8
NTOK = TP + NS * TS
DEPTH = 2
IN_DIM = 5656
EPS = 1e-6
NEG = -30000.0


class Buf:
    def __init__(self, ap, name=""):
        self.ap = ap
        self.name = name
        self.last_w = None
        self.readers = []

    def __getitem__(self, key):
        return View(self, self.ap[key])

    def v(self):
        return View(self, self.ap)


class View:
    def __init__(self, buf, ap):
        self.buf = buf
        self.ap = ap

    def __getitem__(self, key):
        return View(self.buf, self.ap[key])

    def rr(self, s, **kw):
        return View(self.buf, self.ap.rearrange(s, **kw))

    def bc(self, shape):
        return View(self.buf, self.ap.to_broadcast(list(shape)))

    def unsq(self, axis):
        return View(self.buf, self.ap.unsqueeze(axis))

    def bitcast(self, dt):
        return View(self.buf, self.ap.bitcast(dt))


class Op:
    __slots__ = ("eng", "meth", "kw", "reads", "writes", "deps", "idx", "is_dma", "inc_val", "key")


WRITE_KEYS = ("out", "accum_out")
ENGS = ("pe", "act", "dve", "pool", "sp")


class Prog:
    def __init__(self, nc):
        self.nc = nc
        self.ops = []
        self.bar = set()
        self.bpoints = []
        self.capture = None
        self.arena = None
        self.aoff = 0
        self.asize = 0

    def barrier(self):
        last = {}
        dm = set()
        for op in self.ops:
            if op.is_dma:
                dm.add(op.idx)
            else:
                last[op.eng] = op.idx
        self.bar = set(last.values()) | dm
        self.aoff = 0
        self.bpoints.append(len(self.ops))

    def ov_top(self, name, shape, dtype):
        n = 1
        for d in shape[1:]:
            n *= d
        nbytes = n * (4 if dtype == F32 else 2)
        n4 = (nbytes + 31) // 32 * 8
        self.atop = self.asize - n4
        ap = self.arena[:, self.atop:self.atop + n4]
        if dtype != F32:
            ap = ap.bitcast(dtype)
        ap = ap[:, 0:n]
        if len(shape) == 3:
            ap = ap.rearrange("p (a b) -> p a b", a=shape[1])
        return Buf(ap, name)

    def ov(self, name, shape, dtype):
        n = 1
        for d in shape[1:]:
            n *= d
        nbytes = n * (4 if dtype == F32 else 2)
        n4 = (nbytes + 31) // 32 * 8
        assert self.aoff + n4 <= getattr(self, "atop", self.asize), (name, self.aoff, n4, self.asize)
        ap = self.arena[:, self.aoff:self.aoff + n4]
        self.aoff += n4
        if dtype != F32:
            ap = ap.bitcast(dtype)
        ap = ap[:, 0:n]
        if len(shape) == 3:
            ap = ap.rearrange("p (a b) -> p a b", a=shape[1])
        return Buf(ap, name)

    def sb(self, name, shape, dtype):
        t = self.nc.alloc_sbuf_tensor(name, list(shape), dtype)
        return Buf(t.ap(), name)

    def ps(self, name, shape, dtype=F32):
        t = self.nc.alloc_psum_tensor(name, list(shape), dtype)
        return Buf(t.ap(), name)

    def do(self, eng, meth, extra_reads=(), extra_writes=(), **kw):
        if self.capture is not None:
            self.capture.append((eng, meth, extra_reads, extra_writes, kw))
            return None
        reads, writes, real = [], [], {}
        for k, v in kw.items():
            if isinstance(v, View):
                (writes if k in WRITE_KEYS else reads).append(v.buf)
                real[k] = v.ap
            elif isinstance(v, Buf):
                (writes if k in WRITE_KEYS else reads).append(v)
                real[k] = v.ap
            else:
                real[k] = v
        for b in extra_reads:
            reads.append(b.buf if isinstance(b, View) else b)
        for b in extra_writes:
            writes.append(b.buf if isinstance(b, View) else b)
        op = Op()
        op.eng, op.meth, op.kw, op.reads, op.writes = eng, meth, real, reads, writes
        op.is_dma = meth in ("dma_start",)
        op.inc_val = None
        op.key = None
        op.idx = len(self.ops)
        deps = set()
        for r in reads:
            if r.last_w is not None:
                deps.add(r.last_w)
        for w in writes:
            if w.last_w is not None:
                deps.add(w.last_w)
            deps.update(w.readers)
        for r in reads:
            r.readers.append(op.idx)
        for w in writes:
            w.last_w = op.idx
            w.readers = []
        deps.discard(op.idx)
        deps |= self.bar
        if eng == "pe":
            deps = {d for d in deps if self.ops[d].eng != "pe"}
        op.deps = deps
        self.ops.append(op)
        return op

    def emit(self):
        nc = self.nc
        ops = self.ops
        has_dep = [False] * len(ops)
        for op in ops:
            for d in op.deps:
                has_dep[d] = True
        eng_sem = {e: nc.alloc_semaphore("sem_" + e) for e in ENGS}
        eng_cnt = {e: 0 for e in ENGS}
        dma_sems = {}
        cur = {}
        swsems = {}
        free = []
        bset = set(self.bpoints)
        for op in ops:
            if op.idx in bset:
                free.extend(cur.values())
                cur = {}
            if op.is_dma:
                key = op.writes[0] if op.writes else op.reads[0]
                if op.eng == "pool":
                    ent = swsems.get(id(key))
                    if ent is None:
                        ent = [nc.alloc_semaphore("swsem%d" % len(swsems)), 0]
                        swsems[id(key)] = ent
                        dma_sems["sw%d" % len(swsems)] = ent
                else:
                    ent = cur.get(id(key))
                    if ent is None:
                        if free:
                            ent = free.pop()
                        else:
                            ent = [nc.alloc_semaphore("dsem%d" % len(dma_sems)), 0]
                            dma_sems[len(dma_sems)] = ent
                        cur[id(key)] = ent
                ent[1] += 16
                op.inc_val = (ent[0], ent[1], 16)
            elif has_dep[op.idx]:
                eng_cnt[op.eng] += 1
                op.inc_val = (eng_sem[op.eng], eng_cnt[op.eng], 1)
        streams = {e: [] for e in ENGS}
        for op in ops:
            streams[op.eng].append(op)
        engmap = {"pe": "tensor", "act": "scalar", "dve": "vector", "pool": "gpsimd", "sp": "sync"}

        def run_stream(ename):
            def body(eng):
                waited = {}
                for op in streams[ename]:
                    need = {}
                    for d in op.deps:
                        s, val, _ = ops[d].inc_val
                        k = id(s)
                        if waited.get(k, 0) >= val:
                            continue
                        if k not in need or need[k][1] < val:
                            need[k] = (s, val)
                    for k, (s, val) in need.items():
                        eng.wait_ge(s, val)
                        waited[k] = val
                    ins = getattr(eng, op.meth)(**op.kw)
                    if op.inc_val is not None:
                        ins.then_inc(op.inc_val[0], op.inc_val[2])
                if ename == "sp":
                    for key, (s, tot) in dma_sems.items():
                        eng.wait_ge(s, tot)
            return body

        with nc.Block() as block:
            for e in ENGS:
                getattr(block, engmap[e])(run_stream(e))


def build_program():
    nc = bass.Bass("TRN2", target_bir_lowering=False)
    P = Prog(nc)

    def din(name, shape):
        return nc.dram_tensor(name, list(shape), F32, kind="ExternalInput").ap()

    def dout(name, shape):
        return nc.dram_tensor(name, list(shape), F32, kind="ExternalOutput").ap()

    xp = din("xp", [TP, D])
    xs = din("xs", [NS * TS, D])
    c17 = din("c17", [1 + NS, D])
    st_ssm = din("st_ssm", [DEPTH, NS, 16, 64, 128])
    st_sconv = din("st_sconv", [DEPTH, NS, 3, 1536])
    st_mc = din("st_mc", [DEPTH, NS, 4, 128, 128])
    st_mn = din("st_mn", [DEPTH, NS, 4, 128])
    st_mm = din("st_mm", [DEPTH, NS, 4])
    st_lh = din("st_lh", [DEPTH, NS, 512])
    st_lconv = din("st_lconv", [DEPTH, NS, 3, 512])
    ada_w = din("ada_w", [DEPTH, D, 6 * D])
    ada_b = din("ada_b", [DEPTH, 6 * D])
    norm1_w = din("norm1_w", [DEPTH, D])
    norm2_w = din("norm2_w", [DEPTH, D])
    w_in = din("w_in", [DEPTH, D, IN_DIM])
    ssd_conv_w = din("ssd_conv_w", [DEPTH, 4, 1536])
    ssd_conv_b = din("ssd_conv_b", [DEPTH, 1536])
    ssd_dt_bias = din("ssd_dt_bias", [DEPTH, 16])
    ssd_a_log = din("ssd_a_log", [DEPTH, 16])
    ssd_d = din("ssd_d", [DEPTH, 16])
    ssd_norm_w = din("ssd_norm_w", [DEPTH, 1024])
    ml_i_bias = din("ml_i_bias", [DEPTH, 4])
    ml_f_bias = din("ml_f_bias", [DEPTH, 4])
    ml_norm_w = din("ml_norm_w", [DEPTH, 512])
    lru_conv_w = din("lru_conv_w", [DEPTH, 4, 512])
    lru_conv_b = din("lru_conv_b", [DEPTH, 512])
    lru_wa = din("lru_wa", [DEPTH, 4, 128, 128])
    lru_ba = din("lru_ba", [DEPTH, 512])
    lru_wx = din("lru_wx", [DEPTH, 4, 128, 128])
    lru_bx = din("lru_bx", [DEPTH, 512])
    lru_lambda = din("lru_lambda", [DEPTH, 512])
    w_out = din("w_out", [DEPTH, 2 * D, D])
    mlp_up = din("mlp_up", [DEPTH, D, 4 * D])
    mlp_down = din("mlp_down", [DEPTH, 4 * D, D])
    final_norm_w = din("final_norm_w", [1, D])
    y_p = dout("y_p", [TP, D])
    y_s = dout("y_s", [NS * TS, D])
    o_ssm = dout("o_ssm", [DEPTH, 1 + NS, 16, 64, 128])
    o_sconv = dout("o_sconv", [DEPTH, 1 + NS, 3, 1536])
    o_mc = dout("o_mc", [DEPTH, 1 + NS, 4, 128, 128])
    o_mn = dout("o_mn", [DEPTH, 1 + NS, 4, 128])
    o_mm = dout("o_mm", [DEPTH, 1 + NS, 4])
    o_lh = dout("o_lh", [DEPTH, 1 + NS, 512])
    o_lconv = dout("o_lconv", [DEPTH, 1 + NS, 3, 512])

    def mm(out, lhsT, rhs, start=True, stop=True):
        P.do("pe", "matmul", out=out, lhsT=lhsT, rhs=rhs, start=start, stop=stop)

    def tr(out, in_, ident):
        P.do("pe", "transpose", out=out, in_=in_, identity=ident)

    def act(out, in_, func, bias=0.0, scale=1.0, eng="act", **kw):
        P.do("act", "activation", out=out, in_=in_, func=func, bias=bias, scale=scale, **kw)

    def tt(out, in0, in1, op, eng="dve"):
        P.do(eng, "tensor_tensor", out=out, in0=in0, in1=in1, op=op)

    def ts(out, in0, s1, s2, op0, op1=None, eng="dve"):
        if op1 is None:
            P.do(eng, "tensor_scalar", out=out, in0=in0, scalar1=s1, scalar2=None, op0=op0)
        else:
            P.do(eng, "tensor_scalar", out=out, in0=in0, scalar1=s1, scalar2=s2, op0=op0, op1=op1)

    def stt(out, in0, scalar, in1, op0, op1):
        P.do("dve", "scalar_tensor_tensor", out=out, in0=in0, scalar=scalar, in1=in1, op0=op0, op1=op1)

    def cp(out, in_, eng="dve"):
        P.do(eng, "tensor_copy", out=out, in_=in_)

    def dma(out, in_, eng="sp"):
        P.do(eng, "dma_start", out=out, in_=in_)

    def dma_slow(out, in_, eng="sp"):
        P.do(eng, "dma_start", out=out, in_=in_, allow_slow_non_contiguous=True)

    def memset(buf_view, val, eng="pool"):
        P.do(eng, "memset", ap=buf_view.ap, constant=val, extra_writes=[buf_view.buf])

    identf = P.sb("identf", [128, 128], F32)
    identb = P.sb("identb", [128, 128], BF16)
    ones_mean = P.sb("ones_mean", [128, 128], BF16)
    onesf = P.sb("onesf", [128, 128], F32)
    ucum = P.sb("ucum", [128, 128], F32)
    negrep = P.sb("negrep", [128, 4, 128], BF16)
    negT = P.sb("negT", [128, 4, 128], BF16)
    sel127 = P.sb("sel127", [128, 128], F32)
    sel7 = P.sb("sel7", [128, 128], F32)
    memset(identf.v(), 1.0)
    P.do("pool", "affine_select", out=identf.v(), in_=identf.v(), pattern=[[-1, 128]], compare_op=ALU.is_equal,
         fill=0.0, base=0, channel_multiplier=1)
    cp(identb.v(), identf.v(), eng="pool")
    memset(ones_mean.v(), 1.0 / 1024.0)
    memset(onesf.v(), 1.0)
    memset(ucum.v(), 1.0)
    P.do("pool", "affine_select", out=ucum.v(), in_=ucum.v(), pattern=[[1, 128]], compare_op=ALU.is_ge,
         fill=0.0, base=0, channel_multiplier=-1)
    negtmp = P.sb("negtmp", [128, 128], F32)
    memset(negtmp.v(), 0.0)
    P.do("pool", "affine_select", out=negtmp.v(), in_=negtmp.v(), pattern=[[1, 128]], compare_op=ALU.is_ge,
         fill=NEG, base=0, channel_multiplier=-1)
    for i in range(4):
        cp(negrep[:, i, :], negtmp.v(), eng="pool")
    memset(negtmp.v(), 0.0)
    P.do("pool", "affine_select", out=negtmp.v(), in_=negtmp.v(), pattern=[[-1, 128]], compare_op=ALU.is_ge,
         fill=NEG, base=0, channel_multiplier=1)
    for i in range(4):
        cp(negT[:, i, :], negtmp.v(), eng="pool")
    for selt, row in ((sel127, 127), (sel7, 7)):
        memset(selt.v(), 1.0)
        P.do("pool", "affine_select", out=selt.v(), in_=selt.v(), pattern=[[0, 128]], compare_op=ALU.is_equal,
             fill=0.0, base=-row, channel_multiplier=1)

    PA = P.ps("PA", [128, 1024], F32)
    PB = P.ps("PB", [128, 2048], F32)
    PC = P.ps("PC", [128, 512], F32)
    PF = P.ps("PF", [128, 512], F32)

    x = P.sb("x", [128, 8, NTOK], F32)
    hn_d = nc.dram_tensor("hn_d", [NTOK // 128, 128, 1024], BF16, kind="Internal").ap()
    NCH = NTOK // 128
    xc = [Buf(x.ap[:, :, c * 128:(c + 1) * 128], "x%d" % c) for c in range(NCH)]
    hd = [Buf(hn_d[c].rearrange("p (k t) -> p k t", k=8), "hd%d" % c) for c in range(NCH)]


    def acp(out, in_):
        P.do("act", "activation", out=out, in_=in_, func=AF.Copy)

    Wbuf = Wo = None
    stage = P.sb("stage", [128, 128], F32)
    parA = P.sb("parA", [128, 104], F32)
    parB = P.sb("parB", [128, 60], F32)
    mod = P.sb("mod", [128, 48, 17], F32)
    s1 = P.sb("s1", [128, 8, 17], F32)
    s2 = P.sb("s2", [128, 8, 17], F32)
    bc_dtb = P.sb("bc_dtb", [128, 16], F32)
    bc_A = P.sb("bc_A", [128, 16], F32)
    bc_D = P.sb("bc_D", [128, 16], F32)
    bc_ib = P.sb("bc_ib", [128, 4], F32)
    bc_fb = P.sb("bc_fb", [128, 4], F32)
    lru_c1 = P.sb("lru_c1", [128, 4], F32)
    fnw = P.sb("fnw", [128, 8], F32)
    cs_fm = P.sb("cs_fm", [128, 8, 17], BF16)
    eps_t = P.sb("eps_t", [128, 1], F32)
    onesb = P.sb("onesb", [128, 1], BF16)
    memset(eps_t.v(), EPS)
    memset(onesb.v(), 1.0)
    sconv_io = P.sb("sconv_io", [128, 12, 48], F32)
    lconv_io = P.sb("lconv_io", [128, 4, 48], F32)
    lh_io = P.sb("lh_io", [128, 4, 16], F32)
    mn_io = P.sb("mn_io", [128, 64], F32)
    A4 = nc.sbuf_bytes_remaining // 4 - 64
    A4 = A4 // 8 * 8
    P.arena = nc.alloc_sbuf_tensor("arena", [128, A4], F32).ap()
    P.asize = A4

    xins = [P.ov("xin%d" % i, [128, 1024], F32) for i in range(3)]
    pbs = [PA, Buf(PB.ap[:, 0:1024], "pb1"), Buf(PB.ap[:, 1024:2048], "pb2")]
    for c in range(NCH):
        src = xp[c * 128:(c + 1) * 128, :] if c < 16 else xs[:, :]
        xin = xins[c % 3]
        pb = pbs[c % 3]
        dma(xin.v(), src)
        for k in range(8):
            tr(pb[:, k * 128:(k + 1) * 128], xin[:, k * 128:(k + 1) * 128], identf.v())
        if c % 2 == 0:
            cp(xc[c].v(), pb.v().rr("p (k t) -> p k t", k=8))
        else:
            acp(xc[c].v(), pb.v().rr("p (k t) -> p k t", k=8))
    P.barrier()

    cin = P.ov("cin", [128, 1024], F32)
    dma(cin[0:17, :], c17)
    act(cin[0:17, :], cin[0:17, :], AF.Silu)
    for k in range(8):
        tr(PC[:, k * 17:(k + 1) * 17], cin[0:17, k * 128:(k + 1) * 128], identf[0:17, 0:17])
    cp(cs_fm.v(), PC[:, 0:136].rr("p (k s) -> p k s", k=8))

    def load_fm(dst_view, rows_list):
        r0 = 0
        for ap in rows_list:
            r = ap.shape[0]
            dma(stage[r0:r0 + r, :], ap)
            r0 += r
        tr(PC[:, 0:r0], stage[0:r0, :], identf[0:r0, 0:r0])
        cp(dst_view, PC[:, 0:r0])

    def load_bc(dst_view, ap_row, n):
        dma(dst_view, ap_row.to_broadcast([128, n]))

    load_fm(fnw.v(), [final_norm_w.rearrange("o (k p) -> (o k) p", p=128)])

    def alloc_norm():
        return ([P.ov("adaslab%d" % i, [128, 8, 512], BF16) for i in range(2)],
                [P.ov("sq%d" % i, [128, 8, 128], BF16) for i in range(2)],
                [P.ov("rstd%d" % i, [128, 128], F32) for i in range(2)],
                [P.ov("ntmp%d" % i, [128, 8, 128], F32) for i in range(2)])
    adaslabs = sqs = rstds = ntmps = None
    nbanks = [PC, PF]

    def rmsnorm_mod(c, sc_t, sh_off, dstb):
        sq, rstd, ntmp, nb = sqs[c % 2], rstds[c % 2], ntmps[c % 2], nbanks[c % 2]
        P.do("pool" if c % 2 else "act", "tensor_tensor" if c % 2 else "activation",
             **(dict(out=sq.v(), in0=xc[c].v(), in1=xc[c].v(), op=ALU.mult) if c % 2 else
                dict(out=sq.v(), in_=xc[c].v(), func=AF.Square)))
        for k in range(8):
            mm(nb[:, 0:128], ones_mean.v(), sq[:, k, :], start=(k == 0), stop=(k == 7))
        act(rstd.v(), nb[:, 0:128], AF.Ln, bias=eps_t[:, 0:1], scale=1.0)
        act(rstd.v(), rstd.v(), AF.Exp, scale=-0.5)
        tt(ntmp.v(), xc[c].v(), rstd.v().unsq(1).bc([128, 8, 128]), ALU.mult)
        if c < 16:
            for k in range(8):
                act(dstb[:, k, :], ntmp[:, k, :], AF.Identity, bias=mod[:, sh_off + k, 0:1], scale=sc_t[:, k, 0:1])
        else:
            for k in range(8):
                tt(ntmp[:, k, :].rr("p (b t) -> p b t", t=8), ntmp[:, k, :].rr("p (b t) -> p b t", t=8),
                   sc_t[:, k, 1:17].unsq(2).bc([128, 16, 8]), ALU.mult)
                tt(dstb[:, k, :].rr("p (b t) -> p b t", t=8), ntmp[:, k, :].rr("p (b t) -> p b t", t=8),
                   mod[:, sh_off + k, 1:17].unsq(2).bc([128, 16, 8]), ALU.add)


    zs = xpad = cacc = xbc = xB = sm = Rb = dec = MT = xdt = xend = yy = y2 = ynb = yfm = Snat = STb = ss = None

    def alloc_ssd():
        return dict(zs=P.ov("zs", [128, 1024], BF16), xpad=P.ov("xpad", [128, 12, 131], F32),
                    cacc=P.ov("cacc", [128, 128], F32), xbc=P.ov("xbc", [128, 12, 128], BF16),
                    xB=P.ov("xB", [128, 1280], BF16), sm=P.ov("sm", [128, 256], F32),
                    Rb=P.ov("Rb", [128, 4, 128], F32), dec=P.ov("dec", [128, 4, 128], F32),
                    MT=P.ov("MT", [128, 16, 128], BF16), xdt=P.ov("xdt", [128, 1024], BF16),
                    yy=P.ov("yy", [128, 1024], F32), ynb=P.ov("ynb", [128, 1024], BF16),
                    yfm=P.ov("yfm", [128, 8, 128], BF16), Snat=P.ov("Snat", [128, 8, 128], F32),
                    ss=P.ov("ss", [128, 8], F32))

    def w_out_and_update(l, tok0, L, nk, seqcol):
        c = tok0 // 128
        o = tok0 - c * 128
        for oc in range(8):
            for k in range(nk):
                mm(PB[:, oc * 128:oc * 128 + L], Wo[:, k, oc * 128:(oc + 1) * 128], yfm[:, k, 0:L],
                   start=(k == 0), stop=(k == nk - 1))
        for oc in range(8):
            stt(xc[c][:, oc, o:o + L], PB[:, oc * 128:oc * 128 + L], mod[:, 16 + oc, seqcol:seqcol + 1],
                xc[c][:, oc, o:o + L], ALU.mult, ALU.add)

    def units():
        for c in range(16):
            yield (c * 128, 128, 0, c == 0, c == 15, None)
        for b in range(NS):
            yield (TP + b * TS, TS, 1 + b, True, True, b)

    def ssd_unit(l, tok0, L, seqcol, first, last, b):
        c = tok0 // 128
        o = tok0 - c * 128
        if b is None:
            H = hbuf[c % 2]
            dma(H.v(), hd[c].v())
            o = 0
        else:
            H = hsamp
        sel = sel127 if L == 128 else sel7
        if b is not None:
            dma(Snat.v(), st_ssm[l, b].rearrange("(hp h2) p n -> (h2 p) hp n", h2=2))
            for j in range(3):
                dma_slow(xpad[:, :, j], st_sconv[l, b, j].rearrange("(k p) -> p k", p=128))
        elif first:
            memset(Snat.v(), 0.0)
            memset(xpad[:, :, 0:3], 0.0)
        for j in range(2):
            for k in range(8):
                mm(PA[0:L, j * 512:(j + 1) * 512], H[:, k, o:o + L], Wbuf[:, k, j * 512:(j + 1) * 512],
                   start=(k == 0), stop=(k == 7))
        act(zs[0:L, :], PA[0:L, :], AF.Silu)
        for oc in range(12):
            for k in range(8):
                mm(PB[:, oc * 128:oc * 128 + L], Wbuf[:, k, 1024 + oc * 128:1024 + (oc + 1) * 128], H[:, k, o:o + L],
                   start=(k == 0), stop=(k == 7))
        acp(xpad[:, :, 3:3 + L], PB[:, 0:1536].rr("p (k t) -> p k t", k=12)[:, :, 0:L])
        for k in range(8):
            mm(PC[0:L, 0:16], H[:, k, o:o + L], Wbuf[:, k, 2560:2576], start=(k == 0), stop=(k == 7))
        dt = sm[0:L, 0:16]
        dtA = sm[0:L, 16:32]
        cum = sm[0:L, 32:48]
        tt(dt, PC[0:L, 0:16], bc_dtb[0:L, :], ALU.add)
        act(dt, dt, AF.Exp)
        act(dt, dt, AF.Ln, bias=1.0)
        tt(dtA, dt, bc_A[0:L, :], ALU.mult)
        for oc in range(12):
            ts(cacc[:, 0:L], xpad[:, oc, 0:L], parB[:, oc:oc + 1], parB[:, 48 + oc:49 + oc], ALU.mult, ALU.add)
            for j in range(1, 4):
                stt(cacc[:, 0:L], xpad[:, oc, j:j + L], parB[:, j * 12 + oc:j * 12 + oc + 1], cacc[:, 0:L], ALU.mult, ALU.add)
            act(xbc[:, oc, 0:L], cacc[:, 0:L], AF.Silu)
        if last:
            for j in range(3):
                dma_slow(o_sconv[l, seqcol, j].rearrange("(k p) -> p k", p=128), xpad[:, :, L + j])
        elif b is None:
            cp(xpad[:, :, 0:3], xpad[:, :, L:L + 3], eng="pool")
        PAb = PA.v().bitcast(BF16)
        for oc in range(10):
            tr(PAb[0:L, oc * 128:(oc + 1) * 128], xbc[:, oc, 0:L], identb.v())
        acp(xB[0:L, :], PAb[0:L, 0:1280])
        mm(PC[0:L, 16:32], ucum[0:L, 0:L], dtA)
        cp(cum, PC[0:L, 16:32])
        for g in range(2):
            mm(PF[0:L, g * 128:g * 128 + L], xbc[:, 8 + g, 0:L], xbc[:, 10 + g, 0:L])
        for q in range(4):
            g = q // 2
            tt(Rb[0:L, :, 0:L], ucum[0:L, 0:L].unsq(1).bc([L, 4, L]), dtA[:, 4 * q:4 * q + 4].unsq(2).bc([L, 4, L]), ALU.mult)
            outv = PB[0:L, q * 512:(q + 1) * 512].rr("p (h t) -> p h t", h=4)[:, :, 0:L]
            mm(outv, onesf[0:L, 0:L], Rb[0:L, :, 0:L], start=True, stop=False)
            mm(outv, identb[0:L, 0:L], negrep[0:L, :, 0:L], start=False, stop=True)
            tt(dec[0:L, :, 0:L], outv, cum[:, 4 * q:4 * q + 4].unsq(2).bc([L, 4, L]), ALU.subtract)
            act(dec[0:L, :, 0:L], dec[0:L, :, 0:L], AF.Exp)
            tt(MT[0:L, 4 * q:4 * q + 4, 0:L], dec[0:L, :, 0:L],
               PF[0:L, g * 128:g * 128 + L].unsq(1).bc([L, 4, L]), ALU.mult)
        tt(xdt[0:L, :].rr("t (h p) -> t h p", p=64), xB[0:L, 0:1024].rr("t (h p) -> t h p", p=64),
           dt.unsq(2).bc([L, 16, 64]), ALU.mult)
        for h in range(16):
            mm(PA[0:L, h * 64:(h + 1) * 64], MT[0:L, h, 0:L], xdt[0:L, h * 64:(h + 1) * 64])
        for hp in range(8):
            tr(PB[:, hp * 128:(hp + 1) * 128], Snat[:, hp, :], identf.v())
        acp(STb.v(), PB[:, 0:1024])
        for g in range(2):
            mm(PB[0:L, 1024 + g * 512:1024 + (g + 1) * 512], xbc[:, 10 + g, 0:L], STb[:, g * 512:(g + 1) * 512])
        ecum = sm[0:L, 48:64]
        act(ecum, cum, AF.Exp)
        tt(yy[0:L, :].rr("t (h p) -> t h p", p=64), PB[0:L, 1024:2048].rr("t (h p) -> t h p", p=64),
           ecum.unsq(2).bc([L, 16, 64]), ALU.mult)
        tt(yy[0:L, :], yy[0:L, :], PA[0:L, :], ALU.add)
        tt(y2[0:L, :].rr("t (h p) -> t h p", p=64), xB[0:L, 0:1024].rr("t (h p) -> t h p", p=64),
           bc_D[0:L, :].unsq(2).bc([L, 16, 64]), ALU.mult)
        tt(yy[0:L, :], yy[0:L, :], y2[0:L, :], ALU.add)
        tt(yy[0:L, :], yy[0:L, :], zs[0:L, :], ALU.mult)
        act(y2[0:L, :], yy[0:L, :], AF.Square, accum_out=ss[0:L, 0:1])
        act(ss[0:L, 1:2], ss[0:L, 0:1], AF.Ln, bias=eps_t[0:L, 0:1], scale=1.0 / 1024.0)
        act(ss[0:L, 2:3], ss[0:L, 1:2], AF.Exp, scale=-0.5)
        ts(ynb[0:L, :], yy[0:L, :], ss[0:L, 2:3], None, ALU.mult)
        for k in range(8):
            tr(PAb[:, k * 128:k * 128 + L], ynb[0:L, k * 128:(k + 1) * 128], identb[0:L, 0:L])
        tt(yfm[:, :, 0:L], PAb[:, 0:1024].rr("p (k t) -> p k t", k=8)[:, :, 0:L],
           parA[:, 64:72].unsq(2).bc([128, 8, L]), ALU.mult)
        w_out_and_update(l, tok0, L, 8, seqcol)
        mm(PC[:, 32:48], sel[0:L, :], cum)
        cl = sm[:, 64:80]
        cp(cl, PC[:, 32:48])
        eend = sm[0:L, 80:96]
        tt(eend, cl[0:L, :], cum, ALU.subtract)
        act(eend, eend, AF.Exp)
        tt(xend[0:L, :].rr("t (h p) -> t h p", p=64), xdt[0:L, :].rr("t (h p) -> t h p", p=64),
           eend.unsq(2).bc([L, 16, 64]), ALU.mult)
        dcy = sm[:, 96:112]
        act(dcy, cl, AF.Exp)
        for hp in range(8):
            g = hp // 4
            mm(PA[:, hp * 128:(hp + 1) * 128], xend[0:L, hp * 128:(hp + 1) * 128], xB[0:L, 1024 + g * 128:1024 + (g + 1) * 128])
        dcyv = dcy.rr("p (hp h2) -> p hp h2", h2=2)
        for h2 in range(2):
            rs = slice(64 * h2, 64 * h2 + 64)
            tt(Snat[rs, :, :], Snat[rs, :, :], dcyv[rs, :, h2:h2 + 1].bc([64, 8, 128]), ALU.mult)
            tt(Snat[rs, :, :], Snat[rs, :, :], PA[rs, :].rr("p (hp n) -> p hp n", hp=8), ALU.add)
        if last:
            dma(o_ssm[l, seqcol].rearrange("(hp h2) p n -> (h2 p) hp n", h2=2), Snat.v())

    hbuf = hsamp = None
    lpad = lx = lxb = lri = la = lu = lh = lg = hstate = wg = None

    def alloc_lru():
        return dict(lpad=P.ov("lpad", [128, 4, 131], F32), lx=P.ov("lx", [128, 4, 128], F32),
                    lxb=P.ov("lxb", [128, 4, 128], BF16), lri=P.ov("lri", [128, 8, 128], F32),
                    la=P.ov("la", [128, 4, 128], F32), lu=P.ov("lu", [128, 4, 128], F32),
                    lh=P.ov("lh", [128, 4, 128], F32), lg=P.ov("lg", [128, 4, 128], F32),
                    hstate=P.ov("hstate", [128, 4], F32), wg=P.ov("wg", [128, 8, 128], BF16),
                    yfm=P.ov("yfm", [128, 8, 128], BF16))

    def lru_unit(l, tok0, L, seqcol, first, last, b):
        c = tok0 // 128
        o = tok0 - c * 128
        if b is None:
            H = hbuf[c % 2]
            dma(H.v(), hd[c].v())
            o = 0
        else:
            H = hsamp
        if b is not None:
            dma_slow(hstate.v(), st_lh[l, b].rearrange("(k p) -> p k", p=128))
            for j in range(3):
                dma_slow(lpad[:, :, j], st_lconv[l, b, j].rearrange("(k p) -> p k", p=128))
        elif first:
            memset(hstate.v(), 0.0)
            memset(lpad[:, :, 0:3], 0.0)
        for oc in range(8):
            for k in range(8):
                mm(PA[:, oc * 128:oc * 128 + L], Wbuf[:, k, oc * 128:(oc + 1) * 128], H[:, k, o:o + L],
                   start=(k == 0), stop=(k == 7))
        PAv = PA.v().rr("p (k t) -> p k t", k=8)
        acp(lpad[:, :, 3:3 + L], PAv[:, 0:4, 0:L])
        act(lg[:, :, 0:L], PAv[:, 4:8, 0:L], AF.Gelu)
        def cwv(j):
            return parA[:, 72 + j * 4:72 + (j + 1) * 4].unsq(2).bc([128, 4, L])
        tt(lx[:, :, 0:L], lpad[:, :, 0:L], cwv(0), ALU.mult)
        for j in range(1, 4):
            tt(lu[:, :, 0:L], lpad[:, :, j:j + L], cwv(j), ALU.mult)
            tt(lx[:, :, 0:L], lx[:, :, 0:L], lu[:, :, 0:L], ALU.add)
        tt(lx[:, :, 0:L], lx[:, :, 0:L], parA[:, 88:92].unsq(2).bc([128, 4, L]), ALU.add)
        cp(lxb[:, :, 0:L], lx[:, :, 0:L])
        if last:
            for j in range(3):
                dma_slow(o_lconv[l, seqcol, j].rearrange("(k p) -> p k", p=128), lpad[:, :, L + j])
        elif b is None:
            cp(lpad[:, :, 0:3], lpad[:, :, L:L + 3], eng="pool")
        for k in range(4):
            mm(PC[:, k * 128:k * 128 + L], wg[:, k, :], lxb[:, k, 0:L])
        for k in range(4):
            mm(PF[:, k * 128:k * 128 + L], wg[:, 4 + k, :], lxb[:, k, 0:L])
        tt(lri[:, 0:4, 0:L], PC.v().rr("p (k t) -> p k t", k=4)[:, :, 0:L], parA[:, 92:96].unsq(2).bc([128, 4, L]), ALU.add)
        tt(lri[:, 4:8, 0:L], PF.v().rr("p (k t) -> p k t", k=4)[:, :, 0:L], parA[:, 96:100].unsq(2).bc([128, 4, L]), ALU.add)
        act(lri[:, :, 0:L], lri[:, :, 0:L], AF.Sigmoid)
        for k in range(4):
            act(la[:, k, 0:L], lri[:, k, 0:L], AF.Exp, scale=lru_c1[:, k:k + 1])
        tt(lu[:, :, 0:L], la[:, :, 0:L], la[:, :, 0:L], ALU.mult)
        ts(lu[:, :, 0:L], lu[:, :, 0:L], -1.0, 1.0, ALU.mult, ALU.add)
        ts(lu[:, :, 0:L], lu[:, :, 0:L], 0.0, None, ALU.max)
        act(lu[:, :, 0:L], lu[:, :, 0:L], AF.Sqrt)
        if b is None and first:
            memset(lu[:, :, 0:1], 1.0)
        tt(lu[:, :, 0:L], lu[:, :, 0:L], lri[:, 4:8, 0:L], ALU.mult)
        tt(lu[:, :, 0:L], lu[:, :, 0:L], lx[:, :, 0:L], ALU.mult)
        for k in range(4):
            P.do("dve", "tensor_tensor_scan", out=lh[:, k, 0:L], data0=la[:, k, 0:L], data1=lu[:, k, 0:L],
                 initial=hstate[:, k:k + 1], op0=ALU.mult, op1=ALU.add)
        cp(hstate.v(), lh[:, :, L - 1])
        if last:
            dma_slow(o_lh[l, seqcol].rearrange("(k p) -> p k", p=128), hstate.v())
        tt(yfm[:, 0:4, 0:L], lh[:, :, 0:L], lg[:, :, 0:L], ALU.mult)
        w_out_and_update(l, tok0, L, 4, seqcol)

    bc_mlw = None
    qk = ktm = vtm = vw = osig = g4 = R4 = d4 = w4 = Cnat = CTb = nfm = nfb = mbc = num = num2 = None

    def alloc_ml():
        return dict(qk=P.ov("qk", [128, 8, 128], BF16), ktm=P.ov("ktm", [128, 512], BF16),
                    vtm=P.ov("vtm", [128, 512], BF16), vw=P.ov("vw", [128, 512], BF16),
                    osig=P.ov("osig", [128, 512], F32), g4=P.ov("g4", [128, 128], F32),
                    R4=P.ov("R4", [128, 4, 128], F32), d4=P.ov("d4", [128, 4, 128], F32),
                    w4=P.ov("w4", [128, 4, 128], BF16), Cnat=P.ov("Cnat", [128, 4, 128], F32),
                    CTb=P.ov("CTb", [128, 4, 128], BF16), nfm=P.ov("nfm", [128, 4], F32),
                    nfb=P.ov("nfb", [128, 4], BF16), mbc=P.ov("mbc", [128, 4], F32),
                    num=P.ov("num", [128, 512], F32), num2=P.ov("num2", [128, 512], F32), bc_mlw=P.ov("bc_mlw", [128, 512], F32),
                    ynb=P.ov("ynb", [128, 1024], BF16), yfm=P.ov("yfm", [128, 8, 128], BF16))

    KSC = 128.0 ** -0.5

    def ml_unit(l, tok0, L, seqcol, first, last, b):
        c = tok0 // 128
        o = tok0 - c * 128
        if b is None:
            H = hbuf[c % 2]
            dma(H.v(), hd[c].v())
            o = 0
        else:
            H = hsamp
        sel = sel127 if L == 128 else sel7
        if b is not None:
            dma(Cnat.v(), st_mc[l, b].rearrange("h v d -> v h d"))
            dma_slow(nfm.v(), st_mn[l, b].rearrange("h d -> d h"))
            dma(mbc.v(), st_mm[l, b:b + 1, :].to_broadcast([128, 4]))
        elif first:
            memset(Cnat.v(), 0.0)
            memset(nfm.v(), 0.0)
            memset(mbc.v(), 0.0)
        for oc in range(8):
            for k in range(8):
                mm(PA[:, oc * 128:oc * 128 + L], Wbuf[:, k, oc * 128:(oc + 1) * 128], H[:, k, o:o + L],
                   start=(k == 0), stop=(k == 7))
        PAv = PA.v().rr("p (k t) -> p k t", k=8)
        acp(qk[:, 0:4, 0:L], PAv[:, 0:4, 0:L])
        act(qk[:, 4:8, 0:L], PAv[:, 4:8, 0:L], AF.Copy, scale=KSC)
        for j in range(3):
            for k in range(8):
                mm(PB[0:L, j * 512:(j + 1) * 512], H[:, k, o:o + L], Wbuf[:, k, 512 + j * 512:512 + (j + 1) * 512],
                   start=(k == 0), stop=(k == 7))
        for k in range(8):
            mm(PC[0:L, 0:8], H[:, k, o:o + L], Wbuf[:, k, 2048:2056], start=(k == 0), stop=(k == 7))
        act(ktm[0:L, :], PB[0:L, 0:512], AF.Copy, scale=KSC)
        acp(vtm[0:L, :], PB[0:L, 512:1024])
        act(osig[0:L, :], PB[0:L, 1024:1536], AF.Sigmoid)
        ig = g4[0:L, 0:4]
        lf = g4[0:L, 4:8]
        bcs = g4[0:L, 8:12]
        a_s = g4[0:L, 12:16]
        tt(ig, PC[0:L, 0:4], bc_ib[0:L, :], ALU.add)
        tt(lf, PC[0:L, 4:8], bc_fb[0:L, :], ALU.add)
        act(lf, lf, AF.Exp, scale=-1.0)
        act(lf, lf, AF.Ln, bias=1.0)
        ts(lf, lf, -1.0, None, ALU.mult)
        mm(PC[0:L, 8:12], ucum[0:L, 0:L], lf)
        cp(bcs, PC[0:L, 8:12])
        tt(a_s, ig, bcs, ALU.subtract)
        tt(R4[0:L, :, 0:L], identf[0:L, 0:L].unsq(1).bc([L, 4, L]), a_s.unsq(2).bc([L, 4, L]), ALU.mult)
        outv = PF.v().rr("p (h t) -> p h t", h=4)[0:L, :, 0:L]
        mm(outv, onesf[0:L, 0:L], R4[0:L, :, 0:L], start=True, stop=False)
        mm(outv, identb[0:L, 0:L], negT[0:L, :, 0:L], start=False, stop=True)
        cm = g4[0:L, 16:20]
        P.do("dve", "tensor_reduce", out=cm, in_=outv, axis=AX.X, op=ALU.max)
        r = g4[0:L, 20:24]
        tt(r, cm, mbc[0:L, :], ALU.max)
        mt = g4[0:L, 24:28]
        tt(mt, bcs, r, ALU.add)
        negr = g4[0:L, 28:32]
        ts(negr, r, -1.0, None, ALU.mult)
        tt(R4[0:L, :, 0:L], identf[0:L, 0:L].unsq(1).bc([L, 4, L]), negr.unsq(2).bc([L, 4, L]), ALU.mult)
        mm(outv, onesf[0:L, 0:L], R4[0:L, :, 0:L], start=True, stop=False)
        mm(outv, identb[0:L, 0:L], negrep[0:L, :, 0:L], start=False, stop=True)
        tt(d4[0:L, :, 0:L], outv, a_s.unsq(2).bc([L, 4, L]), ALU.add)
        act(d4[0:L, :, 0:L], d4[0:L, :, 0:L], AF.Exp)
        outc = PC.v().rr("p (h t) -> p h t", h=4)
        for h in range(4):
            mm(outc[0:L, h, 0:L], qk[:, 4 + h, 0:L], qk[:, h, 0:L])
        tt(w4[0:L, :, 0:L], d4[0:L, :, 0:L], outc[0:L, :, 0:L], ALU.mult)
        for h in range(4):
            mm(PA[0:L, h * 128:(h + 1) * 128], w4[0:L, h, 0:L], vtm[0:L, h * 128:(h + 1) * 128])
        for h in range(4):
            mm(PA[0:L, 512 + h:513 + h], w4[0:L, h, 0:L], onesb[0:L, :])
        for h in range(4):
            tr(PB[:, h * 128:(h + 1) * 128], Cnat[:, h, :], identf.v())
        acp(CTb.v(), PB[:, 0:512].rr("p (h v) -> p h v", h=4))
        cp(nfb.v(), nfm.v())
        for h in range(4):
            mm(PB[0:L, 512 + h * 128:512 + (h + 1) * 128], qk[:, h, 0:L], CTb[:, h, :])
        for h in range(4):
            mm(PB[0:L, 1024 + h:1025 + h], qk[:, h, 0:L], nfb[:, h:h + 1])
        inter = g4[0:L, 32:36]
        tt(inter, mbc[0:L, :], r, ALU.subtract)
        act(inter, inter, AF.Exp)
        tt(num[0:L, :].rr("t (h v) -> t h v", h=4), PB[0:L, 512:1024].rr("t (h v) -> t h v", h=4),
           inter.unsq(2).bc([L, 4, 128]), ALU.mult)
        tt(num[0:L, :], num[0:L, :], PA[0:L, 0:512], ALU.add)
        den = g4[0:L, 36:40]
        tt(den, PB[0:L, 1024:1028], inter, ALU.mult)
        tt(den, den, PA[0:L, 512:516], ALU.add)
        act(den, den, AF.Abs)
        emt = g4[0:L, 40:44]
        act(emt, mt, AF.Exp, scale=-1.0)
        tt(den, den, emt, ALU.max)
        P.do("dve", "reciprocal", out=den, in_=den)
        tt(num[0:L, :].rr("t (h v) -> t h v", h=4), num[0:L, :].rr("t (h v) -> t h v", h=4),
           den.unsq(2).bc([L, 4, 128]), ALU.mult)
        tt(num2[0:L, :], num[0:L, :], num[0:L, :], ALU.mult)
        ssq = g4[0:L, 44:48]
        P.do("dve", "tensor_reduce", out=ssq, in_=num2[0:L, :].rr("t (h v) -> t h v", h=4), axis=AX.X, op=ALU.add)
        act(ssq, ssq, AF.Sqrt, bias=eps_t[0:L, 0:1], scale=1.0 / 128.0)
        P.do("dve", "reciprocal", out=ssq, in_=ssq)
        tt(num[0:L, :].rr("t (h v) -> t h v", h=4), num[0:L, :].rr("t (h v) -> t h v", h=4),
           ssq.unsq(2).bc([L, 4, 128]), ALU.mult)
        tt(num[0:L, :], num[0:L, :], bc_mlw[0:L, :], ALU.mult)
        tt(ynb[0:L, 0:512], num[0:L, :], osig[0:L, :], ALU.mult)
        PAb = PA.v().bitcast(BF16)
        for k in range(4):
            tr(PAb[:, k * 128:k * 128 + L], ynb[0:L, k * 128:(k + 1) * 128], identb[0:L, 0:L])
        cp(yfm[:, 0:4, 0:L], PAb[:, 0:512].rr("p (k t) -> p k t", k=4)[:, :, 0:L])
        w_out_and_update(l, tok0, L, 4, seqcol)
        bm = g4[0:L, 48:56]
        cp(bm[:, 0:4], bcs)
        cp(bm[:, 4:8], mt)
        mm(PC[:, 0:8], sel[0:L, :], bm)
        last8 = g4[:, 56:64]
        cp(last8, PC[:, 0:8])
        wend = g4[0:L, 64:68]
        tt(wend, last8[0:L, 0:4], last8[0:L, 4:8], ALU.subtract)
        tt(wend, wend, a_s, ALU.add)
        act(wend, wend, AF.Exp)
        dc = g4[:, 68:72]
        tt(dc, last8[:, 0:4], mbc.v(), ALU.add)
        tt(dc, dc, last8[:, 4:8], ALU.subtract)
        act(dc, dc, AF.Exp)
        cp(mbc.v(), last8[:, 4:8])
        tt(vw[0:L, :].rr("t (h v) -> t h v", h=4), vtm[0:L, :].rr("t (h v) -> t h v", h=4),
           wend.unsq(2).bc([L, 4, 128]), ALU.mult)
        wendb = g4[0:L, 72:76].bitcast(BF16)[:, 0:4]
        cp(wendb, wend)
        for h in range(4):
            mm(PB[:, h * 128:(h + 1) * 128], vw[0:L, h * 128:(h + 1) * 128], ktm[0:L, h * 128:(h + 1) * 128])
        for h in range(4):
            mm(PB[:, 512 + h:513 + h], ktm[0:L, h * 128:(h + 1) * 128], wendb[:, h:h + 1])
        tt(Cnat.v(), Cnat.v(), dc.unsq(2).bc([128, 4, 128]), ALU.mult)
        tt(Cnat.v(), Cnat.v(), PB[:, 0:512].rr("p (h d) -> p h d", h=4), ALU.add)
        tt(nfm.v(), nfm.v(), dc, ALU.mult)
        tt(nfm.v(), nfm.v(), PB[:, 512:516], ALU.add)
        if last:
            dma(o_mc[l, seqcol].rearrange("h v d -> v h d"), Cnat.v())
            dma_slow(o_mn[l, seqcol].rearrange("h d -> d h"), nfm.v())
            dma(o_mm[l, seqcol:seqcol + 1, :], mbc[0:1, :])

    hid = upw = dnw = rl = modraw = None


    def run_pipeline(gens):
        def collect(g, until_early):
            P.capture = []
            try:
                while True:
                    v = next(g)
                    if until_early and v == "EARLY_DONE":
                        break
            except StopIteration:
                pass
            ops_ = P.capture
            P.capture = None
            return ops_

        sim = {"eng": {e: 0.0 for e in ENGS}, "w": {}, "r": {}}

        def bufs_of(it):
            eng, meth, er, ew, kw = it
            reads, writes = [], []
            for k, v in kw.items():
                bb = v.buf if isinstance(v, View) else (v if isinstance(v, Buf) else None)
                if bb is not None:
                    (writes if k in WRITE_KEYS else reads).append(bb)
            for x_ in er:
                reads.append(x_.buf if isinstance(x_, View) else x_)
            for x_ in ew:
                writes.append(x_.buf if isinstance(x_, View) else x_)
            return reads, writes

        def fsize(v):
            ap = v.ap if isinstance(v, (View, Buf)) else v
            n = 1
            for d in ap.shape[1:]:
                n *= d
            return n

        def dur_of(it):
            eng, meth, er, ew, kw = it
            if meth == "dma_start":
                o_ = kw["out"]
                return 2.0 + fsize(o_) * 128 * 4 / 150e3
            if eng == "pe":
                n = fsize(kw["rhs"]) if "rhs" in kw else 128
                d_ = max(n, 64) / 1200.0
                lt = kw.get("lhsT", kw.get("in_"))
                if lt is not None and (lt.ap if isinstance(lt, (View, Buf)) else lt).dtype == F32:
                    d_ *= 4 if meth == "matmul" else 1
                return d_
            o_ = kw.get("out", kw.get("ap"))
            n = fsize(o_) if o_ is not None else 64
            if eng == "dve":
                return 0.07 + n / 960.0
            if eng == "act":
                return 0.22 + n / 1200.0
            return 0.12 + n / 400.0

        def est_start(it):
            reads, writes = bufs_of(it)
            t = sim["eng"][it[0]]
            for r_ in reads:
                t = max(t, sim["w"].get(id(r_), 0.0) + 0.15)
            for w_ in writes:
                t = max(t, sim["w"].get(id(w_), 0.0) + 0.15, sim["r"].get(id(w_), 0.0) + 0.15)
            return t, reads, writes

        def commit(it, t, reads, writes):
            d_ = dur_of(it)
            issue = 0.03 if it[1] != "dma_start" else 0.1
            sim["eng"][it[0]] = (t + d_) if it[1] != "dma_start" else (t + issue)
            for r_ in reads:
                sim["r"][id(r_)] = max(sim["r"].get(id(r_), 0.0), t + d_)
            for w_ in writes:
                sim["w"][id(w_)] = t + d_
            P.do(it[0], it[1], it[2], it[3], **it[4])

        def merge(A, B):
            lst = list(B) + list(A)
            n = len(lst)
            lastw, readers = {}, {}
            preds = [set() for _ in range(n)]
            rws = []
            for idx, it in enumerate(lst):
                reads, writes = bufs_of(it)
                rws.append((reads, writes))
                for r_ in reads:
                    if id(r_) in lastw:
                        preds[idx].add(lastw[id(r_)])
                for w_ in writes:
                    if id(w_) in lastw:
                        preds[idx].add(lastw[id(w_)])
                    preds[idx].update(readers.get(id(w_), ()))
                for r_ in reads:
                    readers.setdefault(id(r_), []).append(idx)
                for w_ in writes:
                    lastw[id(w_)] = idx
                    readers[id(w_)] = []
                preds[idx].discard(idx)
            succs = [[] for _ in range(n)]
            indeg = [0] * n
            for idx in range(n):
                indeg[idx] = len(preds[idx])
                for p_ in preds[idx]:
                    succs[p_].append(idx)
            ready = [i for i in range(n) if indeg[i] == 0]
            while ready:
                best = None
                bt = None
                for i in ready:
                    t = est_start(lst[i])
                    if bt is None or (t[0], i) < (bt[0], best):
                        best, bt = i, t
                ready.remove(best)
                commit(lst[best], *bt)
                for s_ in succs[best]:
                    indeg[s_] -= 1
                    if indeg[s_] == 0:
                        ready.append(s_)
                n -= 1
            assert n == 0, "scheduler dropped ops"

        WIN = 32
        gens = list(gens)
        for w0 in range(0, len(gens), WIN):
            allops = []
            for g in gens[w0:w0 + WIN]:
                allops.extend(collect(g, False))
            merge(allops, [])

    def run_scheduled(fn):
        def _g():
            fn()
            yield
        run_pipeline([_g()])

    EA = EB = EC = EY = LA = LB = LC = None
    xbc2 = xB2 = xdt2 = sm2 = yi2 = cacc_t = caccs = ctmp_t = cts = Rbs = decs = None
    xpads = zss = yys = ynbs2 = Snats = sss = yfms_s = zsfm_all = None

    def ssd_gen(l, tok0, L, seqcol, first, last, b, ctx):
        c = tok0 // 128
        sel = sel127 if L == 128 else sel7
        xbc, xB, xdt, sm, yi = xbc2[ctx], xB2[ctx], xdt2[ctx], sm2[ctx], yi2[ctx]
        if b is None:
            xpad_, zs_, yy_, ynb_, Snat_, ss_, yfm_ = xpad, zs, yy, ynb, Snat, ss, yfm
        else:
            xpad_, zs_, yy_, ynb_, Snat_, ss_, yfm_ = xpads[b], zss[ctx], yys[ctx], ynbs2[ctx], Snats[ctx], sss[ctx], yfms_s[b]
        xend_ = zs_
        STb_ = ynb_
        if b is None:
            H = hbuf[ctx]
            dma(H.v(), hd[c].v())
            o = 0
        else:
            H = hsamp
            o = tok0 - c * 128
        if b is not None:
            cp(xpad_[:, :, 0:3], sconv_io[:, :, 3 * b:3 * b + 3], eng="pool")
        elif first:
            memset(xpad_[:, :, 0:3], 0.0)
        for r in range(3 if b is None else 0):
            bank = (EA, EB)[r % 2]
            for oo in range(4):
                oc = r * 4 + oo
                for k in range(8):
                    mm(bank[:, oo * 128:oo * 128 + L], Wbuf[:, k, 1024 + oc * 128:1024 + (oc + 1) * 128], H[:, k, o:o + L],
                       start=(k == 0), stop=(k == 7))
            acp(xpad_[:, r * 4:(r + 1) * 4, 3:3 + L], bank.v().rr("p (k t) -> p k t", k=4)[:, :, 0:L])
            yield
        for k in range(8):
            mm(EC[0:L, 0:16], H[:, k, o:o + L], Wbuf[:, k, 2560:2576], start=(k == 0), stop=(k == 7))
        dt = sm[0:L, 0:16]
        dtA = sm[0:L, 16:32]
        cum = sm[0:L, 32:48]
        tt(dt, EC[0:L, 0:16], bc_dtb[0:L, :], ALU.add)
        act(dt, dt, AF.Exp)
        act(dt, dt, AF.Ln, bias=1.0)
        tt(dtA, dt, bc_A[0:L, :], ALU.mult)
        mm(EC[0:L, 16:32], ucum[0:L, 0:L], dtA)
        cp(cum, EC[0:L, 16:32])
        yield
        call = View(caccs[0], cacc_t.ap[:, :, 0:L])
        ctall = View(cts[0], ctmp_t.ap[:, :, 0:L])
        if L == 128:
            for oc in range(12):
                ts(caccs[oc][:, 0:L], xpad_[:, oc, 0:L], parB[:, oc:oc + 1], parB[:, 48 + oc:49 + oc], ALU.mult, ALU.add)
            for j in range(1, 4):
                for oc in range(12):
                    stt(caccs[oc][:, 0:L], xpad_[:, oc, j:j + L], parB[:, j * 12 + oc:j * 12 + oc + 1], caccs[oc][:, 0:L], ALU.mult, ALU.add)
        else:
            def cwv(j):
                return parB[:, j * 12:(j + 1) * 12].unsq(2).bc([128, 12, L])
            P.do("dve", "tensor_tensor", out=call, in0=xpad_[:, :, 0:L], in1=cwv(0), op=ALU.mult, extra_writes=caccs[1:])
            for j in range(1, 4):
                P.do("dve", "tensor_tensor", out=ctall, in0=xpad_[:, :, j:j + L], in1=cwv(j), op=ALU.mult, extra_writes=cts[1:])
                P.do("dve", "tensor_tensor", out=call, in0=call, in1=ctall, op=ALU.add, extra_reads=cts[1:], extra_writes=caccs[1:])
            P.do("dve", "tensor_tensor", out=call, in0=call, in1=parB[:, 48:60].unsq(2).bc([128, 12, L]), op=ALU.add, extra_writes=caccs[1:])
        P.do("act", "activation", out=ctall, in_=call, func=AF.Exp, bias=0.0, scale=-1.0, extra_reads=caccs[1:], extra_writes=cts[1:])
        P.do("act", "activation", out=ctall, in_=ctall, func=AF.Ln, bias=1.0, scale=1.0, extra_writes=cts[1:])
        P.do("act", "activation", out=ctall, in_=ctall, func=AF.Exp, bias=0.0, scale=-1.0, extra_writes=cts[1:])
        P.do("dve", "tensor_tensor", out=xbc[:, :, 0:L], in0=call, in1=ctall, op=ALU.mult, extra_reads=caccs[1:] + cts[1:])
        yield
        if b is not None:
            cp(sconv_io[:, :, 3 * b:3 * b + 3], xpad_[:, :, L:L + 3], eng="pool")
        elif last:
            for j in range(3):
                dma_slow(o_sconv[l, seqcol, j].rearrange("(k p) -> p k", p=128), xpad_[:, :, L + j])
        else:
            cp(xpad_[:, :, 0:3], xpad_[:, :, L:L + 3], eng="pool")
        EBb = EB.v().bitcast(BF16)
        EAb = EA.v().bitcast(BF16)
        for oc in range(8):
            tr(EBb[0:L, oc * 128:(oc + 1) * 128], xbc[:, oc, 0:L], identb.v())
        acp(xB[0:L, 0:1024], EBb[0:L, 0:1024])
        for oc in range(8, 10):
            tr(EAb[0:L, (oc - 8) * 128:(oc - 7) * 128], xbc[:, oc, 0:L], identb.v())
        acp(xB[0:L, 1024:1280], EAb[0:L, 0:256])
        for g in range(2):
            mm(EC[0:L, 128 + g * 128:128 + g * 128 + L], xbc[:, 8 + g, 0:L], xbc[:, 10 + g, 0:L])
        tt(xdt[0:L, :].rr("t (h p) -> t h p", p=64), xB[0:L, 0:1024].rr("t (h p) -> t h p", p=64),
           dt.unsq(2).bc([L, 16, 64]), ALU.mult)
        yield
        for q in range(4):
            g = q // 2
            bank = (EA, EB)[q % 2]
            Rb = Rbs[q % 2]
            dec = decs[q % 2]
            tt(Rb[0:L, :, 0:L], ucum[0:L, 0:L].unsq(1).bc([L, 4, L]), dtA[:, 4 * q:4 * q + 4].unsq(2).bc([L, 4, L]), ALU.mult)
            outv = bank[0:L, :].rr("p (h t) -> p h t", h=4)[:, :, 0:L]
            if L == 128:
                mm(outv, onesf[0:L, 0:L], Rb[0:L, :, 0:L], start=True, stop=False)
                mm(outv, identb[0:L, 0:L], negrep[0:L, :, 0:L], start=False, stop=True)
            else:
                for hh in range(4):
                    mm(outv[:, hh, :], onesf[0:L, 0:L], Rb[0:L, hh, 0:L], start=True, stop=False)
                    mm(outv[:, hh, :], identb[0:L, 0:L], negrep[0:L, hh, 0:L], start=False, stop=True)
            tt(dec[0:L, :, 0:L], outv, cum[:, 4 * q:4 * q + 4].unsq(2).bc([L, 4, L]), ALU.subtract)
            act(dec[0:L, :, 0:L], dec[0:L, :, 0:L], AF.Exp)
            tt(MT[0:L, 4 * q:4 * q + 4, 0:L], dec[0:L, :, 0:L],
               EC[0:L, 128 + g * 128:128 + g * 128 + L].unsq(1).bc([L, 4, L]), ALU.mult)
            for h in range(4 * q, 4 * q + 4):
                mm(EY[0:L, h * 64:(h + 1) * 64], MT[0:L, h, 0:L], xdt[0:L, h * 64:(h + 1) * 64])
            yield
        acp(yi[0:L, :], EY[0:L, :])
        yield "EARLY_DONE"
        if b is not None:
            dma(Snat_.v(), st_ssm[l, b].rearrange("(hp h2) p n -> (h2 p) hp n", h2=2))
        elif first:
            memset(Snat_.v(), 0.0)
        y2 = yi
        ecum = sm[0:L, 48:64]
        act(ecum, cum, AF.Exp)
        mm(LC[:, 0:16], sel[0:L, :], cum)
        cl = sm[:, 64:80]
        cp(cl, LC[:, 0:16])
        eend = sm[0:L, 80:96]
        tt(eend, cl[0:L, :], cum, ALU.subtract)
        act(eend, eend, AF.Exp)
        tt(xend_[0:L, :].rr("t (h p) -> t h p", p=64), xdt[0:L, :].rr("t (h p) -> t h p", p=64),
           eend.unsq(2).bc([L, 16, 64]), ALU.mult, eng="pool")
        dcy = sm[:, 96:112]
        act(dcy, cl, AF.Exp)
        dcyv = dcy.rr("p (hp h2) -> p hp h2", h2=2)
        for g_ in range(2):
            for hh in range(4):
                tr(LC[:, hh * 128:(hh + 1) * 128], Snat_[:, 4 * g_ + hh, :], identf.v())
            acp(STb_[:, g_ * 512:(g_ + 1) * 512], LC.v())
        yield
        for r in range(2):
            bank = (LA, LB)[r]
            for i in range(4):
                hp = r * 4 + i
                mm(bank[:, i * 128:(i + 1) * 128], xend_[0:L, hp * 128:(hp + 1) * 128], xB[0:L, 1024 + r * 128:1024 + (r + 1) * 128])
            for h2 in range(2):
                rs = slice(64 * h2, 64 * h2 + 64)
                tt(Snat_[rs, 4 * r:4 * r + 4, :], Snat_[rs, 4 * r:4 * r + 4, :], dcyv[rs, 4 * r:4 * r + 4, h2:h2 + 1].bc([64, 4, 128]), ALU.mult, eng="pool")
                tt(Snat_[rs, 4 * r:4 * r + 4, :], Snat_[rs, 4 * r:4 * r + 4, :], bank[rs, :].rr("p (hp n) -> p hp n", hp=4), ALU.add)
            yield
        if last:
            dma(o_ssm[l, seqcol].rearrange("(hp h2) p n -> (h2 p) hp n", h2=2), Snat_.v())
        for g_ in range(2):
            bank = (LA, LB)[g_]
            mm(bank[0:L, :], xbc[:, 10 + g_, 0:L], STb_[:, g_ * 512:(g_ + 1) * 512])
            tt(yy_[0:L, g_ * 512:(g_ + 1) * 512].rr("t (h p) -> t h p", p=64), bank[0:L, :].rr("t (h p) -> t h p", p=64),
               ecum[:, 8 * g_:8 * g_ + 8].unsq(2).bc([L, 8, 64]), ALU.mult)
        yield
        tt(yy_[0:L, :], yy_[0:L, :], yi[0:L, :], ALU.add)
        for j in range(2):
            bank = (LA, LB)[j]
            if b is None:
                for k in range(8):
                    mm(bank[0:L, :], H[:, k, o:o + L], Wbuf[:, k, j * 512:(j + 1) * 512], start=(k == 0), stop=(k == 7))
                zt = y2[0:L, j * 512:(j + 1) * 512]
                act(zt, bank[0:L, :], AF.Exp, scale=-1.0)
                act(zt, zt, AF.Ln, bias=1.0)
                act(zt, zt, AF.Exp, scale=-1.0)
                tt(zs_[0:L, j * 512:(j + 1) * 512], bank[0:L, :], zt, ALU.mult)
            else:
                bankb = bank.v().bitcast(BF16)
                for kk in range(4):
                    tr(bankb[0:L, kk * 128:(kk + 1) * 128], zsfm_all[:, 4 * j + kk, o:o + L], identb.v())
                acp(zs_[0:L, j * 512:(j + 1) * 512], bankb[0:L, 0:512])
        yield
        tt(y2[0:L, :].rr("t (h p) -> t h p", p=64), xB[0:L, 0:1024].rr("t (h p) -> t h p", p=64),
           bc_D[0:L, :].unsq(2).bc([L, 16, 64]), ALU.mult, eng="pool")
        tt(yy_[0:L, :], yy_[0:L, :], y2[0:L, :], ALU.add)
        tt(yy_[0:L, :], yy_[0:L, :], zs_[0:L, :], ALU.mult)
        act(y2[0:L, :], yy_[0:L, :], AF.Square, accum_out=ss_[0:L, 0:1])
        act(ss_[0:L, 1:2], ss_[0:L, 0:1], AF.Ln, bias=eps_t[0:L, 0:1], scale=1.0 / 1024.0)
        act(ss_[0:L, 2:3], ss_[0:L, 1:2], AF.Exp, scale=-0.5)
        ts(ynb_[0:L, :], yy_[0:L, :], ss_[0:L, 2:3], None, ALU.mult)
        yield
        LCb = LC.v().bitcast(BF16)
        for k in range(8):
            tr(LCb[:, k * 128:k * 128 + L], ynb_[0:L, k * 128:(k + 1) * 128], identb[0:L, 0:L])
        tt(yfm_[:, :, 0:L], LCb[:, 0:1024].rr("p (k t) -> p k t", k=8)[:, :, 0:L],
           parA[:, 64:72].unsq(2).bc([128, 8, L]), ALU.mult)
        yield
        for r in range(2 if b is None else 0):
            bank = (LA, LB)[r]
            for oo in range(4):
                oc = r * 4 + oo
                for k in range(8):
                    mm(bank[:, oo * 128:oo * 128 + L], Wo[:, k, oc * 128:(oc + 1) * 128], yfm_[:, k, 0:L],
                       start=(k == 0), stop=(k == 7))
            for oo in range(4):
                oc = r * 4 + oo
                stt(xc[c][:, oc, o:o + L], bank[:, oo * 128:oo * 128 + L], mod[:, 16 + oc, seqcol:seqcol + 1],
                    xc[c][:, oc, o:o + L], ALU.mult, ALU.add)
            yield

    qk2 = ktm2 = vtm2 = osig2 = g42 = numi2 = None
    R4s = d4s = w4s = CTbs = nfbs = nums = num2s = ynbs = vws = None
    qk_s = kvo_s = yfms_m = kvo_all = None

    def ml_gen(l, tok0, L, seqcol, first, last, b, ctx):
        c = tok0 // 128
        sel = sel127 if L == 128 else sel7
        uidx = ctx
        ctx4 = uidx % 4
        p2 = uidx % 2
        qk, ktm, vtm, osig, g4, numi = qk2[ctx4], ktm2[ctx4], vtm2[ctx4], osig2[ctx4], g42[ctx4], numi2[ctx4]
        R4, d4, w4 = R4s[p2], d4s[p2], w4s[p2]
        CTb, nfb, num, num2, ynb, yfm, vw = CTbs[p2], nfbs[p2], nums[p2], num2s[p2], ynbs[p2], yfms[p2], vws[p2]
        if b is not None:
            qk = qk_s[b]
            yfm = yfms_m[b]
        if b is None:
            H = hbuf[ctx4]
            dma(H.v(), hd[c].v())
            o = 0
        else:
            H = hsamp
            o = tok0 - c * 128
        for r in range(2 if b is None else 0):
            bank = (EA, EB)[r]
            for oo in range(4):
                oc = r * 4 + oo
                for k in range(8):
                    mm(bank[:, oo * 128:oo * 128 + L], Wbuf[:, k, oc * 128:(oc + 1) * 128], H[:, k, o:o + L],
                       start=(k == 0), stop=(k == 7))
            bv = bank.v().rr("p (k t) -> p k t", k=4)[:, :, 0:L]
            if r == 0:
                acp(qk[:, 0:4, 0:L], bv)
            else:
                act(qk[:, 4:8, 0:L], bv, AF.Copy, scale=KSC)
        yield
        if b is not None:
            EAb = EA.v().bitcast(BF16)
            EBb = EB.v().bitcast(BF16)
            kva = View(kvo_s[0], kvo_all.ap)
            for kk in range(12):
                dstb = EAb if kk < 8 else EBb
                k2 = kk if kk < 8 else kk - 8
                P.do("pe", "transpose", out=dstb[0:L, k2 * 128:(k2 + 1) * 128], in_=kva[:, kk, o:o + L],
                     identity=identb.v(), extra_reads=kvo_s[1:])
            acp(ktm[0:L, :], EAb[0:L, 0:512])
            acp(vtm[0:L, :], EAb[0:L, 512:1024])
            acp(osig[0:L, :], EBb[0:L, 0:512])
        for j in range(3 if b is None else 0):
            bank = (EA, EB)[j % 2]
            for k in range(8):
                mm(bank[0:L, :], H[:, k, o:o + L], Wbuf[:, k, 512 + j * 512:512 + (j + 1) * 512], start=(k == 0), stop=(k == 7))
            if j == 0:
                act(ktm[0:L, :], bank[0:L, :], AF.Copy, scale=KSC)
            elif j == 1:
                acp(vtm[0:L, :], bank[0:L, :])
            else:
                act(osig[0:L, :], bank[0:L, :], AF.Exp, scale=-1.0)
                act(osig[0:L, :], osig[0:L, :], AF.Ln, bias=1.0)
                act(osig[0:L, :], osig[0:L, :], AF.Exp, scale=-1.0)
        for k in range(8):
            mm(EC[0:L, 0:8], H[:, k, o:o + L], Wbuf[:, k, 2048:2056], start=(k == 0), stop=(k == 7))
        yield
        ig = g4[0:L, 0:4]
        lf = g4[0:L, 4:8]
        bcs = g4[0:L, 8:12]
        a_s = g4[0:L, 12:16]
        rp = g4[0:L, 16:20]
        negrp = g4[0:L, 20:24]
        tt(ig, EC[0:L, 0:4], bc_ib[0:L, :], ALU.add)
        tt(lf, EC[0:L, 4:8], bc_fb[0:L, :], ALU.add)
        act(lf, lf, AF.Exp, scale=-1.0)
        act(lf, lf, AF.Ln, bias=1.0)
        ts(lf, lf, -1.0, None, ALU.mult)
        mm(EC[0:L, 8:12], ucum[0:L, 0:L], lf)
        cp(bcs, EC[0:L, 8:12])
        tt(a_s, ig, bcs, ALU.subtract)
        tt(R4[0:L, :, 0:L], identf[0:L, 0:L].unsq(1).bc([L, 4, L]), a_s.unsq(2).bc([L, 4, L]), ALU.mult)
        outv = EB.v().rr("p (h t) -> p h t", h=4)[0:L, :, 0:L]
        if L == 128:
            mm(outv, onesf[0:L, 0:L], R4[0:L, :, 0:L], start=True, stop=False)
            mm(outv, identb[0:L, 0:L], negT[0:L, :, 0:L], start=False, stop=True)
        else:
            for hh in range(4):
                mm(outv[:, hh, :], onesf[0:L, 0:L], R4[0:L, hh, 0:L], start=True, stop=False)
                mm(outv[:, hh, :], identb[0:L, 0:L], negT[0:L, hh, 0:L], start=False, stop=True)
        P.do("dve", "tensor_reduce", out=rp, in_=outv, axis=AX.X, op=ALU.max)
        ts(negrp, rp, -1.0, None, ALU.mult)
        yield
        tt(R4[0:L, :, 0:L], identf[0:L, 0:L].unsq(1).bc([L, 4, L]), negrp.unsq(2).bc([L, 4, L]), ALU.mult)
        outv2 = EA.v().rr("p (h t) -> p h t", h=4)[0:L, :, 0:L]
        if L == 128:
            mm(outv2, onesf[0:L, 0:L], R4[0:L, :, 0:L], start=True, stop=False)
            mm(outv2, identb[0:L, 0:L], negrep[0:L, :, 0:L], start=False, stop=True)
        else:
            for hh in range(4):
                mm(outv2[:, hh, :], onesf[0:L, 0:L], R4[0:L, hh, 0:L], start=True, stop=False)
                mm(outv2[:, hh, :], identb[0:L, 0:L], negrep[0:L, hh, 0:L], start=False, stop=True)
        tt(d4[0:L, :, 0:L], outv2, a_s.unsq(2).bc([L, 4, L]), ALU.add)
        act(d4[0:L, :, 0:L], d4[0:L, :, 0:L], AF.Exp)
        outc = EY[:, 0:512].rr("p (h t) -> p h t", h=4)
        for h in range(4):
            mm(outc[0:L, h, 0:L], qk[:, 4 + h, 0:L], qk[:, h, 0:L])
        tt(w4[0:L, :, 0:L], d4[0:L, :, 0:L], outc[0:L, :, 0:L], ALU.mult)
        for h in range(4):
            mm(EY[0:L, 512 + h * 128:512 + (h + 1) * 128], w4[0:L, h, 0:L], vtm[0:L, h * 128:(h + 1) * 128])
        for h in range(4):
            mm(EC[0:L, 16 + h:17 + h], w4[0:L, h, 0:L], onesb[0:L, :])
        acp(numi[0:L, 0:512], EY[0:L, 512:1024])
        cp(numi[0:L, 512:516], EC[0:L, 16:20])
        yield "EARLY_DONE"
        if b is not None:
            dma(Cnat.v(), st_mc[l, b].rearrange("h v d -> v h d"))
            cp(nfm.v(), mn_io[:, 4 * b:4 * b + 4], eng="pool")
            dma(mbc.v(), st_mm[l, b:b + 1, :].to_broadcast([128, 4]))
        elif first:
            memset(Cnat.v(), 0.0)
            memset(nfm.v(), 0.0)
            memset(mbc.v(), 0.0)
        r_ = g4[0:L, 24:28]
        f_ = g4[0:L, 28:32]
        inter = g4[0:L, 32:36]
        mt = g4[0:L, 36:40]
        emt = g4[0:L, 40:44]
        den = g4[0:L, 44:48]
        den2 = g4[0:L, 48:52]
        ssq = g4[0:L, 52:56]
        tt(r_, rp, mbc[0:L, :], ALU.max)
        tt(f_, rp, r_, ALU.subtract)
        act(f_, f_, AF.Exp)
        tt(inter, mbc[0:L, :], r_, ALU.subtract)
        act(inter, inter, AF.Exp)
        tt(mt, bcs, r_, ALU.add)
        act(emt, mt, AF.Exp, scale=-1.0)
        for h in range(4):
            tr(LC[:, h * 128:(h + 1) * 128], Cnat[:, h, :], identf.v())
        acp(CTb.v(), LC.v().rr("p (h v) -> p h v", h=4))
        cp(nfb.v(), nfm.v())
        for h in range(4):
            mm(LA[0:L, h * 128:(h + 1) * 128], qk[:, h, 0:L], CTb[:, h, :])
        for h in range(4):
            mm(LB[0:L, h:h + 1], qk[:, h, 0:L], nfb[:, h:h + 1])
        yield
        tt(num[0:L, :].rr("t (h v) -> t h v", h=4), numi[0:L, 0:512].rr("t (h v) -> t h v", h=4),
           f_.unsq(2).bc([L, 4, 128]), ALU.mult)
        tt(num2[0:L, :].rr("t (h v) -> t h v", h=4), LA[0:L, :].rr("t (h v) -> t h v", h=4),
           inter.unsq(2).bc([L, 4, 128]), ALU.mult)
        tt(num[0:L, :], num[0:L, :], num2[0:L, :], ALU.add, eng="pool")
        tt(den, numi[0:L, 512:516], f_, ALU.mult)
        tt(den2, LB[0:L, 0:4], inter, ALU.mult)
        tt(den, den, den2, ALU.add)
        act(den, den, AF.Abs)
        tt(den, den, emt, ALU.max)
        P.do("dve", "reciprocal", out=den, in_=den)
        tt(num[0:L, :].rr("t (h v) -> t h v", h=4), num[0:L, :].rr("t (h v) -> t h v", h=4),
           den.unsq(2).bc([L, 4, 128]), ALU.mult)
        yield
        tt(num2[0:L, :], num[0:L, :], num[0:L, :], ALU.mult, eng="pool")
        P.do("dve", "tensor_reduce", out=ssq, in_=num2[0:L, :].rr("t (h v) -> t h v", h=4), axis=AX.X, op=ALU.add)
        act(ssq, ssq, AF.Ln, bias=eps_t[0:L, 0:1], scale=1.0 / 128.0)
        act(ssq, ssq, AF.Exp, scale=-0.5)
        tt(num[0:L, :].rr("t (h v) -> t h v", h=4), num[0:L, :].rr("t (h v) -> t h v", h=4),
           ssq.unsq(2).bc([L, 4, 128]), ALU.mult)
        tt(num[0:L, :], num[0:L, :], bc_mlw[0:L, :], ALU.mult, eng="pool")
        tt(ynb[0:L, 0:512], num[0:L, :], osig[0:L, :], ALU.mult)
        LCb = LC.v().bitcast(BF16)
        for k in range(4):
            tr(LCb[:, k * 128:k * 128 + L], ynb[0:L, k * 128:(k + 1) * 128], identb[0:L, 0:L])
        cp(yfm[:, 0:4, 0:L], LCb[:, 0:512].rr("p (k t) -> p k t", k=4)[:, :, 0:L])
        yield
        for r in range(2 if b is None else 0):
            bank = (LA, LB)[r]
            for oo in range(4):
                oc = r * 4 + oo
                for k in range(4):
                    mm(bank[:, oo * 128:oo * 128 + L], Wo[:, k, oc * 128:(oc + 1) * 128], yfm[:, k, 0:L],
                       start=(k == 0), stop=(k == 3))
            for oo in range(4):
                oc = r * 4 + oo
                stt(xc[c][:, oc, o:o + L], bank[:, oo * 128:oo * 128 + L], mod[:, 16 + oc, seqcol:seqcol + 1],
                    xc[c][:, oc, o:o + L], ALU.mult, ALU.add)
            yield
        bm = g4[0:L, 56:64]
        cp(bm[:, 0:4], bcs)
        cp(bm[:, 4:8], mt)
        mm(LC[:, 0:8], sel[0:L, :], bm)
        last8 = g4[:, 64:72]
        cp(last8, LC[:, 0:8])
        wend = g4[0:L, 72:76]
        tt(wend, last8[0:L, 0:4], last8[0:L, 4:8], ALU.subtract)
        tt(wend, wend, a_s, ALU.add)
        act(wend, wend, AF.Exp)
        dc = g4[:, 76:80]
        tt(dc, last8[:, 0:4], mbc.v(), ALU.add)
        tt(dc, dc, last8[:, 4:8], ALU.subtract)
        act(dc, dc, AF.Exp)
        cp(mbc.v(), last8[:, 4:8])
        tt(vw[0:L, :].rr("t (h v) -> t h v", h=4), vtm[0:L, :].rr("t (h v) -> t h v", h=4),
           wend.unsq(2).bc([L, 4, 128]), ALU.mult)
        wendb = g4[0:L, 80:84].bitcast(BF16)[:, 0:4]
        cp(wendb, wend)
        yield
        for h in range(4):
            mm(LA[:, h * 128:(h + 1) * 128], vw[0:L, h * 128:(h + 1) * 128], ktm[0:L, h * 128:(h + 1) * 128])
        for h in range(4):
            mm(LB[:, h:h + 1], ktm[0:L, h * 128:(h + 1) * 128], wendb[:, h:h + 1])
        tt(Cnat.v(), Cnat.v(), dc.unsq(2).bc([128, 4, 128]), ALU.mult, eng="pool")
        tt(Cnat.v(), Cnat.v(), LA.v().rr("p (h d) -> p h d", h=4), ALU.add)
        tt(nfm.v(), nfm.v(), dc, ALU.mult)
        tt(nfm.v(), nfm.v(), LB[:, 0:4], ALU.add)
        if last:
            dma(o_mc[l, seqcol].rearrange("h v d -> v h d"), Cnat.v())
            if b is not None:
                cp(mn_io[:, 4 * b:4 * b + 4], nfm.v(), eng="pool")
            else:
                dma_slow(o_mn[l, seqcol].rearrange("h d -> d h"), nfm.v())
            dma(o_mm[l, seqcol:seqcol + 1, :], mbc[0:1, :])
        yield

    la2 = lu2 = lg2 = lxs = lx_t = None
    lpads = lx_ts = lxss = lxbs = lris = lhs = yfms = EAs = EBs = LAs = LBs = None
    lpads_s = lgs_s = yfms_l = None

    def lru_gen(l, tok0, L, seqcol, first, last, b, ctx):
        c = tok0 // 128
        uidx = ctx
        ctx4 = uidx % 4
        p2 = uidx % 2
        la, lu, lg = la2[ctx4], lu2[ctx4], lg2[ctx4]
        lpad, lx_t, lxs, lxb, lri = lpads[p2], lx_ts[p2], lxss[p2], lxbs[p2], lris[p2]
        lpad_next = lpads[1 - p2]
        lh, yfm = lhs[p2], yfms[p2]
        if b is not None:
            lpad = lpads_s[b]
            lg = lgs_s[b]
            yfm = yfms_l[b]
        EA, EB = EAs[p2], EBs[p2]
        LA, LB = LAs[p2], LBs[p2]
        if b is None:
            H = hbuf[ctx4]
            dma(H.v(), hd[c].v())
            o = 0
        else:
            H = hsamp
            o = tok0 - c * 128
        if b is not None:
            cp(lpad[:, :, 0:3], lconv_io[:, :, 3 * b:3 * b + 3], eng="pool")
        elif first:
            memset(lpad[:, :, 0:3], 0.0)
        for r in range(2 if b is None else 0):
            bank = (EA, EB)[r]
            for oo in range(4):
                oc = r * 4 + oo
                for k in range(8):
                    mm(bank[:, oo * 128:oo * 128 + L], Wbuf[:, k, oc * 128:(oc + 1) * 128], H[:, k, o:o + L],
                       start=(k == 0), stop=(k == 7))
            bv = bank.v().rr("p (k t) -> p k t", k=4)[:, :, 0:L]
            if r == 0:
                acp(lpad[:, :, 3:3 + L], bv)
            else:
                act(lg[:, :, 0:L], bv, AF.Gelu)
        yield
        for k in range(4):
            ts(lxs[k][:, 0:L], lpad[:, k, 0:L], parA[:, 72 + k:73 + k], parA[:, 88 + k:89 + k], ALU.mult, ALU.add)
        for j in range(1, 4):
            for k in range(4):
                stt(lxs[k][:, 0:L], lpad[:, k, j:j + L], parA[:, 72 + j * 4 + k:73 + j * 4 + k], lxs[k][:, 0:L], ALU.mult, ALU.add)
        lxall = View(lxs[0], lx_t.ap[:, :, 0:L])
        P.do("act", "activation", out=lxb[:, :, 0:L], in_=lxall, func=AF.Copy, extra_reads=lxs[1:])
        if b is not None:
            cp(lconv_io[:, :, 3 * b:3 * b + 3], lpad[:, :, L:L + 3], eng="pool")
        elif last:
            for j in range(3):
                dma_slow(o_lconv[l, seqcol, j].rearrange("(k p) -> p k", p=128), lpad[:, :, L + j])
        else:
            cp(lpad_next[:, :, 0:3], lpad[:, :, L:L + 3], eng="pool")
        yield
        for k in range(4):
            mm(EA[:, k * 128:k * 128 + L], wg[:, k, :], lxb[:, k, 0:L])
        for k in range(4):
            mm(EB[:, k * 128:k * 128 + L], wg[:, 4 + k, :], lxb[:, k, 0:L])
        tt(lri[:, 0:4, 0:L], EA.v().rr("p (k t) -> p k t", k=4)[:, :, 0:L], parA[:, 92:96].unsq(2).bc([128, 4, L]), ALU.add)
        tt(lri[:, 4:8, 0:L], EB.v().rr("p (k t) -> p k t", k=4)[:, :, 0:L], parA[:, 96:100].unsq(2).bc([128, 4, L]), ALU.add)
        act(lri[:, :, 0:L], lri[:, :, 0:L], AF.Exp, scale=-1.0)
        act(lri[:, :, 0:L], lri[:, :, 0:L], AF.Ln, bias=1.0)
        act(lri[:, :, 0:L], lri[:, :, 0:L], AF.Exp, scale=-1.0)
        for k in range(4):
            act(la[:, k, 0:L], lri[:, k, 0:L], AF.Exp, scale=lru_c1[:, k:k + 1])
        yield
        tt(lu[:, :, 0:L], la[:, :, 0:L], la[:, :, 0:L], ALU.mult)
        ts(lu[:, :, 0:L], lu[:, :, 0:L], -1.0, 1.0, ALU.mult, ALU.add)
        ts(lu[:, :, 0:L], lu[:, :, 0:L], 1e-18, None, ALU.max)
        act(lu[:, :, 0:L], lu[:, :, 0:L], AF.Ln)
        act(lu[:, :, 0:L], lu[:, :, 0:L], AF.Exp, scale=0.5)
        if b is None and first:
            memset(lu[:, :, 0:1], 1.0)
        tt(lu[:, :, 0:L], lu[:, :, 0:L], lri[:, 4:8, 0:L], ALU.mult)
        P.do("dve", "tensor_tensor", out=lu[:, :, 0:L], in0=lu[:, :, 0:L], in1=lxall, op=ALU.mult, extra_reads=lxs[1:])
        yield "EARLY_DONE"
        if b is not None:
            cp(hstate.v(), lh_io[:, :, b], eng="pool")
        elif first:
            memset(hstate.v(), 0.0)
        for k in range(4):
            P.do("dve", "tensor_tensor_scan", out=lh[:, k, 0:L], data0=la[:, k, 0:L], data1=lu[:, k, 0:L],
                 initial=hstate[:, k:k + 1], op0=ALU.mult, op1=ALU.add)
        cp(hstate.v(), lh[:, :, L - 1])
        if b is not None:
            cp(lh_io[:, :, b], hstate.v(), eng="pool")
        elif last:
            dma_slow(o_lh[l, seqcol].rearrange("(k p) -> p k", p=128), hstate.v())
        tt(yfm[:, 0:4, 0:L], lh[:, :, 0:L], lg[:, :, 0:L], ALU.mult)
        yield
        for r in range(2 if b is None else 0):
            bank = (LA, LB)[r]
            for oo in range(4):
                oc = r * 4 + oo
                for k in range(4):
                    mm(bank[:, oo * 128:oo * 128 + L], Wo[:, k, oc * 128:(oc + 1) * 128], yfm[:, k, 0:L],
                       start=(k == 0), stop=(k == 3))
            for oo in range(4):
                oc = r * 4 + oo
                stt(xc[c][:, oc, o:o + L], bank[:, oo * 128:oo * 128 + L], mod[:, 16 + oc, seqcol:seqcol + 1],
                    xc[c][:, oc, o:o + L], ALU.mult, ALU.add)
            yield

    for l in range(DEPTH):
        P.barrier()
        adaslabs, sqs, rstds, ntmps = alloc_norm()
        load_fm(parA.v(), [ada_b[l].rearrange("(k p) -> k p", p=128), norm1_w[l].rearrange("(k p) -> k p", p=128),
                           norm2_w[l].rearrange("(k p) -> k p", p=128), ssd_norm_w[l].rearrange("(k p) -> k p", p=128),
                           lru_conv_w[l].rearrange("j (k p) -> (j k) p", p=128), lru_conv_b[l].rearrange("(k p) -> k p", p=128),
                           lru_ba[l].rearrange("(k p) -> k p", p=128), lru_bx[l].rearrange("(k p) -> k p", p=128),
                           lru_lambda[l].rearrange("(k p) -> k p", p=128)])
        load_fm(parB.v(), [ssd_conv_w[l].rearrange("j (k p) -> (j k) p", p=128), ssd_conv_b[l].rearrange("(k p) -> k p", p=128)])
        load_bc(bc_dtb.v(), ssd_dt_bias[l:l + 1, :], 16)
        load_bc(bc_A.v(), ssd_a_log[l:l + 1, :], 16)
        act(bc_A.v(), bc_A.v(), AF.Exp)
        ts(bc_A.v(), bc_A.v(), -1.0, None, ALU.mult)
        load_bc(bc_D.v(), ssd_d[l:l + 1, :], 16)
        load_bc(bc_ib.v(), ml_i_bias[l:l + 1, :], 4)
        load_bc(bc_fb.v(), ml_f_bias[l:l + 1, :], 4)
        act(lru_c1.v(), parA[:, 100:104], AF.Exp, scale=-1.0)
        act(lru_c1.v(), lru_c1.v(), AF.Ln, bias=1.0)
        ts(lru_c1.v(), lru_c1.v(), -8.0, None, ALU.mult)
        if l == 0:
            for sl in range(12):
                adaslab = adaslabs[sl % 2]
                dma(adaslab.v(), ada_w[l, :, sl * 512:(sl + 1) * 512].rearrange("(k p) n -> p k n", p=128), eng="pool")
                for oc in range(4):
                    j = sl * 4 + oc
                    for k in range(8):
                        mm(PB[:, j * 32:j * 32 + 17], adaslab[:, k, oc * 128:(oc + 1) * 128], cs_fm[:, k, :],
                           start=(k == 0), stop=(k == 7))
            tt(mod.v(), PB[:, 0:1536].rr("p (j s) -> p j s", j=48)[:, :, 0:17], parA[:, 0:48].unsq(2).bc([128, 48, 17]), ALU.add)
        else:
            tt(mod.v(), modraw.v(), parA[:, 0:48].unsq(2).bc([128, 48, 17]), ALU.add)
            P.atop = P.asize
        ts(s1.v(), mod[:, 8:16, :], 1.0, None, ALU.add)
        tt(s1.v(), s1.v(), parA[:, 48:56].unsq(2).bc([128, 8, 17]), ALU.mult)
        ts(s2.v(), mod[:, 32:40, :], 1.0, None, ALU.add)
        tt(s2.v(), s2.v(), parA[:, 56:64].unsq(2).bc([128, 8, 17]), ALU.mult)
        hst = [P.ov("hst%d" % i, [128, 8, 128], BF16) for i in range(2)]
        def _norm1():
            for c in range(NCH):
                rmsnorm_mod(c, s1, 0, hst[c % 2])
                dma(hd[c].v(), hst[c % 2].v())
        run_scheduled(_norm1)
        P.barrier()
        Wbuf = P.ov("Wbuf", [128, 8, 2576], BF16)
        Wo = P.ov("Wo", [128, 8, 1024], BF16)
        hbuf = [P.ov("hbuf%d" % i, [128, 8, 128], BF16) for i in range(2)]
        hsamp = P.ov("hsamp", [128, 8, 128], BF16)
        dma(hsamp.v(), hd[16].v())
        xpad = P.ov("xpad", [128, 12, 131], F32)
        cacc_t = P.ov("cacc", [128, 12, 128], F32)
        caccs = [Buf(cacc_t.ap[:, i, :], "cacc%d" % i) for i in range(12)]
        ctmp_t = P.ov("ctmp", [128, 12, 128], F32)
        cts = [Buf(ctmp_t.ap[:, 4 * i:4 * i + 4, :], "ct%d" % i) for i in range(3)]
        Rbs = [cts[2], P.ov("Rb1", [128, 4, 128], F32)]
        decs = [cts[0], cts[1]]
        MT = P.ov("MT", [128, 16, 128], BF16)
        xbc2 = [P.ov("xbc%d" % i, [128, 12, 128], BF16) for i in range(2)]
        xB2 = [P.ov("xB%d" % i, [128, 1280], BF16) for i in range(2)]
        xdt2 = [P.ov("xdt%d" % i, [128, 1024], BF16) for i in range(2)]
        sm2 = [P.ov("sm%d" % i, [128, 128], F32) for i in range(2)]
        yi2 = [P.ov("yi%d" % i, [128, 1024], F32) for i in range(2)]
        zs = P.ov("zs", [128, 1024], BF16)
        yy = P.ov("yy", [128, 1024], F32)
        ynb = P.ov("ynb", [128, 1024], BF16)
        yfm = P.ov("yfm", [128, 8, 128], BF16)
        Snat = P.ov("Snat", [128, 8, 128], F32)
        ss = P.ov("ss", [128, 8], F32)
        xend = zs
        STb = ynb
        EA, EB, EC = Buf(PA.ap[:, 0:512], "EA"), Buf(PA.ap[:, 512:1024], "EB"), Buf(PC.ap, "EC")
        EY = Buf(PB.ap[:, 0:1024], "EY")
        LA, LB, LC = Buf(PB.ap[:, 1024:1536], "LA"), Buf(PB.ap[:, 1536:2048], "LB"), Buf(PF.ap, "LC")
        dma(Wbuf[:, :, 0:2576], w_in[l, :, 0:2576].rearrange("(k p) n -> p k n", p=128), eng="pool")
        dma(Wo.v(), w_out[l, 0:1024, :].rearrange("(k p) n -> p k n", p=128), eng="pool")
        scv = st_sconv[l].rearrange("b j c -> (b j) c")
        dma(yy[0:48, :], scv[:, 0:1024])
        dma(yi2[0][0:48, 0:512], scv[:, 1024:1536])
        for k in range(12):
            src = yy[0:48, k * 128:(k + 1) * 128] if k < 8 else yi2[0][0:48, (k - 8) * 128:(k - 7) * 128]
            bank = EA if k < 8 else EB
            kk = k if k < 8 else k - 8
            tr(bank[:, kk * 48:(kk + 1) * 48], src, identf[0:48, 0:48])
        cp(sconv_io[:, 0:8, :], EA[:, 0:384].rr("p (k t) -> p k t", k=8))
        cp(sconv_io[:, 8:12, :], EB[:, 0:192].rr("p (k t) -> p k t", k=4))
        ulist = list(units())
        run_pipeline([ssd_gen(l, tok0, L, seqcol, first, last, b, i % 2)
                      for i, (tok0, L, seqcol, first, last, b) in enumerate(ulist[0:16])])
        P.barrier()
        Wbuf = P.ov("Wbuf", [128, 8, 2576], BF16)
        Wo = P.ov("Wo", [128, 8, 1024], BF16)
        hbuf = [P.ov("hbuf%d" % i, [128, 8, 128], BF16) for i in range(2)]
        hsamp = P.ov("hsamp", [128, 8, 128], BF16)
        xpad_all = P.ov("xpad_all", [128, 12, 176], F32)
        xpa4 = xpad_all.ap.rearrange("p k (b j) -> p k b j", b=16)
        xpads = [Buf(xpa4[:, :, b_, :], "xpad_s%d" % b_) for b_ in range(16)]
        cacc_t = P.ov("cacc", [128, 12, 8], F32)
        caccs = [Buf(cacc_t.ap[:, i, :], "cacc%d" % i) for i in range(12)]
        ctmp_t = P.ov("ctmp", [128, 12, 8], F32)
        cts = [Buf(ctmp_t.ap[:, 4 * i:4 * i + 4, :], "ct%d" % i) for i in range(3)]
        Rbs = [cts[2], P.ov("Rb1", [128, 4, 8], F32)]
        decs = [cts[0], cts[1]]
        MT = P.ov("MT", [128, 16, 8], BF16)
        xbc2 = [P.ov("xbc%d" % i, [128, 12, 8], BF16) for i in range(2)]
        xB2 = [P.ov("xB%d" % i, [128, 1280], BF16) for i in range(2)]
        xdt2 = [P.ov("xdt%d" % i, [128, 1024], BF16) for i in range(2)]
        sm2 = [P.ov("sm%d" % i, [128, 128], F32) for i in range(2)]
        yi2 = [P.ov("yi%d" % i, [128, 1024], F32) for i in range(2)]
        zss = [P.ov("zs%d" % i, [128, 1024], BF16) for i in range(2)]
        yys = [P.ov("yy%d" % i, [128, 1024], F32) for i in range(2)]
        ynbs2 = [P.ov("ynb%d" % i, [128, 1024], BF16) for i in range(2)]
        Snats = [P.ov("Snat%d" % i, [128, 8, 128], F32) for i in range(2)]
        sss = [P.ov("ss%d" % i, [128, 8], F32) for i in range(2)]
        yfm_all = P.ov("yfm_all", [128, 8, 128], BF16)
        yfms_s = [Buf(yfm_all.ap[:, :, 8 * b_:8 * b_ + 8], "yfm_s%d" % b_) for b_ in range(16)]
        zsfm_all = P.ov("zsfm_all", [128, 8, 128], BF16)
        ztmp = P.ov("ztmp", [128, 512], F32)
        yy = yys[0]
        EA, EB, EC = Buf(PA.ap[:, 0:512], "EA"), Buf(PA.ap[:, 512:1024], "EB"), Buf(PC.ap, "EC")
        EY = Buf(PB.ap[:, 0:1024], "EY")
        LA, LB, LC = Buf(PB.ap[:, 1024:1536], "LA"), Buf(PB.ap[:, 1536:2048], "LB"), Buf(PF.ap, "LC")
        for r in range(3):
            bank = (EA, EB)[r % 2]
            for oo in range(4):
                oc = r * 4 + oo
                for k in range(8):
                    mm(bank[:, oo * 128:(oo + 1) * 128], Wbuf[:, k, 1024 + oc * 128:1024 + (oc + 1) * 128], hsamp[:, k, :],
                       start=(k == 0), stop=(k == 7))
            for oo in range(4):
                oc = r * 4 + oo
                P.do("act", "activation", out=View(xpads[0], xpa4[:, oc, :, 3:11]),
                     in_=bank[:, oo * 128:(oo + 1) * 128].rr("p (b t) -> p b t", b=16), func=AF.Copy,
                     extra_writes=xpads[1:])
        for r in range(2):
            bank = (LA, LB)[r]
            for oo in range(4):
                oc = r * 4 + oo
                for k in range(8):
                    mm(bank[:, oo * 128:(oo + 1) * 128], Wbuf[:, k, oc * 128:(oc + 1) * 128], hsamp[:, k, :],
                       start=(k == 0), stop=(k == 7))
            act(ztmp.v(), bank.v(), AF.Exp, scale=-1.0)
            act(ztmp.v(), ztmp.v(), AF.Ln, bias=1.0)
            act(ztmp.v(), ztmp.v(), AF.Exp, scale=-1.0)
            tt(zsfm_all[:, 4 * r:4 * r + 4, :], bank.v().rr("p (k t) -> p k t", k=4), ztmp.v().rr("p (k t) -> p k t", k=4), ALU.mult)
        run_pipeline([ssd_gen(l, tok0, L, seqcol, first, last, b, i % 2)
                      for i, (tok0, L, seqcol, first, last, b) in enumerate(ulist[16:])])
        yfa = View(yfms_s[0], yfm_all.ap)
        for r in range(2):
            bank = (LA, LB)[r]
            for oo in range(4):
                oc = r * 4 + oo
                for k in range(8):
                    P.do("pe", "matmul", out=bank[:, oo * 128:(oo + 1) * 128], lhsT=Wo[:, k, oc * 128:(oc + 1) * 128],
                         rhs=yfa[:, k, :], start=(k == 0), stop=(k == 7), extra_reads=yfms_s[1:])
            for oo in range(4):
                oc = r * 4 + oo
                tt(ztmp[:, 0:128].rr("p (b t) -> p b t", t=8), bank[:, oo * 128:(oo + 1) * 128].rr("p (b t) -> p b t", t=8),
                   mod[:, 16 + oc, 1:17].unsq(2).bc([128, 16, 8]), ALU.mult)
                tt(xc[16][:, oc, :], xc[16][:, oc, :], ztmp[:, 0:128], ALU.add)
        for k in range(12):
            bank = (EA, EB, LA)[k // 4]
            tr(bank[0:48, (k % 4) * 128:(k % 4 + 1) * 128], sconv_io[:, k, :], identf.v())
        cp(yy[0:48, 0:512], EA[0:48, :])
        cp(yy[0:48, 512:1024], EB[0:48, :])
        cp(yi2[0][0:48, 0:512], LA[0:48, :])
        sco = o_sconv[l, 1:1 + NS].rearrange("b j c -> (b j) c")
        dma(sco[:, 0:1024], yy[0:48, :])
        dma(sco[:, 1024:1536], yi2[0][0:48, 0:512])
        P.barrier()
        Wbuf = P.ov("Wbuf", [128, 8, 2056], BF16)
        Wo = P.ov("Wo", [128, 4, 1024], BF16)
        hsamp = P.ov("hsamp", [128, 8, 128], BF16)
        dma(hsamp.v(), hd[16].v())
        hbuf = [P.ov("hbuf%d" % i, [128, 8, 128], BF16) for i in range(4)]
        R4s = [P.ov("R4%d" % i, [128, 4, 128], F32) for i in range(2)]
        d4s = [P.ov("d4%d" % i, [128, 4, 128], F32) for i in range(2)]
        w4s = [P.ov("w4%d" % i, [128, 4, 128], BF16) for i in range(2)]
        qk2 = [P.ov("qk%d" % i, [128, 8, 128], BF16) for i in range(4)]
        ktm2 = [P.ov("ktm%d" % i, [128, 512], BF16) for i in range(4)]
        vtm2 = [P.ov("vtm%d" % i, [128, 512], BF16) for i in range(4)]
        osig2 = [P.ov("osig%d" % i, [128, 512], F32) for i in range(4)]
        g42 = [P.ov("g4%d" % i, [128, 128], F32) for i in range(4)]
        numi2 = [P.ov("numi%d" % i, [128, 520], F32) for i in range(4)]
        CTbs = [P.ov("CTb%d" % i, [128, 4, 128], BF16) for i in range(2)]
        nfbs = [P.ov("nfb%d" % i, [128, 4], BF16) for i in range(2)]
        nums = [P.ov("num%d" % i, [128, 512], F32) for i in range(2)]
        num2s = [P.ov("num2%d" % i, [128, 512], F32) for i in range(2)]
        ynbs = [P.ov("ynb%d" % i, [128, 512], BF16) for i in range(2)]
        yfms = [P.ov("yfm%d" % i, [128, 4, 128], BF16) for i in range(2)]
        vws = [P.ov("vw%d" % i, [128, 512], BF16) for i in range(2)]
        num = nums[0]
        Cnat = P.ov("Cnat", [128, 4, 128], F32)
        nfm = P.ov("nfm", [128, 4], F32)
        mbc = P.ov("mbc", [128, 4], F32)
        bc_mlw = P.ov("bc_mlw", [128, 512], F32)
        load_bc(bc_mlw.v(), ml_norm_w[l:l + 1, :], 512)
        EA, EB, EC = Buf(PA.ap[:, 0:512], "EA"), Buf(PA.ap[:, 512:1024], "EB"), Buf(PC.ap, "EC")
        EY = Buf(PB.ap[:, 0:1024], "EY")
        LA, LB, LC = Buf(PB.ap[:, 1024:1536], "LA"), Buf(PB.ap[:, 1536:2048], "LB"), Buf(PF.ap, "LC")
        dma(Wbuf[:, :, 0:2056], w_in[l, :, 2576:4632].rearrange("(k p) n -> p k n", p=128), eng="pool")
        dma(Wo[:, 0:4, :], w_out[l, 1024:1536, :].rearrange("(k p) n -> p k n", p=128), eng="pool")
        dma(num[0:64, 0:128], st_mn[l].rearrange("b h d -> (b h) d"))
        tr(EA[:, 0:64], num[0:64, 0:128], identf[0:64, 0:64])
        cp(mn_io.v(), EA[:, 0:64])
        qk_all = P.ov("qk_all", [128, 8, 128], BF16)
        qk_s = [Buf(qk_all.ap[:, :, 8 * b_:8 * b_ + 8], "qk_s%d" % b_) for b_ in range(16)]
        kvo_all = P.ov("kvo_all", [128, 12, 128], BF16)
        kvo_s = [Buf(kvo_all.ap[:, :, 8 * b_:8 * b_ + 8], "kvo_s%d" % b_) for b_ in range(16)]
        yfm_all_m = P.ov("yfm_all_m", [128, 4, 128], BF16)
        yfms_m = [Buf(yfm_all_m.ap[:, :, 8 * b_:8 * b_ + 8], "yfm_m%d" % b_) for b_ in range(16)]
        for r in range(5):
            bank = (EA, EB)[r % 2]
            for oo in range(4):
                oc = (r if r < 2 else r - 1) * 4 + oo
                for k in range(8):
                    mm(bank[:, oo * 128:(oo + 1) * 128], Wbuf[:, k, oc * 128:(oc + 1) * 128], hsamp[:, k, :],
                       start=(k == 0), stop=(k == 7))
            bv = bank.v().rr("p (k t) -> p k t", k=4)
            if r == 0:
                P.do("act", "activation", out=View(qk_s[0], qk_all.ap[:, 0:4, :]), in_=bv, func=AF.Copy, extra_writes=qk_s[1:])
            elif r == 1:
                P.do("act", "activation", out=View(qk_s[0], qk_all.ap[:, 4:8, :]), in_=bv, func=AF.Copy, scale=KSC, extra_writes=qk_s[1:])
            elif r == 2:
                P.do("act", "activation", out=View(kvo_s[0], kvo_all.ap[:, 0:4, :]), in_=bv, func=AF.Copy, scale=KSC, extra_writes=kvo_s[1:])
            elif r == 3:
                P.do("act", "activation", out=View(kvo_s[0], kvo_all.ap[:, 4:8, :]), in_=bv, func=AF.Copy, extra_writes=kvo_s[1:])
            else:
                nt = nums[0].v()
                act(nt, bank.v(), AF.Exp, scale=-1.0)
                act(nt, nt, AF.Ln, bias=1.0)
                P.do("act", "activation", out=View(kvo_s[0], kvo_all.ap[:, 8:12, :]), in_=nt.rr("p (k t) -> p k t", k=4),
                     func=AF.Exp, scale=-1.0, extra_writes=kvo_s[1:])
        run_pipeline([ml_gen(l, tok0, L, seqcol, first, last, b, i)
                      for i, (tok0, L, seqcol, first, last, b) in enumerate(units())])
        yfam = View(yfms_m[0], yfm_all_m.ap)
        for r in range(2):
            bank = (LA, LB)[r]
            for oo in range(4):
                oc = r * 4 + oo
                for k in range(4):
                    P.do("pe", "matmul", out=bank[:, oo * 128:(oo + 1) * 128], lhsT=Wo[:, k, oc * 128:(oc + 1) * 128],
                         rhs=yfam[:, k, :], start=(k == 0), stop=(k == 3), extra_reads=yfms_m[1:])
            for oo in range(4):
                oc = r * 4 + oo
                tt(nums[0][:, 0:128].rr("p (b t) -> p b t", t=8), bank[:, oo * 128:(oo + 1) * 128].rr("p (b t) -> p b t", t=8),
                   mod[:, 16 + oc, 1:17].unsq(2).bc([128, 16, 8]), ALU.mult)
                tt(xc[16][:, oc, :], xc[16][:, oc, :], nums[0][:, 0:128], ALU.add)
        tr(EA[0:64, 0:128], mn_io.v(), identf.v())
        cp(num[0:64, 0:128], EA[0:64, 0:128])
        dma(o_mn[l, 1:1 + NS].rearrange("b h d -> (b h) d"), num[0:64, 0:128])
        P.barrier()
        Wbuf = P.ov("Wbuf", [128, 8, 1024], BF16)
        Wo = P.ov("Wo", [128, 4, 1024], BF16)
        hbuf = [P.ov("hbuf%d" % i, [128, 8, 128], BF16) for i in range(2)]
        hsamp = P.ov("hsamp", [128, 8, 128], BF16)
        dma(hsamp.v(), hd[16].v())
        hbuf = [P.ov("hbuf%d" % i, [128, 8, 128], BF16) for i in range(4)]
        lpads = [P.ov("lpad%d" % i, [128, 4, 131], F32) for i in range(2)]
        lx_ts = [P.ov("lx%d" % i, [128, 4, 128], F32) for i in range(2)]
        lxss = [[Buf(t_.ap[:, i, :], "lxs%d" % i) for i in range(4)] for t_ in lx_ts]
        lxbs = [P.ov("lxb%d" % i, [128, 4, 128], BF16) for i in range(2)]
        lris = [P.ov("lri%d" % i, [128, 8, 128], F32) for i in range(2)]
        la2 = [P.ov("la%d" % i, [128, 4, 128], F32) for i in range(4)]
        lu2 = [P.ov("lu%d" % i, [128, 4, 128], F32) for i in range(4)]
        lg2 = [P.ov("lg%d" % i, [128, 4, 128], F32) for i in range(4)]
        lhs = [P.ov("lh%d" % i, [128, 4, 128], F32) for i in range(2)]
        yfms = [P.ov("yfm%d" % i, [128, 8, 128], BF16) for i in range(2)]
        lh = lhs[0]
        lri = lris[0]
        hstate = P.ov("hstate", [128, 4], F32)
        wg = P.ov("wg", [128, 8, 128], BF16)
        EAs = [Buf(PA.ap[:, 0:512], "EA0"), Buf(PB.ap[:, 0:512], "EA1")]
        EBs = [Buf(PA.ap[:, 512:1024], "EB0"), Buf(PB.ap[:, 512:1024], "EB1")]
        LAs = [Buf(PB.ap[:, 1024:1536], "LA0"), Buf(PC.ap, "LA1")]
        LBs = [Buf(PB.ap[:, 1536:2048], "LB0"), Buf(PF.ap, "LB1")]
        EA, EB = EAs[0], EBs[0]
        dma(Wbuf[:, :, 0:1024], w_in[l, :, 4632:5656].rearrange("(k p) n -> p k n", p=128), eng="pool")
        dma(Wo[:, 0:4, :], w_out[l, 1536:2048, :].rearrange("(k p) n -> p k n", p=128), eng="pool")
        dma(wg[:, 0:4, :], lru_wa[l].rearrange("k c d -> c k d"), eng="pool")
        dma(wg[:, 4:8, :], lru_wx[l].rearrange("k c d -> c k d"), eng="pool")
        lhv = lh.v().rr("p k t -> p (k t)")
        lrv = lri.v().rr("p k t -> p (k t)")
        dma(lhv[0:48, :], st_lconv[l].rearrange("b j c -> (b j) c"))
        dma(lrv[0:16, 0:512], st_lh[l])
        for k in range(4):
            tr(EA[:, k * 48:(k + 1) * 48], lhv[0:48, k * 128:(k + 1) * 128], identf[0:48, 0:48])
            tr(EB[:, k * 16:(k + 1) * 16], lrv[0:16, k * 128:(k + 1) * 128], identf[0:16, 0:16])
        cp(lconv_io.v(), EA[:, 0:192].rr("p (k t) -> p k t", k=4))
        cp(lh_io.v(), EB[:, 0:64].rr("p (k t) -> p k t", k=4))
        lpad_all = P.ov("lpad_all", [128, 4, 176], F32)
        lpa4 = lpad_all.ap.rearrange("p k (b j) -> p k b j", b=16)
        lpads_s = [Buf(lpa4[:, :, b_, :], "lpad_s%d" % b_) for b_ in range(16)]
        lg_all = P.ov("lg_all", [128, 4, 128], F32)
        lgs_s = [Buf(lg_all.ap[:, :, 8 * b_:8 * b_ + 8], "lg_s%d" % b_) for b_ in range(16)]
        yfm_all_l = P.ov("yfm_all_l", [128, 4, 128], BF16)
        yfms_l = [Buf(yfm_all_l.ap[:, :, 8 * b_:8 * b_ + 8], "yfm_l%d" % b_) for b_ in range(16)]
        ltmp = P.ov("ltmp", [128, 128], F32)
        for r in range(2):
            bank = (EAs[1], EBs[1])[r]
            for oo in range(4):
                oc = r * 4 + oo
                for k in range(8):
                    mm(bank[:, oo * 128:(oo + 1) * 128], Wbuf[:, k, oc * 128:(oc + 1) * 128], hsamp[:, k, :],
                       start=(k == 0), stop=(k == 7))
            if r == 0:
                for oo in range(4):
                    P.do("act", "activation", out=View(lpads_s[0], lpa4[:, oo, :, 3:11]),
                         in_=bank[:, oo * 128:(oo + 1) * 128].rr("p (b t) -> p b t", b=16), func=AF.Copy,
                         extra_writes=lpads_s[1:])
            else:
                P.do("act", "activation", out=View(lgs_s[0], lg_all.ap), in_=bank.v().rr("p (k t) -> p k t", k=4),
                     func=AF.Gelu, extra_writes=lgs_s[1:])
        run_pipeline([lru_gen(l, tok0, L, seqcol, first, last, b, i)
                      for i, (tok0, L, seqcol, first, last, b) in enumerate(units())])
        yfal = View(yfms_l[0], yfm_all_l.ap)
        for r in range(2):
            bank = (LAs[0], LBs[0])[r]
            for oo in range(4):
                oc = r * 4 + oo
                for k in range(4):
                    P.do("pe", "matmul", out=bank[:, oo * 128:(oo + 1) * 128], lhsT=Wo[:, k, oc * 128:(oc + 1) * 128],
                         rhs=yfal[:, k, :], start=(k == 0), stop=(k == 3), extra_reads=yfms_l[1:])
            for oo in range(4):
                oc = r * 4 + oo
                tt(ltmp.v().rr("p (b t) -> p b t", t=8), bank[:, oo * 128:(oo + 1) * 128].rr("p (b t) -> p b t", t=8),
                   mod[:, 16 + oc, 1:17].unsq(2).bc([128, 16, 8]), ALU.mult)
                tt(xc[16][:, oc, :], xc[16][:, oc, :], ltmp.v(), ALU.add)
        for k in range(4):
            tr(EA[0:48, k * 128:(k + 1) * 128], lconv_io[:, k, :], identf.v())
            tr(EB[0:16, k * 128:(k + 1) * 128], lh_io[:, k, :], identf.v())
        cp(lhv[0:48, :], EA[0:48, :])
        cp(lrv[0:16, 0:512], EB[0:16, :])
        dma(o_lconv[l, 1:1 + NS].rearrange("b j c -> (b j) c"), lhv[0:48, :])
        dma(o_lh[l, 1:1 + NS], lrv[0:16, 0:512])
        P.barrier()
        adaslabs, sqs, rstds, ntmps = alloc_norm()
        hids = [P.ov("hid%d" % i, [128, 4, 512], BF16) for i in range(2)]
        upws = [P.ov("upw%d" % i, [128, 8, 512], BF16) for i in range(2)]
        dnws = [P.ov("dnw%d" % i, [128, 4, 1024], BF16) for i in range(2)]
        rls = [P.ov("rl%d" % i, [128, 512], F32) for i in range(3)]
        upbanks = [Buf(PA.ap[:, 0:512], "mb0"), Buf(PA.ap[:, 512:1024], "mb1"), Buf(PC.ap, "mb2")]
        adabank = Buf(PF.ap, "adab")
        if l + 1 < DEPTH:
            modraw = P.ov_top("modraw", [128, 48, 17], F32)
        dnbanks = [Buf(PB.ap[:, i * 512:(i + 1) * 512], "db%d" % i) for i in range(4)]
        hn = P.ov("hn_mlp", [128, 8, NTOK], BF16)
        hc = [Buf(hn.ap[:, :, c * 128:(c + 1) * 128], "h%d" % c) for c in range(NCH)]
        def _norm2():
            for c in range(NCH):
                rmsnorm_mod(c, s2, 24, hc[c])
        run_scheduled(_norm2)
        P.barrier()
        tiles = [(i * 512, 512) for i in range(4)] + [(2048, 128)]
        cnt = 0
        rcnt = 0
        for e in range(8):
            upw = upws[e % 2]
            dnw = dnws[e % 2]
            dma(upw.v(), mlp_up[l, :, e * 512:(e + 1) * 512].rearrange("(k p) n -> p k n", p=128), eng="pool")
            dma(dnw.v(), mlp_down[l, e * 512:(e + 1) * 512, :].rearrange("(k p) n -> p k n", p=128), eng="pool")
            for (t0, N) in tiles:
                hid = hids[cnt % 2]
                hbufs = [hc[(t0 + i * 128) // 128] for i in range(N // 128)]
                for fo in range(4):
                    ub = upbanks[(cnt * 4 + fo) % 3]
                    rl = rls[rcnt % 3]
                    rcnt += 1
                    for k in range(8):
                        P.do("pe", "matmul", out=ub[:, 0:N], lhsT=upw[:, k, fo * 128:(fo + 1) * 128],
                             rhs=View(hbufs[0], hn.ap[:, k, t0:t0 + N]), start=(k == 0), stop=(k == 7),
                             extra_reads=hbufs[1:])
                    act(rl[:, 0:N], ub[:, 0:N], AF.Relu)
                    tt(hid[:, fo, 0:N], rl[:, 0:N], rl[:, 0:N], ALU.mult, eng="pool")
                for oc in range(8):
                    db = dnbanks[oc % 4]
                    for k in range(4):
                        mm(db[:, 0:N], dnw[:, k, oc * 128:(oc + 1) * 128], hid[:, k, 0:N],
                           start=(k == 0), stop=(k == 3))
                    pv = db[:, 0:N]
                    xbufs = [xc[(t0 + i * 128) // 128] for i in range(N // 128)]
                    xv = View(xbufs[0], x.ap[:, oc, t0:t0 + N])
                    if t0 < TP:
                        P.do("dve", "scalar_tensor_tensor", out=xv, in0=pv, scalar=mod[:, 40 + oc, 0:1], in1=xv,
                             op0=ALU.mult, op1=ALU.add, extra_reads=xbufs[1:], extra_writes=xbufs[1:])
                    else:
                        rl = rls[rcnt % 3]
                        rcnt += 1
                        tt(rl[:, 0:N].rr("p (b t) -> p b t", t=8), pv.rr("p (b t) -> p b t", t=8),
                           mod[:, 40 + oc, 1:17].unsq(2).bc([128, 16, 8]), ALU.mult)
                        tt(xv, xv, rl[:, 0:N], ALU.add)
                cnt += 1
                if l + 1 < DEPTH and cnt % 3 == 1 and cnt // 3 < 12:
                    sl = cnt // 3
                    adaslab = adaslabs[sl % 2]
                    dma(adaslab.v(), ada_w[l + 1, :, sl * 512:(sl + 1) * 512].rearrange("(k p) n -> p k n", p=128), eng="pool")
                    for oc in range(4):
                        j = sl * 4 + oc
                        jj = j % 16
                        for k in range(8):
                            mm(adabank[:, jj * 32:jj * 32 + 17], adaslab[:, k, oc * 128:(oc + 1) * 128], cs_fm[:, k, :],
                               start=(k == 0), stop=(k == 7))
                    if sl % 4 == 3:
                        g0 = (sl // 4) * 16
                        cp(modraw[:, g0:g0 + 16, :], adabank.v().rr("p (j s) -> p j s", j=16)[:, :, 0:17])

    P.barrier()
    adaslabs, sqs, rstds, ntmps = alloc_norm()
    youts = [P.ov("yout%d" % i, [128, 1024], F32) for i in range(2)]
    pbs = [PA, Buf(PB.ap[:, 0:1024], "fpb1")]
    def _final():
        for c in range(NCH):
            sq, rstd, ntmp, nb = sqs[c % 2], rstds[c % 2], ntmps[c % 2], nbanks[c % 2]
            yout, pb = youts[c % 2], pbs[c % 2]
            act(sq.v(), xc[c].v(), AF.Square)
            for k in range(8):
                mm(nb[:, 0:128], ones_mean.v(), sq[:, k, :], start=(k == 0), stop=(k == 7))
            act(rstd.v(), nb[:, 0:128], AF.Ln, bias=eps_t[:, 0:1], scale=1.0)
            act(rstd.v(), rstd.v(), AF.Exp, scale=-0.5)
            tt(ntmp.v(), xc[c].v(), rstd.v().unsq(1).bc([128, 8, 128]), ALU.mult)
            tt(ntmp.v(), ntmp.v(), fnw.v().unsq(2).bc([128, 8, 128]), ALU.mult, eng="pool")
            for k in range(8):
                tr(pb[:, k * 128:(k + 1) * 128], ntmp[:, k, :], identf.v())
            if c % 2 == 0:
                acp(yout.v(), pb.v())
            else:
                cp(yout.v(), pb.v())
            dst = y_p[c * 128:(c + 1) * 128, :] if c < 16 else y_s[:, :]
            dma(dst, yout.v())


    run_scheduled(_final)

    P.emit()
    return nc


_NC_CACHE = {}


def kernel(**inputs):
    f = lambda a: np.ascontiguousarray(np.asarray(a, dtype=np.float32))
    x_prompt = f(inputs["x_prompt"]); x_sample = f(inputs["x_sample"])
    c_prompt = f(inputs["c_prompt"]); c_sample = f(inputs["c_sample"])
    state_ssm = f(inputs["state_ssm"]); state_ssd_conv = f(inputs["state_ssd_conv"])
    state_mlstm_c = f(inputs["state_mlstm_c"]); state_mlstm_n = f(inputs["state_mlstm_n"])
    state_mlstm_m = f(inputs["state_mlstm_m"]); state_lru_h = f(inputs["state_lru_h"])
    state_lru_conv = f(inputs["state_lru_conv"])
    shared = {}
    for name in ("ada_w", "ada_b", "norm1_w", "norm2_w", "w_in", "ssd_conv_w", "ssd_conv_b", "ssd_dt_bias",
                 "ssd_a_log", "ssd_d", "ssd_norm_w", "ml_i_bias", "ml_f_bias", "ml_norm_w", "lru_conv_w",
                 "lru_conv_b", "lru_wa", "lru_ba", "lru_wx", "lru_bx", "lru_lambda", "w_out", "mlp_up", "mlp_down"):
        shared[name] = f(inputs[name])
    shared["final_norm_w"] = f(inputs["final_norm_w"]).reshape(1, D)
    if "nc" not in _NC_CACHE:
        _NC_CACHE["nc"] = build_program()
    nc = _NC_CACHE["nc"]
    in_maps = []
    for i in range(NCORES):
        sl = slice(i * NS, (i + 1) * NS)
        m = dict(shared)
        m["xp"] = x_prompt[i]
        m["xs"] = x_sample[sl].reshape(NS * TS, D)
        m["c17"] = np.concatenate([c_prompt[i:i + 1], c_sample[sl]], axis=0)
        m["st_ssm"] = np.ascontiguousarray(state_ssm[:, sl])
        m["st_sconv"] = np.ascontiguousarray(state_ssd_conv[:, sl])
        m["st_mc"] = np.ascontiguousarray(state_mlstm_c[:, sl])
        m["st_mn"] = np.ascontiguousarray(state_mlstm_n[:, sl])
        m["st_mm"] = np.ascontiguousarray(state_mlstm_m[:, sl])
        m["st_lh"] = np.ascontiguousarray(state_lru_h[:, sl])
        m["st_lconv"] = np.ascontiguousarray(state_lru_conv[:, sl])
        in_maps.append(m)
    res = run_bass_kernel_spmd(nc, in_maps, core_ids=list(range(NCORES)))
    R = res.results
    y_prompt = np.stack([R[i]["y_p"] for i in range(NCORES)], axis=0)
    y_sample = np.concatenate([R[i]["y_s"].reshape(NS, TS, D) for i in range(NCORES)], axis=0)
    outs = [y_prompt, y_sample]
    names = ["o_ssm", "o_sconv", "o_mc", "o_mn", "o_mm", "o_lh", "o_lconv"]
    for nm in names:
        outs.append(np.concatenate([R[i][nm][:, 0:1] for i in range(NCORES)], axis=1))
    for nm in names:
        outs.append(np.concatenate([R[i][nm][:, 1:] for i in range(NCORES)], axis=1))
    return tuple(np.ascontiguousarray(o, dtype=np.float32) for o in outs)
```

```python
import numpy as np
import concourse.bass as bass
import concourse.mybir as mybir
from concourse.bass_utils import run_bass_kernel_spmd

F32 = mybir.dt.float32
BF16 = mybir.dt.bfloat16
ALU = mybir.AluOpType
AF = mybir.ActivationFunctionType
AX = mybir.AxisListType

NCORES = 8
D = 1024
TP = 2048
NS = 16
TS = 8
NTOK = TP + NS * TS
DEPTH = 2
IN_DIM = 5656
EPS = 1e-6
NEG = -30000.0


class Buf:
    def __init__(self, ap, name=""):
        self.ap = ap
        self.name = name
        self.last_w = None
        self.readers = []

    def __getitem__(self, key):
        return View(self, self.ap[key])

    def v(self):
        return View(self, self.ap)


class View:
    def __init__(self, buf, ap):
        self.buf = buf
        self.ap = ap

    def __getitem__(self, key):
        return View(self.buf, self.ap[key])

    def rr(self, s, **kw):
        return View(self.buf, self.ap.rearrange(s, **kw))

    def bc(self, shape):
        return View(self.buf, self.ap.to_broadcast(list(shape)))

    def unsq(self, axis):
        return View(self.buf, self.ap.unsqueeze(axis))

    def bitcast(self, dt):
        return View(self.buf, self.ap.bitcast(dt))


class Op:
    __slots__ = ("eng", "meth", "kw", "reads", "writes", "deps", "idx", "is_dma", "inc_val", "key")


WRITE_KEYS = ("out", "accum_out")
ENGS = ("pe", "act", "dve", "pool", "sp")


class Prog:
    def __init__(self, nc):
        self.nc = nc
        self.ops = []
        self.bar = set()
        self.bpoints = []
        self.capture = None
        self.arena = None
        self.aoff = 0
        self.asize = 0

    def barrier(self):
        last = {}
        dm = set()
        for op in self.ops:
            if op.is_dma:
                dm.add(op.idx)
            else:
                last[op.eng] = op.idx
        self.bar = set(last.values()) | dm
        self.aoff = 0
        self.bpoints.append(len(self.ops))

    def ov_top(self, name, shape, dtype):
        n = 1
        for d in shape[1:]:
            n *= d
        nbytes = n * (4 if dtype == F32 else 2)
        n4 = (nbytes + 31) // 32 * 8
        self.atop = self.asize - n4
        ap = self.arena[:, self.atop:self.atop + n4]
        if dtype != F32:
            ap = ap.bitcast(dtype)
        ap = ap[:, 0:n]
        if len(shape) == 3:
            ap = ap.rearrange("p (a b) -> p a b", a=shape[1])
        return Buf(ap, name)

    def ov(self, name, shape, dtype):
        n = 1
        for d in shape[1:]:
            n *= d
        nbytes = n * (4 if dtype == F32 else 2)
        n4 = (nbytes + 31) // 32 * 8
        assert self.aoff + n4 <= getattr(self, "atop", self.asize), (name, self.aoff, n4, self.asize)
        ap = self.arena[:, self.aoff:self.aoff + n4]
        self.aoff += n4
        if dtype != F32:
            ap = ap.bitcast(dtype)
        ap = ap[:, 0:n]
        if len(shape) == 3:
            ap = ap.rearrange("p (a b) -> p a b", a=shape[1])
        return Buf(ap, name)

    def sb(self, name, shape, dtype):
        t = self.nc.alloc_sbuf_tensor(name, list(shape), dtype)
        return Buf(t.ap(), name)

    def ps(self, name, shape, dtype=F32):
        t = self.nc.alloc_psum_tensor(name, list(shape), dtype)
        return Buf(t.ap(), name)

    def do(self, eng, meth, extra_reads=(), extra_writes=(), **kw):
        if self.capture is not None:
            self.capture.append((eng, meth, extra_reads, extra_writes, kw))
            return None
        reads, writes, real = [], [], {}
        for k, v in kw.items():
            if isinstance(v, View):
                (writes if k in WRITE_KEYS else reads).append(v.buf)
                real[k] = v.ap
            elif isinstance(v, Buf):
                (writes if k in WRITE_KEYS else reads).append(v)
                real[k] = v.ap
            else:
                real[k] = v
        for b in extra_reads:
            reads.append(b.buf if isinstance(b, View) else b)
        for b in extra_writes:
            writes.append(b.buf if isinstance(b, View) else b)
        op = Op()
        op.eng, op.meth, op.kw, op.reads, op.writes = eng, meth, real, reads, writes
        op.is_dma = meth in ("dma_start",)
        op.inc_val = None
        op.key = None
        op.idx = len(self.ops)
        deps = set()
        for r in reads:
            if r.last_w is not None:
                deps.add(r.last_w)
        for w in writes:
            if w.last_w is not None:
                deps.add(w.last_w)
            deps.update(w.readers)
        for r in reads:
            r.readers.append(op.idx)
        for w in writes:
            w.last_w = op.idx
            w.readers = []
        deps.discard(op.idx)
        deps |= self.bar
        if eng == "pe":
            deps = {d for d in deps if self.ops[d].eng != "pe"}
        op.deps = deps
        self.ops.append(op)
        return op

    def emit(self):
        nc = self.nc
        ops = self.ops
        has_dep = [False] * len(ops)
        for op in ops:
            for d in op.deps:
                has_dep[d] = True
        eng_sem = {e: nc.alloc_semaphore("sem_" + e) for e in ENGS}
        eng_cnt = {e: 0 for e in ENGS}
        dma_sems = {}
        cur = {}
        swsems = {}
        free = []
        bset = set(self.bpoints)
        for op in ops:
            if op.idx in bset:
                free.extend(cur.values())
                cur = {}
            if op.is_dma:
                key = op.writes[0] if op.writes else op.reads[0]
                if op.eng == "pool":
                    ent = swsems.get(id(key))
                    if ent is None:
                        ent = [nc.alloc_semaphore("swsem%d" % len(swsems)), 0]
                        swsems[id(key)] = ent
                        dma_sems["sw%d" % len(swsems)] = ent
                else:
                    ent = cur.get(id(key))
                    if ent is None:
                        if free:
                            ent = free.pop()
                        else:
                            ent = [nc.alloc_semaphore("dsem%d" % len(dma_sems)), 0]
                            dma_sems[len(dma_sems)] = ent
                        cur[id(key)] = ent
                ent[1] += 16
                op.inc_val = (ent[0], ent[1], 16)
            elif has_dep[op.idx]:
                eng_cnt[op.eng] += 1
                op.inc_val = (eng_sem[op.eng], eng_cnt[op.eng], 1)
        streams = {e: [] for e in ENGS}
        for op in ops:
            streams[op.eng].append(op)
        engmap = {"pe": "tensor", "act": "scalar", "dve": "vector", "pool": "gpsimd", "sp": "sync"}

        def run_stream(ename):
            def body(eng):
                waited = {}
                for op in streams[ename]:
                    need = {}
                    for d in op.deps:
                        s, val, _ = ops[d].inc_val
                        k = id(s)
                        if waited.get(k, 0) >= val:
                            continue
                        if k not in need or need[k][1] < val:
                            need[k] = (s, val)
                    for k, (s, val) in need.items():
                        eng.wait_ge(s, val)
                        waited[k] = val
                    ins = getattr(eng, op.meth)(**op.kw)
                    if op.inc_val is not None:
                        ins.then_inc(op.inc_val[0], op.inc_val[2])
                if ename == "sp":
                    for key, (s, tot) in dma_sems.items():
                        eng.wait_ge(s, tot)
            return body

        with nc.Block() as block:
            for e in ENGS:
                getattr(block, engmap[e])(run_stream(e))


def build_program():
    nc = bass.Bass("TRN2", target_bir_lowering=False)
    P = Prog(nc)

    def din(name, shape):
        return nc.dram_tensor(name, list(shape), F32, kind="ExternalInput").ap()

    def dout(name, shape):
        return nc.dram_tensor(name, list(shape), F32, kind="ExternalOutput").ap()

    xp = din("xp", [TP, D])
    xs = din("xs", [NS * TS, D])
    c17 = din("c17", [1 + NS, D])
    st_ssm = din("st_ssm", [DEPTH, NS, 16, 64, 128])
    st_sconv = din("st_sconv", [DEPTH, NS, 3, 1536])
    st_mc = din("st_mc", [DEPTH, NS, 4, 128, 128])
    st_mn = din("st_mn", [DEPTH, NS, 4, 128])
    st_mm = din("st_mm", [DEPTH, NS, 4])
    st_lh = din("st_lh", [DEPTH, NS, 512])
    st_lconv = din("st_lconv", [DEPTH, NS, 3, 512])
    ada_w = din("ada_w", [DEPTH, D, 6 * D])
    ada_b = din("ada_b", [DEPTH, 6 * D])
    norm1_w = din("norm1_w", [DEPTH, D])
    norm2_w = din("norm2_w", [DEPTH, D])
    w_in = din("w_in", [DEPTH, D, IN_DIM])
    ssd_conv_w = din("ssd_conv_w", [DEPTH, 4, 1536])
    ssd_conv_b = din("ssd_conv_b", [DEPTH, 1536])
    ssd_dt_bias = din("ssd_dt_bias", [DEPTH, 16])
    ssd_a_log = din("ssd_a_log", [DEPTH, 16])
    ssd_d = din("ssd_d", [DEPTH, 16])
    ssd_norm_w = din("ssd_norm_w", [DEPTH, 1024])
    ml_i_bias = din("ml_i_bias", [DEPTH, 4])
    ml_f_bias = din("ml_f_bias", [DEPTH, 4])
    ml_norm_w = din("ml_norm_w", [DEPTH, 512])
    lru_conv_w = din("lru_conv_w", [DEPTH, 4, 512])
    lru_conv_b = din("lru_conv_b", [DEPTH, 512])
    lru_wa = din("lru_wa", [DEPTH, 4, 128, 128])
    lru_ba = din("lru_ba", [DEPTH, 512])
    lru_wx = din("lru_wx", [DEPTH, 4, 128, 128])
    lru_bx = din("lru_bx", [DEPTH, 512])
    lru_lambda = din("lru_lambda", [DEPTH, 512])
    w_out = din("w_out", [DEPTH, 2 * D, D])
    mlp_up = din("mlp_up", [DEPTH, D, 4 * D])
    mlp_down = din("mlp_down", [DEPTH, 4 * D, D])
    final_norm_w = din("final_norm_w", [1, D])
    y_p = dout("y_p", [TP, D])
    y_s = dout("y_s", [NS * TS, D])
    o_ssm = dout("o_ssm", [DEPTH, 1 + NS, 16, 64, 128])
    o_sconv = dout("o_sconv", [DEPTH, 1 + NS, 3, 1536])
    o_mc = dout("o_mc", [DEPTH, 1 + NS, 4, 128, 128])
    o_mn = dout("o_mn", [DEPTH, 1 + NS, 4, 128])
    o_mm = dout("o_mm", [DEPTH, 1 + NS, 4])
    o_lh = dout("o_lh", [DEPTH, 1 + NS, 512])
    o_lconv = dout("o_lconv", [DEPTH, 1 + NS, 3, 512])

    def mm(out, lhsT, rhs, start=True, stop=True):
        P.do("pe", "matmul", out=out, lhsT=lhsT, rhs=rhs, start=start, stop=stop)

    def tr(out, in_, ident):
        P.do("pe", "transpose", out=out, in_=in_, identity=ident)

    def act(out, in_, func, bias=0.0, scale=1.0, eng="act", **kw):
        P.do("act", "activation", out=out, in_=in_, func=func, bias=bias, scale=scale, **kw)

    def tt(out, in0, in1, op, eng="dve"):
        P.do(eng, "tensor_tensor", out=out, in0=in0, in1=in1, op=op)

    def ts(out, in0, s1, s2, op0, op1=None, eng="dve"):
        if op1 is None:
            P.do(eng, "tensor_scalar", out=out, in0=in0, scalar1=s1, scalar2=None, op0=op0)
        else:
            P.do(eng, "tensor_scalar", out=out, in0=in0, scalar1=s1, scalar2=s2, op0=op0, op1=op1)

    def stt(out, in0, scalar, in1, op0, op1):
        P.do("dve", "scalar_tensor_tensor", out=out, in0=in0, scalar=scalar, in1=in1, op0=op0, op1=op1)

    def cp(out, in_, eng="dve"):
        P.do(eng, "tensor_copy", out=out, in_=in_)

    def dma(out, in_, eng="sp"):
        P.do(eng, "dma_start", out=out, in_=in_)

    def dma_slow(out, in_, eng="sp"):
        P.do(eng, "dma_start", out=out, in_=in_, allow_slow_non_contiguous=True)

    def memset(buf_view, val, eng="pool"):
        P.do(eng, "memset", ap=buf_view.ap, constant=val, extra_writes=[buf_view.buf])

    identf = P.sb("identf", [128, 128], F32)
    identb = P.sb("identb", [128, 128], BF16)
    ones_mean = P.sb("ones_mean", [128, 128], BF16)
    onesf = P.sb("onesf", [128, 128], F32)
    ucum = P.sb("ucum", [128, 128], F32)
    negrep = P.sb("negrep", [128, 4, 128], BF16)
    negT = P.sb("negT", [128, 4, 128], BF16)
    sel127 = P.sb("sel127", [128, 128], F32)
    sel7 = P.sb("sel7", [128, 128], F32)
    memset(identf.v(), 1.0)
    P.do("pool", "affine_select", out=identf.v(), in_=identf.v(), pattern=[[-1, 128]], compare_op=ALU.is_equal,
         fill=0.0, base=0, channel_multiplier=1)
    cp(identb.v(), identf.v(), eng="pool")
    memset(ones_mean.v(), 1.0 / 1024.0)
    memset(onesf.v(), 1.0)
    memset(ucum.v(), 1.0)
    P.do("pool", "affine_select", out=ucum.v(), in_=ucum.v(), pattern=[[1, 128]], compare_op=ALU.is_ge,
         fill=0.0, base=0, channel_multiplier=-1)
    negtmp = P.sb("negtmp", [128, 128], F32)
    memset(negtmp.v(), 0.0)
    P.do("pool", "affine_select", out=negtmp.v(), in_=negtmp.v(), pattern=[[1, 128]], compare_op=ALU.is_ge,
         fill=NEG, base=0, channel_multiplier=-1)
    for i in range(4):
        cp(negrep[:, i, :], negtmp.v(), eng="pool")
    memset(negtmp.v(), 0.0)
    P.do("pool", "affine_select", out=negtmp.v(), in_=negtmp.v(), pattern=[[-1, 128]], compare_op=ALU.is_ge,
         fill=NEG, base=0, channel_multiplier=1)
    for i in range(4):
        cp(negT[:, i, :], negtmp.v(), eng="pool")
    for selt, row in ((sel127, 127), (sel7, 7)):
        memset(selt.v(), 1.0)
        P.do("pool", "affine_select", out=selt.v(), in_=selt.v(), pattern=[[0, 128]], compare_op=ALU.is_equal,
             fill=0.0, base=-row, channel_multiplier=1)

    PA = P.ps("PA", [128, 1024], F32)
    PB = P.ps("PB", [128, 2048], F32)
    PC = P.ps("PC", [128, 512], F32)
    PF = P.ps("PF", [128, 512], F32)

    x = P.sb("x", [128, 8, NTOK], F32)
    hn_d = nc.dram_tensor("hn_d", [NTOK // 128, 128, 1024], BF16, kind="Internal").ap()
    NCH = NTOK // 128
    xc = [Buf(x.ap[:, :, c * 128:(c + 1) * 128], "x%d" % c) for c in range(NCH)]
    hd = [Buf(hn_d[c].rearrange("p (k t) -> p k t", k=8), "hd%d" % c) for c in range(NCH)]


    def acp(out, in_):
        P.do("act", "activation", out=out, in_=in_, func=AF.Copy)

    Wbuf = Wo = None
    stage = P.sb("stage", [128, 128], F32)
    parA = P.sb("parA", [128, 104], F32)
    parB = P.sb("parB", [128, 60], F32)
    mod = P.sb("mod", [128, 48, 17], F32)
    s1 = P.sb("s1", [128, 8, 17], F32)
    s2 = P.sb("s2", [128, 8, 17], F32)
    bc_dtb = P.sb("bc_dtb", [128, 16], F32)
    bc_A = P.sb("bc_A", [128, 16], F32)
    bc_D = P.sb("bc_D", [128, 16], F32)
    bc_ib = P.sb("bc_ib", [128, 4], F32)
    bc_fb = P.sb("bc_fb", [128, 4], F32)
    lru_c1 = P.sb("lru_c1", [128, 4], F32)
    fnw = P.sb("fnw", [128, 8], F32)
    cs_fm = P.sb("cs_fm", [128, 8, 17], BF16)
    eps_t = P.sb("eps_t", [128, 1], F32)
    onesb = P.sb("onesb", [128, 1], BF16)
    memset(eps_t.v(), EPS)
    memset(onesb.v(), 1.0)
    sconv_io = P.sb("sconv_io", [128, 12, 48], F32)
    lconv_io = P.sb("lconv_io", [128, 4, 48], F32)
    lh_io = P.sb("lh_io", [128, 4, 16], F32)
    mn_io = P.sb("mn_io", [128, 64], F32)
    A4 = nc.sbuf_bytes_remaining // 4 - 64
    A4 = A4 // 8 * 8
    P.arena = nc.alloc_sbuf_tensor("arena", [128, A4], F32).ap()
    P.asize = A4

    xins = [P.ov("xin%d" % i, [128, 1024], F32) for i in range(3)]
    pbs = [PA, Buf(PB.ap[:, 0:1024], "pb1"), Buf(PB.ap[:, 1024:2048], "pb2")]
    for c in range(NCH):
        src = xp[c * 128:(c + 1) * 128, :] if c < 16 else xs[:, :]
        xin = xins[c % 3]
        pb = pbs[c % 3]
        dma(xin.v(), src)
        for k in range(8):
            tr(pb[:, k * 128:(k + 1) * 128], xin[:, k * 128:(k + 1) * 128], identf.v())
        if c % 2 == 0:
            cp(xc[c].v(), pb.v().rr("p (k t) -> p k t", k=8))
        else:
            acp(xc[c].v(), pb.v().rr("p (k t) -> p k t", k=8))
    P.barrier()

    cin = P.ov("cin", [128, 1024], F32)
    dma(cin[0:17, :], c17)
    act(cin[0:17, :], cin[0:17, :], AF.Silu)
    for k in range(8):
        tr(PC[:, k * 17:(k + 1) * 17], cin[0:17, k * 128:(k + 1) * 128], identf[0:17, 0:17])
    cp(cs_fm.v(), PC[:, 0:136].rr("p (k s) -> p k s", k=8))

    def load_fm(dst_view, rows_list):
        r0 = 0
        for ap in rows_list:
            r = ap.shape[0]
            dma(stage[r0:r0 + r, :], ap)
            r0 += r
        tr(PC[:, 0:r0], stage[0:r0, :], identf[0:r0, 0:r0])
        cp(dst_view, PC[:, 0:r0])

    def load_bc(dst_view, ap_row, n):
        dma(dst_view, ap_row.to_broadcast([128, n]))

    load_fm(fnw.v(), [final_norm_w.rearrange("o (k p) -> (o k) p", p=128)])

    def alloc_norm():
        return ([P.ov("adaslab%d" % i, [128, 8, 512], BF16) for i in range(2)],
                [P.ov("sq%d" % i, [128, 8, 128], BF16) for i in range(2)],
                [P.ov("rstd%d" % i, [128, 128], F32) for i in range(2)],
                [P.ov("ntmp%d" % i, [128, 8, 128], F32) for i in range(2)])
    adaslabs = sqs = rstds = ntmps = None
    nbanks = [PC, PF]

    def rmsnorm_mod(c, sc_t, sh_off, dstb):
        sq, rstd, ntmp, nb = sqs[c % 2], rstds[c % 2], ntmps[c % 2], nbanks[c % 2]
        P.do("pool" if c % 2 else "act", "tensor_tensor" if c % 2 else "activation",
             **(dict(out=sq.v(), in0=xc[c].v(), in1=xc[c].v(), op=ALU.mult) if c % 2 else
                dict(out=sq.v(), in_=xc[c].v(), func=AF.Square)))
        for k in range(8):
            mm(nb[:, 0:128], ones_mean.v(), sq[:, k, :], start=(k == 0), stop=(k == 7))
        act(rstd.v(), nb[:, 0:128], AF.Ln, bias=eps_t[:, 0:1], scale=1.0)
        act(rstd.v(), rstd.v(), AF.Exp, scale=-0.5)
        tt(ntmp.v(), xc[c].v(), rstd.v().unsq(1).bc([128, 8, 128]), ALU.mult)
        if c < 16:
            for k in range(8):
                act(dstb[:, k, :], ntmp[:, k, :], AF.Identity, bias=mod[:, sh_off + k, 0:1], scale=sc_t[:, k, 0:1])
        else:
            for k in range(8):
                tt(ntmp[:, k, :].rr("p (b t) -> p b t", t=8), ntmp[:, k, :].rr("p (b t) -> p b t", t=8),
                   sc_t[:, k, 1:17].unsq(2).bc([128, 16, 8]), ALU.mult)
                tt(dstb[:, k, :].rr("p (b t) -> p b t", t=8), ntmp[:, k, :].rr("p (b t) -> p b t", t=8),
                   mod[:, sh_off + k, 1:17].unsq(2).bc([128, 16, 8]), ALU.add)


    zs = xpad = cacc = xbc = xB = sm = Rb = dec = MT = xdt = xend = yy = y2 = ynb = yfm = Snat = STb = ss = None

    def alloc_ssd():
        return dict(zs=P.ov("zs", [128, 1024], BF16), xpad=P.ov("xpad", [128, 12, 131], F32),
                    cacc=P.ov("cacc", [128, 128], F32), xbc=P.ov("xbc", [128, 12, 128], BF16),
                    xB=P.ov("xB", [128, 1280], BF16), sm=P.ov("sm", [128, 256], F32),
                    Rb=P.ov("Rb", [128, 4, 128], F32), dec=P.ov("dec", [128, 4, 128], F32),
                    MT=P.ov("MT", [128, 16, 128], BF16), xdt=P.ov("xdt", [128, 1024], BF16),
                    yy=P.ov("yy", [128, 1024], F32), ynb=P.ov("ynb", [128, 1024], BF16),
                    yfm=P.ov("yfm", [128, 8, 128], BF16), Snat=P.ov("Snat", [128, 8, 128], F32),
                    ss=P.ov("ss", [128, 8], F32))

    def w_out_and_update(l, tok0, L, nk, seqcol):
        c = tok0 // 128
        o = tok0 - c * 128
        for oc in range(8):
            for k in range(nk):
                mm(PB[:, oc * 128:oc * 128 + L], Wo[:, k, oc * 128:(oc + 1) * 128], yfm[:, k, 0:L],
                   start=(k == 0), stop=(k == nk - 1))
        for oc in range(8):
            stt(xc[c][:, oc, o:o + L], PB[:, oc * 128:oc * 128 + L], mod[:, 16 + oc, seqcol:seqcol + 1],
                xc[c][:, oc, o:o + L], ALU.mult, ALU.add)

    def units():
        for c in range(16):
            yield (c * 128, 128, 0, c == 0, c == 15, None)
        for b in range(NS):
            yield (TP + b * TS, TS, 1 + b, True, True, b)

    def ssd_unit(l, tok0, L, seqcol, first, last, b):
        c = tok0 // 128
        o = tok0 - c * 128
        if b is None:
            H = hbuf[c % 2]
            dma(H.v(), hd[c].v())
            o = 0
        else:
            H = hsamp
        sel = sel127 if L == 128 else sel7
        if b is not None:
            dma(Snat.v(), st_ssm[l, b].rearrange("(hp h2) p n -> (h2 p) hp n", h2=2))
            for j in range(3):
                dma_slow(xpad[:, :, j], st_sconv[l, b, j].rearrange("(k p) -> p k", p=128))
        elif first:
            memset(Snat.v(), 0.0)
            memset(xpad[:, :, 0:3], 0.0)
        for j in range(2):
            for k in range(8):
                mm(PA[0:L, j * 512:(j + 1) * 512], H[:, k, o:o + L], Wbuf[:, k, j * 512:(j + 1) * 512],
                   start=(k == 0), stop=(k == 7))
        act(zs[0:L, :], PA[0:L, :], AF.Silu)
        for oc in range(12):
            for k in range(8):
                mm(PB[:, oc * 128:oc * 128 + L], Wbuf[:, k, 1024 + oc * 128:1024 + (oc + 1) * 128], H[:, k, o:o + L],
                   start=(k == 0), stop=(k == 7))
        acp(xpad[:, :, 3:3 + L], PB[:, 0:1536].rr("p (k t) -> p k t", k=12)[:, :, 0:L])
        for k in range(8):
            mm(PC[0:L, 0:16], H[:, k, o:o + L], Wbuf[:, k, 2560:2576], start=(k == 0), stop=(k == 7))
        dt = sm[0:L, 0:16]
        dtA = sm[0:L, 16:32]
        cum = sm[0:L, 32:48]
        tt(dt, PC[0:L, 0:16], bc_dtb[0:L, :], ALU.add)
        act(dt, dt, AF.Exp)
        act(dt, dt, AF.Ln, bias=1.0)
        tt(dtA, dt, bc_A[0:L, :], ALU.mult)
        for oc in range(12):
            ts(cacc[:, 0:L], xpad[:, oc, 0:L], parB[:, oc:oc + 1], parB[:, 48 + oc:49 + oc], ALU.mult, ALU.add)
            for j in range(1, 4):
                stt(cacc[:, 0:L], xpad[:, oc, j:j + L], parB[:, j * 12 + oc:j * 12 + oc + 1], cacc[:, 0:L], ALU.mult, ALU.add)
            act(xbc[:, oc, 0:L], cacc[:, 0:L], AF.Silu)
        if last:
            for j in range(3):
                dma_slow(o_sconv[l, seqcol, j].rearrange("(k p) -> p k", p=128), xpad[:, :, L + j])
        elif b is None:
            cp(xpad[:, :, 0:3], xpad[:, :, L:L + 3], eng="pool")
        PAb = PA.v().bitcast(BF16)
        for oc in range(10):
            tr(PAb[0:L, oc * 128:(oc + 1) * 128], xbc[:, oc, 0:L], identb.v())
        acp(xB[0:L, :], PAb[0:L, 0:1280])
        mm(PC[0:L, 16:32], ucum[0:L, 0:L], dtA)
        cp(cum, PC[0:L, 16:32])
        for g in range(2):
            mm(PF[0:L, g * 128:g * 128 + L], xbc[:, 8 + g, 0:L], xbc[:, 10 + g, 0:L])
        for q in range(4):
            g = q // 2
            tt(Rb[0:L, :, 0:L], ucum[0:L, 0:L].unsq(1).bc([L, 4, L]), dtA[:, 4 * q:4 * q + 4].unsq(2).bc([L, 4, L]), ALU.mult)
            outv = PB[0:L, q * 512:(q + 1) * 512].rr("p (h t) -> p h t", h=4)[:, :, 0:L]
            mm(outv, onesf[0:L, 0:L], Rb[0:L, :, 0:L], start=True, stop=False)
            mm(outv, identb[0:L, 0:L], negrep[0:L, :, 0:L], start=False, stop=True)
            tt(dec[0:L, :, 0:L], outv, cum[:, 4 * q:4 * q + 4].unsq(2).bc([L, 4, L]), ALU.subtract)
            act(dec[0:L, :, 0:L], dec[0:L, :, 0:L], AF.Exp)
            tt(MT[0:L, 4 * q:4 * q + 4, 0:L], dec[0:L, :, 0:L],
               PF[0:L, g * 128:g * 128 + L].unsq(1).bc([L, 4, L]), ALU.mult)
        tt(xdt[0:L, :].rr("t (h p) -> t h p", p=64), xB[0:L, 0:1024].rr("t (h p) -> t h p", p=64),
           dt.unsq(2).bc([L, 16, 64]), ALU.mult)
        for h in range(16):
            mm(PA[0:L, h * 64:(h + 1) * 64], MT[0:L, h, 0:L], xdt[0:L, h * 64:(h + 1) * 64])
        for hp in range(8):
            tr(PB[:, hp * 128:(hp + 1) * 128], Snat[:, hp, :], identf.v())
        acp(STb.v(), PB[:, 0:1024])
        for g in range(2):
            mm(PB[0:L, 1024 + g * 512:1024 + (g + 1) * 512], xbc[:, 10 + g, 0:L], STb[:, g * 512:(g + 1) * 512])
        ecum = sm[0:L, 48:64]
        act(ecum, cum, AF.Exp)
        tt(yy[0:L, :].rr("t (h p) -> t h p", p=64), PB[0:L, 1024:2048].rr("t (h p) -> t h p", p=64),
           ecum.unsq(2).bc([L, 16, 64]), ALU.mult)
        tt(yy[0:L, :], yy[0:L, :], PA[0:L, :], ALU.add)
        tt(y2[0:L, :].rr("t (h p) -> t h p", p=64), xB[0:L, 0:1024].rr("t (h p) -> t h p", p=64),
           bc_D[0:L, :].unsq(2).bc([L, 16, 64]), ALU.mult)
        tt(yy[0:L, :], yy[0:L, :], y2[0:L, :], ALU.add)
        tt(yy[0:L, :], yy[0:L, :], zs[0:L, :], ALU.mult)
        act(y2[0:L, :], yy[0:L, :], AF.Square, accum_out=ss[0:L, 0:1])
        act(ss[0:L, 1:2], ss[0:L, 0:1], AF.Ln, bias=eps_t[0:L, 0:1], scale=1.0 / 1024.0)
        act(ss[0:L, 2:3], ss[0:L, 1:2], AF.Exp, scale=-0.5)
        ts(ynb[0:L, :], yy[0:L, :], ss[0:L, 2:3], None, ALU.mult)
        for k in range(8):
            tr(PAb[:, k * 128:k * 128 + L], ynb[0:L, k * 128:(k + 1) * 128], identb[0:L, 0:L])
        tt(yfm[:, :, 0:L], PAb[:, 0:1024].rr("p (k t) -> p k t", k=8)[:, :, 0:L],
           parA[:, 64:72].unsq(2).bc([128, 8, L]), ALU.mult)
        w_out_and_update(l, tok0, L, 8, seqcol)
        mm(PC[:, 32:48], sel[0:L, :], cum)
        cl = sm[:, 64:80]
        cp(cl, PC[:, 32:48])
        eend = sm[0:L, 80:96]
        tt(eend, cl[0:L, :], cum, ALU.subtract)
        act(eend, eend, AF.Exp)
        tt(xend[0:L, :].rr("t (h p) -> t h p", p=64), xdt[0:L, :].rr("t (h p) -> t h p", p=64),
           eend.unsq(2).bc([L, 16, 64]), ALU.mult)
        dcy = sm[:, 96:112]
        act(dcy, cl, AF.Exp)
        for hp in range(8):
            g = hp // 4
            mm(PA[:, hp * 128:(hp + 1) * 128], xend[0:L, hp * 128:(hp + 1) * 128], xB[0:L, 1024 + g * 128:1024 + (g + 1) * 128])
        dcyv = dcy.rr("p (hp h2) -> p hp h2", h2=2)
        for h2 in range(2):
            rs = slice(64 * h2, 64 * h2 + 64)
            tt(Snat[rs, :, :], Snat[rs, :, :], dcyv[rs, :, h2:h2 + 1].bc([64, 8, 128]), ALU.mult)
            tt(Snat[rs, :, :], Snat[rs, :, :], PA[rs, :].rr("p (hp n) -> p hp n", hp=8), ALU.add)
        if last:
            dma(o_ssm[l, seqcol].rearrange("(hp h2) p n -> (h2 p) hp n", h2=2), Snat.v())

    hbuf = hsamp = None
    lpad = lx = lxb = lri = la = lu = lh = lg = hstate = wg = None

    def alloc_lru():
        return dict(lpad=P.ov("lpad", [128, 4, 131], F32), lx=P.ov("lx", [128, 4, 128], F32),
                    lxb=P.ov("lxb", [128, 4, 128], BF16), lri=P.ov("lri", [128, 8, 128], F32),
                    la=P.ov("la", [128, 4, 128], F32), lu=P.ov("lu", [128, 4, 128], F32),
                    lh=P.ov("lh", [128, 4, 128], F32), lg=P.ov("lg", [128, 4, 128], F32),
                    hstate=P.ov("hstate", [128, 4], F32), wg=P.ov("wg", [128, 8, 128], BF16),
                    yfm=P.ov("yfm", [128, 8, 128], BF16))

    def lru_unit(l, tok0, L, seqcol, first, last, b):
        c = tok0 // 128
        o = tok0 - c * 128
        if b is None:
            H = hbuf[c % 2]
            dma(H.v(), hd[c].v())
            o = 0
        else:
            H = hsamp
        if b is not None:
            dma_slow(hstate.v(), st_lh[l, b].rearrange("(k p) -> p k", p=128))
            for j in range(3):
                dma_slow(lpad[:, :, j], st_lconv[l, b, j].rearrange("(k p) -> p k", p=128))
        elif first:
            memset(hstate.v(), 0.0)
            memset(lpad[:, :, 0:3], 0.0)
        for oc in range(8):
            for k in range(8):
                mm(PA[:, oc * 128:oc * 128 + L], Wbuf[:, k, oc * 128:(oc + 1) * 128], H[:, k, o:o + L],
                   start=(k == 0), stop=(k == 7))
        PAv = PA.v().rr("p (k t) -> p k t", k=8)
        acp(lpad[:, :, 3:3 + L], PAv[:, 0:4, 0:L])
        act(lg[:, :, 0:L], PAv[:, 4:8, 0:L], AF.Gelu)
        def cwv(j):
            return parA[:, 72 + j * 4:72 + (j + 1) * 4].unsq(2).bc([128, 4, L])
        tt(lx[:, :, 0:L], lpad[:, :, 0:L], cwv(0), ALU.mult)
        for j in range(1, 4):
            tt(lu[:, :, 0:L], lpad[:, :, j:j + L], cwv(j), ALU.mult)
            tt(lx[:, :, 0:L], lx[:, :, 0:L], lu[:, :, 0:L], ALU.add)
        tt(lx[:, :, 0:L], lx[:, :, 0:L], parA[:, 88:92].unsq(2).bc([128, 4, L]), ALU.add)
        cp(lxb[:, :, 0:L], lx[:, :, 0:L])
        if last:
            for j in range(3):
                dma_slow(o_lconv[l, seqcol, j].rearrange("(k p) -> p k", p=128), lpad[:, :, L + j])
        elif b is None:
            cp(lpad[:, :, 0:3], lpad[:, :, L:L + 3], eng="pool")
        for k in range(4):
            mm(PC[:, k * 128:k * 128 + L], wg[:, k, :], lxb[:, k, 0:L])
        for k in range(4):
            mm(PF[:, k * 128:k * 128 + L], wg[:, 4 + k, :], lxb[:, k, 0:L])
        tt(lri[:, 0:4, 0:L], PC.v().rr("p (k t) -> p k t", k=4)[:, :, 0:L], parA[:, 92:96].unsq(2).bc([128, 4, L]), ALU.add)
        tt(lri[:, 4:8, 0:L], PF.v().rr("p (k t) -> p k t", k=4)[:, :, 0:L], parA[:, 96:100].unsq(2).bc([128, 4, L]), ALU.add)
        act(lri[:, :, 0:L], lri[:, :, 0:L], AF.Sigmoid)
        for k in range(4):
            act(la[:, k, 0:L], lri[:, k, 0:L], AF.Exp, scale=lru_c1[:, k:k + 1])
        tt(lu[:, :, 0:L], la[:, :, 0:L], la[:, :, 0:L], ALU.mult)
        ts(lu[:, :, 0:L], lu[:, :, 0:L], -1.0, 1.0, ALU.mult, ALU.add)
        ts(lu[:, :, 0:L], lu[:, :, 0:L], 0.0, None, ALU.max)
        act(lu[:, :, 0:L], lu[:, :, 0:L], AF.Sqrt)
        if b is None and first:
            memset(lu[:, :, 0:1], 1.0)
        tt(lu[:, :, 0:L], lu[:, :, 0:L], lri[:, 4:8, 0:L], ALU.mult)
        tt(lu[:, :, 0:L], lu[:, :, 0:L], lx[:, :, 0:L], ALU.mult)
        for k in range(4):
            P.do("dve", "tensor_tensor_scan", out=lh[:, k, 0:L], data0=la[:, k, 0:L], data1=lu[:, k, 0:L],
                 initial=hstate[:, k:k + 1], op0=ALU.mult, op1=ALU.add)
        cp(hstate.v(), lh[:, :, L - 1])
        if last:
            dma_slow(o_lh[l, seqcol].rearrange("(k p) -> p k", p=128), hstate.v())
        tt(yfm[:, 0:4, 0:L], lh[:, :, 0:L], lg[:, :, 0:L], ALU.mult)
        w_out_and_update(l, tok0, L, 4, seqcol)

    bc_mlw = None
    qk = ktm = vtm = vw = osig = g4 = R4 = d4 = w4 = Cnat = CTb = nfm = nfb = mbc = num = num2 = None

    def alloc_ml():
        return dict(qk=P.ov("qk", [128, 8, 128], BF16), ktm=P.ov("ktm", [128, 512], BF16),
                    vtm=P.ov("vtm", [128, 512], BF16), vw=P.ov("vw", [128, 512], BF16),
                    osig=P.ov("osig", [128, 512], F32), g4=P.ov("g4", [128, 128], F32),
                    R4=P.ov("R4", [128, 4, 128], F32), d4=P.ov("d4", [128, 4, 128], F32),
                    w4=P.ov("w4", [128, 4, 128], BF16), Cnat=P.ov("Cnat", [128, 4, 128], F32),
                    CTb=P.ov("CTb", [128, 4, 128], BF16), nfm=P.ov("nfm", [128, 4], F32),
                    nfb=P.ov("nfb", [128, 4], BF16), mbc=P.ov("mbc", [128, 4], F32),
                    num=P.ov("num", [128, 512], F32), num2=P.ov("num2", [128, 512], F32), bc_mlw=P.ov("bc_mlw", [128, 512], F32),
                    ynb=P.ov("ynb", [128, 1024], BF16), yfm=P.ov("yfm", [128, 8, 128], BF16))

    KSC = 128.0 ** -0.5

    def ml_unit(l, tok0, L, seqcol, first, last, b):
        c = tok0 // 128
        o = tok0 - c * 128
        if b is None:
            H = hbuf[c % 2]
            dma(H.v(), hd[c].v())
            o = 0
        else:
            H = hsamp
        sel = sel127 if L == 128 else sel7
        if b is not None:
            dma(Cnat.v(), st_mc[l, b].rearrange("h v d -> v h d"))
            dma_slow(nfm.v(), st_mn[l, b].rearrange("h d -> d h"))
            dma(mbc.v(), st_mm[l, b:b + 1, :].to_broadcast([128, 4]))
        elif first:
            memset(Cnat.v(), 0.0)
            memset(nfm.v(), 0.0)
            memset(mbc.v(), 0.0)
        for oc in range(8):
            for k in range(8):
                mm(PA[:, oc * 128:oc * 128 + L], Wbuf[:, k, oc * 128:(oc + 1) * 128], H[:, k, o:o + L],
                   start=(k == 0), stop=(k == 7))
        PAv = PA.v().rr("p (k t) -> p k t", k=8)
        acp(qk[:, 0:4, 0:L], PAv[:, 0:4, 0:L])
        act(qk[:, 4:8, 0:L], PAv[:, 4:8, 0:L], AF.Copy, scale=KSC)
        for j in range(3):
            for k in range(8):
                mm(PB[0:L, j * 512:(j + 1) * 512], H[:, k, o:o + L], Wbuf[:, k, 512 + j * 512:512 + (j + 1) * 512],
                   start=(k == 0), stop=(k == 7))
        for k in range(8):
            mm(PC[0:L, 0:8], H[:, k, o:o + L], Wbuf[:, k, 2048:2056], start=(k == 0), stop=(k == 7))
        act(ktm[0:L, :], PB[0:L, 0:512], AF.Copy, scale=KSC)
        acp(vtm[0:L, :], PB[0:L, 512:1024])
        act(osig[0:L, :], PB[0:L, 1024:1536], AF.Sigmoid)
        ig = g4[0:L, 0:4]
        lf = g4[0:L, 4:8]
        bcs = g4[0:L, 8:12]
        a_s = g4[0:L, 12:16]
        tt(ig, PC[0:L, 0:4], bc_ib[0:L, :], ALU.add)
        tt(lf, PC[0:L, 4:8], bc_fb[0:L, :], ALU.add)
        act(lf, lf, AF.Exp, scale=-1.0)
        act(lf, lf, AF.Ln, bias=1.0)
        ts(lf, lf, -1.0, None, ALU.mult)
        mm(PC[0:L, 8:12], ucum[0:L, 0:L], lf)
        cp(bcs, PC[0:L, 8:12])
        tt(a_s, ig, bcs, ALU.subtract)
        tt(R4[0:L, :, 0:L], identf[0:L, 0:L].unsq(1).bc([L, 4, L]), a_s.unsq(2).bc([L, 4, L]), ALU.mult)
        outv = PF.v().rr("p (h t) -> p h t", h=4)[0:L, :, 0:L]
        mm(outv, onesf[0:L, 0:L], R4[0:L, :, 0:L], start=True, stop=False)
        mm(outv, identb[0:L, 0:L], negT[0:L, :, 0:L], start=False, stop=True)
        cm = g4[0:L, 16:20]
        P.do("dve", "tensor_reduce", out=cm, in_=outv, axis=AX.X, op=ALU.max)
        r = g4[0:L, 20:24]
        tt(r, cm, mbc[0:L, :], ALU.max)
        mt = g4[0:L, 24:28]
        tt(mt, bcs, r, ALU.add)
        negr = g4[0:L, 28:32]
        ts(negr, r, -1.0, None, ALU.mult)
        tt(R4[0:L, :, 0:L], identf[0:L, 0:L].unsq(1).bc([L, 4, L]), negr.unsq(2).bc([L, 4, L]), ALU.mult)
        mm(outv, onesf[0:L, 0:L], R4[0:L, :, 0:L], start=True, stop=False)
        mm(outv, identb[0:L, 0:L], negrep[0:L, :, 0:L], start=False, stop=True)
        tt(d4[0:L, :, 0:L], outv, a_s.unsq(2).bc([L, 4, L]), ALU.add)
        act(d4[0:L, :, 0:L], d4[0:L, :, 0:L], AF.Exp)
        outc = PC.v().rr("p (h t) -> p h t", h=4)
        for h in range(4):
            mm(outc[0:L, h, 0:L], qk[:, 4 + h, 0:L], qk[:, h, 0:L])
        tt(w4[0:L, :, 0:L], d4[0:L, :, 0:L], outc[0:L, :, 0:L], ALU.mult)
        for h in range(4):
            mm(PA[0:L, h * 128:(h + 1) * 128], w4[0:L, h, 0:L], vtm[0:L, h * 128:(h + 1) * 128])
        for h in range(4):
            mm(PA[0:L, 512 + h:513 + h], w4[0:L, h, 0:L], onesb[0:L, :])
        for h in range(4):
            tr(PB[:, h * 128:(h + 1) * 128], Cnat[:, h, :], identf.v())
        acp(CTb.v(), PB[:, 0:512].rr("p (h v) -> p h v", h=4))
        cp(nfb.v(), nfm.v())
        for h in range(4):
            mm(PB[0:L, 512 + h * 128:512 + (h + 1) * 128], qk[:, h, 0:L], CTb[:, h, :])
        for h in range(4):
            mm(PB[0:L, 1024 + h:1025 + h], qk[:, h, 0:L], nfb[:, h:h + 1])
        inter = g4[0:L, 32:36]
        tt(inter, mbc[0:L, :], r, ALU.subtract)
        act(inter, inter, AF.Exp)
        tt(num[0:L, :].rr("t (h v) -> t h v", h=4), PB[0:L, 512:1024].rr("t (h v) -> t h v", h=4),
           inter.unsq(2).bc([L, 4, 128]), ALU.mult)
        tt(num[0:L, :], num[0:L, :], PA[0:L, 0:512], ALU.add)
        den = g4[0:L, 36:40]
        tt(den, PB[0:L, 1024:1028], inter, ALU.mult)
        tt(den, den, PA[0:L, 512:516], ALU.add)
        act(den, den, AF.Abs)
        emt = g4[0:L, 40:44]
        act(emt, mt, AF.Exp, scale=-1.0)
        tt(den, den, emt, ALU.max)
        P.do("dve", "reciprocal", out=den, in_=den)
        tt(num[0:L, :].rr("t (h v) -> t h v", h=4), num[0:L, :].rr("t (h v) -> t h v", h=4),
           den.unsq(2).bc([L, 4, 128]), ALU.mult)
        tt(num2[0:L, :], num[0:L, :], num[0:L, :], ALU.mult)
        ssq = g4[0:L, 44:48]
        P.do("dve", "tensor_reduce", out=ssq, in_=num2[0:L, :].rr("t (h v) -> t h v", h=4), axis=AX.X, op=ALU.add)
        act(ssq, ssq, AF.Sqrt, bias=eps_t[0:L, 0:1], scale=1.0 / 128.0)
        P.do("dve", "reciprocal", out=ssq, in_=ssq)
        tt(num[0:L, :].rr("t (h v) -> t h v", h=4), num[0:L, :].rr("t (h v) -> t h v", h=4),
           ssq.unsq(2).bc([L, 4, 128]), ALU.mult)
        tt(num[0:L, :], num[0:L, :], bc_mlw[0:L, :], ALU.mult)
        tt(ynb[0:L, 0:512], num[0:L, :], osig[0:L, :], ALU.mult)
        PAb = PA.v().bitcast(BF16)
        for k in range(4):
            tr(PAb[:, k * 128:k * 128 + L], ynb[0:L, k * 128:(k + 1) * 128], identb[0:L, 0:L])
        cp(yfm[:, 0:4, 0:L], PAb[:, 0:512].rr("p (k t) -> p k t", k=4)[:, :, 0:L])
        w_out_and_update(l, tok0, L, 4, seqcol)
        bm = g4[0:L, 48:56]
        cp(bm[:, 0:4], bcs)
        cp(bm[:, 4:8], mt)
        mm(PC[:, 0:8], sel[0:L, :], bm)
        last8 = g4[:, 56:64]
        cp(last8, PC[:, 0:8])
        wend = g4[0:L, 64:68]
        tt(wend, last8[0:L, 0:4], last8[0:L, 4:8], ALU.subtract)
        tt(wend, wend, a_s, ALU.add)
        act(wend, wend, AF.Exp)
        dc = g4[:, 68:72]
        tt(dc, last8[:, 0:4], mbc.v(), ALU.add)
        tt(dc, dc, last8[:, 4:8], ALU.subtract)
        act(dc, dc, AF.Exp)
        cp(mbc.v(), last8[:, 4:8])
        tt(vw[0:L, :].rr("t (h v) -> t h v", h=4), vtm[0:L, :].rr("t (h v) -> t h v", h=4),
           wend.unsq(2).bc([L, 4, 128]), ALU.mult)
        wendb = g4[0:L, 72:76].bitcast(BF16)[:, 0:4]
        cp(wendb, wend)
        for h in range(4):
            mm(PB[:, h * 128:(h + 1) * 128], vw[0:L, h * 128:(h + 1) * 128], ktm[0:L, h * 128:(h + 1) * 128])
        for h in range(4):
            mm(PB[:, 512 + h:513 + h], ktm[0:L, h * 128:(h + 1) * 128], wendb[:, h:h + 1])
        tt(Cnat.v(), Cnat.v(), dc.unsq(2).bc([128, 4, 128]), ALU.mult)
        tt(Cnat.v(), Cnat.v(), PB[:, 0:512].rr("p (h d) -> p h d", h=4), ALU.add)
        tt(nfm.v(), nfm.v(), dc, ALU.mult)
        tt(nfm.v(), nfm.v(), PB[:, 512:516], ALU.add)
        if last:
            dma(o_mc[l, seqcol].rearrange("h v d -> v h d"), Cnat.v())
            dma_slow(o_mn[l, seqcol].rearrange("h d -> d h"), nfm.v())
            dma(o_mm[l, seqcol:seqcol + 1, :], mbc[0:1, :])

    hid = upw = dnw = rl = modraw = None


    def run_pipeline(gens):
        def collect(g, until_early):
            P.capture = []
            try:
                while True:
                    v = next(g)
                    if until_early and v == "EARLY_DONE":
                        break
            except StopIteration:
                pass
            ops_ = P.capture
            P.capture = None
            return ops_

        sim = {"eng": {e: 0.0 for e in ENGS}, "w": {}, "r": {}}

        def bufs_of(it):
            eng, meth, er, ew, kw = it
            reads, writes = [], []
            for k, v in kw.items():
                bb = v.buf if isinstance(v, View) else (v if isinstance(v, Buf) else None)
                if bb is not None:
                    (writes if k in WRITE_KEYS else reads).append(bb)
            for x_ in er:
                reads.append(x_.buf if isinstance(x_, View) else x_)
            for x_ in ew:
                writes.append(x_.buf if isinstance(x_, View) else x_)
            return reads, writes

        def fsize(v):
            ap = v.ap if isinstance(v, (View, Buf)) else v
            n = 1
            for d in ap.shape[1:]:
                n *= d
            return n

        def dur_of(it):
            eng, meth, er, ew, kw = it
            if meth == "dma_start":
                o_ = kw["out"]
                return 2.0 + fsize(o_) * 128 * 4 / 150e3
            if eng == "pe":
                n = fsize(kw["rhs"]) if "rhs" in kw else 128
                d_ = max(n, 64) / 1200.0
                lt = kw.get("lhsT", kw.get("in_"))
                if lt is not None and (lt.ap if isinstance(lt, (View, Buf)) else lt).dtype == F32:
                    d_ *= 4 if meth == "matmul" else 1
                return d_
            o_ = kw.get("out", kw.get("ap"))
            n = fsize(o_) if o_ is not None else 64
            if eng == "dve":
                return 0.07 + n / 960.0
            if eng == "act":
                return 0.22 + n / 1200.0
            return 0.12 + n / 400.0

        def est_start(it):
            reads, writes = bufs_of(it)
            t = sim["eng"][it[0]]
            for r_ in reads:
                t = max(t, sim["w"].get(id(r_), 0.0) + 0.15)
            for w_ in writes:
                t = max(t, sim["w"].get(id(w_), 0.0) + 0.15, sim["r"].get(id(w_), 0.0) + 0.15)
            return t, reads, writes

        def commit(it, t, reads, writes):
            d_ = dur_of(it)
            issue = 0.03 if it[1] != "dma_start" else 0.1
            sim["eng"][it[0]] = (t + d_) if it[1] != "dma_start" else (t + issue)
            for r_ in reads:
                sim["r"][id(r_)] = max(sim["r"].get(id(r_), 0.0), t + d_)
            for w_ in writes:
                sim["w"][id(w_)] = t + d_
            P.do(it[0], it[1], it[2], it[3], **it[4])

        def merge(A, B):
            lst = list(B) + list(A)
            n = len(lst)
            lastw, readers = {}, {}
            preds = [set() for _ in range(n)]
            rws = []
            for idx, it in enumerate(lst):
                reads, writes = bufs_of(it)
                rws.append((reads, writes))
                for r_ in reads:
                    if id(r_) in lastw:
                        preds[idx].add(lastw[id(r_)])
                for w_ in writes:
                    if id(w_) in lastw:
                        preds[idx].add(lastw[id(w_)])
                    preds[idx].update(readers.get(id(w_), ()))
                for r_ in reads:
                    readers.setdefault(id(r_), []).append(idx)
                for w_ in writes:
                    lastw[id(w_)] = idx
                    readers[id(w_)] = []
                preds[idx].discard(idx)
            succs = [[] for _ in range(n)]
            indeg = [0] * n
            for idx in range(n):
                indeg[idx] = len(preds[idx])
                for p_ in preds[idx]:
                    succs[p_].append(idx)
            ready = [i for i in range(n) if indeg[i] == 0]
            while ready:
                best = None
                bt = None
                for i in ready:
                    t = est_start(lst[i])
                    if bt is None or (t[0], i) < (bt[0], best):
                        best, bt = i, t
                ready.remove(best)
                commit(lst[best], *bt)
                for s_ in succs[best]:
                    indeg[s_] -= 1
                    if indeg[s_] == 0:
                        ready.append(s_)
                n -= 1
            assert n == 0, "scheduler dropped ops"

        WIN = 32
        gens = list(gens)
        for w0 in range(0, len(gens), WIN):
            allops = []
            for g in gens[w0:w0 + WIN]:
                allops.extend(collect(g, False))
            merge(allops, [])

    def run_scheduled(fn):
        def _g():
            fn()
            yield
        run_pipeline([_g()])

    EA = EB = EC = EY = LA = LB = LC = None
    xbc2 = xB2 = xdt2 = sm2 = yi2 = cacc_t = caccs = ctmp_t = cts = Rbs = decs = None
    xpads = zss = yys = ynbs2 = Snats = sss = yfms_s = zsfm_all = None

    def ssd_gen(l, tok0, L, seqcol, first, last, b, ctx):
        c = tok0 // 128
        sel = sel127 if L == 128 else sel7
        xbc, xB, xdt, sm, yi = xbc2[ctx], xB2[ctx], xdt2[ctx], sm2[ctx], yi2[ctx]
        if b is None:
            xpad_, zs_, yy_, ynb_, Snat_, ss_, yfm_ = xpad, zs, yy, ynb, Snat, ss, yfm
        else:
            xpad_, zs_, yy_, ynb_, Snat_, ss_, yfm_ = xpads[b], zss[ctx], yys[ctx], ynbs2[ctx], Snats[ctx], sss[ctx], yfms_s[b]
        xend_ = zs_
        STb_ = ynb_
        if b is None:
            H = hbuf[ctx]
            dma(H.v(), hd[c].v())
            o = 0
        else:
            H = hsamp
            o = tok0 - c * 128
        if b is not None:
            cp(xpad_[:, :, 0:3], sconv_io[:, :, 3 * b:3 * b + 3], eng="pool")
        elif first:
            memset(xpad_[:, :, 0:3], 0.0)
        for r in range(3 if b is None else 0):
            bank = (EA, EB)[r % 2]
            for oo in range(4):
                oc = r * 4 + oo
                for k in range(8):
                    mm(bank[:, oo * 128:oo * 128 + L], Wbuf[:, k, 1024 + oc * 128:1024 + (oc + 1) * 128], H[:, k, o:o + L],
                       start=(k == 0), stop=(k == 7))
            acp(xpad_[:, r * 4:(r + 1) * 4, 3:3 + L], bank.v().rr("p (k t) -> p k t", k=4)[:, :, 0:L])
            yield
        for k in range(8):
            mm(EC[0:L, 0:16], H[:, k, o:o + L], Wbuf[:, k, 2560:2576], start=(k == 0), stop=(k == 7))
        dt = sm[0:L, 0:16]
        dtA = sm[0:L, 16:32]
        cum = sm[0:L, 32:48]
        tt(dt, EC[0:L, 0:16], bc_dtb[0:L, :], ALU.add)
        act(dt, dt, AF.Exp)
        act(dt, dt, AF.Ln, bias=1.0)
        tt(dtA, dt, bc_A[0:L, :], ALU.mult)
        mm(EC[0:L, 16:32], ucum[0:L, 0:L], dtA)
        cp(cum, EC[0:L, 16:32])
        yield
        call = View(caccs[0], cacc_t.ap[:, :, 0:L])
        ctall = View(cts[0], ctmp_t.ap[:, :, 0:L])
        if L == 128:
            for oc in range(12):
                ts(caccs[oc][:, 0:L], xpad_[:, oc, 0:L], parB[:, oc:oc + 1], parB[:, 48 + oc:49 + oc], ALU.mult, ALU.add)
            for j in range(1, 4):
                for oc in range(12):
                    stt(caccs[oc][:, 0:L], xpad_[:, oc, j:j + L], parB[:, j * 12 + oc:j * 12 + oc + 1], caccs[oc][:, 0:L], ALU.mult, ALU.add)
        else:
            def cwv(j):
                return parB[:, j * 12:(j + 1) * 12].unsq(2).bc([128, 12, L])
            P.do("dve", "tensor_tensor", out=call, in0=xpad_[:, :, 0:L], in1=cwv(0), op=ALU.mult, extra_writes=caccs[1:])
            for j in range(1, 4):
                P.do("dve", "tensor_tensor", out=ctall, in0=xpad_[:, :, j:j + L], in1=cwv(j), op=ALU.mult, extra_writes=cts[1:])
                P.do("dve", "tensor_tensor", out=call, in0=call, in1=ctall, op=ALU.add, extra_reads=cts[1:], extra_writes=caccs[1:])
            P.do("dve", "tensor_tensor", out=call, in0=call, in1=parB[:, 48:60].unsq(2).bc([128, 12, L]), op=ALU.add, extra_writes=caccs[1:])
        P.do("act", "activation", out=ctall, in_=call, func=AF.Exp, bias=0.0, scale=-1.0, extra_reads=caccs[1:], extra_writes=cts[1:])
        P.do("act", "activation", out=ctall, in_=ctall, func=AF.Ln, bias=1.0, scale=1.0, extra_writes=cts[1:])
        P.do("act", "activation", out=ctall, in_=ctall, func=AF.Exp, bias=0.0, scale=-1.0, extra_writes=cts[1:])
        P.do("dve", "tensor_tensor", out=xbc[:, :, 0:L], in0=call, in1=ctall, op=ALU.mult, extra_reads=caccs[1:] + cts[1:])
        yield
        if b is not None:
            cp(sconv_io[:, :, 3 * b:3 * b + 3], xpad_[:, :, L:L + 3], eng="pool")
        elif last:
            for j in range(3):
                dma_slow(o_sconv[l, seqcol, j].rearrange("(k p) -> p k", p=128), xpad_[:, :, L + j])
        else:
            cp(xpad_[:, :, 0:3], xpad_[:, :, L:L + 3], eng="pool")
        EBb = EB.v().bitcast(BF16)
        EAb = EA.v().bitcast(BF16)
        for oc in range(8):
            tr(EBb[0:L, oc * 128:(oc + 1) * 128], xbc[:, oc, 0:L], identb.v())
        acp(xB[0:L, 0:1024], EBb[0:L, 0:1024])
        for oc in range(8, 10):
            tr(EAb[0:L, (oc - 8) * 128:(oc - 7) * 128], xbc[:, oc, 0:L], identb.v())
        acp(xB[0:L, 1024:1280], EAb[0:L, 0:256])
        for g in range(2):
            mm(EC[0:L, 128 + g * 128:128 + g * 128 + L], xbc[:, 8 + g, 0:L], xbc[:, 10 + g, 0:L])
        tt(xdt[0:L, :].rr("t (h p) -> t h p", p=64), xB[0:L, 0:1024].rr("t (h p) -> t h p", p=64),
           dt.unsq(2).bc([L, 16, 64]), ALU.mult)
        yield
        for q in range(4):
            g = q // 2
            bank = (EA, EB)[q % 2]
            Rb = Rbs[q % 2]
            dec = decs[q % 2]
            tt(Rb[0:L, :, 0:L], ucum[0:L, 0:L].unsq(1).bc([L, 4, L]), dtA[:, 4 * q:4 * q + 4].unsq(2).bc([L, 4, L]), ALU.mult)
            outv = bank[0:L, :].rr("p (h t) -> p h t", h=4)[:, :, 0:L]
            if L == 128:
                mm(outv, onesf[0:L, 0:L], Rb[0:L, :, 0:L], start=True, stop=False)
                mm(outv, identb[0:L, 0:L], negrep[0:L, :, 0:L], start=False, stop=True)
            else:
                for hh in range(4):
                    mm(outv[:, hh, :], onesf[0:L, 0:L], Rb[0:L, hh, 0:L], start=True, stop=False)
                    mm(outv[:, hh, :], identb[0:L, 0:L], negrep[0:L, hh, 0:L], start=False, stop=True)
            tt(dec[0:L, :, 0:L], outv, cum[:, 4 * q:4 * q + 4].unsq(2).bc([L, 4, L]), ALU.subtract)
            act(dec[0:L, :, 0:L], dec[0:L, :, 0:L], AF.Exp)
            tt(MT[0:L, 4 * q:4 * q + 4, 0:L], dec[0:L, :, 0:L],
               EC[0:L, 128 + g * 128:128 + g * 128 + L].unsq(1).bc([L, 4, L]), ALU.mult)
            for h in range(4 * q, 4 * q + 4):
                mm(EY[0:L, h * 64:(h + 1) * 64], MT[0:L, h, 0:L], xdt[0:L, h * 64:(h + 1) * 64])
            yield
        acp(yi[0:L, :], EY[0:L, :])
        yield "EARLY_DONE"
        if b is not None:
            dma(Snat_.v(), st_ssm[l, b].rearrange("(hp h2) p n -> (h2 p) hp n", h2=2))
        elif first:
            memset(Snat_.v(), 0.0)
        y2 = yi
        ecum = sm[0:L, 48:64]
        act(ecum, cum, AF.Exp)
        mm(LC[:, 0:16], sel[0:L, :], cum)
        cl = sm[:, 64:80]
        cp(cl, LC[:, 0:16])
        eend = sm[0:L, 80:96]
        tt(eend, cl[0:L, :], cum, ALU.subtract)
        act(eend, eend, AF.Exp)
        tt(xend_[0:L, :].rr("t (h p) -> t h p", p=64), xdt[0:L, :].rr("t (h p) -> t h p", p=64),
           eend.unsq(2).bc([L, 16, 64]), ALU.mult, eng="pool")
        dcy = sm[:, 96:112]
        act(dcy, cl, AF.Exp)
        dcyv = dcy.rr("p (hp h2) -> p hp h2", h2=2)
        for g_ in range(2):
            for hh in range(4):
                tr(LC[:, hh * 128:(hh + 1) * 128], Snat_[:, 4 * g_ + hh, :], identf.v())
            acp(STb_[:, g_ * 512:(g_ + 1) * 512], LC.v())
        yield
        for r in range(2):
            bank = (LA, LB)[r]
            for i in range(4):
                hp = r * 4 + i
                mm(bank[:, i * 128:(i + 1) * 128], xend_[0:L, hp * 128:(hp + 1) * 128], xB[0:L, 1024 + r * 128:1024 + (r + 1) * 128])
            for h2 in range(2):
                rs = slice(64 * h2, 64 * h2 + 64)
                tt(Snat_[rs, 4 * r:4 * r + 4, :], Snat_[rs, 4 * r:4 * r + 4, :], dcyv[rs, 4 * r:4 * r + 4, h2:h2 + 1].bc([64, 4, 128]), ALU.mult, eng="pool")
                tt(Snat_[rs, 4 * r:4 * r + 4, :], Snat_[rs, 4 * r:4 * r + 4, :], bank[rs, :].rr("p (hp n) -> p hp n", hp=4), ALU.add)
            yield
        if last:
            dma(o_ssm[l, seqcol].rearrange("(hp h2) p n -> (h2 p) hp n", h2=2), Snat_.v())
        for g_ in range(2):
            bank = (LA, LB)[g_]
            mm(bank[0:L, :], xbc[:, 10 + g_, 0:L], STb_[:, g_ * 512:(g_ + 1) * 512])
            tt(yy_[0:L, g_ * 512:(g_ + 1) * 512].rr("t (h p) -> t h p", p=64), bank[0:L, :].rr("t (h p) -> t h p", p=64),
               ecum[:, 8 * g_:8 * g_ + 8].unsq(2).bc([L, 8, 64]), ALU.mult)
        yield
        tt(yy_[0:L, :], yy_[0:L, :], yi[0:L, :], ALU.add)
        for j in range(2):
            bank = (LA, LB)[j]
            if b is None:
                for k in range(8):
                    mm(bank[0:L, :], H[:, k, o:o + L], Wbuf[:, k, j * 512:(j + 1) * 512], start=(k == 0), stop=(k == 7))
                zt = y2[0:L, j * 512:(j + 1) * 512]
                act(zt, bank[0:L, :], AF.Exp, scale=-1.0)
                act(zt, zt, AF.Ln, bias=1.0)
                act(zt, zt, AF.Exp, scale=-1.0)
                tt(zs_[0:L, j * 512:(j + 1) * 512], bank[0:L, :], zt, ALU.mult)
            else:
                bankb = bank.v().bitcast(BF16)
                for kk in range(4):
                    tr(bankb[0:L, kk * 128:(kk + 1) * 128], zsfm_all[:, 4 * j + kk, o:o + L], identb.v())
                acp(zs_[0:L, j * 512:(j + 1) * 512], bankb[0:L, 0:512])
        yield
        tt(y2[0:L, :].rr("t (h p) -> t h p", p=64), xB[0:L, 0:1024].rr("t (h p) -> t h p", p=64),
           bc_D[0:L, :].unsq(2).bc([L, 16, 64]), ALU.mult, eng="pool")
        tt(yy_[0:L, :], yy_[0:L, :], y2[0:L, :], ALU.add)
        tt(yy_[0:L, :], yy_[0:L, :], zs_[0:L, :], ALU.mult)
        act(y2[0:L, :], yy_[0:L, :], AF.Square, accum_out=ss_[0:L, 0:1])
        act(ss_[0:L, 1:2], ss_[0:L, 0:1], AF.Ln, bias=eps_t[0:L, 0:1], scale=1.0 / 1024.0)
        act(ss_[0:L, 2:3], ss_[0:L, 1:2], AF.Exp, scale=-0.5)
        ts(ynb_[0:L, :], yy_[0:L, :], ss_[0:L, 2:3], None, ALU.mult)
        yield
        LCb = LC.v().bitcast(BF16)
        for k in range(8):
            tr(LCb[:, k * 128:k * 128 + L], ynb_[0:L, k * 128:(k + 1) * 128], identb[0:L, 0:L])
        tt(yfm_[:, :, 0:L], LCb[:, 0:1024].rr("p (k t) -> p k t", k=8)[:, :, 0:L],
           parA[:, 64:72].unsq(2).bc([128, 8, L]), ALU.mult)
        yield
        for r in range(2 if b is None else 0):
            bank = (LA, LB)[r]
            for oo in range(4):
                oc = r * 4 + oo
                for k in range(8):
                    mm(bank[:, oo * 128:oo * 128 + L], Wo[:, k, oc * 128:(oc + 1) * 128], yfm_[:, k, 0:L],
                       start=(k == 0), stop=(k == 7))
            for oo in range(4):
                oc = r * 4 + oo
                stt(xc[c][:, oc, o:o + L], bank[:, oo * 128:oo * 128 + L], mod[:, 16 + oc, seqcol:seqcol + 1],
                    xc[c][:, oc, o:o + L], ALU.mult, ALU.add)
            yield

    qk2 = ktm2 = vtm2 = osig2 = g42 = numi2 = None
    R4s = d4s = w4s = CTbs = nfbs = nums = num2s = ynbs = vws = None
    qk_s = kvo_s = yfms_m = kvo_all = None

    def ml_gen(l, tok0, L, seqcol, first, last, b, ctx):
        c = tok0 // 128
        sel = sel127 if L == 128 else sel7
        uidx = ctx
        ctx4 = uidx % 4
        p2 = uidx % 2
        qk, ktm, vtm, osig, g4, numi = qk2[ctx4], ktm2[ctx4], vtm2[ctx4], osig2[ctx4], g42[ctx4], numi2[ctx4]
        R4, d4, w4 = R4s[p2], d4s[p2], w4s[p2]
        CTb, nfb, num, num2, ynb, yfm, vw = CTbs[p2], nfbs[p2], nums[p2], num2s[p2], ynbs[p2], yfms[p2], vws[p2]
        if b is not None:
            qk = qk_s[b]
            yfm = yfms_m[b]
        if b is None:
            H = hbuf[ctx4]
            dma(H.v(), hd[c].v())
            o = 0
        else:
            H = hsamp
            o = tok0 - c * 128
        for r in range(2 if b is None else 0):
            bank = (EA, EB)[r]
            for oo in range(4):
                oc = r * 4 + oo
                for k in range(8):
                    mm(bank[:, oo * 128:oo * 128 + L], Wbuf[:, k, oc * 128:(oc + 1) * 128], H[:, k, o:o + L],
                       start=(k == 0), stop=(k == 7))
            bv = bank.v().rr("p (k t) -> p k t", k=4)[:, :, 0:L]
            if r == 0:
                acp(qk[:, 0:4, 0:L], bv)
            else:
                act(qk[:, 4:8, 0:L], bv, AF.Copy, scale=KSC)
        yield
        if b is not None:
            EAb = EA.v().bitcast(BF16)
            EBb = EB.v().bitcast(BF16)
            kva = View(kvo_s[0], kvo_all.ap)
            for kk in range(12):
                dstb = EAb if kk < 8 else EBb
                k2 = kk if kk < 8 else kk - 8
                P.do("pe", "transpose", out=dstb[0:L, k2 * 128:(k2 + 1) * 128], in_=kva[:, kk, o:o + L],
                     identity=identb.v(), extra_reads=kvo_s[1:])
            acp(ktm[0:L, :], EAb[0:L, 0:512])
            acp(vtm[0:L, :], EAb[0:L, 512:1024])
            acp(osig[0:L, :], EBb[0:L, 0:512])
        for j in range(3 if b is None else 0):
            bank = (EA, EB)[j % 2]
            for k in range(8):
                mm(bank[0:L, :], H[:, k, o:o + L], Wbuf[:, k, 512 + j * 512:512 + (j + 1) * 512], start=(k == 0), stop=(k == 7))
            if j == 0:
                act(ktm[0:L, :], bank[0:L, :], AF.Copy, scale=KSC)
            elif j == 1:
                acp(vtm[0:L, :], bank[0:L, :])
            else:
                act(osig[0:L, :], bank[0:L, :], AF.Exp, scale=-1.0)
                act(osig[0:L, :], osig[0:L, :], AF.Ln, bias=1.0)
                act(osig[0:L, :], osig[0:L, :], AF.Exp, scale=-1.0)
        for k in range(8):
            mm(EC[0:L, 0:8], H[:, k, o:o + L], Wbuf[:, k, 2048:2056], start=(k == 0), stop=(k == 7))
        yield
        ig = g4[0:L, 0:4]
        lf = g4[0:L, 4:8]
        bcs = g4[0:L, 8:12]
        a_s = g4[0:L, 12:16]
        rp = g4[0:L, 16:20]
        negrp = g4[0:L, 20:24]
        tt(ig, EC[0:L, 0:4], bc_ib[0:L, :], ALU.add)
        tt(lf, EC[0:L, 4:8], bc_fb[0:L, :], ALU.add)
        act(lf, lf, AF.Exp, scale=-1.0)
        act(lf, lf, AF.Ln, bias=1.0)
        ts(lf, lf, -1.0, None, ALU.mult)
        mm(EC[0:L, 8:12], ucum[0:L, 0:L], lf)
        cp(bcs, EC[0:L, 8:12])
        tt(a_s, ig, bcs, ALU.subtract)
        tt(R4[0:L, :, 0:L], identf[0:L, 0:L].unsq(1).bc([L, 4, L]), a_s.unsq(2).bc([L, 4, L]), ALU.mult)
        outv = EB.v().rr("p (h t) -> p h t", h=4)[0:L, :, 0:L]
        if L == 128:
            mm(outv, onesf[0:L, 0:L], R4[0:L, :, 0:L], start=True, stop=False)
            mm(outv, identb[0:L, 0:L], negT[0:L, :, 0:L], start=False, stop=True)
        else:
            for hh in range(4):
                mm(outv[:, hh, :], onesf[0:L, 0:L], R4[0:L, hh, 0:L], start=True, stop=False)
                mm(outv[:, hh, :], identb[0:L, 0:L], negT[0:L, hh, 0:L], start=False, stop=True)
        P.do("dve", "tensor_reduce", out=rp, in_=outv, axis=AX.X, op=ALU.max)
        ts(negrp, rp, -1.0, None, ALU.mult)
        yield
        tt(R4[0:L, :, 0:L], identf[0:L, 0:L].unsq(1).bc([L, 4, L]), negrp.unsq(2).bc([L, 4, L]), ALU.mult)
        outv2 = EA.v().rr("p (h t) -> p h t", h=4)[0:L, :, 0:L]
        if L == 128:
            mm(outv2, onesf[0:L, 0:L], R4[0:L, :, 0:L], start=True, stop=False)
            mm(outv2, identb[0:L, 0:L], negrep[0:L, :, 0:L], start=False, stop=True)
        else:
            for hh in range(4):
                mm(outv2[:, hh, :], onesf[0:L, 0:L], R4[0:L, hh, 0:L], start=True, stop=False)
                mm(outv2[:, hh, :], identb[0:L, 0:L], negrep[0:L, hh, 0:L], start=False, stop=True)
        tt(d4[0:L, :, 0:L], outv2, a_s.unsq(2).bc([L, 4, L]), ALU.add)
        act(d4[0:L, :, 0:L], d4[0:L, :, 0:L], AF.Exp)
        outc = EY[:, 0:512].rr("p (h t) -> p h t", h=4)
        for h in range(4):
            mm(outc[0:L, h, 0:L], qk[:, 4 + h, 0:L], qk[:, h, 0:L])
        tt(w4[0:L, :, 0:L], d4[0:L, :, 0:L], outc[0:L, :, 0:L], ALU.mult)
        for h in range(4):
            mm(EY[0:L, 512 + h * 128:512 + (h + 1) * 128], w4[0:L, h, 0:L], vtm[0:L, h * 128:(h + 1) * 128])
        for h in range(4):
            mm(EC[0:L, 16 + h:17 + h], w4[0:L, h, 0:L], onesb[0:L, :])
        acp(numi[0:L, 0:512], EY[0:L, 512:1024])
        cp(numi[0:L, 512:516], EC[0:L, 16:20])
        yield "EARLY_DONE"
        if b is not None:
            dma(Cnat.v(), st_mc[l, b].rearrange("h v d -> v h d"))
            cp(nfm.v(), mn_io[:, 4 * b:4 * b + 4], eng="pool")
            dma(mbc.v(), st_mm[l, b:b + 1, :].to_broadcast([128, 4]))
        elif first:
            memset(Cnat.v(), 0.0)
            memset(nfm.v(), 0.0)
            memset(mbc.v(), 0.0)
        r_ = g4[0:L, 24:28]
        f_ = g4[0:L, 28:32]
        inter = g4[0:L, 32:36]
        mt = g4[0:L, 36:40]
        emt = g4[0:L, 40:44]
        den = g4[0:L, 44:48]
        den2 = g4[0:L, 48:52]
        ssq = g4[0:L, 52:56]
        tt(r_, rp, mbc[0:L, :], ALU.max)
        tt(f_, rp, r_, ALU.subtract)
        act(f_, f_, AF.Exp)
        tt(inter, mbc[0:L, :], r_, ALU.subtract)
        act(inter, inter, AF.Exp)
        tt(mt, bcs, r_, ALU.add)
        act(emt, mt, AF.Exp, scale=-1.0)
        for h in range(4):
            tr(LC[:, h * 128:(h + 1) * 128], Cnat[:, h, :], identf.v())
        acp(CTb.v(), LC.v().rr("p (h v) -> p h v", h=4))
        cp(nfb.v(), nfm.v())
        for h in range(4):
            mm(LA[0:L, h * 128:(h + 1) * 128], qk[:, h, 0:L], CTb[:, h, :])
        for h in range(4):
            mm(LB[0:L, h:h + 1], qk[:, h, 0:L], nfb[:, h:h + 1])
        yield
        tt(num[0:L, :].rr("t (h v) -> t h v", h=4), numi[0:L, 0:512].rr("t (h v) -> t h v", h=4),
           f_.unsq(2).bc([L, 4, 128]), ALU.mult)
        tt(num2[0:L, :].rr("t (h v) -> t h v", h=4), LA[0:L, :].rr("t (h v) -> t h v", h=4),
           inter.unsq(2).bc([L, 4, 128]), ALU.mult)
        tt(num[0:L, :], num[0:L, :], num2[0:L, :], ALU.add, eng="pool")
        tt(den, numi[0:L, 512:516], f_, ALU.mult)
        tt(den2, LB[0:L, 0:4], inter, ALU.mult)
        tt(den, den, den2, ALU.add)
        act(den, den, AF.Abs)
        tt(den, den, emt, ALU.max)
        P.do("dve", "reciprocal", out=den, in_=den)
        tt(num[0:L, :].rr("t (h v) -> t h v", h=4), num[0:L, :].rr("t (h v) -> t h v", h=4),
           den.unsq(2).bc([L, 4, 128]), ALU.mult)
        yield
        tt(num2[0:L, :], num[0:L, :], num[0:L, :], ALU.mult, eng="pool")
        P.do("dve", "tensor_reduce", out=ssq, in_=num2[0:L, :].rr("t (h v) -> t h v", h=4), axis=AX.X, op=ALU.add)
        act(ssq, ssq, AF.Ln, bias=eps_t[0:L, 0:1], scale=1.0 / 128.0)
        act(ssq, ssq, AF.Exp, scale=-0.5)
        tt(num[0:L, :].rr("t (h v) -> t h v", h=4), num[0:L, :].rr("t (h v) -> t h v", h=4),
           ssq.unsq(2).bc([L, 4, 128]), ALU.mult)
        tt(num[0:L, :], num[0:L, :], bc_mlw[0:L, :], ALU.mult, eng="pool")
        tt(ynb[0:L, 0:512], num[0:L, :], osig[0:L, :], ALU.mult)
        LCb = LC.v().bitcast(BF16)
        for k in range(4):
            tr(LCb[:, k * 128:k * 128 + L], ynb[0:L, k * 128:(k + 1) * 128], identb[0:L, 0:L])
        cp(yfm[:, 0:4, 0:L], LCb[:, 0:512].rr("p (k t) -> p k t", k=4)[:, :, 0:L])
        yield
        for r in range(2 if b is None else 0):
            bank = (LA, LB)[r]
            for oo in range(4):
                oc = r * 4 + oo
                for k in range(4):
                    mm(bank[:, oo * 128:oo * 128 + L], Wo[:, k, oc * 128:(oc + 1) * 128], yfm[:, k, 0:L],
                       start=(k == 0), stop=(k == 3))
            for oo in range(4):
                oc = r * 4 + oo
                stt(xc[c][:, oc, o:o + L], bank[:, oo * 128:oo * 128 + L], mod[:, 16 + oc, seqcol:seqcol + 1],
                    xc[c][:, oc, o:o + L], ALU.mult, ALU.add)
            yield
        bm = g4[0:L, 56:64]
        cp(bm[:, 0:4], bcs)
        cp(bm[:, 4:8], mt)
        mm(LC[:, 0:8], sel[0:L, :], bm)
        last8 = g4[:, 64:72]
        cp(last8, LC[:, 0:8])
        wend = g4[0:L, 72:76]
        tt(wend, last8[0:L, 0:4], last8[0:L, 4:8], ALU.subtract)
        tt(wend, wend, a_s, ALU.add)
        act(wend, wend, AF.Exp)
        dc = g4[:, 76:80]
        tt(dc, last8[:, 0:4], mbc.v(), ALU.add)
        tt(dc, dc, last8[:, 4:8], ALU.subtract)
        act(dc, dc, AF.Exp)
        cp(mbc.v(), last8[:, 4:8])
        tt(vw[0:L, :].rr("t (h v) -> t h v", h=4), vtm[0:L, :].rr("t (h v) -> t h v", h=4),
           wend.unsq(2).bc([L, 4, 128]), ALU.mult)
        wendb = g4[0:L, 80:84].bitcast(BF16)[:, 0:4]
        cp(wendb, wend)
        yield
        for h in range(4):
            mm(LA[:, h * 128:(h + 1) * 128], vw[0:L, h * 128:(h + 1) * 128], ktm[0:L, h * 128:(h + 1) * 128])
        for h in range(4):
            mm(LB[:, h:h + 1], ktm[0:L, h * 128:(h + 1) * 128], wendb[:, h:h + 1])
        tt(Cnat.v(), Cnat.v(), dc.unsq(2).bc([128, 4, 128]), ALU.mult, eng="pool")
        tt(Cnat.v(), Cnat.v(), LA.v().rr("p (h d) -> p h d", h=4), ALU.add)
        tt(nfm.v(), nfm.v(), dc, ALU.mult)
        tt(nfm.v(), nfm.v(), LB[:, 0:4], ALU.add)
        if last:
            dma(o_mc[l, seqcol].rearrange("h v d -> v h d"), Cnat.v())
            if b is not None:
                cp(mn_io[:, 4 * b:4 * b + 4], nfm.v(), eng="pool")
            else:
                dma_slow(o_mn[l, seqcol].rearrange("h d -> d h"), nfm.v())
            dma(o_mm[l, seqcol:seqcol + 1, :], mbc[0:1, :])
        yield

    la2 = lu2 = lg2 = lxs = lx_t = None
    lpads = lx_ts = lxss = lxbs = lris = lhs = yfms = EAs = EBs = LAs = LBs = None
    lpads_s = lgs_s = yfms_l = None

    def lru_gen(l, tok0, L, seqcol, first, last, b, ctx):
        c = tok0 // 128
        uidx = ctx
        ctx4 = uidx % 4
        p2 = uidx % 2
        la, lu, lg = la2[ctx4], lu2[ctx4], lg2[ctx4]
        lpad, lx_t, lxs, lxb, lri = lpads[p2], lx_ts[p2], lxss[p2], lxbs[p2], lris[p2]
        lpad_next = lpads[1 - p2]
        lh, yfm = lhs[p2], yfms[p2]
        if b is not None:
            lpad = lpads_s[b]
            lg = lgs_s[b]
            yfm = yfms_l[b]
        EA, EB = EAs[p2], EBs[p2]
        LA, LB = LAs[p2], LBs[p2]
        if b is None:
            H = hbuf[ctx4]
            dma(H.v(), hd[c].v())
            o = 0
        else:
            H = hsamp
            o = tok0 - c * 128
        if b is not None:
            cp(lpad[:, :, 0:3], lconv_io[:, :, 3 * b:3 * b + 3], eng="pool")
        elif first:
            memset(lpad[:, :, 0:3], 0.0)
        for r in range(2 if b is None else 0):
            bank = (EA, EB)[r]
            for oo in range(4):
                oc = r * 4 + oo
                for k in range(8):
                    mm(bank[:, oo * 128:oo * 128 + L], Wbuf[:, k, oc * 128:(oc + 1) * 128], H[:, k, o:o + L],
                       start=(k == 0), stop=(k == 7))
            bv = bank.v().rr("p (k t) -> p k t", k=4)[:, :, 0:L]
            if r == 0:
                acp(lpad[:, :, 3:3 + L], bv)
            else:
                act(lg[:, :, 0:L], bv, AF.Gelu)
        yield
        for k in range(4):
            ts(lxs[k][:, 0:L], lpad[:, k, 0:L], parA[:, 72 + k:73 + k], parA[:, 88 + k:89 + k], ALU.mult, ALU.add)
        for j in range(1, 4):
            for k in range(4):
                stt(lxs[k][:, 0:L], lpad[:, k, j:j + L], parA[:, 72 + j * 4 + k:73 + j * 4 + k], lxs[k][:, 0:L], ALU.mult, ALU.add)
        lxall = View(lxs[0], lx_t.ap[:, :, 0:L])
        P.do("act", "activation", out=lxb[:, :, 0:L], in_=lxall, func=AF.Copy, extra_reads=lxs[1:])
        if b is not None:
            cp(lconv_io[:, :, 3 * b:3 * b + 3], lpad[:, :, L:L + 3], eng="pool")
        elif last:
            for j in range(3):
                dma_slow(o_lconv[l, seqcol, j].rearrange("(k p) -> p k", p=128), lpad[:, :, L + j])
        else:
            cp(lpad_next[:, :, 0:3], lpad[:, :, L:L + 3], eng="pool")
        yield
        for k in range(4):
            mm(EA[:, k * 128:k * 128 + L], wg[:, k, :], lxb[:, k, 0:L])
        for k in range(4):
            mm(EB[:, k * 128:k * 128 + L], wg[:, 4 + k, :], lxb[:, k, 0:L])
        tt(lri[:, 0:4, 0:L], EA.v().rr("p (k t) -> p k t", k=4)[:, :, 0:L], parA[:, 92:96].unsq(2).bc([128, 4, L]), ALU.add)
        tt(lri[:, 4:8, 0:L], EB.v().rr("p (k t) -> p k t", k=4)[:, :, 0:L], parA[:, 96:100].unsq(2).bc([128, 4, L]), ALU.add)
        act(lri[:, :, 0:L], lri[:, :, 0:L], AF.Exp, scale=-1.0)
        act(lri[:, :, 0:L], lri[:, :, 0:L], AF.Ln, bias=1.0)
        act(lri[:, :, 0:L], lri[:, :, 0:L], AF.Exp, scale=-1.0)
        for k in range(4):
            act(la[:, k, 0:L], lri[:, k, 0:L], AF.Exp, scale=lru_c1[:, k:k + 1])
        yield
        tt(lu[:, :, 0:L], la[:, :, 0:L], la[:, :, 0:L], ALU.mult)
        ts(lu[:, :, 0:L], lu[:, :, 0:L], -1.0, 1.0, ALU.mult, ALU.add)
        ts(lu[:, :, 0:L], lu[:, :, 0:L], 1e-18, None, ALU.max)
        act(lu[:, :, 0:L], lu[:, :, 0:L], AF.Ln)
        act(lu[:, :, 0:L], lu[:, :, 0:L], AF.Exp, scale=0.5)
        if b is None and first:
            memset(lu[:, :, 0:1], 1.0)
        tt(lu[:, :, 0:L], lu[:, :, 0:L], lri[:, 4:8, 0:L], ALU.mult)
        P.do("dve", "tensor_tensor", out=lu[:, :, 0:L], in0=lu[:, :, 0:L], in1=lxall, op=ALU.mult, extra_reads=lxs[1:])
        yield "EARLY_DONE"
        if b is not None:
            cp(hstate.v(), lh_io[:, :, b], eng="pool")
        elif first:
            memset(hstate.v(), 0.0)
        for k in range(4):
            P.do("dve", "tensor_tensor_scan", out=lh[:, k, 0:L], data0=la[:, k, 0:L], data1=lu[:, k, 0:L],
                 initial=hstate[:, k:k + 1], op0=ALU.mult, op1=ALU.add)
        cp(hstate.v(), lh[:, :, L - 1])
        if b is not None:
            cp(lh_io[:, :, b], hstate.v(), eng="pool")
        elif last:
            dma_slow(o_lh[l, seqcol].rearrange("(k p) -> p k", p=128), hstate.v())
        tt(yfm[:, 0:4, 0:L], lh[:, :, 0:L], lg[:, :, 0:L], ALU.mult)
        yield
        for r in range(2 if b is None else 0):
            bank = (LA, LB)[r]
            for oo in range(4):
                oc = r * 4 + oo
                for k in range(4):
                    mm(bank[:, oo * 128:oo * 128 + L], Wo[:, k, oc * 128:(oc + 1) * 128], yfm[:, k, 0:L],
                       start=(k == 0), stop=(k == 3))
            for oo in range(4):
                oc = r * 4 + oo
                stt(xc[c][:, oc, o:o + L], bank[:, oo * 128:oo * 128 + L], mod[:, 16 + oc, seqcol:seqcol + 1],
                    xc[c][:, oc, o:o + L], ALU.mult, ALU.add)
            yield

    for l in range(DEPTH):
        P.barrier()
        adaslabs, sqs, rstds, ntmps = alloc_norm()
        load_fm(parA.v(), [ada_b[l].rearrange("(k p) -> k p", p=128), norm1_w[l].rearrange("(k p) -> k p", p=128),
                           norm2_w[l].rearrange("(k p) -> k p", p=128), ssd_norm_w[l].rearrange("(k p) -> k p", p=128),
                           lru_conv_w[l].rearrange("j (k p) -> (j k) p", p=128), lru_conv_b[l].rearrange("(k p) -> k p", p=128),
                           lru_ba[l].rearrange("(k p) -> k p", p=128), lru_bx[l].rearrange("(k p) -> k p", p=128),
                           lru_lambda[l].rearrange("(k p) -> k p", p=128)])
        load_fm(parB.v(), [ssd_conv_w[l].rearrange("j (k p) -> (j k) p", p=128), ssd_conv_b[l].rearrange("(k p) -> k p", p=128)])
        load_bc(bc_dtb.v(), ssd_dt_bias[l:l + 1, :], 16)
        load_bc(bc_A.v(), ssd_a_log[l:l + 1, :], 16)
        act(bc_A.v(), bc_A.v(), AF.Exp)
        ts(bc_A.v(), bc_A.v(), -1.0, None, ALU.mult)
        load_bc(bc_D.v(), ssd_d[l:l + 1, :], 16)
        load_bc(bc_ib.v(), ml_i_bias[l:l + 1, :], 4)
        load_bc(bc_fb.v(), ml_f_bias[l:l + 1, :], 4)
        act(lru_c1.v(), parA[:, 100:104], AF.Exp, scale=-1.0)
        act(lru_c1.v(), lru_c1.v(), AF.Ln, bias=1.0)
        ts(lru_c1.v(), lru_c1.v(), -8.0, None, ALU.mult)
        if l == 0:
            for sl in range(12):
                adaslab = adaslabs[sl % 2]
                dma(adaslab.v(), ada_w[l, :, sl * 512:(sl + 1) * 512].rearrange("(k p) n -> p k n", p=128), eng="pool")
                for oc in range(4):
                    j = sl * 4 + oc
                    for k in range(8):
                        mm(PB[:, j * 32:j * 32 + 17], adaslab[:, k, oc * 128:(oc + 1) * 128], cs_fm[:, k, :],
                           start=(k == 0), stop=(k == 7))
            tt(mod.v(), PB[:, 0:1536].rr("p (j s) -> p j s", j=48)[:, :, 0:17], parA[:, 0:48].unsq(2).bc([128, 48, 17]), ALU.add)
        else:
            tt(mod.v(), modraw.v(), parA[:, 0:48].unsq(2).bc([128, 48, 17]), ALU.add)
            P.atop = P.asize
        ts(s1.v(), mod[:, 8:16, :], 1.0, None, ALU.add)
        tt(s1.v(), s1.v(), parA[:, 48:56].unsq(2).bc([128, 8, 17]), ALU.mult)
        ts(s2.v(), mod[:, 32:40, :], 1.0, None, ALU.add)
        tt(s2.v(), s2.v(), parA[:, 56:64].unsq(2).bc([128, 8, 17]), ALU.mult)
        hst = [P.ov("hst%d" % i, [128, 8, 128], BF16) for i in range(2)]
        def _norm1():
            for c in range(NCH):
                rmsnorm_mod(c, s1, 0, hst[c % 2])
                dma(hd[c].v(), hst[c % 2].v())
        run_scheduled(_norm1)
        P.barrier()
        Wbuf = P.ov("Wbuf", [128, 8, 2576], BF16)
        Wo = P.ov("Wo", [128, 8, 1024], BF16)
        hbuf = [P.ov("hbuf%d" % i, [128, 8, 128], BF16) for i in range(2)]
        hsamp = P.ov("hsamp", [128, 8, 128], BF16)
        dma(hsamp.v(), hd[16].v())
        xpad = P.ov("xpad", [128, 12, 131], F32)
        cacc_t = P.ov("cacc", [128, 12, 128], F32)
        caccs = [Buf(cacc_t.ap[:, i, :], "cacc%d" % i) for i in range(12)]
        ctmp_t = P.ov("ctmp", [128, 12, 128], F32)
        cts = [Buf(ctmp_t.ap[:, 4 * i:4 * i + 4, :], "ct%d" % i) for i in range(3)]
        Rbs = [cts[2], P.ov("Rb1", [128, 4, 128], F32)]
        decs = [cts[0], cts[1]]
        MT = P.ov("MT", [128, 16, 128], BF16)
        xbc2 = [P.ov("xbc%d" % i, [128, 12, 128], BF16) for i in range(2)]
        xB2 = [P.ov("xB%d" % i, [128, 1280], BF16) for i in range(2)]
        xdt2 = [P.ov("xdt%d" % i, [128, 1024], BF16) for i in range(2)]
        sm2 = [P.ov("sm%d" % i, [128, 128], F32) for i in range(2)]
        yi2 = [P.ov("yi%d" % i, [128, 1024], F32) for i in range(2)]
        zs = P.ov("zs", [128, 1024], BF16)
        yy = P.ov("yy", [128, 1024], F32)
        ynb = P.ov("ynb", [128, 1024], BF16)
        yfm = P.ov("yfm", [128, 8, 128], BF16)
        Snat = P.ov("Snat", [128, 8, 128], F32)
        ss = P.ov("ss", [128, 8], F32)
        xend = zs
        STb = ynb
        EA, EB, EC = Buf(PA.ap[:, 0:512], "EA"), Buf(PA.ap[:, 512:1024], "EB"), Buf(PC.ap, "EC")
        EY = Buf(PB.ap[:, 0:1024], "EY")
        LA, LB, LC = Buf(PB.ap[:, 1024:1536], "LA"), Buf(PB.ap[:, 1536:2048], "LB"), Buf(PF.ap, "LC")
        dma(Wbuf[:, :, 0:2576], w_in[l, :, 0:2576].rearrange("(k p) n -> p k n", p=128), eng="pool")
        dma(Wo.v(), w_out[l, 0:1024, :].rearrange("(k p) n -> p k n", p=128), eng="pool")
        scv = st_sconv[l].rearrange("b j c -> (b j) c")
        dma(yy[0:48, :], scv[:, 0:1024])
        dma(yi2[0][0:48, 0:512], scv[:, 1024:1536])
        for k in range(12):
            src = yy[0:48, k * 128:(k + 1) * 128] if k < 8 else yi2[0][0:48, (k - 8) * 128:(k - 7) * 128]
            bank = EA if k < 8 else EB
            kk = k if k < 8 else k - 8
            tr(bank[:, kk * 48:(kk + 1) * 48], src, identf[0:48, 0:48])
        cp(sconv_io[:, 0:8, :], EA[:, 0:384].rr("p (k t) -> p k t", k=8))
        cp(sconv_io[:, 8:12, :], EB[:, 0:192].rr("p (k t) -> p k t", k=4))
        ulist = list(units())
        run_pipeline([ssd_gen(l, tok0, L, seqcol, first, last, b, i % 2)
                      for i, (tok0, L, seqcol, first, last, b) in enumerate(ulist[0:16])])
        P.barrier()
        Wbuf = P.ov("Wbuf", [128, 8, 2576], BF16)
        Wo = P.ov("Wo", [128, 8, 1024], BF16)
        hbuf = [P.ov("hbuf%d" % i, [128, 8, 128], BF16) for i in range(2)]
        hsamp = P.ov("hsamp", [128, 8, 128], BF16)
        xpad_all = P.ov("xpad_all", [128, 12, 176], F32)
        xpa4 = xpad_all.ap.rearrange("p k (b j) -> p k b j", b=16)
        xpads = [Buf(xpa4[:, :, b_, :], "xpad_s%d" % b_) for b_ in range(16)]
        cacc_t = P.ov("cacc", [128, 12, 8], F32)
        caccs = [Buf(cacc_t.ap[:, i, :], "cacc%d" % i) for i in range(12)]
        ctmp_t = P.ov("ctmp", [128, 12, 8], F32)
        cts = [Buf(ctmp_t.ap[:, 4 * i:4 * i + 4, :], "ct%d" % i) for i in range(3)]
        Rbs = [cts[2], P.ov("Rb1", [128, 4, 8], F32)]
        decs = [cts[0], cts[1]]
        MT = P.ov("MT", [128, 16, 8], BF16)
        xbc2 = [P.ov("xbc%d" % i, [128, 12, 8], BF16) for i in range(2)]
        xB2 = [P.ov("xB%d" % i, [128, 1280], BF16) for i in range(2)]
        xdt2 = [P.ov("xdt%d" % i, [128, 1024], BF16) for i in range(2)]
        sm2 = [P.ov("sm%d" % i, [128, 128], F32) for i in range(2)]
        yi2 = [P.ov("yi%d" % i, [128, 1024], F32) for i in range(2)]
        zss = [P.ov("zs%d" % i, [128, 1024], BF16) for i in range(2)]
        yys = [P.ov("yy%d" % i, [128, 1024], F32) for i in range(2)]
        ynbs2 = [P.ov("ynb%d" % i, [128, 1024], BF16) for i in range(2)]
        Snats = [P.ov("Snat%d" % i, [128, 8, 128], F32) for i in range(2)]
        sss = [P.ov("ss%d" % i, [128, 8], F32) for i in range(2)]
        yfm_all = P.ov("yfm_all", [128, 8, 128], BF16)
        yfms_s = [Buf(yfm_all.ap[:, :, 8 * b_:8 * b_ + 8], "yfm_s%d" % b_) for b_ in range(16)]
        zsfm_all = P.ov("zsfm_all", [128, 8, 128], BF16)
        ztmp = P.ov("ztmp", [128, 512], F32)
        yy = yys[0]
        EA, EB, EC = Buf(PA.ap[:, 0:512], "EA"), Buf(PA.ap[:, 512:1024], "EB"), Buf(PC.ap, "EC")
        EY = Buf(PB.ap[:, 0:1024], "EY")
        LA, LB, LC = Buf(PB.ap[:, 1024:1536], "LA"), Buf(PB.ap[:, 1536:2048], "LB"), Buf(PF.ap, "LC")
        for r in range(3):
            bank = (EA, EB)[r % 2]
            for oo in range(4):
                oc = r * 4 + oo
                for k in range(8):
                    mm(bank[:, oo * 128:(oo + 1) * 128], Wbuf[:, k, 1024 + oc * 128:1024 + (oc + 1) * 128], hsamp[:, k, :],
                       start=(k == 0), stop=(k == 7))
            for oo in range(4):
                oc = r * 4 + oo
                P.do("act", "activation", out=View(xpads[0], xpa4[:, oc, :, 3:11]),
                     in_=bank[:, oo * 128:(oo + 1) * 128].rr("p (b t) -> p b t", b=16), func=AF.Copy,
                     extra_writes=xpads[1:])
        for r in range(2):
            bank = (LA, LB)[r]
            for oo in range(4):
                oc = r * 4 + oo
                for k in range(8):
                    mm(bank[:, oo * 128:(oo + 1) * 128], Wbuf[:, k, oc * 128:(oc + 1) * 128], hsamp[:, k, :],
                       start=(k == 0), stop=(k == 7))
            act(ztmp.v(), bank.v(), AF.Exp, scale=-1.0)
            act(ztmp.v(), ztmp.v(), AF.Ln, bias=1.0)
            act(ztmp.v(), ztmp.v(), AF.Exp, scale=-1.0)
            tt(zsfm_all[:, 4 * r:4 * r + 4, :], bank.v().rr("p (k t) -> p k t", k=4), ztmp.v().rr("p (k t) -> p k t", k=4), ALU.mult)
        run_pipeline([ssd_gen(l, tok0, L, seqcol, first, last, b, i % 2)
                      for i, (tok0, L, seqcol, first, last, b) in enumerate(ulist[16:])])
        yfa = View(yfms_s[0], yfm_all.ap)
        for r in range(2):
            bank = (LA, LB)[r]
            for oo in range(4):
                oc = r * 4 + oo
                for k in range(8):
                    P.do("pe", "matmul", out=bank[:, oo * 128:(oo + 1) * 128], lhsT=Wo[:, k, oc * 128:(oc + 1) * 128],
                         rhs=yfa[:, k, :], start=(k == 0), stop=(k == 7), extra_reads=yfms_s[1:])
            for oo in range(4):
                oc = r * 4 + oo
                tt(ztmp[:, 0:128].rr("p (b t) -> p b t", t=8), bank[:, oo * 128:(oo + 1) * 128].rr("p (b t) -> p b t", t=8),
                   mod[:, 16 + oc, 1:17].unsq(2).bc([128, 16, 8]), ALU.mult)
                tt(xc[16][:, oc, :], xc[16][:, oc, :], ztmp[:, 0:128], ALU.add)
        for k in range(12):
            bank = (EA, EB, LA)[k // 4]
            tr(bank[0:48, (k % 4) * 128:(k % 4 + 1) * 128], sconv_io[:, k, :], identf.v())
        cp(yy[0:48, 0:512], EA[0:48, :])
        cp(yy[0:48, 512:1024], EB[0:48, :])
        cp(yi2[0][0:48, 0:512], LA[0:48, :])
        sco = o_sconv[l, 1:1 + NS].rearrange("b j c -> (b j) c")
        dma(sco[:, 0:1024], yy[0:48, :])
        dma(sco[:, 1024:1536], yi2[0][0:48, 0:512])
        P.barrier()
        Wbuf = P.ov("Wbuf", [128, 8, 2056], BF16)
        Wo = P.ov("Wo", [128, 4, 1024], BF16)
        hsamp = P.ov("hsamp", [128, 8, 128], BF16)
        dma(hsamp.v(), hd[16].v())
        hbuf = [P.ov("hbuf%d" % i, [128, 8, 128], BF16) for i in range(4)]
        R4s = [P.ov("R4%d" % i, [128, 4, 128], F32) for i in range(2)]
        d4s = [P.ov("d4%d" % i, [128, 4, 128], F32) for i in range(2)]
        w4s = [P.ov("w4%d" % i, [128, 4, 128], BF16) for i in range(2)]
        qk2 = [P.ov("qk%d" % i, [128, 8, 128], BF16) for i in range(4)]
        ktm2 = [P.ov("ktm%d" % i, [128, 512], BF16) for i in range(4)]
        vtm2 = [P.ov("vtm%d" % i, [128, 512], BF16) for i in range(4)]
        osig2 = [P.ov("osig%d" % i, [128, 512], F32) for i in range(4)]
        g42 = [P.ov("g4%d" % i, [128, 128], F32) for i in range(4)]
        numi2 = [P.ov("numi%d" % i, [128, 520], F32) for i in range(4)]
        CTbs = [P.ov("CTb%d" % i, [128, 4, 128], BF16) for i in range(2)]
        nfbs = [P.ov("nfb%d" % i, [128, 4], BF16) for i in range(2)]
        nums = [P.ov("num%d" % i, [128, 512], F32) for i in range(2)]
        num2s = [P.ov("num2%d" % i, [128, 512], F32) for i in range(2)]
        ynbs = [P.ov("ynb%d" % i, [128, 512], BF16) for i in range(2)]
        yfms = [P.ov("yfm%d" % i, [128, 4, 128], BF16) for i in range(2)]
        vws = [P.ov("vw%d" % i, [128, 512], BF16) for i in range(2)]
        num = nums[0]
        Cnat = P.ov("Cnat", [128, 4, 128], F32)
        nfm = P.ov("nfm", [128, 4], F32)
        mbc = P.ov("mbc", [128, 4], F32)
        bc_mlw = P.ov("bc_mlw", [128, 512], F32)
        load_bc(bc_mlw.v(), ml_norm_w[l:l + 1, :], 512)
        EA, EB, EC = Buf(PA.ap[:, 0:512], "EA"), Buf(PA.ap[:, 512:1024], "EB"), Buf(PC.ap, "EC")
        EY = Buf(PB.ap[:, 0:1024], "EY")
        LA, LB, LC = Buf(PB.ap[:, 1024:1536], "LA"), Buf(PB.ap[:, 1536:2048], "LB"), Buf(PF.ap, "LC")
        dma(Wbuf[:, :, 0:2056], w_in[l, :, 2576:4632].rearrange("(k p) n -> p k n", p=128), eng="pool")
        dma(Wo[:, 0:4, :], w_out[l, 1024:1536, :].rearrange("(k p) n -> p k n", p=128), eng="pool")
        dma(num[0:64, 0:128], st_mn[l].rearrange("b h d -> (b h) d"))
        tr(EA[:, 0:64], num[0:64, 0:128], identf[0:64, 0:64])
        cp(mn_io.v(), EA[:, 0:64])
        qk_all = P.ov("qk_all", [128, 8, 128], BF16)
        qk_s = [Buf(qk_all.ap[:, :, 8 * b_:8 * b_ + 8], "qk_s%d" % b_) for b_ in range(16)]
        kvo_all = P.ov("kvo_all", [128, 12, 128], BF16)
        kvo_s = [Buf(kvo_all.ap[:, :, 8 * b_:8 * b_ + 8], "kvo_s%d" % b_) for b_ in range(16)]
        yfm_all_m = P.ov("yfm_all_m", [128, 4, 128], BF16)
        yfms_m = [Buf(yfm_all_m.ap[:, :, 8 * b_:8 * b_ + 8], "yfm_m%d" % b_) for b_ in range(16)]
        for r in range(5):
            bank = (EA, EB)[r % 2]
            for oo in range(4):
                oc = (r if r < 2 else r - 1) * 4 + oo
                for k in range(8):
                    mm(bank[:, oo * 128:(oo + 1) * 128], Wbuf[:, k, oc * 128:(oc + 1) * 128], hsamp[:, k, :],
                       start=(k == 0), stop=(k == 7))
            bv = bank.v().rr("p (k t) -> p k t", k=4)
            if r == 0:
                P.do("act", "activation", out=View(qk_s[0], qk_all.ap[:, 0:4, :]), in_=bv, func=AF.Copy, extra_writes=qk_s[1:])
            elif r == 1:
                P.do("act", "activation", out=View(qk_s[0], qk_all.ap[:, 4:8, :]), in_=bv, func=AF.Copy, scale=KSC, extra_writes=qk_s[1:])
            elif r == 2:
                P.do("act", "activation", out=View(kvo_s[0], kvo_all.ap[:, 0:4, :]), in_=bv, func=AF.Copy, scale=KSC, extra_writes=kvo_s[1:])
            elif r == 3:
                P.do("act", "activation", out=View(kvo_s[0], kvo_all.ap[:, 4:8, :]), in_=bv, func=AF.Copy, extra_writes=kvo_s[1:])
            else:
                nt = nums[0].v()
                act(nt, bank.v(), AF.Exp, scale=-1.0)
                act(nt, nt, AF.Ln, bias=1.0)
                P.do("act", "activation", out=View(kvo_s[0], kvo_all.ap[:, 8:12, :]), in_=nt.rr("p (k t) -> p k t", k=4),
                     func=AF.Exp, scale=-1.0, extra_writes=kvo_s[1:])
        run_pipeline([ml_gen(l, tok0, L, seqcol, first, last, b, i)
                      for i, (tok0, L, seqcol, first, last, b) in enumerate(units())])
        yfam = View(yfms_m[0], yfm_all_m.ap)
        for r in range(2):
            bank = (LA, LB)[r]
            for oo in range(4):
                oc = r * 4 + oo
                for k in range(4):
                    P.do("pe", "matmul", out=bank[:, oo * 128:(oo + 1) * 128], lhsT=Wo[:, k, oc * 128:(oc + 1) * 128],
                         rhs=yfam[:, k, :], start=(k == 0), stop=(k == 3), extra_reads=yfms_m[1:])
            for oo in range(4):
                oc = r * 4 + oo
                tt(nums[0][:, 0:128].rr("p (b t) -> p b t", t=8), bank[:, oo * 128:(oo + 1) * 128].rr("p (b t) -> p b t", t=8),
                   mod[:, 16 + oc, 1:17].unsq(2).bc([128, 16, 8]), ALU.mult)
                tt(xc[16][:, oc, :], xc[16][:, oc, :], nums[0][:, 0:128], ALU.add)
        tr(EA[0:64, 0:128], mn_io.v(), identf.v())
        cp(num[0:64, 0:128], EA[0:64, 0:128])
        dma(o_mn[l, 1:1 + NS].rearrange("b h d -> (b h) d"), num[0:64, 0:128])
        P.barrier()
        Wbuf = P.ov("Wbuf", [128, 8, 1024], BF16)
        Wo = P.ov("Wo", [128, 4, 1024], BF16)
        hbuf = [P.ov("hbuf%d" % i, [128, 8, 128], BF16) for i in range(2)]
        hsamp = P.ov("hsamp", [128, 8, 128], BF16)
        dma(hsamp.v(), hd[16].v())
        hbuf = [P.ov("hbuf%d" % i, [128, 8, 128], BF16) for i in range(4)]
        lpads = [P.ov("lpad%d" % i, [128, 4, 131], F32) for i in range(2)]
        lx_ts = [P.ov("lx%d" % i, [128, 4, 128], F32) for i in range(2)]
        lxss = [[Buf(t_.ap[:, i, :], "lxs%d" % i) for i in range(4)] for t_ in lx_ts]
        lxbs = [P.ov("lxb%d" % i, [128, 4, 128], BF16) for i in range(2)]
        lris = [P.ov("lri%d" % i, [128, 8, 128], F32) for i in range(2)]
        la2 = [P.ov("la%d" % i, [128, 4, 128], F32) for i in range(4)]
        lu2 = [P.ov("lu%d" % i, [128, 4, 128], F32) for i in range(4)]
        lg2 = [P.ov("lg%d" % i, [128, 4, 128], F32) for i in range(4)]
        lhs = [P.ov("lh%d" % i, [128, 4, 128], F32) for i in range(2)]
        yfms = [P.ov("yfm%d" % i, [128, 8, 128], BF16) for i in range(2)]
        lh = lhs[0]
        lri = lris[0]
        hstate = P.ov("hstate", [128, 4], F32)
        wg = P.ov("wg", [128, 8, 128], BF16)
        EAs = [Buf(PA.ap[:, 0:512], "EA0"), Buf(PB.ap[:, 0:512], "EA1")]
        EBs = [Buf(PA.ap[:, 512:1024], "EB0"), Buf(PB.ap[:, 512:1024], "EB1")]
        LAs = [Buf(PB.ap[:, 1024:1536], "LA0"), Buf(PC.ap, "LA1")]
        LBs = [Buf(PB.ap[:, 1536:2048], "LB0"), Buf(PF.ap, "LB1")]
        EA, EB = EAs[0], EBs[0]
        dma(Wbuf[:, :, 0:1024], w_in[l, :, 4632:5656].rearrange("(k p) n -> p k n", p=128), eng="pool")
        dma(Wo[:, 0:4, :], w_out[l, 1536:2048, :].rearrange("(k p) n -> p k n", p=128), eng="pool")
        dma(wg[:, 0:4, :], lru_wa[l].rearrange("k c d -> c k d"), eng="pool")
        dma(wg[:, 4:8, :], lru_wx[l].rearrange("k c d -> c k d"), eng="pool")
        lhv = lh.v().rr("p k t -> p (k t)")
        lrv = lri.v().rr("p k t -> p (k t)")
        dma(lhv[0:48, :], st_lconv[l].rearrange("b j c -> (b j) c"))
        dma(lrv[0:16, 0:512], st_lh[l])
        for k in range(4):
            tr(EA[:, k * 48:(k + 1) * 48], lhv[0:48, k * 128:(k + 1) * 128], identf[0:48, 0:48])
            tr(EB[:, k * 16:(k + 1) * 16], lrv[0:16, k * 128:(k + 1) * 128], identf[0:16, 0:16])
        cp(lconv_io.v(), EA[:, 0:192].rr("p (k t) -> p k t", k=4))
        cp(lh_io.v(), EB[:, 0:64].rr("p (k t) -> p k t", k=4))
        lpad_all = P.ov("lpad_all", [128, 4, 176], F32)
        lpa4 = lpad_all.ap.rearrange("p k (b j) -> p k b j", b=16)
        lpads_s = [Buf(lpa4[:, :, b_, :], "lpad_s%d" % b_) for b_ in range(16)]
        lg_all = P.ov("lg_all", [128, 4, 128], F32)
        lgs_s = [Buf(lg_all.ap[:, :, 8 * b_:8 * b_ + 8], "lg_s%d" % b_) for b_ in range(16)]
        yfm_all_l = P.ov("yfm_all_l", [128, 4, 128], BF16)
        yfms_l = [Buf(yfm_all_l.ap[:, :, 8 * b_:8 * b_ + 8], "yfm_l%d" % b_) for b_ in range(16)]
        ltmp = P.ov("ltmp", [128, 128], F32)
        for r in range(2):
            bank = (EAs[1], EBs[1])[r]
            for oo in range(4):
                oc = r * 4 + oo
                for k in range(8):
                    mm(bank[:, oo * 128:(oo + 1) * 128], Wbuf[:, k, oc * 128:(oc + 1) * 128], hsamp[:, k, :],
                       start=(k == 0), stop=(k == 7))
            if r == 0:
                for oo in range(4):
                    P.do("act", "activation", out=View(lpads_s[0], lpa4[:, oo, :, 3:11]),
                         in_=bank[:, oo * 128:(oo + 1) * 128].rr("p (b t) -> p b t", b=16), func=AF.Copy,
                         extra_writes=lpads_s[1:])
            else:
                P.do("act", "activation", out=View(lgs_s[0], lg_all.ap), in_=bank.v().rr("p (k t) -> p k t", k=4),
                     func=AF.Gelu, extra_writes=lgs_s[1:])
        run_pipeline([lru_gen(l, tok0, L, seqcol, first, last, b, i)
                      for i, (tok0, L, seqcol, first, last, b) in enumerate(units())])
        yfal = View(yfms_l[0], yfm_all_l.ap)
        for r in range(2):
            bank = (LAs[0], LBs[0])[r]
            for oo in range(4):
                oc = r * 4 + oo
                for k in range(4):
                    P.do("pe", "matmul", out=bank[:, oo * 128:(oo + 1) * 128], lhsT=Wo[:, k, oc * 128:(oc + 1) * 128],
                         rhs=yfal[:, k, :], start=(k == 0), stop=(k == 3), extra_reads=yfms_l[1:])
            for oo in range(4):
                oc = r * 4 + oo
                tt(ltmp.v().rr("p (b t) -> p b t", t=8), bank[:, oo * 128:(oo + 1) * 128].rr("p (b t) -> p b t", t=8),
                   mod[:, 16 + oc, 1:17].unsq(2).bc([128, 16, 8]), ALU.mult)
                tt(xc[16][:, oc, :], xc[16][:, oc, :], ltmp.v(), ALU.add)
        for k in range(4):
            tr(EA[0:48, k * 128:(k + 1) * 128], lconv_io[:, k, :], identf.v())
            tr(EB[0:16, k * 128:(k + 1) * 128], lh_io[:, k, :], identf.v())
        cp(lhv[0:48, :], EA[0:48, :])
        cp(lrv[0:16, 0:512], EB[0:16, :])
        dma(o_lconv[l, 1:1 + NS].rearrange("b j c -> (b j) c"), lhv[0:48, :])
        dma(o_lh[l, 1:1 + NS], lrv[0:16, 0:512])
        P.barrier()
        adaslabs, sqs, rstds, ntmps = alloc_norm()
        hids = [P.ov("hid%d" % i, [128, 4, 512], BF16) for i in range(2)]
        upws = [P.ov("upw%d" % i, [128, 8, 512], BF16) for i in range(2)]
        dnws = [P.ov("dnw%d" % i, [128, 4, 1024], BF16) for i in range(2)]
        rls = [P.ov("rl%d" % i, [128, 512], F32) for i in range(3)]
        upbanks = [Buf(PA.ap[:, 0:512], "mb0"), Buf(PA.ap[:, 512:1024], "mb1"), Buf(PC.ap, "mb2")]
        adabank = Buf(PF.ap, "adab")
        if l + 1 < DEPTH:
            modraw = P.ov_top("modraw", [128, 48, 17], F32)
        dnbanks = [Buf(PB.ap[:, i * 512:(i + 1) * 512], "db%d" % i) for i in range(4)]
        hn = P.ov("hn_mlp", [128, 8, NTOK], BF16)
        hc = [Buf(hn.ap[:, :, c * 128:(c + 1) * 128], "h%d" % c) for c in range(NCH)]
        def _norm2():
            for c in range(NCH):
                rmsnorm_mod(c, s2, 24, hc[c])
        run_scheduled(_norm2)
        P.barrier()
        tiles = [(i * 512, 512) for i in range(4)] + [(2048, 128)]
        def _mlp():
            nonlocal_cnt = [0, 0]
            cnt = 0
            rcnt = 0
            for e in range(8):
                upw = upws[e % 2]
                dnw = dnws[e % 2]
                dma(upw.v(), mlp_up[l, :, e * 512:(e + 1) * 512].rearrange("(k p) n -> p k n", p=128), eng="pool")
                dma(dnw.v(), mlp_down[l, e * 512:(e + 1) * 512, :].rearrange("(k p) n -> p k n", p=128), eng="pool")
                for (t0, N) in tiles:
                    hid = hids[cnt % 2]
                    hbufs = [hc[(t0 + i * 128) // 128] for i in range(N // 128)]
                    for fo in range(4):
                        ub = upbanks[(cnt * 4 + fo) % 3]
                        rl = rls[rcnt % 3]
                        rcnt += 1
                        for k in range(8):
                            P.do("pe", "matmul", out=ub[:, 0:N], lhsT=upw[:, k, fo * 128:(fo + 1) * 128],
                                 rhs=View(hbufs[0], hn.ap[:, k, t0:t0 + N]), start=(k == 0), stop=(k == 7),
                                 extra_reads=hbufs[1:])
                        act(rl[:, 0:N], ub[:, 0:N], AF.Relu)
                        tt(hid[:, fo, 0:N], rl[:, 0:N], rl[:, 0:N], ALU.mult, eng="pool")
                    for oc in range(8):
                        db = dnbanks[oc % 4]
                        for k in range(4):
                            mm(db[:, 0:N], dnw[:, k, oc * 128:(oc + 1) * 128], hid[:, k, 0:N],
                               start=(k == 0), stop=(k == 3))
                        pv = db[:, 0:N]
                        xbufs = [xc[(t0 + i * 128) // 128] for i in range(N // 128)]
                        xv = View(xbufs[0], x.ap[:, oc, t0:t0 + N])
                        if t0 < TP:
                            P.do("dve", "scalar_tensor_tensor", out=xv, in0=pv, scalar=mod[:, 40 + oc, 0:1], in1=xv,
                                 op0=ALU.mult, op1=ALU.add, extra_reads=xbufs[1:], extra_writes=xbufs[1:])
                        else:
                            rl = rls[rcnt % 3]
                            rcnt += 1
                            tt(rl[:, 0:N].rr("p (b t) -> p b t", t=8), pv.rr("p (b t) -> p b t", t=8),
                               mod[:, 40 + oc, 1:17].unsq(2).bc([128, 16, 8]), ALU.mult)
                            tt(xv, xv, rl[:, 0:N], ALU.add)
                    cnt += 1
                    if l + 1 < DEPTH and cnt % 3 == 1 and cnt // 3 < 12:
                        sl = cnt // 3
                        adaslab = adaslabs[sl % 2]
                        dma(adaslab.v(), ada_w[l + 1, :, sl * 512:(sl + 1) * 512].rearrange("(k p) n -> p k n", p=128), eng="pool")
                        for oc in range(4):
                            j = sl * 4 + oc
                            jj = j % 16
                            for k in range(8):
                                mm(adabank[:, jj * 32:jj * 32 + 17], adaslab[:, k, oc * 128:(oc + 1) * 128], cs_fm[:, k, :],
                                   start=(k == 0), stop=(k == 7))
                        if sl % 4 == 3:
                            g0 = (sl // 4) * 16
                            cp(modraw[:, g0:g0 + 16, :], adabank.v().rr("p (j s) -> p j s", j=16)[:, :, 0:17])
        run_scheduled(_mlp)

    P.barrier()
    adaslabs, sqs, rstds, ntmps = alloc_norm()
    youts = [P.ov("yout%d" % i, [128, 1024], F32) for i in range(2)]
    pbs = [PA, Buf(PB.ap[:, 0:1024], "fpb1")]
    def _final():
        for c in range(NCH):
            sq, rstd, ntmp, nb = sqs[c % 2], rstds[c % 2], ntmps[c % 2], nbanks[c % 2]
            yout, pb = youts[c % 2], pbs[c % 2]
            act(sq.v(), xc[c].v(), AF.Square)
            for k in range(8):
                mm(nb[:, 0:128], ones_mean.v(), sq[:, k, :], start=(k == 0), stop=(k == 7))
            act(rstd.v(), nb[:, 0:128], AF.Ln, bias=eps_t[:, 0:1], scale=1.0)
            act(rstd.v(), rstd.v(), AF.Exp, scale=-0.5)
            tt(ntmp.v(), xc[c].v(), rstd.v().unsq(1).bc([128, 8, 128]), ALU.mult)
            tt(ntmp.v(), ntmp.v(), fnw.v().unsq(2).bc([128, 8, 128]), ALU.mult, eng="pool")
            for k in range(8):
                tr(pb[:, k * 128:(k + 1) * 128], ntmp[:, k, :], identf.v())
            if c % 2 == 0:
                acp(yout.v(), pb.v())
            else:
                cp(yout.v(), pb.v())
            dst = y_p[c * 128:(c + 1) * 128, :] if c < 16 else y_s[:, :]
            dma(dst, yout.v())


    run_scheduled(_final)

    P.emit()
    return nc


_NC_CACHE = {}


def kernel(**inputs):
    f = lambda a: np.ascontiguousarray(np.asarray(a, dtype=np.float32))
    x_prompt = f(inputs["x_prompt"]); x_sample = f(inputs["x_sample"])
    c_prompt = f(inputs["c_prompt"]); c_sample = f(inputs["c_sample"])
    state_ssm = f(inputs["state_ssm"]); state_ssd_conv = f(inputs["state_ssd_conv"])
    state_mlstm_c = f(inputs["state_mlstm_c"]); state_mlstm_n = f(inputs["state_mlstm_n"])
    state_mlstm_m = f(inputs["state_mlstm_m"]); state_lru_h = f(inputs["state_lru_h"])
    state_lru_conv = f(inputs["state_lru_conv"])
    shared = {}
    for name in ("ada_w", "ada_b", "norm1_w", "norm2_w", "w_in", "ssd_conv_w", "ssd_conv_b", "ssd_dt_bias",
                 "ssd_a_log", "ssd_d", "ssd_norm_w", "ml_i_bias", "ml_f_bias", "ml_norm_w", "lru_conv_w",
                 "lru_conv_b", "lru_wa", "lru_ba", "lru_wx", "lru_bx", "lru_lambda", "w_out", "mlp_up", "mlp_down"):
        shared[name] = f(inputs[name])
    shared["final_norm_w"] = f(inputs["final_norm_w"]).reshape(1, D)
    if "nc" not in _NC_CACHE:
        _NC_CACHE["nc"] = build_program()
    nc = _NC_CACHE["nc"]
    in_maps = []
    for i in range(NCORES):
        sl = slice(i * NS, (i + 1) * NS)
        m = dict(shared)
        m["xp"] = x_prompt[i]
        m["xs"] = x_sample[sl].reshape(NS * TS, D)
        m["c17"] = np.concatenate([c_prompt[i:i + 1], c_sample[sl]], axis=0)
        m["st_ssm"] = np.ascontiguousarray(state_ssm[:, sl])
        m["st_sconv"] = np.ascontiguousarray(state_ssd_conv[:, sl])
        m["st_mc"] = np.ascontiguousarray(state_mlstm_c[:, sl])
        m["st_mn"] = np.ascontiguousarray(state_mlstm_n[:, sl])
        m["st_mm"] = np.ascontiguousarray(state_mlstm_m[:, sl])
        m["st_lh"] = np.ascontiguousarray(state_lru_h[:, sl])
        m["st_lconv"] = np.ascontiguousarray(state_lru_conv[:, sl])
        in_maps.append(m)
    res = run_bass_kernel_spmd(nc, in_maps, core_ids=list(range(NCORES)))
    R = res.results
    y_prompt = np.stack([R[i]["y_p"] for i in range(NCORES)], axis=0)
    y_sample = np.concatenate([R[i]["y_s"].reshape(NS, TS, D) for i in range(NCORES)], axis=0)
    outs = [y_prompt, y_sample]
    names = ["o_ssm", "o_sconv", "o_mc", "o_mn", "o_mm", "o_lh", "o_lconv"]
    for nm in names:
        outs.append(np.concatenate([R[i][nm][:, 0:1] for i in range(NCORES)], axis=1))
    for nm in names:
        outs.append(np.concatenate([R[i][nm][:, 1:] for i in range(NCORES)], axis=1))
    return tuple(np.ascontiguousarray(o, dtype=np.float32) for o in outs)
```

```python
import numpy as np
import concourse.bass as bass
import concourse.mybir as mybir
from concourse.bass_utils import run_bass_kernel_spmd

F32 = mybir.dt.float32
BF16 = mybir.dt.bfloat16
ALU = mybir.AluOpType
AF = mybir.ActivationFunctionType
AX = mybir.AxisListType

NCORES = 8
D = 1024
TP = 2048
NS = 16
TS = 8
NTOK = TP + NS * TS
DEPTH = 2
IN_DIM = 5656
EPS = 1e-6
NEG = -30000.0


class Buf:
    def __init__(self, ap, name=""):
        self.ap = ap
        self.name = name
        self.last_w = None
        self.readers = []

    def __getitem__(self, key):
        return View(self, self.ap[key])

    def v(self):
        return View(self, self.ap)


class View:
    def __init__(self, buf, ap):
        self.buf = buf
        self.ap = ap

    def __getitem__(self, key):
        return View(self.buf, self.ap[key])

    def rr(self, s, **kw):
        return View(self.buf, self.ap.rearrange(s, **kw))

    def bc(self, shape):
        return View(self.buf, self.ap.to_broadcast(list(shape)))

    def unsq(self, axis):
        return View(self.buf, self.ap.unsqueeze(axis))

    def bitcast(self, dt):
        return View(self.buf, self.ap.bitcast(dt))


class Op:
    __slots__ = ("eng", "meth", "kw", "reads", "writes", "deps", "idx", "is_dma", "inc_val", "key")


WRITE_KEYS = ("out", "accum_out")
ENGS = ("pe", "act", "dve", "pool", "sp")


class Prog:
    def __init__(self, nc):
        self.nc = nc
        self.ops = []
        self.bar = set()
        self.bpoints = []
        self.capture = None
        self.arena = None
        self.aoff = 0
        self.asize = 0

    def barrier(self):
        last = {}
        dm = set()
        for op in self.ops:
            if op.is_dma:
                dm.add(op.idx)
            else:
                last[op.eng] = op.idx
        self.bar = set(last.values()) | dm
        self.aoff = 0
        self.bpoints.append(len(self.ops))

    def ov_top(self, name, shape, dtype):
        n = 1
        for d in shape[1:]:
            n *= d
        nbytes = n * (4 if dtype == F32 else 2)
        n4 = (nbytes + 31) // 32 * 8
        self.atop = self.asize - n4
        ap = self.arena[:, self.atop:self.atop + n4]
        if dtype != F32:
            ap = ap.bitcast(dtype)
        ap = ap[:, 0:n]
        if len(shape) == 3:
            ap = ap.rearrange("p (a b) -> p a b", a=shape[1])
        return Buf(ap, name)

    def ov(self, name, shape, dtype):
        n = 1
        for d in shape[1:]:
            n *= d
        nbytes = n * (4 if dtype == F32 else 2)
        n4 = (nbytes + 31) // 32 * 8
        assert self.aoff + n4 <= getattr(self, "atop", self.asize), (name, self.aoff, n4, self.asize)
        ap = self.arena[:, self.aoff:self.aoff + n4]
        self.aoff += n4
        if dtype != F32:
            ap = ap.bitcast(dtype)
        ap = ap[:, 0:n]
        if len(shape) == 3:
            ap = ap.rearrange("p (a b) -> p a b", a=shape[1])
        return Buf(ap, name)

    def sb(self, name, shape, dtype):
        t = self.nc.alloc_sbuf_tensor(name, list(shape), dtype)
        return Buf(t.ap(), name)

    def ps(self, name, shape, dtype=F32):
        t = self.nc.alloc_psum_tensor(name, list(shape), dtype)
        return Buf(t.ap(), name)

    def do(self, eng, meth, extra_reads=(), extra_writes=(), **kw):
        if self.capture is not None:
            self.capture.append((eng, meth, extra_reads, extra_writes, kw))
            return None
        reads, writes, real = [], [], {}
        for k, v in kw.items():
            if isinstance(v, View):
                (writes if k in WRITE_KEYS else reads).append(v.buf)
                real[k] = v.ap
            elif isinstance(v, Buf):
                (writes if k in WRITE_KEYS else reads).append(v)
                real[k] = v.ap
            else:
                real[k] = v
        for b in extra_reads:
            reads.append(b.buf if isinstance(b, View) else b)
        for b in extra_writes:
            writes.append(b.buf if isinstance(b, View) else b)
        op = Op()
        op.eng, op.meth, op.kw, op.reads, op.writes = eng, meth, real, reads, writes
        op.is_dma = meth in ("dma_start",)
        op.inc_val = None
        op.key = None
        op.idx = len(self.ops)
        deps = set()
        for r in reads:
            if r.last_w is not None:
                deps.add(r.last_w)
        for w in writes:
            if w.last_w is not None:
                deps.add(w.last_w)
            deps.update(w.readers)
        for r in reads:
            r.readers.append(op.idx)
        for w in writes:
            w.last_w = op.idx
            w.readers = []
        deps.discard(op.idx)
        deps |= self.bar
        if eng == "pe":
            deps = {d for d in deps if self.ops[d].eng != "pe"}
        op.deps = deps
        self.ops.append(op)
        return op

    def emit(self):
        nc = self.nc
        ops = self.ops
        has_dep = [False] * len(ops)
        for op in ops:
            for d in op.deps:
                has_dep[d] = True
        eng_sem = {e: nc.alloc_semaphore("sem_" + e) for e in ENGS}
        eng_cnt = {e: 0 for e in ENGS}
        dma_sems = {}
        cur = {}
        swsems = {}
        free = []
        bset = set(self.bpoints)
        for op in ops:
            if op.idx in bset:
                free.extend(cur.values())
                cur = {}
            if op.is_dma:
                key = op.writes[0] if op.writes else op.reads[0]
                if op.eng == "pool":
                    ent = swsems.get(id(key))
                    if ent is None:
                        ent = [nc.alloc_semaphore("swsem%d" % len(swsems)), 0]
                        swsems[id(key)] = ent
                        dma_sems["sw%d" % len(swsems)] = ent
                else:
                    ent = cur.get(id(key))
                    if ent is None:
                        if free:
                            ent = free.pop()
                        else:
                            ent = [nc.alloc_semaphore("dsem%d" % len(dma_sems)), 0]
                            dma_sems[len(dma_sems)] = ent
                        cur[id(key)] = ent
                ent[1] += 16
                op.inc_val = (ent[0], ent[1], 16)
            elif has_dep[op.idx]:
                eng_cnt[op.eng] += 1
                op.inc_val = (eng_sem[op.eng], eng_cnt[op.eng], 1)
        streams = {e: [] for e in ENGS}
        for op in ops:
            streams[op.eng].append(op)
        engmap = {"pe": "tensor", "act": "scalar", "dve": "vector", "pool": "gpsimd", "sp": "sync"}

        def run_stream(ename):
            def body(eng):
                waited = {}
                for op in streams[ename]:
                    need = {}
                    for d in op.deps:
                        s, val, _ = ops[d].inc_val
                        k = id(s)
                        if waited.get(k, 0) >= val:
                            continue
                        if k not in need or need[k][1] < val:
                            need[k] = (s, val)
                    for k, (s, val) in need.items():
                        eng.wait_ge(s, val)
                        waited[k] = val
                    ins = getattr(eng, op.meth)(**op.kw)
                    if op.inc_val is not None:
                        ins.then_inc(op.inc_val[0], op.inc_val[2])
                if ename == "sp":
                    for key, (s, tot) in dma_sems.items():
                        eng.wait_ge(s, tot)
            return body

        with nc.Block() as block:
            for e in ENGS:
                getattr(block, engmap[e])(run_stream(e))


def build_program():
    nc = bass.Bass("TRN2", target_bir_lowering=False)
    P = Prog(nc)

    def din(name, shape):
        return nc.dram_tensor(name, list(shape), F32, kind="ExternalInput").ap()

    def dout(name, shape):
        return nc.dram_tensor(name, list(shape), F32, kind="ExternalOutput").ap()

    xp = din("xp", [TP, D])
    xs = din("xs", [NS * TS, D])
    c17 = din("c17", [1 + NS, D])
    st_ssm = din("st_ssm", [DEPTH, NS, 16, 64, 128])
    st_sconv = din("st_sconv", [DEPTH, NS, 3, 1536])
    st_mc = din("st_mc", [DEPTH, NS, 4, 128, 128])
    st_mn = din("st_mn", [DEPTH, NS, 4, 128])
    st_mm = din("st_mm", [DEPTH, NS, 4])
    st_lh = din("st_lh", [DEPTH, NS, 512])
    st_lconv = din("st_lconv", [DEPTH, NS, 3, 512])
    ada_w = din("ada_w", [DEPTH, D, 6 * D])
    ada_b = din("ada_b", [DEPTH, 6 * D])
    norm1_w = din("norm1_w", [DEPTH, D])
    norm2_w = din("norm2_w", [DEPTH, D])
    w_in = din("w_in", [DEPTH, D, IN_DIM])
    ssd_conv_w = din("ssd_conv_w", [DEPTH, 4, 1536])
    ssd_conv_b = din("ssd_conv_b", [DEPTH, 1536])
    ssd_dt_bias = din("ssd_dt_bias", [DEPTH, 16])
    ssd_a_log = din("ssd_a_log", [DEPTH, 16])
    ssd_d = din("ssd_d", [DEPTH, 16])
    ssd_norm_w = din("ssd_norm_w", [DEPTH, 1024])
    ml_i_bias = din("ml_i_bias", [DEPTH, 4])
    ml_f_bias = din("ml_f_bias", [DEPTH, 4])
    ml_norm_w = din("ml_norm_w", [DEPTH, 512])
    lru_conv_w = din("lru_conv_w", [DEPTH, 4, 512])
    lru_conv_b = din("lru_conv_b", [DEPTH, 512])
    lru_wa = din("lru_wa", [DEPTH, 4, 128, 128])
    lru_ba = din("lru_ba", [DEPTH, 512])
    lru_wx = din("lru_wx", [DEPTH, 4, 128, 128])
    lru_bx = din("lru_bx", [DEPTH, 512])
    lru_lambda = din("lru_lambda", [DEPTH, 512])
    w_out = din("w_out", [DEPTH, 2 * D, D])
    mlp_up = din("mlp_up", [DEPTH, D, 4 * D])
    mlp_down = din("mlp_down", [DEPTH, 4 * D, D])
    final_norm_w = din("final_norm_w", [1, D])
    y_p = dout("y_p", [TP, D])
    y_s = dout("y_s", [NS * TS, D])
    o_ssm = dout("o_ssm", [DEPTH, 1 + NS, 16, 64, 128])
    o_sconv = dout("o_sconv", [DEPTH, 1 + NS, 3, 1536])
    o_mc = dout("o_mc", [DEPTH, 1 + NS, 4, 128, 128])
    o_mn = dout("o_mn", [DEPTH, 1 + NS, 4, 128])
    o_mm = dout("o_mm", [DEPTH, 1 + NS, 4])
    o_lh = dout("o_lh", [DEPTH, 1 + NS, 512])
    o_lconv = dout("o_lconv", [DEPTH, 1 + NS, 3, 512])

    def mm(out, lhsT, rhs, start=True, stop=True):
        P.do("pe", "matmul", out=out, lhsT=lhsT, rhs=rhs, start=start, stop=stop)

    def tr(out, in_, ident):
        P.do("pe", "transpose", out=out, in_=in_, identity=ident)

    def act(out, in_, func, bias=0.0, scale=1.0, eng="act", **kw):
        P.do("act", "activation", out=out, in_=in_, func=func, bias=bias, scale=scale, **kw)

    def tt(out, in0, in1, op, eng="dve"):
        P.do(eng, "tensor_tensor", out=out, in0=in0, in1=in1, op=op)

    def ts(out, in0, s1, s2, op0, op1=None, eng="dve"):
        if op1 is None:
            P.do(eng, "tensor_scalar", out=out, in0=in0, scalar1=s1, scalar2=None, op0=op0)
        else:
            P.do(eng, "tensor_scalar", out=out, in0=in0, scalar1=s1, scalar2=s2, op0=op0, op1=op1)

    def stt(out, in0, scalar, in1, op0, op1):
        P.do("dve", "scalar_tensor_tensor", out=out, in0=in0, scalar=scalar, in1=in1, op0=op0, op1=op1)

    def cp(out, in_, eng="dve"):
        P.do(eng, "tensor_copy", out=out, in_=in_)

    def dma(out, in_, eng="sp"):
        P.do(eng, "dma_start", out=out, in_=in_)

    def dma_slow(out, in_, eng="sp"):
        P.do(eng, "dma_start", out=out, in_=in_, allow_slow_non_contiguous=True)

    def memset(buf_view, val, eng="pool"):
        P.do(eng, "memset", ap=buf_view.ap, constant=val, extra_writes=[buf_view.buf])

    identf = P.sb("identf", [128, 128], F32)
    identb = P.sb("identb", [128, 128], BF16)
    ones_mean = P.sb("ones_mean", [128, 128], BF16)
    onesf = P.sb("onesf", [128, 128], F32)
    ucum = P.sb("ucum", [128, 128], F32)
    negrep = P.sb("negrep", [128, 4, 128], BF16)
    negT = P.sb("negT", [128, 4, 128], BF16)
    sel127 = P.sb("sel127", [128, 128], F32)
    sel7 = P.sb("sel7", [128, 128], F32)
    memset(identf.v(), 1.0)
    P.do("pool", "affine_select", out=identf.v(), in_=identf.v(), pattern=[[-1, 128]], compare_op=ALU.is_equal,
         fill=0.0, base=0, channel_multiplier=1)
    cp(identb.v(), identf.v(), eng="pool")
    memset(ones_mean.v(), 1.0 / 1024.0)
    memset(onesf.v(), 1.0)
    memset(ucum.v(), 1.0)
    P.do("pool", "affine_select", out=ucum.v(), in_=ucum.v(), pattern=[[1, 128]], compare_op=ALU.is_ge,
         fill=0.0, base=0, channel_multiplier=-1)
    negtmp = P.sb("negtmp", [128, 128], F32)
    memset(negtmp.v(), 0.0)
    P.do("pool", "affine_select", out=negtmp.v(), in_=negtmp.v(), pattern=[[1, 128]], compare_op=ALU.is_ge,
         fill=NEG, base=0, channel_multiplier=-1)
    for i in range(4):
        cp(negrep[:, i, :], negtmp.v(), eng="pool")
    memset(negtmp.v(), 0.0)
    P.do("pool", "affine_select", out=negtmp.v(), in_=negtmp.v(), pattern=[[-1, 128]], compare_op=ALU.is_ge,
         fill=NEG, base=0, channel_multiplier=1)
    for i in range(4):
        cp(negT[:, i, :], negtmp.v(), eng="pool")
    for selt, row in ((sel127, 127), (sel7, 7)):
        memset(selt.v(), 1.0)
        P.do("pool", "affine_select", out=selt.v(), in_=selt.v(), pattern=[[0, 128]], compare_op=ALU.is_equal,
             fill=0.0, base=-row, channel_multiplier=1)

    PA = P.ps("PA", [128, 1024], F32)
    PB = P.ps("PB", [128, 2048], F32)
    PC = P.ps("PC", [128, 512], F32)
    PF = P.ps("PF", [128, 512], F32)

    x = P.sb("x", [128, 8, NTOK], F32)
    hn_d = nc.dram_tensor("hn_d", [NTOK // 128, 128, 1024], BF16, kind="Internal").ap()
    NCH = NTOK // 128
    xc = [Buf(x.ap[:, :, c * 128:(c + 1) * 128], "x%d" % c) for c in range(NCH)]
    hd = [Buf(hn_d[c].rearrange("p (k t) -> p k t", k=8), "hd%d" % c) for c in range(NCH)]


    def acp(out, in_):
        P.do("act", "activation", out=out, in_=in_, func=AF.Copy)

    Wbuf = Wo = None
    stage = P.sb("stage", [128, 128], F32)
    parA = P.sb("parA", [128, 104], F32)
    parB = P.sb("parB", [128, 60], F32)
    mod = P.sb("mod", [128, 48, 17], F32)
    s1 = P.sb("s1", [128, 8, 17], F32)
    s2 = P.sb("s2", [128, 8, 17], F32)
    bc_dtb = P.sb("bc_dtb", [128, 16], F32)
    bc_A = P.sb("bc_A", [128, 16], F32)
    bc_D = P.sb("bc_D", [128, 16], F32)
    bc_ib = P.sb("bc_ib", [128, 4], F32)
    bc_fb = P.sb("bc_fb", [128, 4], F32)
    lru_c1 = P.sb("lru_c1", [128, 4], F32)
    fnw = P.sb("fnw", [128, 8], F32)
    cs_fm = P.sb("cs_fm", [128, 8, 17], BF16)
    eps_t = P.sb("eps_t", [128, 1], F32)
    onesb = P.sb("onesb", [128, 1], BF16)
    memset(eps_t.v(), EPS)
    memset(onesb.v(), 1.0)
    sconv_io = P.sb("sconv_io", [128, 12, 48], F32)
    lconv_io = P.sb("lconv_io", [128, 4, 48], F32)
    lh_io = P.sb("lh_io", [128, 4, 16], F32)
    mn_io = P.sb("mn_io", [128, 64], F32)
    A4 = nc.sbuf_bytes_remaining // 4 - 64
    A4 = A4 // 8 * 8
    P.arena = nc.alloc_sbuf_tensor("arena", [128, A4], F32).ap()
    P.asize = A4

    xins = [P.ov("xin%d" % i, [128, 1024], F32) for i in range(3)]
    pbs = [PA, Buf(PB.ap[:, 0:1024], "pb1"), Buf(PB.ap[:, 1024:2048], "pb2")]
    _xload_pending = True

    cin = P.ov("cin", [128, 1024], F32)
    dma(cin[0:17, :], c17)
    act(cin[0:17, :], cin[0:17, :], AF.Silu)
    for k in range(8):
        tr(PC[:, k * 17:(k + 1) * 17], cin[0:17, k * 128:(k + 1) * 128], identf[0:17, 0:17])
    cp(cs_fm.v(), PC[:, 0:136].rr("p (k s) -> p k s", k=8))

    def load_fm(dst_view, rows_list):
        r0 = 0
        for ap in rows_list:
            r = ap.shape[0]
            dma(stage[r0:r0 + r, :], ap)
            r0 += r
        tr(PC[:, 0:r0], stage[0:r0, :], identf[0:r0, 0:r0])
        cp(dst_view, PC[:, 0:r0])

    def load_bc(dst_view, ap_row, n):
        dma(dst_view, ap_row.to_broadcast([128, n]))

    load_fm(fnw.v(), [final_norm_w.rearrange("o (k p) -> (o k) p", p=128)])

    def alloc_norm():
        return ([P.ov("adaslab%d" % i, [128, 8, 512], BF16) for i in range(2)],
                [P.ov("sq%d" % i, [128, 8, 128], BF16) for i in range(2)],
                [P.ov("rstd%d" % i, [128, 128], F32) for i in range(2)],
                [P.ov("ntmp%d" % i, [128, 8, 128], F32) for i in range(2)])
    adaslabs = sqs = rstds = ntmps = None
    nbanks = [PC, PF]

    def rmsnorm_mod(c, sc_t, sh_off, dstb):
        sq, rstd, ntmp, nb = sqs[c % 2], rstds[c % 2], ntmps[c % 2], nbanks[c % 2]
        P.do("pool" if c % 2 else "act", "tensor_tensor" if c % 2 else "activation",
             **(dict(out=sq.v(), in0=xc[c].v(), in1=xc[c].v(), op=ALU.mult) if c % 2 else
                dict(out=sq.v(), in_=xc[c].v(), func=AF.Square)))
        for k in range(8):
            mm(nb[:, 0:128], ones_mean.v(), sq[:, k, :], start=(k == 0), stop=(k == 7))
        act(rstd.v(), nb[:, 0:128], AF.Ln, bias=eps_t[:, 0:1], scale=1.0)
        act(rstd.v(), rstd.v(), AF.Exp, scale=-0.5)
        tt(ntmp.v(), xc[c].v(), rstd.v().unsq(1).bc([128, 8, 128]), ALU.mult)
        if c < 16:
            for k in range(8):
                act(dstb[:, k, :], ntmp[:, k, :], AF.Identity, bias=mod[:, sh_off + k, 0:1], scale=sc_t[:, k, 0:1])
        else:
            for k in range(8):
                tt(ntmp[:, k, :].rr("p (b t) -> p b t", t=8), ntmp[:, k, :].rr("p (b t) -> p b t", t=8),
                   sc_t[:, k, 1:17].unsq(2).bc([128, 16, 8]), ALU.mult)
                tt(dstb[:, k, :].rr("p (b t) -> p b t", t=8), ntmp[:, k, :].rr("p (b t) -> p b t", t=8),
                   mod[:, sh_off + k, 1:17].unsq(2).bc([128, 16, 8]), ALU.add)


    zs = xpad = cacc = xbc = xB = sm = Rb = dec = MT = xdt = xend = yy = y2 = ynb = yfm = Snat = STb = ss = None

    def alloc_ssd():
        return dict(zs=P.ov("zs", [128, 1024], BF16), xpad=P.ov("xpad", [128, 12, 131], F32),
                    cacc=P.ov("cacc", [128, 128], F32), xbc=P.ov("xbc", [128, 12, 128], BF16),
                    xB=P.ov("xB", [128, 1280], BF16), sm=P.ov("sm", [128, 256], F32),
                    Rb=P.ov("Rb", [128, 4, 128], F32), dec=P.ov("dec", [128, 4, 128], F32),
                    MT=P.ov("MT", [128, 16, 128], BF16), xdt=P.ov("xdt", [128, 1024], BF16),
                    yy=P.ov("yy", [128, 1024], F32), ynb=P.ov("ynb", [128, 1024], BF16),
                    yfm=P.ov("yfm", [128, 8, 128], BF16), Snat=P.ov("Snat", [128, 8, 128], F32),
                    ss=P.ov("ss", [128, 8], F32))

    def w_out_and_update(l, tok0, L, nk, seqcol):
        c = tok0 // 128
        o = tok0 - c * 128
        for oc in range(8):
            for k in range(nk):
                mm(PB[:, oc * 128:oc * 128 + L], Wo[:, k, oc * 128:(oc + 1) * 128], yfm[:, k, 0:L],
                   start=(k == 0), stop=(k == nk - 1))
        for oc in range(8):
            stt(xc[c][:, oc, o:o + L], PB[:, oc * 128:oc * 128 + L], mod[:, 16 + oc, seqcol:seqcol + 1],
                xc[c][:, oc, o:o + L], ALU.mult, ALU.add)

    def units():
        for c in range(16):
            yield (c * 128, 128, 0, c == 0, c == 15, None)
        for b in range(NS):
            yield (TP + b * TS, TS, 1 + b, True, True, b)

    def ssd_unit(l, tok0, L, seqcol, first, last, b):
        c = tok0 // 128
        o = tok0 - c * 128
        if b is None:
            H = hbuf[c % 2]
            dma(H.v(), hd[c].v())
            o = 0
        else:
            H = hsamp
        sel = sel127 if L == 128 else sel7
        if b is not None:
            dma(Snat.v(), st_ssm[l, b].rearrange("(hp h2) p n -> (h2 p) hp n", h2=2))
            for j in range(3):
                dma_slow(xpad[:, :, j], st_sconv[l, b, j].rearrange("(k p) -> p k", p=128))
        elif first:
            memset(Snat.v(), 0.0)
            memset(xpad[:, :, 0:3], 0.0)
        for j in range(2):
            for k in range(8):
                mm(PA[0:L, j * 512:(j + 1) * 512], H[:, k, o:o + L], Wbuf[:, k, j * 512:(j + 1) * 512],
                   start=(k == 0), stop=(k == 7))
        act(zs[0:L, :], PA[0:L, :], AF.Silu)
        for oc in range(12):
            for k in range(8):
                mm(PB[:, oc * 128:oc * 128 + L], Wbuf[:, k, 1024 + oc * 128:1024 + (oc + 1) * 128], H[:, k, o:o + L],
                   start=(k == 0), stop=(k == 7))
        acp(xpad[:, :, 3:3 + L], PB[:, 0:1536].rr("p (k t) -> p k t", k=12)[:, :, 0:L])
        for k in range(8):
            mm(PC[0:L, 0:16], H[:, k, o:o + L], Wbuf[:, k, 2560:2576], start=(k == 0), stop=(k == 7))
        dt = sm[0:L, 0:16]
        dtA = sm[0:L, 16:32]
        cum = sm[0:L, 32:48]
        tt(dt, PC[0:L, 0:16], bc_dtb[0:L, :], ALU.add)
        act(dt, dt, AF.Exp)
        act(dt, dt, AF.Ln, bias=1.0)
        tt(dtA, dt, bc_A[0:L, :], ALU.mult)
        for oc in range(12):
            ts(cacc[:, 0:L], xpad[:, oc, 0:L], parB[:, oc:oc + 1], parB[:, 48 + oc:49 + oc], ALU.mult, ALU.add)
            for j in range(1, 4):
                stt(cacc[:, 0:L], xpad[:, oc, j:j + L], parB[:, j * 12 + oc:j * 12 + oc + 1], cacc[:, 0:L], ALU.mult, ALU.add)
            act(xbc[:, oc, 0:L], cacc[:, 0:L], AF.Silu)
        if last:
            for j in range(3):
                dma_slow(o_sconv[l, seqcol, j].rearrange("(k p) -> p k", p=128), xpad[:, :, L + j])
        elif b is None:
            cp(xpad[:, :, 0:3], xpad[:, :, L:L + 3], eng="pool")
        PAb = PA.v().bitcast(BF16)
        for oc in range(10):
            tr(PAb[0:L, oc * 128:(oc + 1) * 128], xbc[:, oc, 0:L], identb.v())
        acp(xB[0:L, :], PAb[0:L, 0:1280])
        mm(PC[0:L, 16:32], ucum[0:L, 0:L], dtA)
        cp(cum, PC[0:L, 16:32])
        for g in range(2):
            mm(PF[0:L, g * 128:g * 128 + L], xbc[:, 8 + g, 0:L], xbc[:, 10 + g, 0:L])
        for q in range(4):
            g = q // 2
            tt(Rb[0:L, :, 0:L], ucum[0:L, 0:L].unsq(1).bc([L, 4, L]), dtA[:, 4 * q:4 * q + 4].unsq(2).bc([L, 4, L]), ALU.mult)
            outv = PB[0:L, q * 512:(q + 1) * 512].rr("p (h t) -> p h t", h=4)[:, :, 0:L]
            mm(outv, onesf[0:L, 0:L], Rb[0:L, :, 0:L], start=True, stop=False)
            mm(outv, identb[0:L, 0:L], negrep[0:L, :, 0:L], start=False, stop=True)
            tt(dec[0:L, :, 0:L], outv, cum[:, 4 * q:4 * q + 4].unsq(2).bc([L, 4, L]), ALU.subtract)
            act(dec[0:L, :, 0:L], dec[0:L, :, 0:L], AF.Exp)
            tt(MT[0:L, 4 * q:4 * q + 4, 0:L], dec[0:L, :, 0:L],
               PF[0:L, g * 128:g * 128 + L].unsq(1).bc([L, 4, L]), ALU.mult)
        tt(xdt[0:L, :].rr("t (h p) -> t h p", p=64), xB[0:L, 0:1024].rr("t (h p) -> t h p", p=64),
           dt.unsq(2).bc([L, 16, 64]), ALU.mult)
        for h in range(16):
            mm(PA[0:L, h * 64:(h + 1) * 64], MT[0:L, h, 0:L], xdt[0:L, h * 64:(h + 1) * 64])
        for hp in range(8):
            tr(PB[:, hp * 128:(hp + 1) * 128], Snat[:, hp, :], identf.v())
        acp(STb.v(), PB[:, 0:1024])
        for g in range(2):
            mm(PB[0:L, 1024 + g * 512:1024 + (g + 1) * 512], xbc[:, 10 + g, 0:L], STb[:, g * 512:(g + 1) * 512])
        ecum = sm[0:L, 48:64]
        act(ecum, cum, AF.Exp)
        tt(yy[0:L, :].rr("t (h p) -> t h p", p=64), PB[0:L, 1024:2048].rr("t (h p) -> t h p", p=64),
           ecum.unsq(2).bc([L, 16, 64]), ALU.mult)
        tt(yy[0:L, :], yy[0:L, :], PA[0:L, :], ALU.add)
        tt(y2[0:L, :].rr("t (h p) -> t h p", p=64), xB[0:L, 0:1024].rr("t (h p) -> t h p", p=64),
           bc_D[0:L, :].unsq(2).bc([L, 16, 64]), ALU.mult)
        tt(yy[0:L, :], yy[0:L, :], y2[0:L, :], ALU.add)
        tt(yy[0:L, :], yy[0:L, :], zs[0:L, :], ALU.mult)
        act(y2[0:L, :], yy[0:L, :], AF.Square, accum_out=ss[0:L, 0:1])
        act(ss[0:L, 1:2], ss[0:L, 0:1], AF.Ln, bias=eps_t[0:L, 0:1], scale=1.0 / 1024.0)
        act(ss[0:L, 2:3], ss[0:L, 1:2], AF.Exp, scale=-0.5)
        ts(ynb[0:L, :], yy[0:L, :], ss[0:L, 2:3], None, ALU.mult)
        for k in range(8):
            tr(PAb[:, k * 128:k * 128 + L], ynb[0:L, k * 128:(k + 1) * 128], identb[0:L, 0:L])
        tt(yfm[:, :, 0:L], PAb[:, 0:1024].rr("p (k t) -> p k t", k=8)[:, :, 0:L],
           parA[:, 64:72].unsq(2).bc([128, 8, L]), ALU.mult)
        w_out_and_update(l, tok0, L, 8, seqcol)
        mm(PC[:, 32:48], sel[0:L, :], cum)
        cl = sm[:, 64:80]
        cp(cl, PC[:, 32:48])
        eend = sm[0:L, 80:96]
        tt(eend, cl[0:L, :], cum, ALU.subtract)
        act(eend, eend, AF.Exp)
        tt(xend[0:L, :].rr("t (h p) -> t h p", p=64), xdt[0:L, :].rr("t (h p) -> t h p", p=64),
           eend.unsq(2).bc([L, 16, 64]), ALU.mult)
        dcy = sm[:, 96:112]
        act(dcy, cl, AF.Exp)
        for hp in range(8):
            g = hp // 4
            mm(PA[:, hp * 128:(hp + 1) * 128], xend[0:L, hp * 128:(hp + 1) * 128], xB[0:L, 1024 + g * 128:1024 + (g + 1) * 128])
        dcyv = dcy.rr("p (hp h2) -> p hp h2", h2=2)
        for h2 in range(2):
            rs = slice(64 * h2, 64 * h2 + 64)
            tt(Snat[rs, :, :], Snat[rs, :, :], dcyv[rs, :, h2:h2 + 1].bc([64, 8, 128]), ALU.mult)
            tt(Snat[rs, :, :], Snat[rs, :, :], PA[rs, :].rr("p (hp n) -> p hp n", hp=8), ALU.add)
        if last:
            dma(o_ssm[l, seqcol].rearrange("(hp h2) p n -> (h2 p) hp n", h2=2), Snat.v())

    hbuf = hsamp = None
    lpad = lx = lxb = lri = la = lu = lh = lg = hstate = wg = None

    def alloc_lru():
        return dict(lpad=P.ov("lpad", [128, 4, 131], F32), lx=P.ov("lx", [128, 4, 128], F32),
                    lxb=P.ov("lxb", [128, 4, 128], BF16), lri=P.ov("lri", [128, 8, 128], F32),
                    la=P.ov("la", [128, 4, 128], F32), lu=P.ov("lu", [128, 4, 128], F32),
                    lh=P.ov("lh", [128, 4, 128], F32), lg=P.ov("lg", [128, 4, 128], F32),
                    hstate=P.ov("hstate", [128, 4], F32), wg=P.ov("wg", [128, 8, 128], BF16),
                    yfm=P.ov("yfm", [128, 8, 128], BF16))

    def lru_unit(l, tok0, L, seqcol, first, last, b):
        c = tok0 // 128
        o = tok0 - c * 128
        if b is None:
            H = hbuf[c % 2]
            dma(H.v(), hd[c].v())
            o = 0
        else:
            H = hsamp
        if b is not None:
            dma_slow(hstate.v(), st_lh[l, b].rearrange("(k p) -> p k", p=128))
            for j in range(3):
                dma_slow(lpad[:, :, j], st_lconv[l, b, j].rearrange("(k p) -> p k", p=128))
        elif first:
            memset(hstate.v(), 0.0)
            memset(lpad[:, :, 0:3], 0.0)
        for oc in range(8):
            for k in range(8):
                mm(PA[:, oc * 128:oc * 128 + L], Wbuf[:, k, oc * 128:(oc + 1) * 128], H[:, k, o:o + L],
                   start=(k == 0), stop=(k == 7))
        PAv = PA.v().rr("p (k t) -> p k t", k=8)
        acp(lpad[:, :, 3:3 + L], PAv[:, 0:4, 0:L])
        act(lg[:, :, 0:L], PAv[:, 4:8, 0:L], AF.Gelu)
        def cwv(j):
            return parA[:, 72 + j * 4:72 + (j + 1) * 4].unsq(2).bc([128, 4, L])
        tt(lx[:, :, 0:L], lpad[:, :, 0:L], cwv(0), ALU.mult)
        for j in range(1, 4):
            tt(lu[:, :, 0:L], lpad[:, :, j:j + L], cwv(j), ALU.mult)
            tt(lx[:, :, 0:L], lx[:, :, 0:L], lu[:, :, 0:L], ALU.add)
        tt(lx[:, :, 0:L], lx[:, :, 0:L], parA[:, 88:92].unsq(2).bc([128, 4, L]), ALU.add)
        cp(lxb[:, :, 0:L], lx[:, :, 0:L])
        if last:
            for j in range(3):
                dma_slow(o_lconv[l, seqcol, j].rearrange("(k p) -> p k", p=128), lpad[:, :, L + j])
        elif b is None:
            cp(lpad[:, :, 0:3], lpad[:, :, L:L + 3], eng="pool")
        for k in range(4):
            mm(PC[:, k * 128:k * 128 + L], wg[:, k, :], lxb[:, k, 0:L])
        for k in range(4):
            mm(PF[:, k * 128:k * 128 + L], wg[:, 4 + k, :], lxb[:, k, 0:L])
        tt(lri[:, 0:4, 0:L], PC.v().rr("p (k t) -> p k t", k=4)[:, :, 0:L], parA[:, 92:96].unsq(2).bc([128, 4, L]), ALU.add)
        tt(lri[:, 4:8, 0:L], PF.v().rr("p (k t) -> p k t", k=4)[:, :, 0:L], parA[:, 96:100].unsq(2).bc([128, 4, L]), ALU.add)
        act(lri[:, :, 0:L], lri[:, :, 0:L], AF.Sigmoid)
        for k in range(4):
            act(la[:, k, 0:L], lri[:, k, 0:L], AF.Exp, scale=lru_c1[:, k:k + 1])
        tt(lu[:, :, 0:L], la[:, :, 0:L], la[:, :, 0:L], ALU.mult)
        ts(lu[:, :, 0:L], lu[:, :, 0:L], -1.0, 1.0, ALU.mult, ALU.add)
        ts(lu[:, :, 0:L], lu[:, :, 0:L], 0.0, None, ALU.max)
        act(lu[:, :, 0:L], lu[:, :, 0:L], AF.Sqrt)
        if b is None and first:
            memset(lu[:, :, 0:1], 1.0)
        tt(lu[:, :, 0:L], lu[:, :, 0:L], lri[:, 4:8, 0:L], ALU.mult)
        tt(lu[:, :, 0:L], lu[:, :, 0:L], lx[:, :, 0:L], ALU.mult)
        for k in range(4):
            P.do("dve", "tensor_tensor_scan", out=lh[:, k, 0:L], data0=la[:, k, 0:L], data1=lu[:, k, 0:L],
                 initial=hstate[:, k:k + 1], op0=ALU.mult, op1=ALU.add)
        cp(hstate.v(), lh[:, :, L - 1])
        if last:
            dma_slow(o_lh[l, seqcol].rearrange("(k p) -> p k", p=128), hstate.v())
        tt(yfm[:, 0:4, 0:L], lh[:, :, 0:L], lg[:, :, 0:L], ALU.mult)
        w_out_and_update(l, tok0, L, 4, seqcol)

    bc_mlw = None
    qk = ktm = vtm = vw = osig = g4 = R4 = d4 = w4 = Cnat = CTb = nfm = nfb = mbc = num = num2 = None

    def alloc_ml():
        return dict(qk=P.ov("qk", [128, 8, 128], BF16), ktm=P.ov("ktm", [128, 512], BF16),
                    vtm=P.ov("vtm", [128, 512], BF16), vw=P.ov("vw", [128, 512], BF16),
                    osig=P.ov("osig", [128, 512], F32), g4=P.ov("g4", [128, 128], F32),
                    R4=P.ov("R4", [128, 4, 128], F32), d4=P.ov("d4", [128, 4, 128], F32),
                    w4=P.ov("w4", [128, 4, 128], BF16), Cnat=P.ov("Cnat", [128, 4, 128], F32),
                    CTb=P.ov("CTb", [128, 4, 128], BF16), nfm=P.ov("nfm", [128, 4], F32),
                    nfb=P.ov("nfb", [128, 4], BF16), mbc=P.ov("mbc", [128, 4], F32),
                    num=P.ov("num", [128, 512], F32), num2=P.ov("num2", [128, 512], F32), bc_mlw=P.ov("bc_mlw", [128, 512], F32),
                    ynb=P.ov("ynb", [128, 1024], BF16), yfm=P.ov("yfm", [128, 8, 128], BF16))

    KSC = 128.0 ** -0.5

    def ml_unit(l, tok0, L, seqcol, first, last, b):
        c = tok0 // 128
        o = tok0 - c * 128
        if b is None:
            H = hbuf[c % 2]
            dma(H.v(), hd[c].v())
            o = 0
        else:
            H = hsamp
        sel = sel127 if L == 128 else sel7
        if b is not None:
            dma(Cnat.v(), st_mc[l, b].rearrange("h v d -> v h d"))
            dma_slow(nfm.v(), st_mn[l, b].rearrange("h d -> d h"))
            dma(mbc.v(), st_mm[l, b:b + 1, :].to_broadcast([128, 4]))
        elif first:
            memset(Cnat.v(), 0.0)
            memset(nfm.v(), 0.0)
            memset(mbc.v(), 0.0)
        for oc in range(8):
            for k in range(8):
                mm(PA[:, oc * 128:oc * 128 + L], Wbuf[:, k, oc * 128:(oc + 1) * 128], H[:, k, o:o + L],
                   start=(k == 0), stop=(k == 7))
        PAv = PA.v().rr("p (k t) -> p k t", k=8)
        acp(qk[:, 0:4, 0:L], PAv[:, 0:4, 0:L])
        act(qk[:, 4:8, 0:L], PAv[:, 4:8, 0:L], AF.Copy, scale=KSC)
        for j in range(3):
            for k in range(8):
                mm(PB[0:L, j * 512:(j + 1) * 512], H[:, k, o:o + L], Wbuf[:, k, 512 + j * 512:512 + (j + 1) * 512],
                   start=(k == 0), stop=(k == 7))
        for k in range(8):
            mm(PC[0:L, 0:8], H[:, k, o:o + L], Wbuf[:, k, 2048:2056], start=(k == 0), stop=(k == 7))
        act(ktm[0:L, :], PB[0:L, 0:512], AF.Copy, scale=KSC)
        acp(vtm[0:L, :], PB[0:L, 512:1024])
        act(osig[0:L, :], PB[0:L, 1024:1536], AF.Sigmoid)
        ig = g4[0:L, 0:4]
        lf = g4[0:L, 4:8]
        bcs = g4[0:L, 8:12]
        a_s = g4[0:L, 12:16]
        tt(ig, PC[0:L, 0:4], bc_ib[0:L, :], ALU.add)
        tt(lf, PC[0:L, 4:8], bc_fb[0:L, :], ALU.add)
        act(lf, lf, AF.Exp, scale=-1.0)
        act(lf, lf, AF.Ln, bias=1.0)
        ts(lf, lf, -1.0, None, ALU.mult)
        mm(PC[0:L, 8:12], ucum[0:L, 0:L], lf)
        cp(bcs, PC[0:L, 8:12])
        tt(a_s, ig, bcs, ALU.subtract)
        tt(R4[0:L, :, 0:L], identf[0:L, 0:L].unsq(1).bc([L, 4, L]), a_s.unsq(2).bc([L, 4, L]), ALU.mult)
        outv = PF.v().rr("p (h t) -> p h t", h=4)[0:L, :, 0:L]
        mm(outv, onesf[0:L, 0:L], R4[0:L, :, 0:L], start=True, stop=False)
        mm(outv, identb[0:L, 0:L], negT[0:L, :, 0:L], start=False, stop=True)
        cm = g4[0:L, 16:20]
        P.do("dve", "tensor_reduce", out=cm, in_=outv, axis=AX.X, op=ALU.max)
        r = g4[0:L, 20:24]
        tt(r, cm, mbc[0:L, :], ALU.max)
        mt = g4[0:L, 24:28]
        tt(mt, bcs, r, ALU.add)
        negr = g4[0:L, 28:32]
        ts(negr, r, -1.0, None, ALU.mult)
        tt(R4[0:L, :, 0:L], identf[0:L, 0:L].unsq(1).bc([L, 4, L]), negr.unsq(2).bc([L, 4, L]), ALU.mult)
        mm(outv, onesf[0:L, 0:L], R4[0:L, :, 0:L], start=True, stop=False)
        mm(outv, identb[0:L, 0:L], negrep[0:L, :, 0:L], start=False, stop=True)
        tt(d4[0:L, :, 0:L], outv, a_s.unsq(2).bc([L, 4, L]), ALU.add)
        act(d4[0:L, :, 0:L], d4[0:L, :, 0:L], AF.Exp)
        outc = PC.v().rr("p (h t) -> p h t", h=4)
        for h in range(4):
            mm(outc[0:L, h, 0:L], qk[:, 4 + h, 0:L], qk[:, h, 0:L])
        tt(w4[0:L, :, 0:L], d4[0:L, :, 0:L], outc[0:L, :, 0:L], ALU.mult)
        for h in range(4):
            mm(PA[0:L, h * 128:(h + 1) * 128], w4[0:L, h, 0:L], vtm[0:L, h * 128:(h + 1) * 128])
        for h in range(4):
            mm(PA[0:L, 512 + h:513 + h], w4[0:L, h, 0:L], onesb[0:L, :])
        for h in range(4):
            tr(PB[:, h * 128:(h + 1) * 128], Cnat[:, h, :], identf.v())
        acp(CTb.v(), PB[:, 0:512].rr("p (h v) -> p h v", h=4))
        cp(nfb.v(), nfm.v())
        for h in range(4):
            mm(PB[0:L, 512 + h * 128:512 + (h + 1) * 128], qk[:, h, 0:L], CTb[:, h, :])
        for h in range(4):
            mm(PB[0:L, 1024 + h:1025 + h], qk[:, h, 0:L], nfb[:, h:h + 1])
        inter = g4[0:L, 32:36]
        tt(inter, mbc[0:L, :], r, ALU.subtract)
        act(inter, inter, AF.Exp)
        tt(num[0:L, :].rr("t (h v) -> t h v", h=4), PB[0:L, 512:1024].rr("t (h v) -> t h v", h=4),
           inter.unsq(2).bc([L, 4, 128]), ALU.mult)
        tt(num[0:L, :], num[0:L, :], PA[0:L, 0:512], ALU.add)
        den = g4[0:L, 36:40]
        tt(den, PB[0:L, 1024:1028], inter, ALU.mult)
        tt(den, den, PA[0:L, 512:516], ALU.add)
        act(den, den, AF.Abs)
        emt = g4[0:L, 40:44]
        act(emt, mt, AF.Exp, scale=-1.0)
        tt(den, den, emt, ALU.max)
        P.do("dve", "reciprocal", out=den, in_=den)
        tt(num[0:L, :].rr("t (h v) -> t h v", h=4), num[0:L, :].rr("t (h v) -> t h v", h=4),
           den.unsq(2).bc([L, 4, 128]), ALU.mult)
        tt(num2[0:L, :], num[0:L, :], num[0:L, :], ALU.mult)
        ssq = g4[0:L, 44:48]
        P.do("dve", "tensor_reduce", out=ssq, in_=num2[0:L, :].rr("t (h v) -> t h v", h=4), axis=AX.X, op=ALU.add)
        act(ssq, ssq, AF.Sqrt, bias=eps_t[0:L, 0:1], scale=1.0 / 128.0)
        P.do("dve", "reciprocal", out=ssq, in_=ssq)
        tt(num[0:L, :].rr("t (h v) -> t h v", h=4), num[0:L, :].rr("t (h v) -> t h v", h=4),
           ssq.unsq(2).bc([L, 4, 128]), ALU.mult)
        tt(num[0:L, :], num[0:L, :], bc_mlw[0:L, :], ALU.mult)
        tt(ynb[0:L, 0:512], num[0:L, :], osig[0:L, :], ALU.mult)
        PAb = PA.v().bitcast(BF16)
        for k in range(4):
            tr(PAb[:, k * 128:k * 128 + L], ynb[0:L, k * 128:(k + 1) * 128], identb[0:L, 0:L])
        cp(yfm[:, 0:4, 0:L], PAb[:, 0:512].rr("p (k t) -> p k t", k=4)[:, :, 0:L])
        w_out_and_update(l, tok0, L, 4, seqcol)
        bm = g4[0:L, 48:56]
        cp(bm[:, 0:4], bcs)
        cp(bm[:, 4:8], mt)
        mm(PC[:, 0:8], sel[0:L, :], bm)
        last8 = g4[:, 56:64]
        cp(last8, PC[:, 0:8])
        wend = g4[0:L, 64:68]
        tt(wend, last8[0:L, 0:4], last8[0:L, 4:8], ALU.subtract)
        tt(wend, wend, a_s, ALU.add)
        act(wend, wend, AF.Exp)
        dc = g4[:, 68:72]
        tt(dc, last8[:, 0:4], mbc.v(), ALU.add)
        tt(dc, dc, last8[:, 4:8], ALU.subtract)
        act(dc, dc, AF.Exp)
        cp(mbc.v(), last8[:, 4:8])
        tt(vw[0:L, :].rr("t (h v) -> t h v", h=4), vtm[0:L, :].rr("t (h v) -> t h v", h=4),
           wend.unsq(2).bc([L, 4, 128]), ALU.mult)
        wendb = g4[0:L, 72:76].bitcast(BF16)[:, 0:4]
        cp(wendb, wend)
        for h in range(4):
            mm(PB[:, h * 128:(h + 1) * 128], vw[0:L, h * 128:(h + 1) * 128], ktm[0:L, h * 128:(h + 1) * 128])
        for h in range(4):
            mm(PB[:, 512 + h:513 + h], ktm[0:L, h * 128:(h + 1) * 128], wendb[:, h:h + 1])
        tt(Cnat.v(), Cnat.v(), dc.unsq(2).bc([128, 4, 128]), ALU.mult)
        tt(Cnat.v(), Cnat.v(), PB[:, 0:512].rr("p (h d) -> p h d", h=4), ALU.add)
        tt(nfm.v(), nfm.v(), dc, ALU.mult)
        tt(nfm.v(), nfm.v(), PB[:, 512:516], ALU.add)
        if last:
            dma(o_mc[l, seqcol].rearrange("h v d -> v h d"), Cnat.v())
            dma_slow(o_mn[l, seqcol].rearrange("h d -> d h"), nfm.v())
            dma(o_mm[l, seqcol:seqcol + 1, :], mbc[0:1, :])

    hid = upw = dnw = rl = modraw = None


    def run_pipeline(gens):
        def collect(g, until_early):
            P.capture = []
            try:
                while True:
                    v = next(g)
                    if until_early and v == "EARLY_DONE":
                        break
            except StopIteration:
                pass
            ops_ = P.capture
            P.capture = None
            return ops_

        sim = {"eng": {e: 0.0 for e in ENGS}, "w": {}, "r": {}}

        def bufs_of(it):
            eng, meth, er, ew, kw = it
            reads, writes = [], []
            for k, v in kw.items():
                bb = v.buf if isinstance(v, View) else (v if isinstance(v, Buf) else None)
                if bb is not None:
                    (writes if k in WRITE_KEYS else reads).append(bb)
            for x_ in er:
                reads.append(x_.buf if isinstance(x_, View) else x_)
            for x_ in ew:
                writes.append(x_.buf if isinstance(x_, View) else x_)
            return reads, writes

        def fsize(v):
            ap = v.ap if isinstance(v, (View, Buf)) else v
            n = 1
            for d in ap.shape[1:]:
                n *= d
            return n

        def dur_of(it):
            eng, meth, er, ew, kw = it
            if meth == "dma_start":
                o_ = kw["out"]
                return 2.0 + fsize(o_) * 128 * 4 / 150e3
            if eng == "pe":
                n = fsize(kw["rhs"]) if "rhs" in kw else 128
                d_ = max(n, 64) / 1200.0
                lt = kw.get("lhsT", kw.get("in_"))
                if lt is not None and (lt.ap if isinstance(lt, (View, Buf)) else lt).dtype == F32:
                    d_ *= 4 if meth == "matmul" else 1
                return d_
            o_ = kw.get("out", kw.get("ap"))
            n = fsize(o_) if o_ is not None else 64
            if eng == "dve":
                return 0.07 + n / 960.0
            if eng == "act":
                return 0.22 + n / 1200.0
            return 0.12 + n / 400.0

        def est_start(it):
            reads, writes = bufs_of(it)
            t = sim["eng"][it[0]]
            for r_ in reads:
                t = max(t, sim["w"].get(id(r_), 0.0) + 0.15)
            for w_ in writes:
                t = max(t, sim["w"].get(id(w_), 0.0) + 0.15, sim["r"].get(id(w_), 0.0) + 0.15)
            return t, reads, writes

        def commit(it, t, reads, writes):
            d_ = dur_of(it)
            issue = 0.03 if it[1] != "dma_start" else 0.1
            sim["eng"][it[0]] = (t + d_) if it[1] != "dma_start" else (t + issue)
            for r_ in reads:
                sim["r"][id(r_)] = max(sim["r"].get(id(r_), 0.0), t + d_)
            for w_ in writes:
                sim["w"][id(w_)] = t + d_
            P.do(it[0], it[1], it[2], it[3], **it[4])

        def merge(A, B):
            lst = list(B) + list(A)
            n = len(lst)
            lastw, readers = {}, {}
            preds = [set() for _ in range(n)]
            rws = []
            for idx, it in enumerate(lst):
                reads, writes = bufs_of(it)
                rws.append((reads, writes))
                for r_ in reads:
                    if id(r_) in lastw:
                        preds[idx].add(lastw[id(r_)])
                for w_ in writes:
                    if id(w_) in lastw:
                        preds[idx].add(lastw[id(w_)])
                    preds[idx].update(readers.get(id(w_), ()))
                for r_ in reads:
                    readers.setdefault(id(r_), []).append(idx)
                for w_ in writes:
                    lastw[id(w_)] = idx
                    readers[id(w_)] = []
                preds[idx].discard(idx)
            succs = [[] for _ in range(n)]
            indeg = [0] * n
            for idx in range(n):
                indeg[idx] = len(preds[idx])
                for p_ in preds[idx]:
                    succs[p_].append(idx)
            ready = [i for i in range(n) if indeg[i] == 0]
            while ready:
                best = None
                bt = None
                for i in ready:
                    t = est_start(lst[i])
                    if bt is None or (t[0], i) < (bt[0], best):
                        best, bt = i, t
                ready.remove(best)
                commit(lst[best], *bt)
                for s_ in succs[best]:
                    indeg[s_] -= 1
                    if indeg[s_] == 0:
                        ready.append(s_)
                n -= 1
            assert n == 0, "scheduler dropped ops"

        WIN = 32
        gens = list(gens)
        for w0 in range(0, len(gens), WIN):
            allops = []
            for g in gens[w0:w0 + WIN]:
                allops.extend(collect(g, False))
            merge(allops, [])

    def run_scheduled(fn):
        def _g():
            fn()
            yield
        run_pipeline([_g()])

    EA = EB = EC = EY = LA = LB = LC = None
    xbc2 = xB2 = xdt2 = sm2 = yi2 = cacc_t = caccs = ctmp_t = cts = Rbs = decs = None
    xpads = zss = yys = ynbs2 = Snats = sss = yfms_s = zsfm_all = None

    def ssd_gen(l, tok0, L, seqcol, first, last, b, ctx):
        c = tok0 // 128
        sel = sel127 if L == 128 else sel7
        xbc, xB, xdt, sm, yi = xbc2[ctx], xB2[ctx], xdt2[ctx], sm2[ctx], yi2[ctx]
        if b is None:
            xpad_, zs_, yy_, ynb_, Snat_, ss_, yfm_ = xpad, zs, yy, ynb, Snat, ss, yfm
        else:
            xpad_, zs_, yy_, ynb_, Snat_, ss_, yfm_ = xpads[b], zss[ctx], yys[ctx], ynbs2[ctx], Snats[ctx], sss[ctx], yfms_s[b]
        xend_ = zs_
        STb_ = ynb_
        if b is None:
            H = hbuf[ctx]
            dma(H.v(), hd[c].v())
            o = 0
        else:
            H = hsamp
            o = tok0 - c * 128
        if b is not None:
            cp(xpad_[:, :, 0:3], sconv_io[:, :, 3 * b:3 * b + 3], eng="pool")
        elif first:
            memset(xpad_[:, :, 0:3], 0.0)
        for r in range(3 if b is None else 0):
            bank = (EA, EB)[r % 2]
            for oo in range(4):
                oc = r * 4 + oo
                for k in range(8):
                    mm(bank[:, oo * 128:oo * 128 + L], Wbuf[:, k, 1024 + oc * 128:1024 + (oc + 1) * 128], H[:, k, o:o + L],
                       start=(k == 0), stop=(k == 7))
            acp(xpad_[:, r * 4:(r + 1) * 4, 3:3 + L], bank.v().rr("p (k t) -> p k t", k=4)[:, :, 0:L])
            yield
        for k in range(8):
            mm(EC[0:L, 0:16], H[:, k, o:o + L], Wbuf[:, k, 2560:2576], start=(k == 0), stop=(k == 7))
        dt = sm[0:L, 0:16]
        dtA = sm[0:L, 16:32]
        cum = sm[0:L, 32:48]
        tt(dt, EC[0:L, 0:16], bc_dtb[0:L, :], ALU.add)
        act(dt, dt, AF.Exp)
        act(dt, dt, AF.Ln, bias=1.0)
        tt(dtA, dt, bc_A[0:L, :], ALU.mult)
        mm(EC[0:L, 16:32], ucum[0:L, 0:L], dtA)
        cp(cum, EC[0:L, 16:32])
        yield
        call = View(caccs[0], cacc_t.ap[:, :, 0:L])
        ctall = View(cts[0], ctmp_t.ap[:, :, 0:L])
        if L == 128:
            for oc in range(12):
                ts(caccs[oc][:, 0:L], xpad_[:, oc, 0:L], parB[:, oc:oc + 1], parB[:, 48 + oc:49 + oc], ALU.mult, ALU.add)
            for j in range(1, 4):
                for oc in range(12):
                    stt(caccs[oc][:, 0:L], xpad_[:, oc, j:j + L], parB[:, j * 12 + oc:j * 12 + oc + 1], caccs[oc][:, 0:L], ALU.mult, ALU.add)
        else:
            def cwv(j):
                return parB[:, j * 12:(j + 1) * 12].unsq(2).bc([128, 12, L])
            P.do("dve", "tensor_tensor", out=call, in0=xpad_[:, :, 0:L], in1=cwv(0), op=ALU.mult, extra_writes=caccs[1:])
            for j in range(1, 4):
                P.do("dve", "tensor_tensor", out=ctall, in0=xpad_[:, :, j:j + L], in1=cwv(j), op=ALU.mult, extra_writes=cts[1:])
                P.do("dve", "tensor_tensor", out=call, in0=call, in1=ctall, op=ALU.add, extra_reads=cts[1:], extra_writes=caccs[1:])
            P.do("dve", "tensor_tensor", out=call, in0=call, in1=parB[:, 48:60].unsq(2).bc([128, 12, L]), op=ALU.add, extra_writes=caccs[1:])
        P.do("act", "activation", out=ctall, in_=call, func=AF.Exp, bias=0.0, scale=-1.0, extra_reads=caccs[1:], extra_writes=cts[1:])
        P.do("act", "activation", out=ctall, in_=ctall, func=AF.Ln, bias=1.0, scale=1.0, extra_writes=cts[1:])
        P.do("act", "activation", out=ctall, in_=ctall, func=AF.Exp, bias=0.0, scale=-1.0, extra_writes=cts[1:])
        P.do("dve", "tensor_tensor", out=xbc[:, :, 0:L], in0=call, in1=ctall, op=ALU.mult, extra_reads=caccs[1:] + cts[1:])
        yield
        if b is not None:
            cp(sconv_io[:, :, 3 * b:3 * b + 3], xpad_[:, :, L:L + 3], eng="pool")
        elif last:
            for j in range(3):
                dma_slow(o_sconv[l, seqcol, j].rearrange("(k p) -> p k", p=128), xpad_[:, :, L + j])
        else:
            cp(xpad_[:, :, 0:3], xpad_[:, :, L:L + 3], eng="pool")
        EBb = EB.v().bitcast(BF16)
        EAb = EA.v().bitcast(BF16)
        for oc in range(8):
            tr(EBb[0:L, oc * 128:(oc + 1) * 128], xbc[:, oc, 0:L], identb.v())
        acp(xB[0:L, 0:1024], EBb[0:L, 0:1024])
        for oc in range(8, 10):
            tr(EAb[0:L, (oc - 8) * 128:(oc - 7) * 128], xbc[:, oc, 0:L], identb.v())
        acp(xB[0:L, 1024:1280], EAb[0:L, 0:256])
        for g in range(2):
            mm(EC[0:L, 128 + g * 128:128 + g * 128 + L], xbc[:, 8 + g, 0:L], xbc[:, 10 + g, 0:L])
        tt(xdt[0:L, :].rr("t (h p) -> t h p", p=64), xB[0:L, 0:1024].rr("t (h p) -> t h p", p=64),
           dt.unsq(2).bc([L, 16, 64]), ALU.mult)
        yield
        for q in range(4):
            g = q // 2
            bank = (EA, EB)[q % 2]
            Rb = Rbs[q % 2]
            dec = decs[q % 2]
            tt(Rb[0:L, :, 0:L], ucum[0:L, 0:L].unsq(1).bc([L, 4, L]), dtA[:, 4 * q:4 * q + 4].unsq(2).bc([L, 4, L]), ALU.mult)
            outv = bank[0:L, :].rr("p (h t) -> p h t", h=4)[:, :, 0:L]
            if L == 128:
                mm(outv, onesf[0:L, 0:L], Rb[0:L, :, 0:L], start=True, stop=False)
                mm(outv, identb[0:L, 0:L], negrep[0:L, :, 0:L], start=False, stop=True)
            else:
                for hh in range(4):
                    mm(outv[:, hh, :], onesf[0:L, 0:L], Rb[0:L, hh, 0:L], start=True, stop=False)
                    mm(outv[:, hh, :], identb[0:L, 0:L], negrep[0:L, hh, 0:L], start=False, stop=True)
            tt(dec[0:L, :, 0:L], outv, cum[:, 4 * q:4 * q + 4].unsq(2).bc([L, 4, L]), ALU.subtract)
            act(dec[0:L, :, 0:L], dec[0:L, :, 0:L], AF.Exp)
            tt(MT[0:L, 4 * q:4 * q + 4, 0:L], dec[0:L, :, 0:L],
               EC[0:L, 128 + g * 128:128 + g * 128 + L].unsq(1).bc([L, 4, L]), ALU.mult)
            for h in range(4 * q, 4 * q + 4):
                mm(EY[0:L, h * 64:(h + 1) * 64], MT[0:L, h, 0:L], xdt[0:L, h * 64:(h + 1) * 64])
            yield
        acp(yi[0:L, :], EY[0:L, :])
        yield "EARLY_DONE"
        if b is not None:
            dma(Snat_.v(), st_ssm[l, b].rearrange("(hp h2) p n -> (h2 p) hp n", h2=2))
        elif first:
            memset(Snat_.v(), 0.0)
        y2 = yi
        ecum = sm[0:L, 48:64]
        act(ecum, cum, AF.Exp)
        mm(LC[:, 0:16], sel[0:L, :], cum)
        cl = sm[:, 64:80]
        cp(cl, LC[:, 0:16])
        eend = sm[0:L, 80:96]
        tt(eend, cl[0:L, :], cum, ALU.subtract)
        act(eend, eend, AF.Exp)
        tt(xend_[0:L, :].rr("t (h p) -> t h p", p=64), xdt[0:L, :].rr("t (h p) -> t h p", p=64),
           eend.unsq(2).bc([L, 16, 64]), ALU.mult, eng="pool")
        dcy = sm[:, 96:112]
        act(dcy, cl, AF.Exp)
        dcyv = dcy.rr("p (hp h2) -> p hp h2", h2=2)
        for g_ in range(2):
            for hh in range(4):
                tr(LC[:, hh * 128:(hh + 1) * 128], Snat_[:, 4 * g_ + hh, :], identf.v())
            acp(STb_[:, g_ * 512:(g_ + 1) * 512], LC.v())
        yield
        for r in range(2):
            bank = (LA, LB)[r]
            for i in range(4):
                hp = r * 4 + i
                mm(bank[:, i * 128:(i + 1) * 128], xend_[0:L, hp * 128:(hp + 1) * 128], xB[0:L, 1024 + r * 128:1024 + (r + 1) * 128])
            for h2 in range(2):
                rs = slice(64 * h2, 64 * h2 + 64)
                tt(Snat_[rs, 4 * r:4 * r + 4, :], Snat_[rs, 4 * r:4 * r + 4, :], dcyv[rs, 4 * r:4 * r + 4, h2:h2 + 1].bc([64, 4, 128]), ALU.mult, eng="pool")
                tt(Snat_[rs, 4 * r:4 * r + 4, :], Snat_[rs, 4 * r:4 * r + 4, :], bank[rs, :].rr("p (hp n) -> p hp n", hp=4), ALU.add)
            yield
        if last:
            dma(o_ssm[l, seqcol].rearrange("(hp h2) p n -> (h2 p) hp n", h2=2), Snat_.v())
        for g_ in range(2):
            bank = (LA, LB)[g_]
            mm(bank[0:L, :], xbc[:, 10 + g_, 0:L], STb_[:, g_ * 512:(g_ + 1) * 512])
            tt(yy_[0:L, g_ * 512:(g_ + 1) * 512].rr("t (h p) -> t h p", p=64), bank[0:L, :].rr("t (h p) -> t h p", p=64),
               ecum[:, 8 * g_:8 * g_ + 8].unsq(2).bc([L, 8, 64]), ALU.mult)
        yield
        tt(yy_[0:L, :], yy_[0:L, :], yi[0:L, :], ALU.add)
        for j in range(2):
            bank = (LA, LB)[j]
            if b is None:
                for k in range(8):
                    mm(bank[0:L, :], H[:, k, o:o + L], Wbuf[:, k, j * 512:(j + 1) * 512], start=(k == 0), stop=(k == 7))
                zt = y2[0:L, j * 512:(j + 1) * 512]
                act(zt, bank[0:L, :], AF.Exp, scale=-1.0)
                act(zt, zt, AF.Ln, bias=1.0)
                act(zt, zt, AF.Exp, scale=-1.0)
                tt(zs_[0:L, j * 512:(j + 1) * 512], bank[0:L, :], zt, ALU.mult)
            else:
                bankb = bank.v().bitcast(BF16)
                for kk in range(4):
                    tr(bankb[0:L, kk * 128:(kk + 1) * 128], zsfm_all[:, 4 * j + kk, o:o + L], identb.v())
                acp(zs_[0:L, j * 512:(j + 1) * 512], bankb[0:L, 0:512])
        yield
        tt(y2[0:L, :].rr("t (h p) -> t h p", p=64), xB[0:L, 0:1024].rr("t (h p) -> t h p", p=64),
           bc_D[0:L, :].unsq(2).bc([L, 16, 64]), ALU.mult, eng="pool")
        tt(yy_[0:L, :], yy_[0:L, :], y2[0:L, :], ALU.add)
        tt(yy_[0:L, :], yy_[0:L, :], zs_[0:L, :], ALU.mult)
        act(y2[0:L, :], yy_[0:L, :], AF.Square, accum_out=ss_[0:L, 0:1])
        act(ss_[0:L, 1:2], ss_[0:L, 0:1], AF.Ln, bias=eps_t[0:L, 0:1], scale=1.0 / 1024.0)
        act(ss_[0:L, 2:3], ss_[0:L, 1:2], AF.Exp, scale=-0.5)
        ts(ynb_[0:L, :], yy_[0:L, :], ss_[0:L, 2:3], None, ALU.mult)
        yield
        LCb = LC.v().bitcast(BF16)
        for k in range(8):
            tr(LCb[:, k * 128:k * 128 + L], ynb_[0:L, k * 128:(k + 1) * 128], identb[0:L, 0:L])
        tt(yfm_[:, :, 0:L], LCb[:, 0:1024].rr("p (k t) -> p k t", k=8)[:, :, 0:L],
           parA[:, 64:72].unsq(2).bc([128, 8, L]), ALU.mult)
        yield
        for r in range(2 if b is None else 0):
            bank = (LA, LB)[r]
            for oo in range(4):
                oc = r * 4 + oo
                for k in range(8):
                    mm(bank[:, oo * 128:oo * 128 + L], Wo[:, k, oc * 128:(oc + 1) * 128], yfm_[:, k, 0:L],
                       start=(k == 0), stop=(k == 7))
            for oo in range(4):
                oc = r * 4 + oo
                stt(xc[c][:, oc, o:o + L], bank[:, oo * 128:oo * 128 + L], mod[:, 16 + oc, seqcol:seqcol + 1],
                    xc[c][:, oc, o:o + L], ALU.mult, ALU.add)
            yield

    qk2 = ktm2 = vtm2 = osig2 = g42 = numi2 = None
    R4s = d4s = w4s = CTbs = nfbs = nums = num2s = ynbs = vws = None
    qk_s = kvo_s = yfms_m = kvo_all = None

    def ml_gen(l, tok0, L, seqcol, first, last, b, ctx):
        c = tok0 // 128
        sel = sel127 if L == 128 else sel7
        uidx = ctx
        ctx4 = uidx % 4
        p2 = uidx % 2
        qk, ktm, vtm, osig, g4, numi = qk2[ctx4], ktm2[ctx4], vtm2[ctx4], osig2[ctx4], g42[ctx4], numi2[ctx4]
        R4, d4, w4 = R4s[p2], d4s[p2], w4s[p2]
        CTb, nfb, num, num2, ynb, yfm, vw = CTbs[p2], nfbs[p2], nums[p2], num2s[p2], ynbs[p2], yfms[p2], vws[p2]
        if b is not None:
            qk = qk_s[b]
            yfm = yfms_m[b]
        if b is None:
            H = hbuf[ctx4]
            dma(H.v(), hd[c].v())
            o = 0
        else:
            H = hsamp
            o = tok0 - c * 128
        for r in range(2 if b is None else 0):
            bank = (EA, EB)[r]
            for oo in range(4):
                oc = r * 4 + oo
                for k in range(8):
                    mm(bank[:, oo * 128:oo * 128 + L], Wbuf[:, k, oc * 128:(oc + 1) * 128], H[:, k, o:o + L],
                       start=(k == 0), stop=(k == 7))
            bv = bank.v().rr("p (k t) -> p k t", k=4)[:, :, 0:L]
            if r == 0:
                acp(qk[:, 0:4, 0:L], bv)
            else:
                act(qk[:, 4:8, 0:L], bv, AF.Copy, scale=KSC)
        yield
        if b is not None:
            EAb = EA.v().bitcast(BF16)
            EBb = EB.v().bitcast(BF16)
            kva = View(kvo_s[0], kvo_all.ap)
            for kk in range(12):
                dstb = EAb if kk < 8 else EBb
                k2 = kk if kk < 8 else kk - 8
                P.do("pe", "transpose", out=dstb[0:L, k2 * 128:(k2 + 1) * 128], in_=kva[:, kk, o:o + L],
                     identity=identb.v(), extra_reads=kvo_s[1:])
            acp(ktm[0:L, :], EAb[0:L, 0:512])
            acp(vtm[0:L, :], EAb[0:L, 512:1024])
            acp(osig[0:L, :], EBb[0:L, 0:512])
        for j in range(3 if b is None else 0):
            bank = (EA, EB)[j % 2]
            for k in range(8):
                mm(bank[0:L, :], H[:, k, o:o + L], Wbuf[:, k, 512 + j * 512:512 + (j + 1) * 512], start=(k == 0), stop=(k == 7))
            if j == 0:
                act(ktm[0:L, :], bank[0:L, :], AF.Copy, scale=KSC)
            elif j == 1:
                acp(vtm[0:L, :], bank[0:L, :])
            else:
                act(osig[0:L, :], bank[0:L, :], AF.Exp, scale=-1.0)
                act(osig[0:L, :], osig[0:L, :], AF.Ln, bias=1.0)
                act(osig[0:L, :], osig[0:L, :], AF.Exp, scale=-1.0)
        for k in range(8):
            mm(EC[0:L, 0:8], H[:, k, o:o + L], Wbuf[:, k, 2048:2056], start=(k == 0), stop=(k == 7))
        yield
        ig = g4[0:L, 0:4]
        lf = g4[0:L, 4:8]
        bcs = g4[0:L, 8:12]
        a_s = g4[0:L, 12:16]
        rp = g4[0:L, 16:20]
        negrp = g4[0:L, 20:24]
        tt(ig, EC[0:L, 0:4], bc_ib[0:L, :], ALU.add)
        tt(lf, EC[0:L, 4:8], bc_fb[0:L, :], ALU.add)
        act(lf, lf, AF.Exp, scale=-1.0)
        act(lf, lf, AF.Ln, bias=1.0)
        ts(lf, lf, -1.0, None, ALU.mult)
        mm(EC[0:L, 8:12], ucum[0:L, 0:L], lf)
        cp(bcs, EC[0:L, 8:12])
        tt(a_s, ig, bcs, ALU.subtract)
        tt(R4[0:L, :, 0:L], identf[0:L, 0:L].unsq(1).bc([L, 4, L]), a_s.unsq(2).bc([L, 4, L]), ALU.mult)
        outv = EB.v().rr("p (h t) -> p h t", h=4)[0:L, :, 0:L]
        if L == 128:
            mm(outv, onesf[0:L, 0:L], R4[0:L, :, 0:L], start=True, stop=False)
            mm(outv, identb[0:L, 0:L], negT[0:L, :, 0:L], start=False, stop=True)
        else:
            for hh in range(4):
                mm(outv[:, hh, :], onesf[0:L, 0:L], R4[0:L, hh, 0:L], start=True, stop=False)
                mm(outv[:, hh, :], identb[0:L, 0:L], negT[0:L, hh, 0:L], start=False, stop=True)
        P.do("dve", "tensor_reduce", out=rp, in_=outv, axis=AX.X, op=ALU.max)
        ts(negrp, rp, -1.0, None, ALU.mult)
        yield
        tt(R4[0:L, :, 0:L], identf[0:L, 0:L].unsq(1).bc([L, 4, L]), negrp.unsq(2).bc([L, 4, L]), ALU.mult)
        outv2 = EA.v().rr("p (h t) -> p h t", h=4)[0:L, :, 0:L]
        if L == 128:
            mm(outv2, onesf[0:L, 0:L], R4[0:L, :, 0:L], start=True, stop=False)
            mm(outv2, identb[0:L, 0:L], negrep[0:L, :, 0:L], start=False, stop=True)
        else:
            for hh in range(4):
                mm(outv2[:, hh, :], onesf[0:L, 0:L], R4[0:L, hh, 0:L], start=True, stop=False)
                mm(outv2[:, hh, :], identb[0:L, 0:L], negrep[0:L, hh, 0:L], start=False, stop=True)
        tt(d4[0:L, :, 0:L], outv2, a_s.unsq(2).bc([L, 4, L]), ALU.add)
        act(d4[0:L, :, 0:L], d4[0:L, :, 0:L], AF.Exp)
        outc = EY[:, 0:512].rr("p (h t) -> p h t", h=4)
        for h in range(4):
            mm(outc[0:L, h, 0:L], qk[:, 4 + h, 0:L], qk[:, h, 0:L])
        tt(w4[0:L, :, 0:L], d4[0:L, :, 0:L], outc[0:L, :, 0:L], ALU.mult)
        for h in range(4):
            mm(EY[0:L, 512 + h * 128:512 + (h + 1) * 128], w4[0:L, h, 0:L], vtm[0:L, h * 128:(h + 1) * 128])
        for h in range(4):
            mm(EC[0:L, 16 + h:17 + h], w4[0:L, h, 0:L], onesb[0:L, :])
        acp(numi[0:L, 0:512], EY[0:L, 512:1024])
        cp(numi[0:L, 512:516], EC[0:L, 16:20])
        yield "EARLY_DONE"
        if b is not None:
            dma(Cnat.v(), st_mc[l, b].rearrange("h v d -> v h d"))
            cp(nfm.v(), mn_io[:, 4 * b:4 * b + 4], eng="pool")
            dma(mbc.v(), st_mm[l, b:b + 1, :].to_broadcast([128, 4]))
        elif first:
            memset(Cnat.v(), 0.0)
            memset(nfm.v(), 0.0)
            memset(mbc.v(), 0.0)
        r_ = g4[0:L, 24:28]
        f_ = g4[0:L, 28:32]
        inter = g4[0:L, 32:36]
        mt = g4[0:L, 36:40]
        emt = g4[0:L, 40:44]
        den = g4[0:L, 44:48]
        den2 = g4[0:L, 48:52]
        ssq = g4[0:L, 52:56]
        tt(r_, rp, mbc[0:L, :], ALU.max)
        tt(f_, rp, r_, ALU.subtract)
        act(f_, f_, AF.Exp)
        tt(inter, mbc[0:L, :], r_, ALU.subtract)
        act(inter, inter, AF.Exp)
        tt(mt, bcs, r_, ALU.add)
        act(emt, mt, AF.Exp, scale=-1.0)
        for h in range(4):
            tr(LC[:, h * 128:(h + 1) * 128], Cnat[:, h, :], identf.v())
        acp(CTb.v(), LC.v().rr("p (h v) -> p h v", h=4))
        cp(nfb.v(), nfm.v())
        for h in range(4):
            mm(LA[0:L, h * 128:(h + 1) * 128], qk[:, h, 0:L], CTb[:, h, :])
        for h in range(4):
            mm(LB[0:L, h:h + 1], qk[:, h, 0:L], nfb[:, h:h + 1])
        yield
        tt(num[0:L, :].rr("t (h v) -> t h v", h=4), numi[0:L, 0:512].rr("t (h v) -> t h v", h=4),
           f_.unsq(2).bc([L, 4, 128]), ALU.mult)
        tt(num2[0:L, :].rr("t (h v) -> t h v", h=4), LA[0:L, :].rr("t (h v) -> t h v", h=4),
           inter.unsq(2).bc([L, 4, 128]), ALU.mult)
        tt(num[0:L, :], num[0:L, :], num2[0:L, :], ALU.add, eng="pool")
        tt(den, numi[0:L, 512:516], f_, ALU.mult)
        tt(den2, LB[0:L, 0:4], inter, ALU.mult)
        tt(den, den, den2, ALU.add)
        act(den, den, AF.Abs)
        tt(den, den, emt, ALU.max)
        P.do("dve", "reciprocal", out=den, in_=den)
        tt(num[0:L, :].rr("t (h v) -> t h v", h=4), num[0:L, :].rr("t (h v) -> t h v", h=4),
           den.unsq(2).bc([L, 4, 128]), ALU.mult)
        yield
        tt(num2[0:L, :], num[0:L, :], num[0:L, :], ALU.mult, eng="pool")
        P.do("dve", "tensor_reduce", out=ssq, in_=num2[0:L, :].rr("t (h v) -> t h v", h=4), axis=AX.X, op=ALU.add)
        act(ssq, ssq, AF.Ln, bias=eps_t[0:L, 0:1], scale=1.0 / 128.0)
        act(ssq, ssq, AF.Exp, scale=-0.5)
        tt(num[0:L, :].rr("t (h v) -> t h v", h=4), num[0:L, :].rr("t (h v) -> t h v", h=4),
           ssq.unsq(2).bc([L, 4, 128]), ALU.mult)
        tt(num[0:L, :], num[0:L, :], bc_mlw[0:L, :], ALU.mult, eng="pool")
        tt(ynb[0:L, 0:512], num[0:L, :], osig[0:L, :], ALU.mult)
        LCb = LC.v().bitcast(BF16)
        for k in range(4):
            tr(LCb[:, k * 128:k * 128 + L], ynb[0:L, k * 128:(k + 1) * 128], identb[0:L, 0:L])
        cp(yfm[:, 0:4, 0:L], LCb[:, 0:512].rr("p (k t) -> p k t", k=4)[:, :, 0:L])
        yield
        for r in range(2 if b is None else 0):
            bank = (LA, LB)[r]
            for oo in range(4):
                oc = r * 4 + oo
                for k in range(4):
                    mm(bank[:, oo * 128:oo * 128 + L], Wo[:, k, oc * 128:(oc + 1) * 128], yfm[:, k, 0:L],
                       start=(k == 0), stop=(k == 3))
            for oo in range(4):
                oc = r * 4 + oo
                stt(xc[c][:, oc, o:o + L], bank[:, oo * 128:oo * 128 + L], mod[:, 16 + oc, seqcol:seqcol + 1],
                    xc[c][:, oc, o:o + L], ALU.mult, ALU.add)
            yield
        bm = g4[0:L, 56:64]
        cp(bm[:, 0:4], bcs)
        cp(bm[:, 4:8], mt)
        mm(LC[:, 0:8], sel[0:L, :], bm)
        last8 = g4[:, 64:72]
        cp(last8, LC[:, 0:8])
        wend = g4[0:L, 72:76]
        tt(wend, last8[0:L, 0:4], last8[0:L, 4:8], ALU.subtract)
        tt(wend, wend, a_s, ALU.add)
        act(wend, wend, AF.Exp)
        dc = g4[:, 76:80]
        tt(dc, last8[:, 0:4], mbc.v(), ALU.add)
        tt(dc, dc, last8[:, 4:8], ALU.subtract)
        act(dc, dc, AF.Exp)
        cp(mbc.v(), last8[:, 4:8])
        tt(vw[0:L, :].rr("t (h v) -> t h v", h=4), vtm[0:L, :].rr("t (h v) -> t h v", h=4),
           wend.unsq(2).bc([L, 4, 128]), ALU.mult)
        wendb = g4[0:L, 80:84].bitcast(BF16)[:, 0:4]
        cp(wendb, wend)
        yield
        for h in range(4):
            mm(LA[:, h * 128:(h + 1) * 128], vw[0:L, h * 128:(h + 1) * 128], ktm[0:L, h * 128:(h + 1) * 128])
        for h in range(4):
            mm(LB[:, h:h + 1], ktm[0:L, h * 128:(h + 1) * 128], wendb[:, h:h + 1])
        tt(Cnat.v(), Cnat.v(), dc.unsq(2).bc([128, 4, 128]), ALU.mult, eng="pool")
        tt(Cnat.v(), Cnat.v(), LA.v().rr("p (h d) -> p h d", h=4), ALU.add)
        tt(nfm.v(), nfm.v(), dc, ALU.mult)
        tt(nfm.v(), nfm.v(), LB[:, 0:4], ALU.add)
        if last:
            dma(o_mc[l, seqcol].rearrange("h v d -> v h d"), Cnat.v())
            if b is not None:
                cp(mn_io[:, 4 * b:4 * b + 4], nfm.v(), eng="pool")
            else:
                dma_slow(o_mn[l, seqcol].rearrange("h d -> d h"), nfm.v())
            dma(o_mm[l, seqcol:seqcol + 1, :], mbc[0:1, :])
        yield

    la2 = lu2 = lg2 = lxs = lx_t = None
    lpads = lx_ts = lxss = lxbs = lris = lhs = yfms = EAs = EBs = LAs = LBs = None
    lpads_s = lgs_s = yfms_l = None

    def lru_gen(l, tok0, L, seqcol, first, last, b, ctx):
        c = tok0 // 128
        uidx = ctx
        ctx4 = uidx % 4
        p2 = uidx % 2
        la, lu, lg = la2[ctx4], lu2[ctx4], lg2[ctx4]
        lpad, lx_t, lxs, lxb, lri = lpads[p2], lx_ts[p2], lxss[p2], lxbs[p2], lris[p2]
        lpad_next = lpads[1 - p2]
        lh, yfm = lhs[p2], yfms[p2]
        if b is not None:
            lpad = lpads_s[b]
            lg = lgs_s[b]
            yfm = yfms_l[b]
        EA, EB = EAs[p2], EBs[p2]
        LA, LB = LAs[p2], LBs[p2]
        if b is None:
            H = hbuf[ctx4]
            dma(H.v(), hd[c].v())
            o = 0
        else:
            H = hsamp
            o = tok0 - c * 128
        if b is not None:
            cp(lpad[:, :, 0:3], lconv_io[:, :, 3 * b:3 * b + 3], eng="pool")
        elif first:
            memset(lpad[:, :, 0:3], 0.0)
        for r in range(2 if b is None else 0):
            bank = (EA, EB)[r]
            for oo in range(4):
                oc = r * 4 + oo
                for k in range(8):
                    mm(bank[:, oo * 128:oo * 128 + L], Wbuf[:, k, oc * 128:(oc + 1) * 128], H[:, k, o:o + L],
                       start=(k == 0), stop=(k == 7))
            bv = bank.v().rr("p (k t) -> p k t", k=4)[:, :, 0:L]
            if r == 0:
                acp(lpad[:, :, 3:3 + L], bv)
            else:
                act(lg[:, :, 0:L], bv, AF.Gelu)
        yield
        for k in range(4):
            ts(lxs[k][:, 0:L], lpad[:, k, 0:L], parA[:, 72 + k:73 + k], parA[:, 88 + k:89 + k], ALU.mult, ALU.add)
        for j in range(1, 4):
            for k in range(4):
                stt(lxs[k][:, 0:L], lpad[:, k, j:j + L], parA[:, 72 + j * 4 + k:73 + j * 4 + k], lxs[k][:, 0:L], ALU.mult, ALU.add)
        lxall = View(lxs[0], lx_t.ap[:, :, 0:L])
        P.do("act", "activation", out=lxb[:, :, 0:L], in_=lxall, func=AF.Copy, extra_reads=lxs[1:])
        if b is not None:
            cp(lconv_io[:, :, 3 * b:3 * b + 3], lpad[:, :, L:L + 3], eng="pool")
        elif last:
            for j in range(3):
                dma_slow(o_lconv[l, seqcol, j].rearrange("(k p) -> p k", p=128), lpad[:, :, L + j])
        else:
            cp(lpad_next[:, :, 0:3], lpad[:, :, L:L + 3], eng="pool")
        yield
        for k in range(4):
            mm(EA[:, k * 128:k * 128 + L], wg[:, k, :], lxb[:, k, 0:L])
        for k in range(4):
            mm(EB[:, k * 128:k * 128 + L], wg[:, 4 + k, :], lxb[:, k, 0:L])
        tt(lri[:, 0:4, 0:L], EA.v().rr("p (k t) -> p k t", k=4)[:, :, 0:L], parA[:, 92:96].unsq(2).bc([128, 4, L]), ALU.add)
        tt(lri[:, 4:8, 0:L], EB.v().rr("p (k t) -> p k t", k=4)[:, :, 0:L], parA[:, 96:100].unsq(2).bc([128, 4, L]), ALU.add)
        act(lri[:, :, 0:L], lri[:, :, 0:L], AF.Exp, scale=-1.0)
        act(lri[:, :, 0:L], lri[:, :, 0:L], AF.Ln, bias=1.0)
        act(lri[:, :, 0:L], lri[:, :, 0:L], AF.Exp, scale=-1.0)
        for k in range(4):
            act(la[:, k, 0:L], lri[:, k, 0:L], AF.Exp, scale=lru_c1[:, k:k + 1])
        yield
        tt(lu[:, :, 0:L], la[:, :, 0:L], la[:, :, 0:L], ALU.mult)
        ts(lu[:, :, 0:L], lu[:, :, 0:L], -1.0, 1.0, ALU.mult, ALU.add)
        ts(lu[:, :, 0:L], lu[:, :, 0:L], 1e-18, None, ALU.max)
        act(lu[:, :, 0:L], lu[:, :, 0:L], AF.Ln)
        act(lu[:, :, 0:L], lu[:, :, 0:L], AF.Exp, scale=0.5)
        if b is None and first:
            memset(lu[:, :, 0:1], 1.0)
        tt(lu[:, :, 0:L], lu[:, :, 0:L], lri[:, 4:8, 0:L], ALU.mult)
        P.do("dve", "tensor_tensor", out=lu[:, :, 0:L], in0=lu[:, :, 0:L], in1=lxall, op=ALU.mult, extra_reads=lxs[1:])
        yield "EARLY_DONE"
        if b is not None:
            cp(hstate.v(), lh_io[:, :, b], eng="pool")
        elif first:
            memset(hstate.v(), 0.0)
        for k in range(4):
            P.do("dve", "tensor_tensor_scan", out=lh[:, k, 0:L], data0=la[:, k, 0:L], data1=lu[:, k, 0:L],
                 initial=hstate[:, k:k + 1], op0=ALU.mult, op1=ALU.add)
        cp(hstate.v(), lh[:, :, L - 1])
        if b is not None:
            cp(lh_io[:, :, b], hstate.v(), eng="pool")
        elif last:
            dma_slow(o_lh[l, seqcol].rearrange("(k p) -> p k", p=128), hstate.v())
        tt(yfm[:, 0:4, 0:L], lh[:, :, 0:L], lg[:, :, 0:L], ALU.mult)
        yield
        for r in range(2 if b is None else 0):
            bank = (LA, LB)[r]
            for oo in range(4):
                oc = r * 4 + oo
                for k in range(4):
                    mm(bank[:, oo * 128:oo * 128 + L], Wo[:, k, oc * 128:(oc + 1) * 128], yfm[:, k, 0:L],
                       start=(k == 0), stop=(k == 3))
            for oo in range(4):
                oc = r * 4 + oo
                stt(xc[c][:, oc, o:o + L], bank[:, oo * 128:oo * 128 + L], mod[:, 16 + oc, seqcol:seqcol + 1],
                    xc[c][:, oc, o:o + L], ALU.mult, ALU.add)
            yield

    def _xload():
        for c in range(NCH):
            src = xp[c * 128:(c + 1) * 128, :] if c < 16 else xs[:, :]
            xin = xins[c % 3]
            pb = pbs[c % 3]
            dma(xin.v(), src)
            for k in range(8):
                tr(pb[:, k * 128:(k + 1) * 128], xin[:, k * 128:(k + 1) * 128], identf.v())
            if c % 2 == 0:
                cp(xc[c].v(), pb.v().rr("p (k t) -> p k t", k=8))
            else:
                acp(xc[c].v(), pb.v().rr("p (k t) -> p k t", k=8))
    run_scheduled(_xload)

    for l in range(DEPTH):
        P.barrier()
        adaslabs, sqs, rstds, ntmps = alloc_norm()
        load_fm(parA.v(), [ada_b[l].rearrange("(k p) -> k p", p=128), norm1_w[l].rearrange("(k p) -> k p", p=128),
                           norm2_w[l].rearrange("(k p) -> k p", p=128), ssd_norm_w[l].rearrange("(k p) -> k p", p=128),
                           lru_conv_w[l].rearrange("j (k p) -> (j k) p", p=128), lru_conv_b[l].rearrange("(k p) -> k p", p=128),
                           lru_ba[l].rearrange("(k p) -> k p", p=128), lru_bx[l].rearrange("(k p) -> k p", p=128),
                           lru_lambda[l].rearrange("(k p) -> k p", p=128)])
        load_fm(parB.v(), [ssd_conv_w[l].rearrange("j (k p) -> (j k) p", p=128), ssd_conv_b[l].rearrange("(k p) -> k p", p=128)])
        load_bc(bc_dtb.v(), ssd_dt_bias[l:l + 1, :], 16)
        load_bc(bc_A.v(), ssd_a_log[l:l + 1, :], 16)
        act(bc_A.v(), bc_A.v(), AF.Exp)
        ts(bc_A.v(), bc_A.v(), -1.0, None, ALU.mult)
        load_bc(bc_D.v(), ssd_d[l:l + 1, :], 16)
        load_bc(bc_ib.v(), ml_i_bias[l:l + 1, :], 4)
        load_bc(bc_fb.v(), ml_f_bias[l:l + 1, :], 4)
        act(lru_c1.v(), parA[:, 100:104], AF.Exp, scale=-1.0)
        act(lru_c1.v(), lru_c1.v(), AF.Ln, bias=1.0)
        ts(lru_c1.v(), lru_c1.v(), -8.0, None, ALU.mult)
        if l == 0:
            for sl in range(12):
                adaslab = adaslabs[sl % 2]
                dma(adaslab.v(), ada_w[l, :, sl * 512:(sl + 1) * 512].rearrange("(k p) n -> p k n", p=128), eng="pool")
                for oc in range(4):
                    j = sl * 4 + oc
                    for k in range(8):
                        mm(PB[:, j * 32:j * 32 + 17], adaslab[:, k, oc * 128:(oc + 1) * 128], cs_fm[:, k, :],
                           start=(k == 0), stop=(k == 7))
            tt(mod.v(), PB[:, 0:1536].rr("p (j s) -> p j s", j=48)[:, :, 0:17], parA[:, 0:48].unsq(2).bc([128, 48, 17]), ALU.add)
        else:
            tt(mod.v(), modraw.v(), parA[:, 0:48].unsq(2).bc([128, 48, 17]), ALU.add)
            P.atop = P.asize
        ts(s1.v(), mod[:, 8:16, :], 1.0, None, ALU.add)
        tt(s1.v(), s1.v(), parA[:, 48:56].unsq(2).bc([128, 8, 17]), ALU.mult)
        ts(s2.v(), mod[:, 32:40, :], 1.0, None, ALU.add)
        tt(s2.v(), s2.v(), parA[:, 56:64].unsq(2).bc([128, 8, 17]), ALU.mult)
        hst = [P.ov("hst%d" % i, [128, 8, 128], BF16) for i in range(2)]
        def _norm1():
            for c in range(NCH):
                rmsnorm_mod(c, s1, 0, hst[c % 2])
                dma(hd[c].v(), hst[c % 2].v())
        run_scheduled(_norm1)
        P.barrier()
        Wbuf = P.ov("Wbuf", [128, 8, 2576], BF16)
        Wo = P.ov("Wo", [128, 8, 1024], BF16)
        hbuf = [P.ov("hbuf%d" % i, [128, 8, 128], BF16) for i in range(2)]
        hsamp = P.ov("hsamp", [128, 8, 128], BF16)
        dma(hsamp.v(), hd[16].v())
        xpad = P.ov("xpad", [128, 12, 131], F32)
        cacc_t = P.ov("cacc", [128, 12, 128], F32)
        caccs = [Buf(cacc_t.ap[:, i, :], "cacc%d" % i) for i in range(12)]
        ctmp_t = P.ov("ctmp", [128, 12, 128], F32)
        cts = [Buf(ctmp_t.ap[:, 4 * i:4 * i + 4, :], "ct%d" % i) for i in range(3)]
        Rbs = [cts[2], P.ov("Rb1", [128, 4, 128], F32)]
        decs = [cts[0], cts[1]]
        MT = P.ov("MT", [128, 16, 128], BF16)
        xbc2 = [P.ov("xbc%d" % i, [128, 12, 128], BF16) for i in range(2)]
        xB2 = [P.ov("xB%d" % i, [128, 1280], BF16) for i in range(2)]
        xdt2 = [P.ov("xdt%d" % i, [128, 1024], BF16) for i in range(2)]
        sm2 = [P.ov("sm%d" % i, [128, 128], F32) for i in range(2)]
        yi2 = [P.ov("yi%d" % i, [128, 1024], F32) for i in range(2)]
        zs = P.ov("zs", [128, 1024], BF16)
        yy = P.ov("yy", [128, 1024], F32)
        ynb = P.ov("ynb", [128, 1024], BF16)
        yfm = P.ov("yfm", [128, 8, 128], BF16)
        Snat = P.ov("Snat", [128, 8, 128], F32)
        ss = P.ov("ss", [128, 8], F32)
        xend = zs
        STb = ynb
        EA, EB, EC = Buf(PA.ap[:, 0:512], "EA"), Buf(PA.ap[:, 512:1024], "EB"), Buf(PC.ap, "EC")
        EY = Buf(PB.ap[:, 0:1024], "EY")
        LA, LB, LC = Buf(PB.ap[:, 1024:1536], "LA"), Buf(PB.ap[:, 1536:2048], "LB"), Buf(PF.ap, "LC")
        dma(Wbuf[:, :, 0:2576], w_in[l, :, 0:2576].rearrange("(k p) n -> p k n", p=128), eng="pool")
        dma(Wo.v(), w_out[l, 0:1024, :].rearrange("(k p) n -> p k n", p=128), eng="pool")
        scv = st_sconv[l].rearrange("b j c -> (b j) c")
        dma(yy[0:48, :], scv[:, 0:1024])
        dma(yi2[0][0:48, 0:512], scv[:, 1024:1536])
        for k in range(12):
            src = yy[0:48, k * 128:(k + 1) * 128] if k < 8 else yi2[0][0:48, (k - 8) * 128:(k - 7) * 128]
            bank = EA if k < 8 else EB
            kk = k if k < 8 else k - 8
            tr(bank[:, kk * 48:(kk + 1) * 48], src, identf[0:48, 0:48])
        cp(sconv_io[:, 0:8, :], EA[:, 0:384].rr("p (k t) -> p k t", k=8))
        cp(sconv_io[:, 8:12, :], EB[:, 0:192].rr("p (k t) -> p k t", k=4))
        ulist = list(units())
        run_pipeline([ssd_gen(l, tok0, L, seqcol, first, last, b, i % 2)
                      for i, (tok0, L, seqcol, first, last, b) in enumerate(ulist[0:16])])
        P.barrier()
        Wbuf = P.ov("Wbuf", [128, 8, 2576], BF16)
        Wo = P.ov("Wo", [128, 8, 1024], BF16)
        hbuf = [P.ov("hbuf%d" % i, [128, 8, 128], BF16) for i in range(2)]
        hsamp = P.ov("hsamp", [128, 8, 128], BF16)
        xpad_all = P.ov("xpad_all", [128, 12, 176], F32)
        xpa4 = xpad_all.ap.rearrange("p k (b j) -> p k b j", b=16)
        xpads = [Buf(xpa4[:, :, b_, :], "xpad_s%d" % b_) for b_ in range(16)]
        cacc_t = P.ov("cacc", [128, 12, 8], F32)
        caccs = [Buf(cacc_t.ap[:, i, :], "cacc%d" % i) for i in range(12)]
        ctmp_t = P.ov("ctmp", [128, 12, 8], F32)
        cts = [Buf(ctmp_t.ap[:, 4 * i:4 * i + 4, :], "ct%d" % i) for i in range(3)]
        Rbs = [cts[2], P.ov("Rb1", [128, 4, 8], F32)]
        decs = [cts[0], cts[1]]
        MT = P.ov("MT", [128, 16, 8], BF16)
        xbc2 = [P.ov("xbc%d" % i, [128, 12, 8], BF16) for i in range(2)]
        xB2 = [P.ov("xB%d" % i, [128, 1280], BF16) for i in range(2)]
        xdt2 = [P.ov("xdt%d" % i, [128, 1024], BF16) for i in range(2)]
        sm2 = [P.ov("sm%d" % i, [128, 128], F32) for i in range(2)]
        yi2 = [P.ov("yi%d" % i, [128, 1024], F32) for i in range(2)]
        zss = [P.ov("zs%d" % i, [128, 1024], BF16) for i in range(2)]
        yys = [P.ov("yy%d" % i, [128, 1024], F32) for i in range(2)]
        ynbs2 = [P.ov("ynb%d" % i, [128, 1024], BF16) for i in range(2)]
        Snats = [P.ov("Snat%d" % i, [128, 8, 128], F32) for i in range(2)]
        sss = [P.ov("ss%d" % i, [128, 8], F32) for i in range(2)]
        yfm_all = P.ov("yfm_all", [128, 8, 128], BF16)
        yfms_s = [Buf(yfm_all.ap[:, :, 8 * b_:8 * b_ + 8], "yfm_s%d" % b_) for b_ in range(16)]
        zsfm_all = P.ov("zsfm_all", [128, 8, 128], BF16)
        ztmp = P.ov("ztmp", [128, 512], F32)
        yy = yys[0]
        EA, EB, EC = Buf(PA.ap[:, 0:512], "EA"), Buf(PA.ap[:, 512:1024], "EB"), Buf(PC.ap, "EC")
        EY = Buf(PB.ap[:, 0:1024], "EY")
        LA, LB, LC = Buf(PB.ap[:, 1024:1536], "LA"), Buf(PB.ap[:, 1536:2048], "LB"), Buf(PF.ap, "LC")
        for r in range(3):
            bank = (EA, EB)[r % 2]
            for oo in range(4):
                oc = r * 4 + oo
                for k in range(8):
                    mm(bank[:, oo * 128:(oo + 1) * 128], Wbuf[:, k, 1024 + oc * 128:1024 + (oc + 1) * 128], hsamp[:, k, :],
                       start=(k == 0), stop=(k == 7))
            for oo in range(4):
                oc = r * 4 + oo
                P.do("act", "activation", out=View(xpads[0], xpa4[:, oc, :, 3:11]),
                     in_=bank[:, oo * 128:(oo + 1) * 128].rr("p (b t) -> p b t", b=16), func=AF.Copy,
                     extra_writes=xpads[1:])
        for r in range(2):
            bank = (LA, LB)[r]
            for oo in range(4):
                oc = r * 4 + oo
                for k in range(8):
                    mm(bank[:, oo * 128:(oo + 1) * 128], Wbuf[:, k, oc * 128:(oc + 1) * 128], hsamp[:, k, :],
                       start=(k == 0), stop=(k == 7))
            act(ztmp.v(), bank.v(), AF.Exp, scale=-1.0)
            act(ztmp.v(), ztmp.v(), AF.Ln, bias=1.0)
            act(ztmp.v(), ztmp.v(), AF.Exp, scale=-1.0)
            tt(zsfm_all[:, 4 * r:4 * r + 4, :], bank.v().rr("p (k t) -> p k t", k=4), ztmp.v().rr("p (k t) -> p k t", k=4), ALU.mult)
        run_pipeline([ssd_gen(l, tok0, L, seqcol, first, last, b, i % 2)
                      for i, (tok0, L, seqcol, first, last, b) in enumerate(ulist[16:])])
        yfa = View(yfms_s[0], yfm_all.ap)
        for r in range(2):
            bank = (LA, LB)[r]
            for oo in range(4):
                oc = r * 4 + oo
                for k in range(8):
                    P.do("pe", "matmul", out=bank[:, oo * 128:(oo + 1) * 128], lhsT=Wo[:, k, oc * 128:(oc + 1) * 128],
                         rhs=yfa[:, k, :], start=(k == 0), stop=(k == 7), extra_reads=yfms_s[1:])
            for oo in range(4):
                oc = r * 4 + oo
                tt(ztmp[:, 0:128].rr("p (b t) -> p b t", t=8), bank[:, oo * 128:(oo + 1) * 128].rr("p (b t) -> p b t", t=8),
                   mod[:, 16 + oc, 1:17].unsq(2).bc([128, 16, 8]), ALU.mult)
                tt(xc[16][:, oc, :], xc[16][:, oc, :], ztmp[:, 0:128], ALU.add)
        for k in range(12):
            bank = (EA, EB, LA)[k // 4]
            tr(bank[0:48, (k % 4) * 128:(k % 4 + 1) * 128], sconv_io[:, k, :], identf.v())
        cp(yy[0:48, 0:512], EA[0:48, :])
        cp(yy[0:48, 512:1024], EB[0:48, :])
        cp(yi2[0][0:48, 0:512], LA[0:48, :])
        sco = o_sconv[l, 1:1 + NS].rearrange("b j c -> (b j) c")
        dma(sco[:, 0:1024], yy[0:48, :])
        dma(sco[:, 1024:1536], yi2[0][0:48, 0:512])
        P.barrier()
        Wbuf = P.ov("Wbuf", [128, 8, 2056], BF16)
        Wo = P.ov("Wo", [128, 4, 1024], BF16)
        hsamp = P.ov("hsamp", [128, 8, 128], BF16)
        dma(hsamp.v(), hd[16].v())
        hbuf = [P.ov("hbuf%d" % i, [128, 8, 128], BF16) for i in range(4)]
        R4s = [P.ov("R4%d" % i, [128, 4, 128], F32) for i in range(2)]
        d4s = [P.ov("d4%d" % i, [128, 4, 128], F32) for i in range(2)]
        w4s = [P.ov("w4%d" % i, [128, 4, 128], BF16) for i in range(2)]
        qk2 = [P.ov("qk%d" % i, [128, 8, 128], BF16) for i in range(4)]
        ktm2 = [P.ov("ktm%d" % i, [128, 512], BF16) for i in range(4)]
        vtm2 = [P.ov("vtm%d" % i, [128, 512], BF16) for i in range(4)]
        osig2 = [P.ov("osig%d" % i, [128, 512], F32) for i in range(4)]
        g42 = [P.ov("g4%d" % i, [128, 128], F32) for i in range(4)]
        numi2 = [P.ov("numi%d" % i, [128, 520], F32) for i in range(4)]
        CTbs = [P.ov("CTb%d" % i, [128, 4, 128], BF16) for i in range(2)]
        nfbs = [P.ov("nfb%d" % i, [128, 4], BF16) for i in range(2)]
        nums = [P.ov("num%d" % i, [128, 512], F32) for i in range(2)]
        num2s = [P.ov("num2%d" % i, [128, 512], F32) for i in range(2)]
        ynbs = [P.ov("ynb%d" % i, [128, 512], BF16) for i in range(2)]
        yfms = [P.ov("yfm%d" % i, [128, 4, 128], BF16) for i in range(2)]
        vws = [P.ov("vw%d" % i, [128, 512], BF16) for i in range(2)]
        num = nums[0]
        Cnat = P.ov("Cnat", [128, 4, 128], F32)
        nfm = P.ov("nfm", [128, 4], F32)
        mbc = P.ov("mbc", [128, 4], F32)
        bc_mlw = P.ov("bc_mlw", [128, 512], F32)
        load_bc(bc_mlw.v(), ml_norm_w[l:l + 1, :], 512)
        EA, EB, EC = Buf(PA.ap[:, 0:512], "EA"), Buf(PA.ap[:, 512:1024], "EB"), Buf(PC.ap, "EC")
        EY = Buf(PB.ap[:, 0:1024], "EY")
        LA, LB, LC = Buf(PB.ap[:, 1024:1536], "LA"), Buf(PB.ap[:, 1536:2048], "LB"), Buf(PF.ap, "LC")
        dma(Wbuf[:, :, 0:2056], w_in[l, :, 2576:4632].rearrange("(k p) n -> p k n", p=128), eng="pool")
        dma(Wo[:, 0:4, :], w_out[l, 1024:1536, :].rearrange("(k p) n -> p k n", p=128), eng="pool")
        dma(num[0:64, 0:128], st_mn[l].rearrange("b h d -> (b h) d"))
        tr(EA[:, 0:64], num[0:64, 0:128], identf[0:64, 0:64])
        cp(mn_io.v(), EA[:, 0:64])
        qk_all = P.ov("qk_all", [128, 8, 128], BF16)
        qk_s = [Buf(qk_all.ap[:, :, 8 * b_:8 * b_ + 8], "qk_s%d" % b_) for b_ in range(16)]
        kvo_all = P.ov("kvo_all", [128, 12, 128], BF16)
        kvo_s = [Buf(kvo_all.ap[:, :, 8 * b_:8 * b_ + 8], "kvo_s%d" % b_) for b_ in range(16)]
        yfm_all_m = P.ov("yfm_all_m", [128, 4, 128], BF16)
        yfms_m = [Buf(yfm_all_m.ap[:, :, 8 * b_:8 * b_ + 8], "yfm_m%d" % b_) for b_ in range(16)]
        for r in range(5):
            bank = (EA, EB)[r % 2]
            for oo in range(4):
                oc = (r if r < 2 else r - 1) * 4 + oo
                for k in range(8):
                    mm(bank[:, oo * 128:(oo + 1) * 128], Wbuf[:, k, oc * 128:(oc + 1) * 128], hsamp[:, k, :],
                       start=(k == 0), stop=(k == 7))
            bv = bank.v().rr("p (k t) -> p k t", k=4)
            if r == 0:
                P.do("act", "activation", out=View(qk_s[0], qk_all.ap[:, 0:4, :]), in_=bv, func=AF.Copy, extra_writes=qk_s[1:])
            elif r == 1:
                P.do("act", "activation", out=View(qk_s[0], qk_all.ap[:, 4:8, :]), in_=bv, func=AF.Copy, scale=KSC, extra_writes=qk_s[1:])
            elif r == 2:
                P.do("act", "activation", out=View(kvo_s[0], kvo_all.ap[:, 0:4, :]), in_=bv, func=AF.Copy, scale=KSC, extra_writes=kvo_s[1:])
            elif r == 3:
                P.do("act", "activation", out=View(kvo_s[0], kvo_all.ap[:, 4:8, :]), in_=bv, func=AF.Copy, extra_writes=kvo_s[1:])
            else:
                nt = nums[0].v()
                act(nt, bank.v(), AF.Exp, scale=-1.0)
                act(nt, nt, AF.Ln, bias=1.0)
                P.do("act", "activation", out=View(kvo_s[0], kvo_all.ap[:, 8:12, :]), in_=nt.rr("p (k t) -> p k t", k=4),
                     func=AF.Exp, scale=-1.0, extra_writes=kvo_s[1:])
        run_pipeline([ml_gen(l, tok0, L, seqcol, first, last, b, i)
                      for i, (tok0, L, seqcol, first, last, b) in enumerate(units())])
        yfam = View(yfms_m[0], yfm_all_m.ap)
        for r in range(2):
            bank = (LA, LB)[r]
            for oo in range(4):
                oc = r * 4 + oo
                for k in range(4):
                    P.do("pe", "matmul", out=bank[:, oo * 128:(oo + 1) * 128], lhsT=Wo[:, k, oc * 128:(oc + 1) * 128],
                         rhs=yfam[:, k, :], start=(k == 0), stop=(k == 3), extra_reads=yfms_m[1:])
            for oo in range(4):
                oc = r * 4 + oo
                tt(nums[0][:, 0:128].rr("p (b t) -> p b t", t=8), bank[:, oo * 128:(oo + 1) * 128].rr("p (b t) -> p b t", t=8),
                   mod[:, 16 + oc, 1:17].unsq(2).bc([128, 16, 8]), ALU.mult)
                tt(xc[16][:, oc, :], xc[16][:, oc, :], nums[0][:, 0:128], ALU.add)
        tr(EA[0:64, 0:128], mn_io.v(), identf.v())
        cp(num[0:64, 0:128], EA[0:64, 0:128])
        dma(o_mn[l, 1:1 + NS].rearrange("b h d -> (b h) d"), num[0:64, 0:128])
        P.barrier()
        Wbuf = P.ov("Wbuf", [128, 8, 1024], BF16)
        Wo = P.ov("Wo", [128, 4, 1024], BF16)
        hbuf = [P.ov("hbuf%d" % i, [128, 8, 128], BF16) for i in range(2)]
        hsamp = P.ov("hsamp", [128, 8, 128], BF16)
        dma(hsamp.v(), hd[16].v())
        hbuf = [P.ov("hbuf%d" % i, [128, 8, 128], BF16) for i in range(4)]
        lpads = [P.ov("lpad%d" % i, [128, 4, 131], F32) for i in range(2)]
        lx_ts = [P.ov("lx%d" % i, [128, 4, 128], F32) for i in range(2)]
        lxss = [[Buf(t_.ap[:, i, :], "lxs%d" % i) for i in range(4)] for t_ in lx_ts]
        lxbs = [P.ov("lxb%d" % i, [128, 4, 128], BF16) for i in range(2)]
        lris = [P.ov("lri%d" % i, [128, 8, 128], F32) for i in range(2)]
        la2 = [P.ov("la%d" % i, [128, 4, 128], F32) for i in range(4)]
        lu2 = [P.ov("lu%d" % i, [128, 4, 128], F32) for i in range(4)]
        lg2 = [P.ov("lg%d" % i, [128, 4, 128], F32) for i in range(4)]
        lhs = [P.ov("lh%d" % i, [128, 4, 128], F32) for i in range(2)]
        yfms = [P.ov("yfm%d" % i, [128, 8, 128], BF16) for i in range(2)]
        lh = lhs[0]
        lri = lris[0]
        hstate = P.ov("hstate", [128, 4], F32)
        wg = P.ov("wg", [128, 8, 128], BF16)
        EAs = [Buf(PA.ap[:, 0:512], "EA0"), Buf(PB.ap[:, 0:512], "EA1")]
        EBs = [Buf(PA.ap[:, 512:1024], "EB0"), Buf(PB.ap[:, 512:1024], "EB1")]
        LAs = [Buf(PB.ap[:, 1024:1536], "LA0"), Buf(PC.ap, "LA1")]
        LBs = [Buf(PB.ap[:, 1536:2048], "LB0"), Buf(PF.ap, "LB1")]
        EA, EB = EAs[0], EBs[0]
        dma(Wbuf[:, :, 0:1024], w_in[l, :, 4632:5656].rearrange("(k p) n -> p k n", p=128), eng="pool")
        dma(Wo[:, 0:4, :], w_out[l, 1536:2048, :].rearrange("(k p) n -> p k n", p=128), eng="pool")
        dma(wg[:, 0:4, :], lru_wa[l].rearrange("k c d -> c k d"), eng="pool")
        dma(wg[:, 4:8, :], lru_wx[l].rearrange("k c d -> c k d"), eng="pool")
        lhv = lh.v().rr("p k t -> p (k t)")
        lrv = lri.v().rr("p k t -> p (k t)")
        dma(lhv[0:48, :], st_lconv[l].rearrange("b j c -> (b j) c"))
        dma(lrv[0:16, 0:512], st_lh[l])
        for k in range(4):
            tr(EA[:, k * 48:(k + 1) * 48], lhv[0:48, k * 128:(k + 1) * 128], identf[0:48, 0:48])
            tr(EB[:, k * 16:(k + 1) * 16], lrv[0:16, k * 128:(k + 1) * 128], identf[0:16, 0:16])
        cp(lconv_io.v(), EA[:, 0:192].rr("p (k t) -> p k t", k=4))
        cp(lh_io.v(), EB[:, 0:64].rr("p (k t) -> p k t", k=4))
        lpad_all = P.ov("lpad_all", [128, 4, 176], F32)
        lpa4 = lpad_all.ap.rearrange("p k (b j) -> p k b j", b=16)
        lpads_s = [Buf(lpa4[:, :, b_, :], "lpad_s%d" % b_) for b_ in range(16)]
        lg_all = P.ov("lg_all", [128, 4, 128], F32)
        lgs_s = [Buf(lg_all.ap[:, :, 8 * b_:8 * b_ + 8], "lg_s%d" % b_) for b_ in range(16)]
        yfm_all_l = P.ov("yfm_all_l", [128, 4, 128], BF16)
        yfms_l = [Buf(yfm_all_l.ap[:, :, 8 * b_:8 * b_ + 8], "yfm_l%d" % b_) for b_ in range(16)]
        ltmp = P.ov("ltmp", [128, 128], F32)
        for r in range(2):
            bank = (EAs[1], EBs[1])[r]
            for oo in range(4):
                oc = r * 4 + oo
                for k in range(8):
                    mm(bank[:, oo * 128:(oo + 1) * 128], Wbuf[:, k, oc * 128:(oc + 1) * 128], hsamp[:, k, :],
                       start=(k == 0), stop=(k == 7))
            if r == 0:
                for oo in range(4):
                    P.do("act", "activation", out=View(lpads_s[0], lpa4[:, oo, :, 3:11]),
                         in_=bank[:, oo * 128:(oo + 1) * 128].rr("p (b t) -> p b t", b=16), func=AF.Copy,
                         extra_writes=lpads_s[1:])
            else:
                P.do("act", "activation", out=View(lgs_s[0], lg_all.ap), in_=bank.v().rr("p (k t) -> p k t", k=4),
                     func=AF.Gelu, extra_writes=lgs_s[1:])
        run_pipeline([lru_gen(l, tok0, L, seqcol, first, last, b, i)
                      for i, (tok0, L, seqcol, first, last, b) in enumerate(units())])
        yfal = View(yfms_l[0], yfm_all_l.ap)
        for r in range(2):
            bank = (LAs[0], LBs[0])[r]
            for oo in range(4):
                oc = r * 4 + oo
                for k in range(4):
                    P.do("pe", "matmul", out=bank[:, oo * 128:(oo + 1) * 128], lhsT=Wo[:, k, oc * 128:(oc + 1) * 128],
                         rhs=yfal[:, k, :], start=(k == 0), stop=(k == 3), extra_reads=yfms_l[1:])
            for oo in range(4):
                oc = r * 4 + oo
                tt(ltmp.v().rr("p (b t) -> p b t", t=8), bank[:, oo * 128:(oo + 1) * 128].rr("p (b t) -> p b t", t=8),
                   mod[:, 16 + oc, 1:17].unsq(2).bc([128, 16, 8]), ALU.mult)
                tt(xc[16][:, oc, :], xc[16][:, oc, :], ltmp.v(), ALU.add)
        for k in range(4):
            tr(EA[0:48, k * 128:(k + 1) * 128], lconv_io[:, k, :], identf.v())
            tr(EB[0:16, k * 128:(k + 1) * 128], lh_io[:, k, :], identf.v())
        cp(lhv[0:48, :], EA[0:48, :])
        cp(lrv[0:16, 0:512], EB[0:16, :])
        dma(o_lconv[l, 1:1 + NS].rearrange("b j c -> (b j) c"), lhv[0:48, :])
        dma(o_lh[l, 1:1 + NS], lrv[0:16, 0:512])
        P.barrier()
        adaslabs, sqs, rstds, ntmps = alloc_norm()
        hids = [P.ov("hid%d" % i, [128, 4, 512], BF16) for i in range(2)]
        upws = [P.ov("upw%d" % i, [128, 8, 512], BF16) for i in range(2)]
        dnws = [P.ov("dnw%d" % i, [128, 4, 1024], BF16) for i in range(2)]
        rls = [P.ov("rl%d" % i, [128, 512], F32) for i in range(3)]
        upbanks = [Buf(PA.ap[:, 0:512], "mb0"), Buf(PA.ap[:, 512:1024], "mb1"), Buf(PC.ap, "mb2")]
        adabank = Buf(PF.ap, "adab")
        if l + 1 < DEPTH:
            modraw = P.ov_top("modraw", [128, 48, 17], F32)
        dnbanks = [Buf(PB.ap[:, i * 512:(i + 1) * 512], "db%d" % i) for i in range(4)]
        hn = P.ov("hn_mlp", [128, 8, NTOK], BF16)
        hc = [Buf(hn.ap[:, :, c * 128:(c + 1) * 128], "h%d" % c) for c in range(NCH)]
        def _norm2():
            for c in range(NCH):
                rmsnorm_mod(c, s2, 24, hc[c])
        run_scheduled(_norm2)
        P.barrier()
        tiles = [(i * 512, 512) for i in range(4)] + [(2048, 128)]
        def _mlp():
            nonlocal_cnt = [0, 0]
            cnt = 0
            rcnt = 0
            for e in range(8):
                upw = upws[e % 2]
                dnw = dnws[e % 2]
                dma(upw.v(), mlp_up[l, :, e * 512:(e + 1) * 512].rearrange("(k p) n -> p k n", p=128), eng="pool")
                dma(dnw.v(), mlp_down[l, e * 512:(e + 1) * 512, :].rearrange("(k p) n -> p k n", p=128), eng="pool")
                for (t0, N) in tiles:
                    hid = hids[cnt % 2]
                    hbufs = [hc[(t0 + i * 128) // 128] for i in range(N // 128)]
                    for fo in range(4):
                        ub = upbanks[(cnt * 4 + fo) % 3]
                        rl = rls[rcnt % 3]
                        rcnt += 1
                        for k in range(8):
                            P.do("pe", "matmul", out=ub[:, 0:N], lhsT=upw[:, k, fo * 128:(fo + 1) * 128],
                                 rhs=View(hbufs[0], hn.ap[:, k, t0:t0 + N]), start=(k == 0), stop=(k == 7),
                                 extra_reads=hbufs[1:])
                        act(rl[:, 0:N], ub[:, 0:N], AF.Relu)
                        tt(hid[:, fo, 0:N], rl[:, 0:N], rl[:, 0:N], ALU.mult, eng="pool")
                    for oc in range(8):
                        db = dnbanks[oc % 4]
                        for k in range(4):
                            mm(db[:, 0:N], dnw[:, k, oc * 128:(oc + 1) * 128], hid[:, k, 0:N],
                               start=(k == 0), stop=(k == 3))
                        pv = db[:, 0:N]
                        xbufs = [xc[(t0 + i * 128) // 128] for i in range(N // 128)]
                        xv = View(xbufs[0], x.ap[:, oc, t0:t0 + N])
                        if t0 < TP:
                            P.do("dve", "scalar_tensor_tensor", out=xv, in0=pv, scalar=mod[:, 40 + oc, 0:1], in1=xv,
                                 op0=ALU.mult, op1=ALU.add, extra_reads=xbufs[1:], extra_writes=xbufs[1:])
                        else:
                            rl = rls[rcnt % 3]
                            rcnt += 1
                            tt(rl[:, 0:N].rr("p (b t) -> p b t", t=8), pv.rr("p (b t) -> p b t", t=8),
                               mod[:, 40 + oc, 1:17].unsq(2).bc([128, 16, 8]), ALU.mult)
                            tt(xv, xv, rl[:, 0:N], ALU.add)
                    cnt += 1
                    if l + 1 < DEPTH and cnt % 3 == 1 and cnt // 3 < 12:
                        sl = cnt // 3
                        adaslab = adaslabs[sl % 2]
                        dma(adaslab.v(), ada_w[l + 1, :, sl * 512:(sl + 1) * 512].rearrange("(k p) n -> p k n", p=128), eng="pool")
                        for oc in range(4):
                            j = sl * 4 + oc
                            jj = j % 16
                            for k in range(8):
                                mm(adabank[:, jj * 32:jj * 32 + 17], adaslab[:, k, oc * 128:(oc + 1) * 128], cs_fm[:, k, :],
                                   start=(k == 0), stop=(k == 7))
                        if sl % 4 == 3:
                            g0 = (sl // 4) * 16
                            cp(modraw[:, g0:g0 + 16, :], adabank.v().rr("p (j s) -> p j s", j=16)[:, :, 0:17])
        run_scheduled(_mlp)

    P.barrier()
    adaslabs, sqs, rstds, ntmps = alloc_norm()
    youts = [P.ov("yout%d" % i, [128, 1024], F32) for i in range(2)]
    pbs = [PA, Buf(PB.ap[:, 0:1024], "fpb1")]
    def _final():
        for c in range(NCH):
            sq, rstd, ntmp, nb = sqs[c % 2], rstds[c % 2], ntmps[c % 2], nbanks[c % 2]
            yout, pb = youts[c % 2], pbs[c % 2]
            act(sq.v(), xc[c].v(), AF.Square)
            for k in range(8):
                mm(nb[:, 0:128], ones_mean.v(), sq[:, k, :], start=(k == 0), stop=(k == 7))
            act(rstd.v(), nb[:, 0:128], AF.Ln, bias=eps_t[:, 0:1], scale=1.0)
            act(rstd.v(), rstd.v(), AF.Exp, scale=-0.5)
            tt(ntmp.v(), xc[c].v(), rstd.v().unsq(1).bc([128, 8, 128]), ALU.mult)
            tt(ntmp.v(), ntmp.v(), fnw.v().unsq(2).bc([128, 8, 128]), ALU.mult, eng="pool")
            for k in range(8):
                tr(pb[:, k * 128:(k + 1) * 128], ntmp[:, k, :], identf.v())
            if c % 2 == 0:
                acp(yout.v(), pb.v())
            else:
                cp(yout.v(), pb.v())
            dst = y_p[c * 128:(c + 1) * 128, :] if c < 16 else y_s[:, :]
            dma(dst, yout.v())


    run_scheduled(_final)

    P.emit()
    return nc


_NC_CACHE = {}


def kernel(**inputs):
    f = lambda a: np.ascontiguousarray(np.asarray(a, dtype=np.float32))
    x_prompt = f(inputs["x_prompt"]); x_sample = f(inputs["x_sample"])
    c_prompt = f(inputs["c_prompt"]); c_sample = f(inputs["c_sample"])
    state_ssm = f(inputs["state_ssm"]); state_ssd_conv = f(inputs["state_ssd_conv"])
    state_mlstm_c = f(inputs["state_mlstm_c"]); state_mlstm_n = f(inputs["state_mlstm_n"])
    state_mlstm_m = f(inputs["state_mlstm_m"]); state_lru_h = f(inputs["state_lru_h"])
    state_lru_conv = f(inputs["state_lru_conv"])
    shared = {}
    for name in ("ada_w", "ada_b", "norm1_w", "norm2_w", "w_in", "ssd_conv_w", "ssd_conv_b", "ssd_dt_bias",
                 "ssd_a_log", "ssd_d", "ssd_norm_w", "ml_i_bias", "ml_f_bias", "ml_norm_w", "lru_conv_w",
                 "lru_conv_b", "lru_wa", "lru_ba", "lru_wx", "lru_bx", "lru_lambda", "w_out", "mlp_up", "mlp_down"):
        shared[name] = f(inputs[name])
    shared["final_norm_w"] = f(inputs["final_norm_w"]).reshape(1, D)
    if "nc" not in _NC_CACHE:
        _NC_CACHE["nc"] = build_program()
    nc = _NC_CACHE["nc"]
    in_maps = []
    for i in range(NCORES):
        sl = slice(i * NS, (i + 1) * NS)
        m = dict(shared)
        m["xp"] = x_prompt[i]
        m["xs"] = x_sample[sl].reshape(NS * TS, D)
        m["c17"] = np.concatenate([c_prompt[i:i + 1], c_sample[sl]], axis=0)
        m["st_ssm"] = np.ascontiguousarray(state_ssm[:, sl])
        m["st_sconv"] = np.ascontiguousarray(state_ssd_conv[:, sl])
        m["st_mc"] = np.ascontiguousarray(state_mlstm_c[:, sl])
        m["st_mn"] = np.ascontiguousarray(state_mlstm_n[:, sl])
        m["st_mm"] = np.ascontiguousarray(state_mlstm_m[:, sl])
        m["st_lh"] = np.ascontiguousarray(state_lru_h[:, sl])
        m["st_lconv"] = np.ascontiguousarray(state_lru_conv[:, sl])
        in_maps.append(m)
    res = run_bass_kernel_spmd(nc, in_maps, core_ids=list(range(NCORES)))
    R = res.results
    y_prompt = np.stack([R[i]["y_p"] for i in range(NCORES)], axis=0)
    y_sample = np.concatenate([R[i]["y_s"].reshape(NS, TS, D) for i in range(NCORES)], axis=0)
    outs = [y_prompt, y_sample]
    names = ["o_ssm", "o_sconv", "o_mc", "o_mn", "o_mm", "o_lh", "o_lconv"]
    for nm in names:
        outs.append(np.concatenate([R[i][nm][:, 0:1] for i in range(NCORES)], axis=1))
    for nm in names:
        outs.append(np.concatenate([R[i][nm][:, 1:] for i in range(NCORES)], axis=1))
    return tuple(np.ascontiguousarray(o, dtype=np.float32) for o in outs)
```
